# Optimizing a Trainium2 kernel written in Bass

```python
import math
import jax
import jax.numpy as jnp
from jax import lax
import numpy as np

D_MODEL = 1024
BATCH = 2
SEQ = 8192
DEPTH = 2

GRID_W = 64
CTX_LEN = 256
HEAD_DIM = 64
ROPE_THETA = 10000.0
EPS = 1e-6
N_SUB = 3
N_MOD = 3 * N_SUB
D_FF = 2816
S5_WIDTH = D_MODEL // 2
S5_GROUP = 16
S5_GROUPS = S5_WIDTH // S5_GROUP
S5_STATE = 64
S5_MIN_STEP = 1e-3
S5_MAX_STEP = 1e-1
GA_HEADS = (D_MODEL // 2) // HEAD_DIM
GA_KV = GA_HEADS // 4
AB_IN = S5_WIDTH + (GA_HEADS + 2 * GA_KV) * HEAD_DIM
Q_BLOCK = 128
WIN_HEADS = D_MODEL // HEAD_DIM
WIN_KV = WIN_HEADS // 4
WINDOW = 128
WIN_BLOCK = 128
WIN_IN = (WIN_HEADS + 2 * WIN_KV) * HEAD_DIM
N_EVEN = (DEPTH + 1) // 2
N_ODD = DEPTH // 2

kernel_name = 'hybrid_s5_gqa_swa_macaron_dit'


def rms_norm(x, g):
    xf = x.astype(jnp.float32)
    y = xf * lax.rsqrt(jnp.mean(xf * xf, axis=-1, keepdims=True) + EPS)
    return (y * g.astype(jnp.float32)).astype(x.dtype)


def ada_in(h, gain, m, s):
    return rms_norm(h, gain) * (1 + m[:, :, 3 * s + 1]) + m[:, :, 3 * s]


def ada_gate(m, s):
    return m[:, :, 3 * s + 2]


def swiglu(h, wg, wu, wd):
    return (jax.nn.silu(h @ wg) * (h @ wu)) @ wd


def axial_rope(rows):
    r = jnp.repeat(jnp.arange(rows, dtype=jnp.float32), GRID_W)
    col = jnp.tile(jnp.arange(GRID_W, dtype=jnp.float32), rows)
    n_freq = HEAD_DIM // 4
    inv = ROPE_THETA ** (-jnp.arange(n_freq, dtype=jnp.float32) / n_freq)
    ang = jnp.concatenate([r[:, None] * inv, col[:, None] * inv], axis=-1)
    return jnp.cos(ang), jnp.sin(ang)


def apply_rope(x, cos, sin):
    half = HEAD_DIM // 2
    cs = cos[None, :, None, :].astype(x.dtype)
    sn = sin[None, :, None, :].astype(x.dtype)
    x1, x2 = x[..., :half], x[..., half:]
    return jnp.concatenate([x1 * cs - x2 * sn, x1 * sn + x2 * cs], axis=-1)


def split_heads(t, n):
    return t.reshape(t.shape[0], t.shape[1], n, HEAD_DIM)


def gqa_scores(q5, k):
    s = jnp.einsum('bqkgd,bskd->bkgqs', q5, k, preferred_element_type=jnp.float32)
    return s * (HEAD_DIM ** -0.5)


def gqa_values(p, v):
    return jnp.einsum('bkgqs,bskd->bqkgd', p.astype(v.dtype), v)


def sink_softmax(s, sink):
    sk = jnp.broadcast_to(sink.astype(jnp.float32)[:, :, None, None], s.shape[:-1] + (1,))
    p = jax.nn.softmax(jnp.concatenate([s, sk], axis=-1), axis=-1)
    return p[..., :-1]


def attend_full(q, k, v, sink=None):
    b, nq, h, d = q.shape
    hkv = k.shape[2]
    s = gqa_scores(q.reshape(b, nq, hkv, h // hkv, d), k)
    p = jax.nn.softmax(s, axis=-1) if sink is None else sink_softmax(s, sink)
    return gqa_values(p, v).reshape(b, nq, h * d)


def dense_gqa(q, k, v, kc, vc):
    b, n, h, d = q.shape
    hkv = k.shape[2]
    nb = n // Q_BLOCK
    k_all = jnp.concatenate([kc, k], axis=1)
    v_all = jnp.concatenate([vc, v], axis=1)
    qb = q.reshape(b, nb, Q_BLOCK, hkv, h // hkv, d).transpose(1, 0, 2, 3, 4, 5)

    def block(qi):
        p = jax.nn.softmax(gqa_scores(qi, k_all), axis=-1)
        return gqa_values(p, v_all)

    o = lax.map(block, qb)
    return o.transpose(1, 0, 2, 3, 4, 5).reshape(b, n, h * d)


def window_gqa(q, k, v, kc, vc, sink):
    b, n, h, d = q.shape
    hkv = k.shape[2]
    nb = n // WIN_BLOCK
    span = 3 * WIN_BLOCK

    def bands(t):
        tp = jnp.pad(t, ((0, 0), (WIN_BLOCK, WIN_BLOCK), (0, 0), (0, 0)))
        parts = [tp[:, o * WIN_BLOCK:o * WIN_BLOCK + n].reshape(b, nb, WIN_BLOCK, hkv, d) for o in range(3)]
        return jnp.concatenate(parts, axis=2).transpose(1, 0, 2, 3, 4)

    kb, vb = bands(k), bands(v)
    qb = q.reshape(b, nb, WIN_BLOCK, hkv, h // hkv, d).transpose(1, 0, 2, 3, 4, 5)
    qpos = jnp.arange(nb)[:, None, None] * WIN_BLOCK + jnp.arange(WIN_BLOCK)[None, :, None]
    kpos = jnp.arange(nb)[:, None, None] * WIN_BLOCK - WIN_BLOCK + jnp.arange(span)[None, None, :]
    valid = (kpos >= 0) & (kpos < n) & (jnp.abs(kpos - qpos) <= WINDOW)

    def block(args):
        qi, ki, vi, mi = args
        s_loc = jnp.where(mi, gqa_scores(qi, ki), -jnp.inf)
        s_ctx = gqa_scores(qi, kc)
        p = sink_softmax(jnp.concatenate([s_loc, s_ctx], axis=-1), sink)
        return gqa_values(p[..., :span], vi) + gqa_values(p[..., span:], vc)

    o = lax.map(block, (qb, kb, vb, valid))
    return o.transpose(1, 0, 2, 3, 4, 5).reshape(b, n, h * d)


def _lin_rec(e1, e2):
    a1, b1 = e1
    a2, b2 = e2
    return a1 * a2, a2 * b1 + b2


def s5_direction(u_ctx, u_lat, lam_re, lam_im, b_re, b_im, c_re, c_im, log_step, reverse, need_ctx_out):
    lam = lax.complex(jnp.minimum(lam_re.astype(jnp.float32), -1e-4), lam_im.astype(jnp.float32))
    dt = jnp.exp(log_step.astype(jnp.float32))[:, None]
    lam_bar = jnp.exp(lam * dt)
    b_bar = ((lam_bar - 1) / lam)[..., None] * lax.complex(b_re.astype(jnp.float32), b_im.astype(jnp.float32))
    c_mat = lax.complex(c_re.astype(jnp.float32), c_im.astype(jnp.float32))

    def run(u, h0):
        nb_, n = u.shape[0], u.shape[1]
        ug = u.astype(jnp.float32).reshape(nb_, n, S5_GROUPS, S5_GROUP).astype(jnp.complex64)
        bu = jnp.einsum('blgh,gph->blgp', ug, b_bar)
        first, last = (n - 1, 0) if reverse else (0, n - 1)
        if h0 is not None:
            bu = bu.at[:, first].add(lam_bar * h0)
        a = jnp.broadcast_to(lam_bar, bu.shape)
        _, hs = lax.associative_scan(_lin_rec, (a, bu), reverse=reverse, axis=1)
        return hs, hs[:, last]

    def read(hs, like):
        y = jnp.einsum('blgp,ghp->blgh', hs, c_mat).real
        return y.reshape(like.shape).astype(like.dtype)

    h_ctx, h_ctx_final = run(u_ctx, None)
    h_lat, _ = run(u_lat, h_ctx_final)
    y_ctx = read(h_ctx, u_ctx) if need_ctx_out else None
    return read(h_lat, u_lat), y_ctx


def s5_glu(y, glu_w):
    g = jax.nn.gelu(y)
    return (g @ glu_w[0]) * jax.nn.sigmoid(g @ glu_w[1])


def mixer_ab(hx, hc, cos, sin, w_in, w_out, qk_g, lam_re, lam_im, b_re, b_im, c_re, c_im,
             log_step, d_skip, glu_w, need_ctx_out):
    qw, kw = GA_HEADS * HEAD_DIM, GA_KV * HEAD_DIM
    z = hx @ w_in
    u, q, k, v = jnp.split(z, [S5_WIDTH, S5_WIDTH + qw, S5_WIDTH + qw + kw], axis=-1)
    uc = hc @ w_in[:, :S5_WIDTH]
    kc, vc = jnp.split(hc @ w_in[:, S5_WIDTH + qw:], [kw], axis=-1)
    y_lat = d_skip * u
    y_ctx = d_skip * uc if need_ctx_out else None
    for r in range(2):
        yl, yc = s5_direction(uc, u, lam_re[r], lam_im[r], b_re[r], b_im[r], c_re[r], c_im[r],
                              log_step[r], r == 1, need_ctx_out)
        y_lat = y_lat + yl
        if need_ctx_out:
            y_ctx = y_ctx + yc
    qn = apply_rope(rms_norm(split_heads(q, GA_HEADS), qk_g[0]), cos, sin)
    kn = apply_rope(rms_norm(split_heads(k, GA_KV), qk_g[1]), cos, sin)
    kcn = rms_norm(split_heads(kc, GA_KV), qk_g[1])
    vch = split_heads(vc, GA_KV)
    b_lat = dense_gqa(qn, kn, split_heads(v, GA_KV), kcn, vch)
    out_lat = jnp.concatenate([s5_glu(y_lat, glu_w), b_lat], axis=-1) @ w_out
    if not need_ctx_out:
        return out_lat, None
    qc = rms_norm(split_heads(hc @ w_in[:, S5_WIDTH:S5_WIDTH + qw], GA_HEADS), qk_g[0])
    b_ctx = attend_full(qc, kcn, vch)
    out_ctx = jnp.concatenate([s5_glu(y_ctx, glu_w), b_ctx], axis=-1) @ w_out
    return out_lat, out_ctx


def mixer_win(hx, hc, cos, sin, w_in, w_out, qk_g, sink, need_ctx_out):
    qw, kw = WIN_HEADS * HEAD_DIM, WIN_KV * HEAD_DIM
    sink = sink.reshape(WIN_KV, WIN_HEADS // WIN_KV)
    q, k, v = jnp.split(hx @ w_in, [qw, qw + kw], axis=-1)
    kc, vc = jnp.split(hc @ w_in[:, qw:], [kw], axis=-1)
    qn = apply_rope(rms_norm(split_heads(q, WIN_HEADS), qk_g[0]), cos, sin)
    kn = apply_rope(rms_norm(split_heads(k, WIN_KV), qk_g[1]), cos, sin)
    kcn = rms_norm(split_heads(kc, WIN_KV), qk_g[1])
    vch = split_heads(vc, WIN_KV)
    out_lat = window_gqa(qn, kn, split_heads(v, WIN_KV), kcn, vch, sink) @ w_out
    if not need_ctx_out:
        return out_lat, None
    qc = rms_norm(split_heads(hc @ w_in[:, :qw], WIN_HEADS), qk_g[0])
    return out_lat, attend_full(qc, kcn, vch, sink) @ w_out


def setup_inputs(seed: int = 0) -> dict:
    key = jax.random.key(seed)
    ks = iter(jax.random.split(key, 32))
    f32 = jnp.float32

    def nrm(shape, scale):
        return scale * jax.random.normal(next(ks), shape, f32)

    d = D_MODEL
    g, p, hh = S5_GROUPS, S5_STATE, S5_GROUP
    return {
        'x': nrm((BATCH, SEQ, d), 1.0),
        'c': nrm((BATCH, d), 1.0),
        'ctx': nrm((BATCH, CTX_LEN, d), 1.0),
        'c_ctx': nrm((d,), 1.0),
        'mod_w': nrm((DEPTH, d, N_MOD * d), 0.5 * d ** -0.5),
        'mod_b': nrm((DEPTH, N_MOD * d), 0.01),
        'norm_g': 1.0 + nrm((DEPTH, N_SUB, d), 0.02),
        'ffn_wg': nrm((DEPTH, 2, d, D_FF), d ** -0.5),
        'ffn_wu': nrm((DEPTH, 2, d, D_FF), d ** -0.5),
        'ffn_wd': nrm((DEPTH, 2, D_FF, d), D_FF ** -0.5),
        'w_out': nrm((DEPTH, d, d), d ** -0.5),
        'qk_norm': 1.0 + nrm((DEPTH, 2, HEAD_DIM), 0.02),
        'ab_w_in': nrm((N_EVEN, d, AB_IN), d ** -0.5),
        's5_lam_re': -0.5 + nrm((N_EVEN, 2, g, p), 0.01),
        's5_lam_im': jnp.pi * jnp.arange(p, dtype=f32) + nrm((N_EVEN, 2, g, p), 0.01),
        's5_b_re': nrm((N_EVEN, 2, g, p, hh), (2 * hh) ** -0.5),
        's5_b_im': nrm((N_EVEN, 2, g, p, hh), (2 * hh) ** -0.5),
        's5_c_re': nrm((N_EVEN, 2, g, hh, p), p ** -0.5),
        's5_c_im': nrm((N_EVEN, 2, g, hh, p), p ** -0.5),
        's5_log_step': jax.random.uniform(next(ks), (N_EVEN, 2, g), f32,
                                          math.log(S5_MIN_STEP), math.log(S5_MAX_STEP)),
        's5_d': nrm((N_EVEN, S5_WIDTH), 1.0),
        's5_glu_w': nrm((N_EVEN, 2, S5_WIDTH, S5_WIDTH), S5_WIDTH ** -0.5),
        'win_w_in': nrm((N_ODD, d, WIN_IN), d ** -0.5),
        'win_sink': nrm((N_ODD, WIN_HEADS), 0.5),
    }


def reference(x, c, ctx, c_ctx, mod_w, mod_b, norm_g, ffn_wg, ffn_wu, ffn_wd, w_out, qk_norm,
              ab_w_in, s5_lam_re, s5_lam_im, s5_b_re, s5_b_im, s5_c_re, s5_c_im, s5_log_step,
              s5_d, s5_glu_w, win_w_in, win_sink):
    batch, n_tok, d = x.shape
    rows = n_tok // GRID_W
    cos, sin = axial_rope(rows)
    xc = ctx
    for l in range(DEPTH):
        last = l == DEPTH - 1
        mx = (jax.nn.silu(c) @ mod_w[l] + mod_b[l]).reshape(batch, 1, N_MOD, d)
        mc = (jax.nn.silu(c_ctx) @ mod_w[l] + mod_b[l]).reshape(1, 1, N_MOD, d)
        x = x + 0.5 * ada_gate(mx, 0) * swiglu(ada_in(x, norm_g[l, 0], mx, 0), ffn_wg[l, 0], ffn_wu[l, 0], ffn_wd[l, 0])
        xc = xc + 0.5 * ada_gate(mc, 0) * swiglu(ada_in(xc, norm_g[l, 0], mc, 0), ffn_wg[l, 0], ffn_wu[l, 0], ffn_wd[l, 0])
        hx = ada_in(x, norm_g[l, 1], mx, 1)
        hc = ada_in(xc, norm_g[l, 1], mc, 1)
        if l % 2 == 0:
            e = l // 2
            ox, oc = mixer_ab(hx, hc, cos, sin, ab_w_in[e], w_out[l], qk_norm[l], s5_lam_re[e], s5_lam_im[e],
                              s5_b_re[e], s5_b_im[e], s5_c_re[e], s5_c_im[e], s5_log_step[e], s5_d[e],
                              s5_glu_w[e], not last)
        else:
            o = l // 2
            ox, oc = mixer_win(hx, hc, cos, sin, win_w_in[o], w_out[l], qk_norm[l], win_sink[o], not last)
        x = x + ada_gate(mx, 1) * ox
        x = x + 0.5 * ada_gate(mx, 2) * swiglu(ada_in(x, norm_g[l, 2], mx, 2), ffn_wg[l, 1], ffn_wu[l, 1], ffn_wd[l, 1])
        if not last:
            xc = xc + ada_gate(mc, 1) * oc
            xc = xc + 0.5 * ada_gate(mc, 2) * swiglu(ada_in(xc, norm_g[l, 2], mc, 2), ffn_wg[l, 1], ffn_wu[l, 1], ffn_wd[l, 1])
    return x
```

```python
import contextlib
import math
import numpy as np
import ml_dtypes
import concourse.bass as bass
import concourse.mybir as mybir
from concourse.bass_utils import run_bass_kernel_spmd

F32 = mybir.dt.float32
BF16 = mybir.dt.bfloat16
I32 = mybir.dt.int32
ALU = mybir.AluOpType
AF = mybir.ActivationFunctionType
AX = mybir.AxisListType
NPBF = ml_dtypes.bfloat16

NCORES = 8
D = 1024
KC = 8
DFF = 2816
FC = 22
TCTX = 256
TLAT = 2048
T = TCTX + TLAT
TILES = [(0, 256), (256, 512), (768, 512), (1280, 512), (1792, 512)]
EPS = 1e-6
GFF = 4


class _Op:
    __slots__ = ("eng", "pos", "fn", "cwaits", "dwaits", "signal", "dma_key", "dma_k")


class Prog:
    ENGS = ("tensor", "vector", "scalar", "gpsimd", "sync")
    CENG = ("tensor", "vector", "scalar", "gpsimd")

    def __init__(self, nc):
        self.nc = nc
        self.ges = contextlib.ExitStack()
        self.pes = contextlib.ExitStack()
        self.esem = {e: self.ges.enter_context(nc.semaphore(f"s_{e}")) for e in self.CENG}
        self.esig = {e: 0 for e in self.CENG}
        self.dsem = {}
        self.dma_cnt = {}
        self.dma_inc = {}
        self.n_sb = 0
        self.n_phase = 0
        self._reset()

    def _reset(self):
        self.ops = {e: [] for e in self.ENGS}
        self.last_w = {}
        self.readers = {}

    def sb(self, shape, dtype=F32, name=None, persist=False):
        self.n_sb += 1
        nm = (name or "sb") + f"_{self.n_sb}"
        es = self.ges if persist else self.pes
        return es.enter_context(self.nc.sbuf_tensor(nm, list(shape), dtype))

    def ps(self, shape, dtype=F32, name=None):
        self.n_sb += 1
        return self.ges.enter_context(self.nc.psum_tensor(name or f"ps{self.n_sb}", list(shape), dtype))

    def add(self, eng, fn, reads=(), writes=(), dma_key=None, inc=16):
        op = _Op()
        op.eng, op.fn, op.signal = eng, fn, False
        op.pos = len(self.ops[eng])
        op.dma_key = dma_key
        op.dma_k = None
        deps = {}
        for k in reads:
            d = self.last_w.get(k)
            if d is not None:
                deps[id(d)] = d
        for k in writes:
            d = self.last_w.get(k)
            if d is not None:
                deps[id(d)] = d
            for r in self.readers.get(k, ()):
                deps[id(r)] = r
        cw = {}
        dw = {}
        for i, d in deps.items():
            if d.dma_key is not None:
                v = self.dma_inc[d.dma_key] * (d.dma_k + 1)
                dw[d.dma_key] = max(dw.get(d.dma_key, 0), v)
            elif d.eng == eng:
                if eng != "tensor":
                    cw[d.eng] = max(cw.get(d.eng, -1), d.pos)
            else:
                cw[d.eng] = max(cw.get(d.eng, -1), d.pos)
        if dma_key is not None:
            self.dma_inc.setdefault(dma_key, inc)
            k = self.dma_cnt.get(dma_key, 0)
            op.dma_k = k
            self.dma_cnt[dma_key] = k + 1
            if k > 0:
                dw[dma_key] = max(dw.get(dma_key, 0), self.dma_inc[dma_key] * k)
        op.cwaits = []
        for e, p in cw.items():
            d = self.ops[e][p]
            d.signal = True
            op.cwaits.append(d)
        op.dwaits = list(dw.items())
        self.ops[eng].append(op)
        for k in reads:
            self.readers.setdefault(k, []).append(op)
        for k in writes:
            self.last_w[k] = op
            self.readers[k] = []
        return op

    def dma(self, out, in_, reads, writes, key, eng="sync", **kw):
        return self.add(eng, lambda e: e.dma_start(out=out, in_=in_, **kw), reads, writes, dma_key=key)

    def coll(self, kind, op, groups, in_ap, out_ap, reads, writes, key):
        self.n_coll = getattr(self, "n_coll", 0) + 1
        key = f"{key}_{self.n_coll}"
        return self.add("gpsimd", lambda e: e.collective_compute(kind, op, replica_groups=groups, ins=[in_ap], outs=[out_ap]),
                        reads, writes, dma_key=key, inc=1)

    def end_phase(self):
        nc = self.nc
        lasts = []
        for e in self.CENG:
            real = [o for o in self.ops[e] if o.dma_key is None and o.fn is not None]
            if real:
                lasts.append(real[-1])
        for d in lasts:
            d.signal = True
        for e in self.ENGS:
            op = _Op()
            op.eng, op.fn, op.signal, op.dma_key, op.dma_k = e, None, False, None, None
            op.pos = len(self.ops[e])
            op.cwaits = [d for d in lasts if d.eng != e]
            op.dwaits = [(k, self.dma_inc[k] * c) for k, c in self.dma_cnt.items()]
            self.ops[e].append(op)
        sig = {}
        for e in self.CENG:
            c = self.esig[e]
            for op in self.ops[e]:
                if op.signal:
                    c += 1
                    sig[id(op)] = c
            self.esig[e] = c
        for k in self.dma_cnt:
            if k not in self.dsem:
                self.dsem[k] = self.ges.enter_context(nc.semaphore(f"d_{len(self.dsem)}"))
        esem, dsem = self.esem, self.dsem
        with nc.Block() as block:
            def mk(ename):
                ops = self.ops[ename]

                def body(e):
                    for op in ops:
                        for d in op.cwaits:
                            e.wait_ge(esem[d.eng], sig[id(d)])
                        for k, v in op.dwaits:
                            e.wait_ge(dsem[k], v)
                        if op.fn is None:
                            continue
                        ins = op.fn(e)
                        if op.dma_key is not None:
                            ins.then_inc(dsem[op.dma_key], self.dma_inc[op.dma_key])
                        elif op.signal:
                            ins.then_inc(esem[ename], 1)
                return body

            for ename in self.ENGS:
                getattr(block, ename)(mk(ename))
        self.pes.close()
        self.pes = contextlib.ExitStack()
        self._reset()
        self.n_phase += 1

    def close(self):
        self.ges.close()


class Ctx:
    def __init__(self, P):
        self.P = P
        self.psum = [P.ps([128, 512], F32, name=f"psb{i}") for i in range(8)]
        self.ld_i = 0
        self.new_phase()

    def new_phase(self):
        P = self.P
        self.stage = [P.sb([128, 1024], F32, name=f"stage{i}") for i in range(3)]
        self.stage_i = 0

    def load_cast(self, dst16_ap, dst_key, src_ap, n, cast_eng="gpsimd", view=None, psl=None):
        P = self.P
        i = self.stage_i
        self.stage_i = (i + 1) % len(self.stage)
        st = self.stage[i]
        sk = f"stage{i}"
        sv = st[:, 0:n] if psl is None else st[psl, 0:n]
        P.dma(sv if view is None else view(sv), src_ap, [], [sk], f"ld_stage{i}")
        P.add(cast_eng, lambda e: e.tensor_copy(dst16_ap, sv if view is None else view(sv)), [sk], [dst_key])

    def load(self, dst_ap, dst_key, src_ap, reads=()):
        P = self.P
        self.ld_i += 1
        P.dma(dst_ap, src_ap, list(reads), [dst_key], f"ld_misc{self.ld_i % 4}")


def make_cols(P, modall, ng, layer, tiles):
    out = {}
    for vi, v in enumerate(("x", "c")):
        gs, gt = tiles[(layer, v)]
        for s in range(3):
            sc = modall[:, layer, (3 * s + 1) * 8:(3 * s + 2) * 8, vi]
            P.add("vector", lambda e, s=s, sc=sc, gs=gs: e.scalar_tensor_tensor(gs[:, s * 8:(s + 1) * 8], sc, 1.0, ng[:, layer, s * 8:(s + 1) * 8], ALU.add, ALU.mult),
                  ["modall", "normg"], [f"gs_{v}"])
            g = modall[:, layer, (3 * s + 2) * 8:(3 * s + 3) * 8, vi]
            fac = 1.0 if s == 1 else 0.5
            P.add("vector", lambda e, s=s, g=g, gt=gt, fac=fac: e.tensor_scalar(gt[:, s * 8:(s + 1) * 8], g, fac, None, ALU.mult),
                  ["modall"], [f"gate_{v}"])
        out[v] = dict(gs=gs, gate=gt, mod=modall, layer=layer, vi=vi)
    return out


def colsel(cols, v, kind, s, c):
    if kind == "gs":
        return cols[v]["gs"][:, s * 8 + c:s * 8 + c + 1]
    if kind == "gate":
        return cols[v]["gate"][:, s * 8 + c:s * 8 + c + 1]
    if kind == "shift":
        j = (3 * s) * 8 + c
        return cols[v]["mod"][:, cols[v]["layer"], j:j + 1, cols[v]["vi"]]
    raise ValueError(kind)


def emit_norm(P, C, K, xT, hT, cols, s, tiles=TILES):
    for ti, (t0, tn) in enumerate(tiles):
        v = "c" if t0 < TCTX else "x"
        pss = C.psum[6 + (ti % 2)]
        psk = f"psb{6 + (ti % 2)}"
        for c in range(KC):
            sq = K["sq"][c % 2]
            sqk = f"sq{c % 2}"
            P.add("scalar", lambda e, sq=sq, c=c, t0=t0, tn=tn: e.activation(sq[:, 0:tn], xT[:, c, t0:t0 + tn], AF.Square),
                  [f"x{c}_{ti}"], [sqk])
            P.add("tensor", lambda e, sq=sq, c=c, pss=pss, tn=tn: e.matmul(pss[:, 0:tn], K["ones"][:], sq[:, 0:tn], start=(c == 0), stop=(c == KC - 1)),
                  [sqk, "ones"], [psk])
        rs = K["rstd"][ti % 2]
        rsk = f"rstd{ti % 2}"
        P.add("scalar", lambda e, rs=rs, pss=pss, tn=tn: e.activation(rs[:, 0:tn], pss[:, 0:tn], AF.Sqrt, bias=K["epscol"][:, 0:1], scale=1.0 / D),
              [psk, "epscol"], [rsk])
        P.add("vector", lambda e, rs=rs, tn=tn: e.reciprocal(rs[:, 0:tn], rs[:, 0:tn]), [rsk], [rsk])
        for c in range(KC):
            tmp = K["ntmp"][c % 2]
            tk = f"ntmp{c % 2}"
            P.add("vector", lambda e, tmp=tmp, c=c, t0=t0, tn=tn, rs=rs: e.tensor_tensor(tmp[:, 0:tn], xT[:, c, t0:t0 + tn], rs[:, 0:tn], ALU.mult),
                  [f"x{c}_{ti}", rsk], [tk])
            P.add("scalar", lambda e, tmp=tmp, c=c, t0=t0, tn=tn, v=v: e.activation(hT[:, c, t0:t0 + tn], tmp[:, 0:tn], AF.Identity,
                                                                                    bias=colsel(cols, v, "shift", s, c), scale=colsel(cols, v, "gs", s, c)),
                  [tk, f"gs_{v}", "modall"], [f"h{c}_{ti}"])


def emit_ffn(P, C, K, xT, hT, cols, s, wg_d, wu_d, wd_d, tiles=TILES):
    act = K["act"]
    wg16, wu16, wd16 = K["wg16"], K["wu16"], K["wd16"]

    def load_chunk(j):
        sl = j % 2
        C.load_cast(wg16[sl][:].rearrange("p k m -> p (k m)"), f"wg16_{sl}", wg_d[j].rearrange("p k m -> p (k m)"), 1024)
        C.load_cast(wu16[sl][:].rearrange("p k m -> p (k m)"), f"wu16_{sl}", wu_d[j].rearrange("p k m -> p (k m)"), 1024)
        C.load_cast(wd16[j % (2 * GFF)][:], f"wd16_{j % (2 * GFF)}", wd_d[j], 1024)

    groups = [list(range(g, min(g + GFF, FC))) for g in range(0, FC, GFF)]
    load_chunk(0)
    pi = 0
    for grp in groups:
        for j in grp:
            if j + 1 < FC:
                load_chunk(j + 1)
            sl = j % 2
            jj = j % GFF
            for ti, (t0, tn) in enumerate(tiles):
                gb = pi % 2
                pi += 1
                pg, pu = C.psum[gb], C.psum[2 + gb]
                for k in range(KC):
                    P.add("tensor", lambda e, pg=pg, k=k, sl=sl, t0=t0, tn=tn: e.matmul(pg[:, 0:tn], wg16[sl][:, k, :], hT[:, k, t0:t0 + tn], start=(k == 0), stop=(k == KC - 1)),
                          [f"wg16_{sl}", f"h{k}_{ti}"], [f"psb{gb}"])
                for k in range(KC):
                    P.add("tensor", lambda e, pu=pu, k=k, sl=sl, t0=t0, tn=tn: e.matmul(pu[:, 0:tn], wu16[sl][:, k, :], hT[:, k, t0:t0 + tn], start=(k == 0), stop=(k == KC - 1)),
                          [f"wu16_{sl}", f"h{k}_{ti}"], [f"psb{2 + gb}"])
                sg = K["sg"][gb]
                P.add("scalar", lambda e, sg=sg, pg=pg, tn=tn: e.activation(sg[:, 0:tn], pg[:, 0:tn], AF.Silu), [f"psb{gb}"], [f"sg{gb}"])
                P.add("vector", lambda e, sg=sg, pu=pu, jj=jj, t0=t0, tn=tn: e.tensor_tensor(act[:, jj, t0:t0 + tn], sg[:, 0:tn], pu[:, 0:tn], ALU.mult),
                      [f"sg{gb}", f"psb{2 + gb}"], [f"act{jj}_{ti}"])
        for ti, (t0, tn) in enumerate(tiles):
            v = "c" if t0 < TCTX else "x"
            for dc in range(KC):
                db = 4 + (dc % 2)
                pd = C.psum[db]
                for n, j in enumerate(grp):
                    jj = j % GFF
                    jw = j % (2 * GFF)
                    P.add("tensor", lambda e, pd=pd, jj=jj, jw=jw, dc=dc, t0=t0, tn=tn, n=n, ng=len(grp): e.matmul(pd[:, 0:tn], wd16[jw][:, dc * 128:(dc + 1) * 128], act[:, jj, t0:t0 + tn],
                                                                                                  start=(n == 0), stop=(n == ng - 1)),
                          [f"wd16_{jw}", f"act{jj}_{ti}"], [f"psb{db}"])
                P.add("vector", lambda e, pd=pd, dc=dc, t0=t0, tn=tn, v=v: e.scalar_tensor_tensor(xT[:, dc, t0:t0 + tn], pd[:, 0:tn], colsel(cols, v, "gate", s, dc),
                                                                                                 xT[:, dc, t0:t0 + tn], ALU.mult, ALU.add),
                      [f"psb{db}", f"gate_{v}", f"x{dc}_{ti}"], [f"x{dc}_{ti}"])


def alloc_common(P, C, with_ffn=True):
    K = {}
    K["ones"] = P.sb([128, 128], BF16, name="ones")
    P.add("vector", lambda e: e.memset(K["ones"][:], 1.0), [], ["ones"])
    K["epscol"] = P.sb([128, 1], F32, name="epscol")
    P.add("vector", lambda e: e.memset(K["epscol"][:], EPS), [], ["epscol"])
    K["sq"] = [P.sb([128, 512], BF16, name=f"sq{i}") for i in range(2)]
    K["rstd"] = [P.sb([128, 512], F32, name=f"rstd{i}") for i in range(2)]
    K["ntmp"] = [P.sb([128, 512], F32, name=f"ntmp{i}") for i in range(2)]
    if with_ffn:
        K["act"] = P.sb([128, GFF, T], BF16, name="act")
        K["wg16"] = [P.sb([128, 8, 128], BF16, name=f"wg16_{i}") for i in range(2)]
        K["wu16"] = [P.sb([128, 8, 128], BF16, name=f"wu16_{i}") for i in range(2)]
        K["wd16"] = [P.sb([128, 1024], BF16, name=f"wd16_{i}") for i in range(2 * GFF)]
        K["sg"] = [P.sb([128, 512], F32, name=f"sg{i}") for i in range(2)]
    return K


def emit_consts(P, C, K, rot_d, bones_d):
    K["rot"] = P.sb([128, 128], BF16, name="rot_sb")
    K["bones"] = P.sb([128, 128], BF16, name="bones_sb")
    C.load_cast(K["rot"][:], "rot", rot_d, 128)
    C.load_cast(K["bones"][:], "bones", bones_d, 128)


def emit_inproj(P, C, K, hT, win_d, n_fm, kinds, qkg, cosT, sinT, outs, v_d, vw, v_oc0, tiles=TILES):
    w16 = K["win16"]
    zq = K["zq"]
    for oc in range(n_fm):
        sl = oc % 2
        C.load_cast(w16[sl][:].rearrange("p k m -> p (k m)"), f"win16_{sl}", win_d[oc].rearrange("p k m -> p (k m)"), 1024)
        kind = kinds[oc]
        od, odt = outs[oc]
        for ti, (t0, tn) in enumerate(tiles):
            pb = ti % 2
            pz = C.psum[pb]
            for k in range(KC):
                P.add("tensor", lambda e, pz=pz, k=k, sl=sl, t0=t0, tn=tn: e.matmul(pz[:, 0:tn], w16[sl][:, k, :], hT[:, k, t0:t0 + tn], start=(k == 0), stop=(k == KC - 1)),
                      [f"win16_{sl}", f"h{k}_{ti}"], [f"psb{pb}"])
            ob = K["ob32"][pb] if odt == F32 else K["ob16"][pb]
            obk = f"ob{'32' if odt == F32 else '16'}_{pb}"
            if kind == "u":
                P.add("scalar", lambda e, ob=ob, pz=pz, tn=tn: e.copy(ob[:, 0:tn], pz[:, 0:tn]), [f"psb{pb}"], [obk])
            else:
                gcol = qkg[:, 0:1] if kind == "q" else qkg[:, 1:2]
                z = zq[pb]
                zk = f"zq{pb}"
                sq = K["sq"][pb]
                sqk = f"sq{pb}"
                P.add("scalar", lambda e, z=z, pz=pz, tn=tn: e.copy(z[:, 0:tn], pz[:, 0:tn]), [f"psb{pb}"], [zk])
                P.add("scalar", lambda e, sq=sq, z=z, tn=tn: e.activation(sq[:, 0:tn], z[:, 0:tn], AF.Square), [zk], [sqk])
                ph = C.psum[2 + pb]
                P.add("tensor", lambda e, ph=ph, sq=sq, tn=tn: e.matmul(ph[:, 0:tn], K["bones"][:], sq[:, 0:tn], start=True, stop=True), [sqk, "bones"], [f"psb{2 + pb}"])
                rs = K["rstd"][pb]
                rsk = f"rstd{pb}"
                P.add("scalar", lambda e, rs=rs, ph=ph, tn=tn: e.activation(rs[:, 0:tn], ph[:, 0:tn], AF.Sqrt, bias=K["epscol"][:, 0:1], scale=1.0 / 64), [f"psb{2 + pb}", "epscol"], [rsk])
                P.add("vector", lambda e, rs=rs, tn=tn: e.reciprocal(rs[:, 0:tn], rs[:, 0:tn]), [rsk], [rsk])
                P.add("vector", lambda e, z=z, rs=rs, tn=tn, gcol=gcol: e.scalar_tensor_tensor(z[:, 0:tn], z[:, 0:tn], gcol, rs[:, 0:tn], ALU.mult, ALU.mult), [zk, rsk, "qkg"], [zk])
                zb = K["zb"][pb]
                zbk = f"zb{pb}"
                P.add("scalar", lambda e, zb=zb, z=z, tn=tn: e.copy(zb[:, 0:tn], z[:, 0:tn]), [zk], [zbk])
                pr = C.psum[4 + pb]
                P.add("tensor", lambda e, pr=pr, zb=zb, tn=tn: e.matmul(pr[:, 0:tn], K["rot"][:], zb[:, 0:tn], start=True, stop=True), [zbk, "rot"], [f"psb{4 + pb}"])
                t2 = K["ntmp"][pb]
                t2k = f"ntmp{pb}"
                cb_ = pb % len(K["cst"])
                cst, snt = K["cst"][cb_], K["snt"][cb_]
                C.load(cst[:, 0:tn], f"cst{cb_}", cosT[:, t0:t0 + tn])
                C.load(snt[:, 0:tn], f"snt{cb_}", sinT[:, t0:t0 + tn])
                P.add("vector", lambda e, t2=t2, pr=pr, snt=snt, tn=tn: e.tensor_tensor(t2[:, 0:tn], pr[:, 0:tn], snt[:, 0:tn], ALU.mult), [f"psb{4 + pb}", f"snt{cb_}"], [t2k])
                P.add("vector", lambda e, z=z, cst=cst, tn=tn: e.tensor_tensor(z[:, 0:tn], z[:, 0:tn], cst[:, 0:tn], ALU.mult), [zk, f"cst{cb_}"], [zk])
                P.add("vector", lambda e, ob=ob, z=z, t2=t2, tn=tn: e.tensor_tensor(ob[:, 0:tn], z[:, 0:tn], t2[:, 0:tn], ALU.add), [zk, t2k], [obk])
            P.dma(od(t0, tn) if callable(od) else od[:, t0:t0 + tn], ob[:, 0:tn], [obk], [], f"st_{obk}")
    nvc = vw // 128
    wv = K["wv16"]
    for i in range(nvc):
        C.load_cast(wv[:, :, i * 128:(i + 1) * 128], f"wv16_{i}", win_d[v_oc0 + i], 1024, view=lambda a: a.rearrange("p (k m) -> p k m", m=128))
    for tt in range(T // 128):
        pb = tt % 2
        pv = C.psum[6 + pb]
        ti = 0 if tt < 2 else 1 + (tt - 2) // 4
        for k in range(KC):
            P.add("tensor", lambda e, pv=pv, k=k, tt=tt: e.matmul(pv[:, 0:vw], hT[:, k, tt * 128:(tt + 1) * 128], wv[:, k, :], start=(k == 0), stop=(k == KC - 1)),
                  [f"wv16_{i}" for i in range(nvc)] + [f"h{k}_{ti}"], [f"psb{6 + pb}"])
        vb = K["vb"][pb]
        P.add("scalar", lambda e, vb=vb, pv=pv: e.copy(vb[:, 0:vw], pv[:, 0:vw]), [f"psb{6 + pb}"], ["vb0"])
        P.dma(v_d[tt], vb[:, 0:vw], ["vb0"], [], "st_vb0")


def phase_att(P, C, layer, Dr):
    dense = layer == 0
    n_qc, n_kv = (4, 2) if dense else (8, 4)
    n_heads = 2 * n_qc
    NK = 66 if dense else 26
    qT = P.sb([128, n_qc, T], BF16, name="q_sb")
    kd = P.sb([128, n_kv, NK * 128], BF16, name="k_sb")
    va = P.sb([128, n_kv, NK, 65], BF16, name="v_sb")
    q_loc, o_loc = Dr["q_loc"], Dr["o_loc"]
    for c in range(n_qc):
        C.load(qT[:, c, :], f"q{c}", q_loc[:, c, :])
    for kv in range(n_kv):
        P.add("vector", lambda e, kv=kv: e.memset(va[:, kv, :, 64:65], 1.0), [], [f"v{kv}"])
    tkv = lambda s_: s_.rearrange("p (kt d) -> p kt d", d=64)
    tk = lambda ap: ap.rearrange("(kt p) d -> p kt d", p=128)

    def kload(kv, half, dst0, src_rows, src_cols0, ncols):
        pl = slice(64 * half, 64 * half + 64)
        for c0 in range(0, ncols, 1024):
            n = min(1024, ncols - c0)
            C.load_cast(kd[pl, kv, dst0 + c0:dst0 + c0 + n], f"k{kv}", src_rows[:, src_cols0 + c0:src_cols0 + c0 + n], n, psl=pl)

    def vload(kv, kt0, nkt, src):
        for t0 in range(0, nkt, 16):
            n = min(16, nkt - t0)
            C.load_cast(va[:, kv, kt0 + t0:kt0 + t0 + n, 0:64], f"v{kv}", tk(src[t0 * 128:(t0 + n) * 128, 64 * kv:64 * kv + 64]), n * 64, view=tkv)

    if dense:
        kA_all, kB_all, vA_all, vB_all = Dr["kA_all"], Dr["kB_all"], Dr["vA_all"], Dr["vB_all"]
        for kv in range(n_kv):
            for half in range(2):
                kload(kv, half, 0, kA_all[64 * kv:64 * kv + 64], 0, TCTX)
                for r in range(4):
                    kload(kv, half, TCTX + TLAT * r, kA_all[r * 128 + 64 * kv:r * 128 + 64 * kv + 64], TCTX, 1024)
                    kload(kv, half, TCTX + TLAT * r + 1024, kB_all[r * 128 + 64 * kv:r * 128 + 64 * kv + 64], 0, 1024)
            vload(kv, 0, 2, vA_all[0:TCTX])
            for r in range(4):
                vload(kv, 2 + 16 * r, 8, vA_all[r * 1280 + TCTX:(r + 1) * 1280])
                vload(kv, 2 + 16 * r + 8, 8, vB_all[r * 1024:(r + 1) * 1024])
    else:
        k_loc, v_loc, ek_all, ev_all = Dr["k_loc"], Dr["v_loc"], Dr["ek_all"], Dr["ev_all"]
        for kv in range(n_kv):
            ks = slice(64 * (kv % 2), 64 * (kv % 2) + 64)
            kc = kv // 2
            for half in range(2):
                kload(kv, half, 0, k_loc[ks, kc], TCTX, TLAT)
                kload(kv, half, 24 * 128, k_loc[ks, kc], 0, TCTX)
                for r in range(4):
                    kload(kv, half, 16 * 128 + 256 * r, ek_all[r * 128 + 64 * (kv % 2):r * 128 + 64 * (kv % 2) + 64], kc * 256, 256)
            vload(kv, 0, 16, v_loc[TCTX:T])
            vload(kv, 24, 2, v_loc[0:TCTX])
            for r in range(4):
                vload(kv, 16 + 2 * r, 2, ev_all[r * 256:(r + 1) * 256])
    ones32 = P.sb([128, 64], F32, name="ones32")
    P.add("vector", lambda e: e.memset(ones32[:], 1.0), [], ["ones32"])
    es = P.sb([128, 16], F32, name="es")
    if not dense:
        C.load(es[:], "es", Dr["sink"])
        P.add("scalar", lambda e: e.activation(es[:], es[:], AF.Exp), ["es"], ["es"])
        wm = P.sb([128, 4, 6, 512], BF16, name="wm")
        C.load(wm[:], "wm", Dr["wmask"])
        em = P.sb([128, 2, 4, 512], BF16, name="em")
        C.load(em[:], "em", Dr["emask"])
    pT = [P.sb([128, 512], BF16, name=f"pT{i}") for i in range(3)]
    osb = [P.sb([64, 512], F32, name=f"osb{i}") for i in range(2)]
    rden = [P.sb([128, 512], F32, name=f"rden{i}") for i in range(2)]
    ob = [P.sb([64, 512], BF16, name=f"ob{i}") for i in range(2)]
    work = []
    if dense:
        work.append((0, 256, [(0, None), (1, None)]))
        for m in range(4):
            work.append((256 + 512 * m, 512, [(kt, None) for kt in range(66)]))
    else:
        for m in range(4):
            keys = [(4 * m + r - 1, wm[:, m, r, :]) for r in range(6) if 0 <= 4 * m + r - 1 <= 15]
            if m == 0:
                keys += [(16 + 2 * r + 1, em[:, 0, r, :]) for r in range(4)]
            if m == 3:
                keys += [(16 + 2 * r, em[:, 1, r, :]) for r in range(4)]
            keys += [(24, None), (25, None)]
            work.append((256 + 512 * m, 512, keys))
    it = 0
    si = 0
    for (q0, qn, keys) in work:
        for h in range(n_heads):
            c, half, kv = h // 2, h % 2, h // 4
            pl = slice(64 * half, 64 * half + 64)
            ob_i = it % 2
            it += 1
            po = C.psum[6 + ob_i]
            pok = f"psb{6 + ob_i}"
            LA = 2
            nk = len(keys)
            slots = []
            for n in range(nk + LA):
                if n < nk:
                    kt, mk = keys[n]
                    sb_i = si % 3
                    si += 1
                    slots.append(sb_i)
                    pS = C.psum[sb_i]
                    P.add("tensor", lambda e, pS=pS, kv=kv, kt=kt, pl=pl, c=c, q0=q0, qn=qn: e.matmul(pS[:, 0:qn], kd[pl, kv, kt * 128:(kt + 1) * 128], qT[pl, c, q0:q0 + qn], start=True, stop=True),
                          [f"k{kv}", f"q{c}"], [f"psb{sb_i}"])
                    pt = pT[sb_i]
                    P.add("scalar", lambda e, pt=pt, pS=pS, qn=qn: e.activation(pt[:, 0:qn], pS[:, 0:qn], AF.Exp, scale=0.125), [f"psb{sb_i}"], [f"pT{sb_i}"])
                    if mk is not None:
                        P.add("vector", lambda e, pt=pt, mk=mk, qn=qn: e.tensor_tensor(pt[:, 0:qn], pt[:, 0:qn], mk[:, 0:qn], ALU.mult), [f"pT{sb_i}", "wm", "em"], [f"pT{sb_i}"])
                if n >= LA:
                    m_ = n - LA
                    kt2 = keys[m_][0]
                    sb2 = slots[m_]
                    pt2 = pT[sb2]
                    P.add("tensor", lambda e, po=po, pt2=pt2, kv=kv, kt2=kt2, qn=qn, m_=m_, nk=nk: e.matmul(po[0:65, 0:qn], va[:, kv, kt2, :], pt2[:, 0:qn], start=(m_ == 0), stop=(m_ == nk - 1)),
                          [f"v{kv}", f"pT{sb2}"], [pok])
            rd = rden[ob_i]
            rdk = f"rden{ob_i}"
            osx = osb[ob_i]
            P.add("scalar", lambda e, osx=osx, po=po, qn=qn: e.copy(osx[:, 0:qn], po[0:64, 0:qn]), [pok], [f"osb{ob_i}"])
            if dense:
                P.add("vector", lambda e, rd=rd, po=po, qn=qn: e.reciprocal(rd[64:65, 0:qn], po[64:65, 0:qn]), [pok], [rdk])
            else:
                P.add("vector", lambda e, rd=rd, po=po, qn=qn, h=h: e.tensor_scalar(rd[64:65, 0:qn], po[64:65, 0:qn], es[64:65, h:h + 1], None, ALU.add), [pok, "es"], [rdk])
                P.add("vector", lambda e, rd=rd, qn=qn: e.reciprocal(rd[64:65, 0:qn], rd[64:65, 0:qn]), [rdk], [rdk])
            pb = C.psum[3 + ob_i]
            P.add("tensor", lambda e, pb=pb, rd=rd, qn=qn: e.matmul(pb[0:64, 0:qn], ones32[64:65, 0:64], rd[64:65, 0:qn], start=True, stop=True), [rdk, "ones32"], [f"psb{3 + ob_i}"])
            o16 = ob[ob_i]
            P.add("vector", lambda e, o16=o16, osx=osx, pb=pb, qn=qn: e.tensor_tensor(o16[:, 0:qn], osx[:, 0:qn], pb[0:64, 0:qn], ALU.mult), [f"osb{ob_i}", f"psb{3 + ob_i}"], [f"ob{ob_i}"])
            P.dma(o_loc[64 * half:64 * half + 64, c, q0:q0 + qn], o16[:, 0:qn], [f"ob{ob_i}"], [], f"st_ob{ob_i}")


NS5 = TCTX + 4 * TLAT
TWO_PI = 2.0 * math.pi


def phase_s5(P, C, Dr):
    uA_all, uB_all, B_d, C_d, pc_d = Dr["uA_all"], Dr["uB_all"], Dr["Bl"], Dr["Cl"], Dr["pcols"]
    ic_d, ij_d, oh_d = Dr["iota_c"], Dr["iota_j"], Dr["onehot"]
    y_rs_in, yc_loc = Dr["y_rs_in"], Dr["yc_loc"]
    sm = lambda name, n=8, dtype=F32: P.sb([128, n], dtype, name=name)
    pc = P.sb([128, 3, 8], F32, name="pc")
    C.load(pc[:].rearrange("p a b -> p (a b)"), "pc", pc_d.rearrange("p a b -> p (a b)"))
    iota_c = P.sb([128, 2, 66], F32, name="iota_c_sb")
    iota_j = P.sb([128, 2, 128], F32, name="iota_j_sb")
    oh = sm("onehot_sb", 4)
    C.load(iota_c[:], "iota_c", ic_d)
    C.load(iota_j[:], "iota_j", ij_d)
    C.load(oh[:], "oh", oh_d)
    cnt = [0]

    def V(fn, reads, writes, eng="vector"):
        P.add(eng, fn, reads, writes)

    fr_i = P.sb([128, 128], I32, name="fr_i")
    fr_f = P.sb([128, 128], F32, name="fr_f")
    sc_t = P.sb([128, 128], F32, name="sc_t")
    sc_u = P.sb([128, 128], F32, name="sc_u")

    def fracp(dst, src, n, key_dst, key_src):
        V(lambda e: e.tensor_copy(fr_i[:, 0:n], src), [key_src], ["fr_i"])
        V(lambda e: e.tensor_copy(fr_f[:, 0:n], fr_i[:, 0:n]), ["fr_i"], ["fr_f"])
        V(lambda e: e.tensor_tensor(dst, src, fr_f[:, 0:n], ALU.subtract), [key_src, "fr_f"], [key_dst])

    def sincos(sin_dst, cos_dst, ph, n, key_s, key_c, key_ph):
        P.add("scalar", lambda e: e.activation(sin_dst, ph, AF.Sin, scale=TWO_PI - 1e-5), [key_ph], [key_s])
        V(lambda e: e.tensor_scalar(sc_t[:, 0:n], ph, 0.25, None, ALU.add), [key_ph], ["sc_t"])
        fracp(sc_u[:, 0:n], sc_t[:, 0:n], n, "sc_u", "sc_t")
        P.add("scalar", lambda e: e.activation(cos_dst, sc_u[:, 0:n], AF.Sin, scale=TWO_PI - 1e-5), ["sc_u"], [key_c])

    lr, li, dtv, a, th, rr, f = sm("lr"), sm("li"), sm("dtv"), sm("a_ln"), sm("th"), sm("rr"), sm("f")
    V(lambda e: e.tensor_scalar(lr[:], pc[:, 0, :], -1e-4, None, ALU.min), ["pc"], ["lr"])
    V(lambda e: e.tensor_copy(li[:], pc[:, 1, :]), ["pc"], ["li"])
    P.add("scalar", lambda e: e.activation(dtv[:], pc[:, 2, :], AF.Exp), ["pc"], ["dtv"])
    V(lambda e: e.tensor_tensor(a[:], lr[:], dtv[:], ALU.mult), ["lr", "dtv"], ["a"])
    V(lambda e: e.tensor_tensor(th[:], li[:], dtv[:], ALU.mult), ["li", "dtv"], ["th"])
    P.add("scalar", lambda e: e.activation(rr[:], a[:], AF.Exp), ["a"], ["rr"])
    V(lambda e: e.tensor_scalar(f[:], th[:], 1.0 / TWO_PI, None, ALU.mult), ["th"], ["f"])
    f0, sth, cth = sm("f0"), sm("sth"), sm("cth")
    fracp(f0[:], f[:], 8, "f0", "f")
    sincos(sth[:], cth[:], f0[:], 8, "sth", "cth", "f0")
    nr, ni, den, t8a, t8b, kr, ki, nkr, nki = [sm(n) for n in ("nr", "ni", "den", "t8a", "t8b", "kr", "ki", "nkr", "nki")]
    V(lambda e: e.tensor_tensor(nr[:], rr[:], cth[:], ALU.mult), ["rr", "cth"], ["nr"])
    V(lambda e: e.tensor_scalar(nr[:], nr[:], -1.0, None, ALU.add), ["nr"], ["nr"])
    V(lambda e: e.tensor_tensor(ni[:], rr[:], sth[:], ALU.mult), ["rr", "sth"], ["ni"])
    V(lambda e: e.tensor_tensor(den[:], lr[:], lr[:], ALU.mult), ["lr"], ["den"])
    V(lambda e: e.tensor_tensor(t8a[:], li[:], li[:], ALU.mult), ["li"], ["t8a"])
    V(lambda e: e.tensor_tensor(den[:], den[:], t8a[:], ALU.add), ["den", "t8a"], ["den"])
    V(lambda e: e.reciprocal(den[:], den[:]), ["den"], ["den"])
    V(lambda e: e.tensor_tensor(t8a[:], nr[:], lr[:], ALU.mult), ["nr", "lr"], ["t8a"])
    V(lambda e: e.tensor_tensor(t8b[:], ni[:], li[:], ALU.mult), ["ni", "li"], ["t8b"])
    V(lambda e: e.tensor_tensor(kr[:], t8a[:], t8b[:], ALU.add), ["t8a", "t8b"], ["kr"])
    V(lambda e: e.tensor_tensor(kr[:], kr[:], den[:], ALU.mult), ["kr", "den"], ["kr"])
    V(lambda e: e.tensor_tensor(t8a[:], ni[:], lr[:], ALU.mult), ["ni", "lr"], ["t8a"])
    V(lambda e: e.tensor_tensor(t8b[:], nr[:], li[:], ALU.mult), ["nr", "li"], ["t8b"])
    V(lambda e: e.tensor_tensor(ki[:], t8a[:], t8b[:], ALU.subtract), ["t8a", "t8b"], ["ki"])
    V(lambda e: e.tensor_tensor(ki[:], ki[:], den[:], ALU.mult), ["ki", "den"], ["ki"])
    V(lambda e: e.tensor_scalar(nkr[:], kr[:], -1.0, None, ALU.mult), ["kr"], ["nkr"])
    V(lambda e: e.tensor_scalar(nki[:], ki[:], -1.0, None, ALU.mult), ["ki"], ["nki"])
    a128, a128f = sm("a128"), sm("a128f")
    V(lambda e: e.tensor_scalar(a128[:], f0[:], 128.0, None, ALU.mult), ["f0"], ["a128"])
    fracp(a128f[:], a128[:], 8, "a128f", "a128")
    B16 = P.sb([128, 16, 4, 128], BF16, name="B16")
    CR = P.sb([128, 8, 128], BF16, name="CR16")
    CI = P.sb([128, 8, 128], BF16, name="CI16")
    c32 = [P.sb([128, 2, 128], F32, name=f"c32_{i}") for i in range(2)]
    ctmp = [P.sb([128, 128], F32, name=f"ctmp{i}") for i in range(2)]

    def wsetup(d, gp):
        q = d * 4 + gp
        for ri in range(2):
            C.load_cast(B16[:, q * 2 + ri, :, :], f"B16_{q}_{ri}", B_d[d, gp, ri].rearrange("c p m -> p c m"), 512,
                        view=lambda s_: s_.rearrange("p (c m) -> p c m", m=128))
        cb = c32[q % 2]
        ck = f"c32_{q % 2}"
        P.dma(cb[:], C_d[d, gp].rearrange("r k m -> k r m"), [], [ck], f"ld_c32_{q % 2}")
        tm = ctmp[q % 2]
        tk = f"ctmp{q % 2}"
        V(lambda e: e.tensor_scalar(tm[:], cb[:, 0, :], kr[:, q:q + 1], None, ALU.mult), [ck, "kr"], [tk])
        V(lambda e: e.scalar_tensor_tensor(CR[:, q, :], cb[:, 1, :], nki[:, q:q + 1], tm[:], ALU.mult, ALU.add), [ck, "nki", tk], [f"CR_{q}"])
        V(lambda e: e.tensor_scalar(tm[:], cb[:, 1, :], nkr[:, q:q + 1], None, ALU.mult), [ck, "nkr", f"CR_{q}"], [tk])
        V(lambda e: e.scalar_tensor_tensor(CI[:, q, :], cb[:, 0, :], nki[:, q:q + 1], tm[:], ALU.mult, ALU.add), [ck, "nki", tk], [f"CI_{q}"])

    for d in range(2):
        for gp in range(4):
            wsetup(d, gp)
    sinC = P.sb([128, 8, 66], F32, name="sinC")
    cosC = P.sb([128, 8, 66], F32, name="cosC")
    sinJ = P.sb([128, 8, 128], F32, name="sinJ")
    cosJ = P.sb([128, 8, 128], F32, name="cosJ")

    phA = P.sb([128, 128], F32, name="phA")
    phB = P.sb([128, 128], F32, name="phB")

    def tsetup(q):
        d = q // 4
        V(lambda e: e.tensor_scalar(phA[:, 0:66], iota_c[:, d, :], a128f[:, q:q + 1], None, ALU.mult), ["iota_c", "a128f"], ["phA"])
        fracp(phB[:, 0:66], phA[:, 0:66], 66, "phB", "phA")
        sincos(sinC[:, q, :], cosC[:, q, :], phB[:, 0:66], 66, f"sinC{q}", f"cosC{q}", "phB")
        V(lambda e: e.tensor_scalar(phA[:, 0:128], iota_j[:, d, :], f0[:, q:q + 1], None, ALU.mult), ["iota_j", "f0"], ["phA"])
        fracp(phB[:, 0:128], phA[:, 0:128], 128, "phB", "phA")
        sincos(sinJ[:, q, :], cosJ[:, q, :], phB[:, 0:128], 128, f"sinJ{q}", f"cosJ{q}", "phB")

    for q in range(8):
        tsetup(q)
    NB = 512
    big = lambda name, dtype=F32: P.sb([128, NB], dtype, name=name)
    u16 = [P.sb([128, 4, NB], BF16, name=f"u16_{i}") for i in range(2)]
    St = [big(f"St{i}") for i in range(2)]
    Ct = [big(f"Ct{i}") for i in range(2)]
    tB = big("tB")
    tA2 = [big("tA0"), big("tA1")]
    brs2, bis2 = [big("brs0"), big("brs1")], [big("bis0"), big("bis1")]
    btr2, bti2 = [big("btr0"), big("btr1")], [big("bti0"), big("bti1")]
    gr2, gi2 = [big("gr0"), big("gr1")], [big("gi0"), big("gi1")]
    hr16 = [big(f"hr16_{i}", BF16) for i in range(2)]
    hi16 = [big(f"hi16_{i}", BF16) for i in range(2)]
    carry = P.sb([128, 16], F32, name="carry")
    yb = [P.sb([128, 4, NB], F32, name=f"yb{i}") for i in range(2)]

    def rv(t, n):
        return bass.AP(t, n - 1, [[NB, 128], [-1, n]])

    def gp_body(d, first, lay0, sn, ub, gp, tb, hb, seg_n):
        q = d * 4 + gp
        c0, ncn = lay0 // 128, sn // 128
        nst = (sn + 511) // 512
        S, Cc = St[tb], Ct[tb]
        brs, bis = brs2[tb], bis2[tb]
        ch = gp % 2
        tA, btr, bti, gr, gi = tA2[ch], btr2[ch], bti2[ch], gr2[ch], gi2[ch]
        kA_, kbr, kbi, kgr, kgi = f"tA{ch}", f"btr{ch}", f"bti{ch}", f"gr{ch}", f"gi{ch}"
        yb_ = 4 + (seg_n % 2)
        W = slice(0, sn)
        v3 = lambda t: t[:, 0:sn].rearrange("p (c j) -> p c j", j=128)
        cC = lambda t: t[:, q, c0:c0 + ncn].unsqueeze(2).broadcast_to([128, ncn, 128])
        cJ = lambda t: t[:, q, :].unsqueeze(1).broadcast_to([128, ncn, 128])
        G = "gpsimd"
        P.add(G, lambda e, a=cC(sinC), b=cJ(cosJ), v=v3(S): e.tensor_tensor(v, a, b, ALU.mult), [f"sinC{q}", f"cosJ{q}"], [f"St{tb}"])
        P.add(G, lambda e, a=cC(cosC), b=cJ(sinJ), v=v3(tB): e.tensor_tensor(v, a, b, ALU.mult), [f"cosC{q}", f"sinJ{q}"], ["tBg"])
        P.add(G, lambda e: e.tensor_tensor(S[:, W], S[:, W], tB[:, W], ALU.add), [f"St{tb}", "tBg"], [f"St{tb}"])
        P.add(G, lambda e, a=cC(cosC), b=cJ(cosJ), v=v3(Cc): e.tensor_tensor(v, a, b, ALU.mult), [f"cosC{q}", f"cosJ{q}"], [f"Ct{tb}"])
        P.add(G, lambda e, a=cC(sinC), b=cJ(sinJ), v=v3(tB): e.tensor_tensor(v, a, b, ALU.mult), [f"sinC{q}", f"sinJ{q}"], ["tBg"])
        P.add(G, lambda e: e.tensor_tensor(Cc[:, W], Cc[:, W], tB[:, W], ALU.subtract), [f"Ct{tb}", "tBg"], [f"Ct{tb}"])
        for st in range(nst):
            n0 = st * 512
            nn = min(512, sn - n0)
            for ri, (dst, dk) in enumerate(((brs, f"brs{tb}_"), (bis, f"bis{tb}_"))):
                pbi = ri * 2 + tb
                pb = C.psum[pbi]
                for c in range(4):
                    P.add("tensor", lambda e, pb=pb, ri=ri, n0=n0, nn=nn, c=c: e.matmul(pb[:, 0:nn], B16[:, q * 2 + ri, c, :], u16[ub][:, c, n0:n0 + nn], start=(c == 0), stop=(c == 3)),
                          [f"B16_{q}_{ri}", f"u16_{ub}_{c}"], [f"psb{pbi}"])
                P.add("scalar", lambda e, pb=pb, dst=dst, n0=n0, nn=nn: e.copy(dst[:, n0:n0 + nn], pb[:, 0:nn]), [f"psb{pbi}"], [f"{dk}{st}"])
        assert nst == 1
        bk = [f"brs{tb}_{st}" for st in range(nst)]
        ik = [f"bis{tb}_{st}" for st in range(nst)]
        V(lambda e: e.tensor_tensor(tA[:, W], Cc[:, W], brs[:, W], ALU.mult), [f"Ct{tb}"] + bk, [kA_])
        yield
        V(lambda e: e.tensor_tensor(btr[:, W], S[:, W], bis[:, W], ALU.mult), [f"St{tb}"] + ik, [kbr])
        yield
        V(lambda e: e.tensor_tensor(btr[:, W], btr[:, W], tA[:, W], ALU.add), [kbr, kA_], [kbr])
        yield
        V(lambda e: e.tensor_tensor(tA[:, W], Cc[:, W], bis[:, W], ALU.mult), [f"Ct{tb}"] + ik, [kA_])
        yield
        V(lambda e: e.tensor_tensor(bti[:, W], S[:, W], brs[:, W], ALU.mult), [f"St{tb}"] + bk, [kbi])
        yield
        V(lambda e: e.tensor_tensor(bti[:, W], tA[:, W], bti[:, W], ALU.subtract), [kbi, kA_], [kbi])
        yield
        rcol = rr[:, q:q + 1].broadcast_to([128, sn])
        for (g_, bt_, gk, btk, ci) in ((gr, btr, kgr, kbr, 2 * q), (gi, bti, kgi, kbi, 2 * q + 1)):
            init = 0.0 if first else carry[:, ci:ci + 1]
            if d == 0:
                V(lambda e, g_=g_, bt_=bt_, init=init: e.tensor_tensor_scan(g_[:, W], rcol, bt_[:, W], init, ALU.mult, ALU.add), [btk, "rr", f"carry{ci}"], [gk])
                yield
                P.add("scalar", lambda e, g_=g_, ci=ci: e.copy(carry[:, ci:ci + 1], g_[:, sn - 1:sn]), [gk], [f"carry{ci}"])
            else:
                V(lambda e, g_=g_, bt_=bt_, init=init: e.tensor_tensor_scan(rv(g_, sn), rcol, rv(bt_, sn), init, ALU.mult, ALU.add), [btk, "rr", f"carry{ci}"], [gk])
                yield
                P.add("scalar", lambda e, g_=g_, ci=ci: e.copy(carry[:, ci:ci + 1], g_[:, 0:1]), [gk], [f"carry{ci}"])
        hr_, hi_ = hr16[hb], hi16[hb]
        V(lambda e: e.tensor_tensor(tA[:, W], Cc[:, W], gr[:, W], ALU.mult), [f"Ct{tb}", kgr], [kA_])
        yield
        V(lambda e: e.tensor_tensor(btr[:, W], S[:, W], gi[:, W], ALU.mult), [f"St{tb}", kgi], [kbr])
        yield
        V(lambda e: e.tensor_tensor(hr_[:, W], tA[:, W], btr[:, W], ALU.subtract), [kA_, kbr], [f"hr16_{hb}"])
        yield
        V(lambda e: e.tensor_tensor(tA[:, W], S[:, W], gr[:, W], ALU.mult), [f"St{tb}", kgr], [kA_])
        yield
        V(lambda e: e.tensor_tensor(bti[:, W], Cc[:, W], gi[:, W], ALU.mult), [f"Ct{tb}", kgi], [kbi])
        yield
        V(lambda e: e.tensor_tensor(hi_[:, W], tA[:, W], bti[:, W], ALU.add), [kA_, kbi], [f"hi16_{hb}"])
        yield
        for st in range(nst):
            n0 = st * 512
            nn = min(512, sn - n0)
            py = C.psum[yb_]
            P.add("tensor", lambda e, py=py, n0=n0, nn=nn: e.matmul(py[:, 0:nn], CR[:, q, :], hr_[:, n0:n0 + nn], start=(gp == 0), stop=False),
                  [f"CR_{q}", f"hr16_{hb}"], [f"psb{yb_}"])
            P.add("tensor", lambda e, py=py, n0=n0, nn=nn: e.matmul(py[:, 0:nn], CI[:, q, :], hi_[:, n0:n0 + nn], start=False, stop=(gp == 3)),
                  [f"CI_{q}", f"hi16_{hb}"], [f"psb{yb_}"])

    def seg_body(d, si, first, kind, s, ub, it0, seg_n):
        if kind == "ctx":
            sn, r, t0 = 256, 0, 0
            lay0 = 0 if d == 0 else 4 * TLAT
        else:
            sn, r, t0 = 512, s // 4, TCTX + 512 * (s % 4)
            lay0 = (TCTX + 512 * s) if d == 0 else 512 * s
        for c in range(4):
            usrc = uA_all[c][r * 128:(r + 1) * 128, t0:t0 + sn] if t0 < 1280 else uB_all[c][r * 128:(r + 1) * 128, t0 - 1280:t0 - 1280 + sn]
            C.load_cast(u16[ub][:, c, 0:sn], f"u16_{ub}_{c}", usrc, sn)
        nst = (sn + 511) // 512
        it = it0
        for gp0 in (0, 2):
            gens = []
            for gp in (gp0, gp0 + 1):
                tb = it % 2
                it += 1
                gens.append(gp_body(d, first, lay0, sn, ub, gp, tb, it % 2, seg_n))
            while gens:
                for g_ in list(gens):
                    try:
                        next(g_)
                    except StopIteration:
                        gens.remove(g_)
        ybuf = yb[ub]
        ybk = 4 + (seg_n % 2)
        if kind == "ctx":
            P.add("scalar", lambda e: e.copy(ybuf[:, 0, 0:256], C.psum[ybk][:, 0:256]), [f"psb{ybk}"], [f"yb{ub}"])
            P.dma(yc_loc[:, d * 256:(d + 1) * 256], ybuf[:, 0, 0:256], [f"yb{ub}"], [], f"st_yb{ub}")
        else:
            for c in range(4):
                P.add("scalar", lambda e, c=c: e.activation(ybuf[:, c, 0:512], C.psum[ybk][:, 0:512], AF.Identity, scale=oh[:, c:c + 1]),
                      [f"psb{ybk}", "oh"], [f"yb{ub}"])
            for c in range(4):
                P.dma(y_rs_in[d][c][(s // 4) * 128:(s // 4 + 1) * 128, 512 * (s % 4):512 * (s % 4) + 512], ybuf[:, c, 0:512], [f"yb{ub}"], [], f"st_yb{ub}_{c}")
        return it

    it = 0
    n = 0
    for d in range(2):
        order = [("ctx", 0)] + ([("lat", s) for s in range(16)] if d == 0 else [("lat", s) for s in range(15, -1, -1)])
        for si, (kind, s) in enumerate(order):
            it = seg_body(d, si, si == 0, kind, s, n % 2, it, n)
            n += 1


def phase_po(P, C, layer, xT, cols, Dr, tiles=TILES):
    l0 = layer == 0
    o_loc, wo_d = Dr["o_loc"], Dr["wo"][layer]
    wo16 = P.sb([128, 8, 1024], BF16, name="wo16")
    for k in range(8):
        C.load_cast(wo16[:, k, :], f"wo16_{k}", wo_d[k], 1024)
    nch = 4 if l0 else 8
    ot = [P.sb([128, nch, 512], BF16, name=f"ot{i}") for i in range(2)]
    if l0:
        uA, uB, y_rs_out, yc_all = Dr["uA"], Dr["uB"], Dr["y_rs_out"], Dr["yc_all"]
        g0 = P.sb([128, 4, 512], BF16, name="glu0_sb")
        g1 = P.sb([128, 4, 512], BF16, name="glu1_sb")
        for k in range(4):
            C.load_cast(g0[:, k, :], f"g0_{k}", Dr["glu0"][k], 512)
            C.load_cast(g1[:, k, :], f"g1_{k}", Dr["glu1"][k], 512)
        dcol = P.sb([128, 4], F32, name="dcol_sb")
        C.load(dcol[:], "dcol", Dr["dcol"])
        ub = [P.sb([128, 512], F32, name=f"ub{i}") for i in range(2)]
        yfb = [P.sb([128, 512], F32, name=f"yfb{i}") for i in range(2)]
        yrb = [P.sb([128, 512], F32, name=f"yrb{i}") for i in range(2)]
        t1 = [P.sb([128, 512], F32, name=f"t1_{i}") for i in range(2)]
        gt = [P.sb([128, 4, 512], BF16, name=f"gt{i}") for i in range(2)]
        glt = [P.sb([128, 4, 512], BF16, name=f"glt{i}") for i in range(2)]
        sgb = [P.sb([128, 512], F32, name=f"sgb{i}") for i in range(2)]
    it = 0
    for ti, (t0, tn) in enumerate(tiles):
        v = "c" if t0 < TCTX else "x"
        tb = ti % 2
        P.dma(ot[tb][:, :, 0:tn], o_loc[:, 0:nch, t0:t0 + tn], [], [f"ot{tb}"], f"ld_ot{tb}")
        if l0:
            for c in range(4):
                b = it % 2
                it += 1
                if t0 < TCTX:
                    yf_src = yc_all[c * 128:(c + 1) * 128, 0:256]
                    yr_src = yc_all[c * 128:(c + 1) * 128, 256:512]
                else:
                    yf_src = y_rs_out[0][c][:, t0 - TCTX:t0 - TCTX + tn]
                    yr_src = y_rs_out[1][c][:, t0 - TCTX:t0 - TCTX + tn]
                u_src = uA[c][:, t0:t0 + tn] if t0 < 1280 else uB[c][:, t0 - 1280:t0 - 1280 + tn]
                P.dma(ub[b][:, 0:tn], u_src, [], [f"ub{b}"], f"ld_ub{b}")
                P.dma(yfb[b][:, 0:tn], yf_src, [], [f"yfb{b}"], f"ld_yfb{b}")
                P.dma(yrb[b][:, 0:tn], yr_src, [], [f"yrb{b}"], f"ld_yrb{b}")
                y, u_, yr_, tt = yfb[b], ub[b], yrb[b], t1[b]
                P.add("vector", lambda e, y=y, yr_=yr_, tn=tn: e.tensor_tensor(y[:, 0:tn], y[:, 0:tn], yr_[:, 0:tn], ALU.add), [f"yfb{b}", f"yrb{b}"], [f"yfb{b}"])
                P.add("vector", lambda e, y=y, u_=u_, c=c, tn=tn: e.scalar_tensor_tensor(y[:, 0:tn], u_[:, 0:tn], dcol[:, c:c + 1], y[:, 0:tn], ALU.mult, ALU.add),
                      [f"yfb{b}", f"ub{b}", "dcol"], [f"yfb{b}"])
                P.add("scalar", lambda e, tt=tt, y=y, tn=tn: e.activation(tt[:, 0:tn], y[:, 0:tn], AF.Square), [f"yfb{b}"], [f"t1_{b}"])
                P.add("vector", lambda e, tt=tt, tn=tn: e.tensor_scalar(tt[:, 0:tn], tt[:, 0:tn], 0.044715, 1.0, ALU.mult, ALU.add), [f"t1_{b}"], [f"t1_{b}"])
                P.add("vector", lambda e, tt=tt, y=y, tn=tn: e.tensor_tensor(tt[:, 0:tn], tt[:, 0:tn], y[:, 0:tn], ALU.mult), [f"t1_{b}", f"yfb{b}"], [f"t1_{b}"])
                P.add("scalar", lambda e, tt=tt, tn=tn: e.activation(tt[:, 0:tn], tt[:, 0:tn], AF.Sigmoid, scale=1.5957691216), [f"t1_{b}"], [f"t1_{b}"])
                P.add("vector", lambda e, tt=tt, y=y, c=c, tb=tb, tn=tn: e.tensor_tensor(gt[tb][:, c, 0:tn], tt[:, 0:tn], y[:, 0:tn], ALU.mult), [f"t1_{b}", f"yfb{b}"], [f"gt{tb}_{c}"])
            for oc in range(4):
                pb = oc % 2
                pa, pg = C.psum[pb], C.psum[2 + pb]
                for k in range(4):
                    P.add("tensor", lambda e, pa=pa, k=k, oc=oc, tb=tb, tn=tn: e.matmul(pa[:, 0:tn], g0[:, k, oc * 128:(oc + 1) * 128], gt[tb][:, k, 0:tn], start=(k == 0), stop=(k == 3)),
                          [f"g0_{k}", f"gt{tb}_{k}"], [f"psb{pb}"])
                for k in range(4):
                    P.add("tensor", lambda e, pg=pg, k=k, oc=oc, tb=tb, tn=tn: e.matmul(pg[:, 0:tn], g1[:, k, oc * 128:(oc + 1) * 128], gt[tb][:, k, 0:tn], start=(k == 0), stop=(k == 3)),
                          [f"g1_{k}", f"gt{tb}_{k}"], [f"psb{2 + pb}"])
                sg = sgb[pb]
                P.add("scalar", lambda e, sg=sg, pg=pg, tn=tn: e.activation(sg[:, 0:tn], pg[:, 0:tn], AF.Sigmoid), [f"psb{2 + pb}"], [f"sgb{pb}"])
                P.add("vector", lambda e, sg=sg, pa=pa, oc=oc, tb=tb, tn=tn: e.tensor_tensor(glt[tb][:, oc, 0:tn], sg[:, 0:tn], pa[:, 0:tn], ALU.mult), [f"sgb{pb}", f"psb{pb}"], [f"glt{tb}_{oc}"])
        for oc in range(8):
            pb = 4 + oc % 2
            po = C.psum[pb]
            srcs = ([(glt[tb][:, k, 0:tn], f"glt{tb}_{k}", k) for k in range(4)] if l0 else []) + \
                   [(ot[tb][:, k, 0:tn], f"ot{tb}", (4 + k) if l0 else k) for k in range(nch)]
            for n, (rhs, rk, kk) in enumerate(srcs):
                P.add("tensor", lambda e, po=po, rhs=rhs, kk=kk, oc=oc, tn=tn, n=n, ns=len(srcs): e.matmul(po[:, 0:tn], wo16[:, kk, oc * 128:(oc + 1) * 128], rhs, start=(n == 0), stop=(n == ns - 1)),
                      [f"wo16_{kk}", rk], [f"psb{pb}"])
            P.add("vector", lambda e, po=po, oc=oc, t0=t0, tn=tn, v=v: e.scalar_tensor_tensor(xT[:, oc, t0:t0 + tn], po[:, 0:tn], colsel(cols, v, "gate", 1, oc),
                                                                                             xT[:, oc, t0:t0 + tn], ALU.mult, ALU.add),
                  [f"psb{pb}", f"gate_{v}", f"x{oc}_{ti}"], [f"x{oc}_{ti}"])


GROUPS = [[0, 1, 2, 3], [4, 5, 6, 7]]


def build_fused(stop=999):
    nc = bass.Bass("TRN2", target_bir_lowering=False)
    ext = lambda name, shape, dtype=F32: nc.dram_tensor(name, list(shape), dtype, kind="ExternalInput").ap()
    scr = lambda name, shape, dtype=F32: nc.dram_tensor(name, list(shape), dtype).ap()
    xT_d = ext("xT", [128, 8, T])
    cT_d = ext("cT", [128, 16])
    modw_d = ext("modw", [36, 128, 8, 128])
    modb_d = ext("modb", [128, 36])
    normg_d = ext("normg", [128, 2, 24])
    wg_d = ext("wg", [4, FC, 128, 8, 128])
    wu_d = ext("wu", [4, FC, 128, 8, 128])
    wd_d = ext("wd", [4, FC, 128, 1024])
    win_d = [ext("win0", [10, 128, 8, 128]), ext("win1", [12, 128, 8, 128])]
    qkg_d = ext("qkg", [128, 2, 2])
    cos_d, sin_d = ext("cosT", [128, T]), ext("sinT", [128, T])
    rot_d, bones_d = ext("rot", [128, 128]), ext("bones", [128, 128])
    Dr = dict(
        Bl=ext("Bl", [2, 4, 2, 4, 128, 128]), Cl=ext("Cl", [2, 4, 2, 128, 128]), pcols=ext("pcols", [128, 3, 8]),
        iota_c=ext("iota_c", [128, 2, 66]), iota_j=ext("iota_j", [128, 2, 128]), onehot=ext("onehot", [128, 4]),
        dcol=ext("dcol", [128, 4]), glu0=ext("glu0", [4, 128, 512]), glu1=ext("glu1", [4, 128, 512]),
        wo=ext("wo", [2, 8, 128, 1024]), sink=ext("sink", [128, 16]),
        wmask=ext("wmask", [128, 4, 6, 512], BF16), emask=ext("emask", [128, 2, 4, 512], BF16),
    )
    out_d = nc.dram_tensor("outT", [128, 8, TLAT], F32, kind="ExternalOutput").ap()
    Dr.update(
        mod_loc=scr("mod_loc", [128, 72]), mod_all=scr("mod_all", [512, 72]),
        uA=[scr(f"uA{c}", [128, 1280]) for c in range(4)], uB=[scr(f"uB{c}", [128, 1024]) for c in range(4)],
        uA_all=[scr(f"uA_all{c}", [512, 1280]) for c in range(4)], uB_all=[scr(f"uB_all{c}", [512, 1024]) for c in range(4)],
        q_loc=scr("q_loc", [128, 8, T], BF16),
        k_loc=scr("k_loc", [128, 2, T]), kA=scr("kA", [128, 1280]), kB=scr("kB", [128, 1024]),
        kA_all=scr("kA_all", [512, 1280]), kB_all=scr("kB_all", [512, 1024]),
        v_loc=scr("v_loc", [T, 256]), vA=scr("vA", [1280, 128]), vB=scr("vB", [1024, 128]),
        vA_all=scr("vA_all", [5120, 128]), vB_all=scr("vB_all", [4096, 128]),
        y_rs_in=[[scr(f"y_rs_in{d}{c}", [512, TLAT]) for c in range(4)] for d in range(2)],
        y_rs_out=[[scr(f"y_rs_out{d}{c}", [128, TLAT]) for c in range(4)] for d in range(2)],
        yc_loc=scr("yc_loc", [128, 512]), yc_all=scr("yc_all", [512, 512]),
        o_loc=scr("o_loc", [128, 8, T], BF16),
        ek_loc=scr("ek_loc", [128, 512]), ek_all=scr("ek_all", [512, 512]),
        ev_loc=scr("ev_loc", [256, 256]), ev_all=scr("ev_all", [1024, 256]),
    )
    P = Prog(nc)
    xT = P.sb([128, 8, T], F32, name="xT_sb", persist=True)
    modall = P.sb([128, 2, 72, 2], F32, name="modall", persist=True)
    ng = P.sb([128, 2, 24], F32, name="normg_sb", persist=True)
    coltiles = {(l, v): (P.sb([128, 24], F32, name=f"gs_{v}{l}", persist=True), P.sb([128, 24], F32, name=f"gate_{v}{l}", persist=True))
                for l in range(2) for v in ("x", "c")}
    C = Ctx(P)

    c32 = P.sb([128, 16], F32, name="c32")
    c16 = P.sb([128, 16], BF16, name="c16")
    bt = P.sb([128, 36], F32, name="bt")
    res = P.sb([128, 36, 2], F32, name="res")
    w16 = [P.sb([128, 8, 128], BF16, name=f"w16_{i}") for i in range(2)]
    for c in range(KC):
        for ti, (t0, tn) in enumerate(TILES):
            P.dma(xT[:, c, t0:t0 + tn], xT_d[:, c, t0:t0 + tn], [], [f"x{c}_{ti}"], f"ld_x{(c * 5 + ti) % 4}")
    C.load(c32[:], "c32", cT_d)
    C.load(bt[:], "bt", modb_d)
    C.load(ng[:], "normg", normg_d)
    P.add("scalar", lambda e: e.activation(c16[:], c32[:], AF.Silu), ["c32"], ["c16"])
    for oc in range(36):
        sl = oc % 2
        C.load_cast(w16[sl][:].rearrange("p k m -> p (k m)"), f"w16_{sl}", modw_d[oc].rearrange("p k m -> p (k m)"), 1024)
        ps = C.psum[oc % 2]
        for k in range(8):
            P.add("tensor", lambda e, ps=ps, k=k, sl=sl: e.matmul(ps[:, 0:2], w16[sl][:, k, :], c16[:, k * 2:(k + 1) * 2], start=(k == 0), stop=(k == 7)),
                  [f"w16_{sl}", "c16"], [f"psb{oc % 2}"])
        P.add("vector", lambda e, ps=ps, oc=oc: e.tensor_scalar(res[:, oc, :], ps[:, 0:2], bt[:, oc:oc + 1], None, ALU.add),
              [f"psb{oc % 2}", "bt"], ["res"])
    P.dma(Dr["mod_loc"], res[:].rearrange("p a b -> p (a b)"), ["res"], ["mod_loc"], "st_mod")
    P.coll("AllGather", ALU.bypass, GROUPS, Dr["mod_loc"], Dr["mod_all"], ["mod_loc"], ["mod_all"], "cc_mod")
    for l in range(2):
        for r2 in range(2):
            r = 2 * l + r2
            C.load(modall[:, l, 36 * r2:36 * (r2 + 1), :].rearrange("p a b -> p (a b)"), "modall", Dr["mod_all"][r * 128:(r + 1) * 128, :], reads=["mod_all"])
    cols = [make_cols(P, modall, ng, l, coltiles) for l in range(2)]
    P.end_phase()
    if stop == 1:
        P.close()
        return nc

    def phase_ffn(layer, f_idx, s, tiles=TILES):
        C.new_phase()
        K = alloc_common(P, C)
        hT = P.sb([128, 8, T], BF16, name="hT_sb")
        emit_norm(P, C, K, xT, hT, cols[layer], s, tiles=tiles)
        emit_ffn(P, C, K, xT, hT, cols[layer], s, wg_d[2 * layer + f_idx], wu_d[2 * layer + f_idx], wd_d[2 * layer + f_idx], tiles=tiles)
        return K, hT

    def phase_a(layer):
        n_u, n_q, n_k, vw = (4, 4, 1, 128) if layer == 0 else (0, 8, 2, 256)
        n_fm = n_u + n_q + n_k
        K, hT = phase_ffn(layer, 0, 0)
        emit_norm(P, C, K, xT, hT, cols[layer], 1)
        emit_consts(P, C, K, rot_d, bones_d)
        qkg = P.sb([128, 2, 2], F32, name="qkg_sb")
        C.load(qkg[:], "qkg", qkg_d)
        K["cst"] = [P.sb([128, 512], F32, name=f"cst{i}") for i in range(1)]
        K["snt"] = [P.sb([128, 512], F32, name=f"snt{i}") for i in range(1)]
        K["win16"] = [P.sb([128, 8, 128], BF16, name=f"win16_{i}") for i in range(2)]
        K["zq"] = [P.sb([128, 512], F32, name=f"zq{i}") for i in range(2)]
        K["zb"] = [P.sb([128, 512], BF16, name=f"zb{i}") for i in range(2)]
        K["ob32"] = [P.sb([128, 512], F32, name=f"ob32_{i}") for i in range(2)]
        K["ob16"] = [P.sb([128, 512], BF16, name=f"ob16_{i}") for i in range(2)]
        K["wv16"] = P.sb([128, 8, vw], BF16, name="wv16")
        K["vb"] = [P.sb([128, 256], F32, name="vb0")] * 2
        kinds = ["u"] * n_u + ["q"] * n_q + ["k"] * n_k
        def split_dst(a_, b_):
            return lambda t0, tn: (a_[:, t0:t0 + tn] if t0 < 1280 else b_[:, t0 - 1280:t0 - 1280 + tn])
        kdst = [split_dst(Dr["kA"], Dr["kB"])] if layer == 0 else [Dr["k_loc"][:, i, :] for i in range(2)]
        outs = [(split_dst(Dr["uA"][i], Dr["uB"][i]), F32) for i in range(n_u)] + [(Dr["q_loc"][:, i, :], BF16) for i in range(n_q)] + [(kd_, F32) for kd_ in kdst]
        if layer == 0:
            v_d = [Dr["vA"][tt * 128:(tt + 1) * 128, :] if tt < 10 else Dr["vB"][(tt - 10) * 128:(tt - 9) * 128, :] for tt in range(T // 128)]
        else:
            v_d = Dr["v_loc"].rearrange("(tt p) f -> tt p f", p=128)
        emit_inproj(P, C, K, hT, win_d[layer], n_fm, kinds, qkg[:, layer, :], cos_d, sin_d, outs, v_d, vw, n_fm)
        P.end_phase()

    phase_a(0)
    if stop == 2:
        P.close()
        return nc
    for c in range(4):
        P.coll("AllGather", ALU.bypass, GROUPS, Dr["uA"][c], Dr["uA_all"][c], [], [f"uA_all{c}"], "cc_u")
        P.coll("AllGather", ALU.bypass, GROUPS, Dr["uB"][c], Dr["uB_all"][c], [], [f"uB_all{c}"], "cc_u")
    P.coll("AllGather", ALU.bypass, GROUPS, Dr["kA"], Dr["kA_all"], [], ["kA_all"], "cc_k")
    P.coll("AllGather", ALU.bypass, GROUPS, Dr["kB"], Dr["kB_all"], [], ["kB_all"], "cc_k")
    P.coll("AllGather", ALU.bypass, GROUPS, Dr["vA"], Dr["vA_all"], [], ["vA_all"], "cc_v")
    P.coll("AllGather", ALU.bypass, GROUPS, Dr["vB"], Dr["vB_all"], [], ["vB_all"], "cc_v")
    P.end_phase()
    if stop == 3:
        P.close()
        return nc
    C.new_phase()
    phase_s5(P, C, Dr)
    P.end_phase()
    if stop == 4:
        P.close()
        return nc
    for d in range(2):
        for c in range(4):
            P.coll("ReduceScatter", ALU.add, GROUPS, Dr["y_rs_in"][d][c], Dr["y_rs_out"][d][c], [], [f"y_rs_out{d}{c}"], "cc_y")
    P.coll("AllGather", ALU.bypass, GROUPS, Dr["yc_loc"], Dr["yc_all"], [], ["yc_all"], "cc_yc")
    P.end_phase()
    if stop == 5:
        P.close()
        return nc
    C.new_phase()
    phase_att(P, C, 0, Dr)
    P.end_phase()
    if stop == 6:
        P.close()
        return nc
    C.new_phase()
    phase_po(P, C, 0, xT, cols[0], Dr)
    P.end_phase()
    if stop == 7:
        P.close()
        return nc
    phase_ffn(0, 1, 2)
    P.end_phase()
    if stop == 8:
        P.close()
        return nc
    phase_a(1)
    if stop == 9:
        P.close()
        return nc
    for c in range(2):
        for lh, col0 in ((0, TCTX), (1, T - 128)):
            P.dma(Dr["ek_loc"][:, c * 256 + lh * 128:c * 256 + (lh + 1) * 128], Dr["k_loc"][:, c, col0:col0 + 128], [], ["ek_loc"], f"cp_ek{c}{lh}")
    for lh, col0 in ((0, TCTX), (1, T - 128)):
        P.dma(Dr["ev_loc"][lh * 128:(lh + 1) * 128, :], Dr["v_loc"][col0:col0 + 128, :], [], ["ev_loc"], f"cp_ev{lh}")
    P.coll("AllGather", ALU.bypass, GROUPS, Dr["ek_loc"], Dr["ek_all"], ["ek_loc"], ["ek_all"], "cc_ek")
    P.coll("AllGather", ALU.bypass, GROUPS, Dr["ev_loc"], Dr["ev_all"], ["ev_loc"], ["ev_all"], "cc_ev")
    P.end_phase()
    if stop == 10:
        P.close()
        return nc
    C.new_phase()
    phase_att(P, C, 1, Dr)
    P.end_phase()
    if stop == 11:
        P.close()
        return nc
    C.new_phase()
    phase_po(P, C, 1, xT, cols[1], Dr, tiles=TILES[1:])
    P.end_phase()
    if stop == 12:
        P.close()
        return nc
    phase_ffn(1, 1, 2, tiles=TILES[1:])
    for c in range(KC):
        for ti, (t0, tn) in enumerate(TILES[1:]):
            P.dma(out_d[:, c, t0 - TCTX:t0 - TCTX + tn], xT[:, c, t0:t0 + tn], [f"x{c}_{ti}"], [], f"st_x{(c * 5 + ti) % 4}")
    P.end_phase()
    if stop == 13:
        P.close()
        return nc
    P.close()
    return nc
def fm(x2d):
    t, f = x2d.shape
    return np.ascontiguousarray(x2d.T.reshape(f // 128, 128, t).transpose(1, 0, 2))


def unfm(a):
    p, c, t = a.shape
    return np.ascontiguousarray(a.transpose(1, 0, 2).reshape(c * 128, t).T)


def w_oc(W):
    k, n = W.shape
    return np.ascontiguousarray(W.reshape(k // 128, 128, n // 128, 128).transpose(2, 1, 0, 3))


def w_rows(W):
    k, n = W.shape
    return np.ascontiguousarray(W.reshape(k // 128, 128, n))


def rope_tables():
    rows = 8192 // 64
    r = np.repeat(np.arange(rows, dtype=np.float32), 64)
    col = np.tile(np.arange(64, dtype=np.float32), rows)
    inv = (10000.0 ** (-np.arange(16, dtype=np.float32) / 16)).astype(np.float32)
    ang = np.concatenate([r[:, None] * inv, col[:, None] * inv], axis=-1).astype(np.float32)
    cos = np.cos(ang).astype(np.float32).T
    sin = np.sin(ang).astype(np.float32).T
    idx = np.arange(128) % 32
    return cos[idx], sin[idx]


def const_mats():
    rot = np.zeros((128, 128), np.float32)
    for m in range(128):
        j = m % 64
        if j < 32:
            rot[m + 32, m] = -1.0
        else:
            rot[m - 32, m] = 1.0
    bones = np.zeros((128, 128), np.float32)
    bones[:64, :64] = 1.0
    bones[64:, 64:] = 1.0
    return rot, bones


def core_tokens(inp_x, ctx, i):
    b, q = i // 4, i % 4
    return np.concatenate([ctx[b], inp_x[b, q * TLAT:(q + 1) * TLAT]], axis=0)


def fused_inputs(inp):
    cos, sin = rope_tables()
    rot, bones = const_mats()
    W = np.concatenate([inp["mod_w"][0], inp["mod_w"][1]], axis=1)
    Bv = np.concatenate([inp["mod_b"][0], inp["mod_b"][1]], axis=0)
    Wl = W.reshape(8, 128, 144, 128).transpose(2, 1, 0, 3)
    Bl_mod = Bv.reshape(144, 128).T
    normg = np.ascontiguousarray(np.stack([inp["norm_g"][l].reshape(3, 8, 128).transpose(2, 0, 1).reshape(128, 24) for l in range(2)], axis=1))
    wg = np.stack([w_oc(inp["ffn_wg"][l, f]) for l in range(2) for f in range(2)])
    wu = np.stack([w_oc(inp["ffn_wu"][l, f]) for l in range(2) for f in range(2)])
    wd = np.stack([w_rows(inp["ffn_wd"][l, f]) for l in range(2) for f in range(2)])
    win0, win1 = w_oc(inp["ab_w_in"][0]), w_oc(inp["win_w_in"][0])
    qkg = np.ascontiguousarray(np.stack([inp["qk_norm"][l][:, np.arange(128) % 64].T for l in range(2)], axis=1))
    wo = np.stack([w_rows(inp["w_out"][l]) for l in range(2)])
    dcol = np.ascontiguousarray(inp["s5_d"][0].reshape(4, 128).T)
    glu0, glu1 = w_rows(inp["s5_glu_w"][0, 0]), w_rows(inp["s5_glu_w"][0, 1])
    sink = np.ascontiguousarray(np.broadcast_to(inp["win_sink"][0].reshape(1, 16), (128, 16))).astype(np.float32)
    iota_c = np.zeros((128, 2, 66), np.float32)
    iota_c[:, 0, :] = np.arange(66)
    iota_c[:, 1, :] = 65 - np.arange(66)
    iota_j = np.zeros((128, 2, 128), np.float32)
    iota_j[:, 0, :] = np.arange(128)
    iota_j[:, 1, :] = 127 - np.arange(128)
    one, zero = np.ones((1,), NPBF)[0], np.zeros((1,), NPBF)[0]
    kk = np.arange(128)[:, None]
    qq = np.arange(512)[None, :]
    wmask = np.zeros((128, 4, 6, 512), NPBF)
    for m in range(4):
        for r in range(6):
            ok = np.abs((4 * m + r - 1) * 128 + kk - (512 * m + qq)) <= 128
            wmask[:, m, r, :] = np.where(ok, one, zero)
    e = 0
    maps = []
    for i in range(NCORES):
        b, q = i // 4, i % 4
        cT = np.ascontiguousarray(np.stack([inp["c"][b], inp["c_ctx"]], axis=0).T.reshape(8, 128, 2).transpose(1, 0, 2).reshape(128, 16))
        cosT = np.concatenate([np.ones((128, TCTX), np.float32), cos[:, q * TLAT:(q + 1) * TLAT]], axis=1)
        sinT = np.concatenate([np.zeros((128, TCTX), np.float32), sin[:, q * TLAT:(q + 1) * TLAT]], axis=1)
        Bl = np.zeros((2, 4, 2, 4, 128, 128), np.float32)
        Cl = np.zeros((2, 4, 2, 128, 128), np.float32)
        pcols = np.zeros((128, 3, 8), np.float32)
        j = q
        for d in range(2):
            for gp in range(4):
                for gl in range(2):
                    g = 8 * j + 2 * gp + gl
                    ch0 = (2 * gp + gl) * 16
                    for ri, (bsrc, csrc) in enumerate(((inp["s5_b_re"], inp["s5_c_re"]), (inp["s5_b_im"], inp["s5_c_im"]))):
                        Bl[d, gp, ri, j, ch0:ch0 + 16, gl * 64:(gl + 1) * 64] = bsrc[e, d, g].T
                        Cl[d, gp, ri, gl * 64:(gl + 1) * 64, ch0:ch0 + 16] = csrc[e, d, g].T
                    pcols[gl * 64:(gl + 1) * 64, 0, d * 4 + gp] = inp["s5_lam_re"][e, d, g]
                    pcols[gl * 64:(gl + 1) * 64, 1, d * 4 + gp] = inp["s5_lam_im"][e, d, g]
                    pcols[gl * 64:(gl + 1) * 64, 2, d * 4 + gp] = inp["s5_log_step"][e, d, g]
        onehot = np.zeros((128, 4), np.float32)
        onehot[:, q] = 1.0
        emask = np.zeros((128, 2, 4, 512), NPBF)
        for r in range(4):
            if r == q - 1:
                emask[:, 0, r, :] = np.where(np.abs(-128 + kk - qq) <= 128, one, zero)
            if r == q + 1:
                emask[:, 1, r, :] = np.where(np.abs(2048 + kk - 1536 - qq) <= 128, one, zero)
        maps.append(dict(
            xT=fm(core_tokens(inp["x"], inp["ctx"], i)), cT=cT, modw=np.ascontiguousarray(Wl[36 * q:36 * q + 36]),
            modb=np.ascontiguousarray(Bl_mod[:, 36 * q:36 * q + 36]), normg=normg, wg=wg, wu=wu, wd=wd, win0=win0, win1=win1, qkg=qkg,
            cosT=np.ascontiguousarray(cosT), sinT=np.ascontiguousarray(sinT), rot=rot, bones=bones,
            Bl=Bl, Cl=Cl, pcols=pcols, iota_c=iota_c, iota_j=iota_j, onehot=onehot, dcol=dcol, glu0=glu0, glu1=glu1, wo=wo, sink=sink,
            wmask=wmask, emask=emask))
    return maps


def kernel(**inputs):
    inp = {k: np.asarray(v) for k, v in inputs.items()}
    nc = build_fused()
    res = run_bass_kernel_spmd(nc, fused_inputs(inp), core_ids=list(range(NCORES)))
    out = np.zeros((2, 4 * TLAT, D), np.float32)
    for i in range(NCORES):
        b, q = i // 4, i % 4
        out[b, q * TLAT:(q + 1) * TLAT] = unfm(np.asarray(res.results[i]["outT"]))
    return out
```

```python
import contextlib
import math
import numpy as np
import ml_dtypes
import concourse.bass as bass
import concourse.mybir as mybir
from concourse.bass_utils import run_bass_kernel_spmd

F32 = mybir.dt.float32
BF16 = mybir.dt.bfloat16
I32 = mybir.dt.int32
ALU = mybir.AluOpType
AF = mybir.ActivationFunctionType
AX = mybir.AxisListType
NPBF = ml_dtypes.bfloat16

NCORES = 8
D = 1024
KC = 8
DFF = 2816
FC = 22
TCTX = 256
TLAT = 2048
T = TCTX + TLAT
TILES = [(0, 256), (256, 512), (768, 512), (1280, 512), (1792, 512)]
EPS = 1e-6
GFF = 4


class _Op:
    __slots__ = ("eng", "pos", "fn", "cwaits", "dwaits", "signal", "dma_key", "dma_k")


class Prog:
    ENGS = ("tensor", "vector", "scalar", "gpsimd", "sync")
    CENG = ("tensor", "vector", "scalar", "gpsimd")

    def __init__(self, nc):
        self.nc = nc
        self.ges = contextlib.ExitStack()
        self.pes = contextlib.ExitStack()
        self.esem = {e: self.ges.enter_context(nc.semaphore(f"s_{e}")) for e in self.CENG}
        self.esig = {e: 0 for e in self.CENG}
        self.dsem = {}
        self.dma_cnt = {}
        self.dma_inc = {}
        self.n_sb = 0
        self.n_phase = 0
        self._reset()

    def _reset(self):
        self.ops = {e: [] for e in self.ENGS}
        self.last_w = {}
        self.readers = {}

    def sb(self, shape, dtype=F32, name=None, persist=False):
        self.n_sb += 1
        nm = (name or "sb") + f"_{self.n_sb}"
        es = self.ges if persist else self.pes
        return es.enter_context(self.nc.sbuf_tensor(nm, list(shape), dtype))

    def ps(self, shape, dtype=F32, name=None):
        self.n_sb += 1
        return self.ges.enter_context(self.nc.psum_tensor(name or f"ps{self.n_sb}", list(shape), dtype))

    def add(self, eng, fn, reads=(), writes=(), dma_key=None, inc=16):
        op = _Op()
        op.eng, op.fn, op.signal = eng, fn, False
        op.pos = len(self.ops[eng])
        op.dma_key = dma_key
        op.dma_k = None
        deps = {}
        for k in reads:
            d = self.last_w.get(k)
            if d is not None:
                deps[id(d)] = d
        for k in writes:
            d = self.last_w.get(k)
            if d is not None:
                deps[id(d)] = d
            for r in self.readers.get(k, ()):
                deps[id(r)] = r
        cw = {}
        dw = {}
        for i, d in deps.items():
            if d.dma_key is not None:
                v = self.dma_inc[d.dma_key] * (d.dma_k + 1)
                dw[d.dma_key] = max(dw.get(d.dma_key, 0), v)
            elif d.eng == eng:
                if eng != "tensor":
                    cw[d.eng] = max(cw.get(d.eng, -1), d.pos)
            else:
                cw[d.eng] = max(cw.get(d.eng, -1), d.pos)
        if dma_key is not None:
            self.dma_inc.setdefault(dma_key, inc)
            k = self.dma_cnt.get(dma_key, 0)
            op.dma_k = k
            self.dma_cnt[dma_key] = k + 1
            if k > 0:
                dw[dma_key] = max(dw.get(dma_key, 0), self.dma_inc[dma_key] * k)
        op.cwaits = []
        for e, p in cw.items():
            d = self.ops[e][p]
            d.signal = True
            op.cwaits.append(d)
        op.dwaits = list(dw.items())
        self.ops[eng].append(op)
        for k in reads:
            self.readers.setdefault(k, []).append(op)
        for k in writes:
            self.last_w[k] = op
            self.readers[k] = []
        return op

    def dma(self, out, in_, reads, writes, key, eng="sync", **kw):
        return self.add(eng, lambda e: e.dma_start(out=out, in_=in_, **kw), reads, writes, dma_key=key)

    def coll(self, kind, op, groups, in_ap, out_ap, reads, writes, key):
        self.n_coll = getattr(self, "n_coll", 0) + 1
        key = f"{key}_{self.n_coll}"
        return self.add("gpsimd", lambda e: e.collective_compute(kind, op, replica_groups=groups, ins=[in_ap], outs=[out_ap]),
                        reads, writes, dma_key=key, inc=1)

    def end_phase(self):
        nc = self.nc
        lasts = []
        for e in self.CENG:
            real = [o for o in self.ops[e] if o.dma_key is None and o.fn is not None]
            if real:
                lasts.append(real[-1])
        for d in lasts:
            d.signal = True
        for e in self.ENGS:
            op = _Op()
            op.eng, op.fn, op.signal, op.dma_key, op.dma_k = e, None, False, None, None
            op.pos = len(self.ops[e])
            op.cwaits = [d for d in lasts if d.eng != e]
            op.dwaits = [(k, self.dma_inc[k] * c) for k, c in self.dma_cnt.items()]
            self.ops[e].append(op)
        sig = {}
        for e in self.CENG:
            c = self.esig[e]
            for op in self.ops[e]:
                if op.signal:
                    c += 1
                    sig[id(op)] = c
            self.esig[e] = c
        for k in self.dma_cnt:
            if k not in self.dsem:
                self.dsem[k] = self.ges.enter_context(nc.semaphore(f"d_{len(self.dsem)}"))
        esem, dsem = self.esem, self.dsem
        with nc.Block() as block:
            def mk(ename):
                ops = self.ops[ename]

                def body(e):
                    for op in ops:
                        for d in op.cwaits:
                            e.wait_ge(esem[d.eng], sig[id(d)])
                        for k, v in op.dwaits:
                            e.wait_ge(dsem[k], v)
                        if op.fn is None:
                            continue
                        ins = op.fn(e)
                        if op.dma_key is not None:
                            ins.then_inc(dsem[op.dma_key], self.dma_inc[op.dma_key])
                        elif op.signal:
                            ins.then_inc(esem[ename], 1)
                return body

            for ename in self.ENGS:
                getattr(block, ename)(mk(ename))
        self.pes.close()
        self.pes = contextlib.ExitStack()
        self._reset()
        self.n_phase += 1

    def close(self):
        self.ges.close()


class Ctx:
    def __init__(self, P):
        self.P = P
        self.psum = [P.ps([128, 512], F32, name=f"psb{i}") for i in range(8)]
        self.ld_i = 0
        self.new_phase()

    def new_phase(self):
        P = self.P
        self.stage = [P.sb([128, 1024], F32, name=f"stage{i}") for i in range(3)]
        self.stage_i = 0

    def load_cast(self, dst16_ap, dst_key, src_ap, n, cast_eng="gpsimd", view=None, psl=None):
        P = self.P
        i = self.stage_i
        self.stage_i = (i + 1) % len(self.stage)
        st = self.stage[i]
        sk = f"stage{i}"
        sv = st[:, 0:n] if psl is None else st[psl, 0:n]
        P.dma(sv if view is None else view(sv), src_ap, [], [sk], f"ld_stage{i}")
        if cast_eng == "scalar":
            P.add("scalar", lambda e: e.copy(dst16_ap, sv if view is None else view(sv)), [sk], [dst_key])
        else:
            P.add(cast_eng, lambda e: e.tensor_copy(dst16_ap, sv if view is None else view(sv)), [sk], [dst_key])

    def load(self, dst_ap, dst_key, src_ap, reads=()):
        P = self.P
        self.ld_i += 1
        P.dma(dst_ap, src_ap, list(reads), [dst_key], f"ld_misc{self.ld_i % 4}")


def make_cols(P, modall, ng, layer, tiles):
    out = {}
    for vi, v in enumerate(("x", "c")):
        gs, gt = tiles[(layer, v)]
        for s in range(3):
            sc = modall[:, layer, (3 * s + 1) * 8:(3 * s + 2) * 8, vi]
            P.add("vector", lambda e, s=s, sc=sc, gs=gs: e.scalar_tensor_tensor(gs[:, s * 8:(s + 1) * 8], sc, 1.0, ng[:, layer, s * 8:(s + 1) * 8], ALU.add, ALU.mult),
                  ["modall", "normg"], [f"gs_{v}"])
            g = modall[:, layer, (3 * s + 2) * 8:(3 * s + 3) * 8, vi]
            fac = 1.0 if s == 1 else 0.5
            P.add("vector", lambda e, s=s, g=g, gt=gt, fac=fac: e.tensor_scalar(gt[:, s * 8:(s + 1) * 8], g, fac, None, ALU.mult),
                  ["modall"], [f"gate_{v}"])
        out[v] = dict(gs=gs, gate=gt, mod=modall, layer=layer, vi=vi)
    return out


def colsel(cols, v, kind, s, c):
    if kind == "gs":
        return cols[v]["gs"][:, s * 8 + c:s * 8 + c + 1]
    if kind == "gate":
        return cols[v]["gate"][:, s * 8 + c:s * 8 + c + 1]
    if kind == "shift":
        j = (3 * s) * 8 + c
        return cols[v]["mod"][:, cols[v]["layer"], j:j + 1, cols[v]["vi"]]
    raise ValueError(kind)


def emit_norm(P, C, K, xT, hT, cols, s, tiles=TILES):
    for ti, (t0, tn) in enumerate(tiles):
        v = "c" if t0 < TCTX else "x"
        pss = C.psum[6 + (ti % 2)]
        psk = f"psb{6 + (ti % 2)}"
        for c in range(KC):
            sq = K["sq"][c % 2]
            sqk = f"sq{c % 2}"
            P.add("scalar", lambda e, sq=sq, c=c, t0=t0, tn=tn: e.activation(sq[:, 0:tn], xT[:, c, t0:t0 + tn], AF.Square),
                  [f"x{c}_{ti}"], [sqk])
            P.add("tensor", lambda e, sq=sq, c=c, pss=pss, tn=tn: e.matmul(pss[:, 0:tn], K["ones"][:], sq[:, 0:tn], start=(c == 0), stop=(c == KC - 1)),
                  [sqk, "ones"], [psk])
        rs = K["rstd"][ti % 2]
        rsk = f"rstd{ti % 2}"
        P.add("scalar", lambda e, rs=rs, pss=pss, tn=tn: e.activation(rs[:, 0:tn], pss[:, 0:tn], AF.Sqrt, bias=K["epscol"][:, 0:1], scale=1.0 / D),
              [psk, "epscol"], [rsk])
        P.add("vector", lambda e, rs=rs, tn=tn: e.reciprocal(rs[:, 0:tn], rs[:, 0:tn]), [rsk], [rsk])
        for c in range(KC):
            tmp = K["ntmp"][c % 2]
            tk = f"ntmp{c % 2}"
            P.add("vector", lambda e, tmp=tmp, c=c, t0=t0, tn=tn, rs=rs: e.tensor_tensor(tmp[:, 0:tn], xT[:, c, t0:t0 + tn], rs[:, 0:tn], ALU.mult),
                  [f"x{c}_{ti}", rsk], [tk])
            P.add("scalar", lambda e, tmp=tmp, c=c, t0=t0, tn=tn, v=v: e.activation(hT[:, c, t0:t0 + tn], tmp[:, 0:tn], AF.Identity,
                                                                                    bias=colsel(cols, v, "shift", s, c), scale=colsel(cols, v, "gs", s, c)),
                  [tk, f"gs_{v}", "modall"], [f"h{c}_{ti}"])


def emit_ffn(P, C, K, xT, hT, cols, s, wg_d, wu_d, wd_d, tiles=TILES):
    act = K["act"]
    wg16, wu16, wd16 = K["wg16"], K["wu16"], K["wd16"]

    def load_chunk(j):
        sl = j % 2
        C.load_cast(wg16[sl][:].rearrange("p k m -> p (k m)"), f"wg16_{sl}", wg_d[j].rearrange("p k m -> p (k m)"), 1024)
        C.load_cast(wu16[sl][:].rearrange("p k m -> p (k m)"), f"wu16_{sl}", wu_d[j].rearrange("p k m -> p (k m)"), 1024)
        C.load_cast(wd16[j % (2 * GFF)][:], f"wd16_{j % (2 * GFF)}", wd_d[j], 1024)

    groups = [list(range(g, min(g + GFF, FC))) for g in range(0, FC, GFF)]
    load_chunk(0)
    pi = 0
    for grp in groups:
        for j in grp:
            if j + 1 < FC:
                load_chunk(j + 1)
            sl = j % 2
            jj = j % GFF
            for ti, (t0, tn) in enumerate(tiles):
                gb = pi % 2
                pi += 1
                pg, pu = C.psum[gb], C.psum[2 + gb]
                for k in range(KC):
                    P.add("tensor", lambda e, pg=pg, k=k, sl=sl, t0=t0, tn=tn: e.matmul(pg[:, 0:tn], wg16[sl][:, k, :], hT[:, k, t0:t0 + tn], start=(k == 0), stop=(k == KC - 1)),
                          [f"wg16_{sl}", f"h{k}_{ti}"], [f"psb{gb}"])
                for k in range(KC):
                    P.add("tensor", lambda e, pu=pu, k=k, sl=sl, t0=t0, tn=tn: e.matmul(pu[:, 0:tn], wu16[sl][:, k, :], hT[:, k, t0:t0 + tn], start=(k == 0), stop=(k == KC - 1)),
                          [f"wu16_{sl}", f"h{k}_{ti}"], [f"psb{2 + gb}"])
                sg = K["sg"][gb]
                P.add("scalar", lambda e, sg=sg, pg=pg, tn=tn: e.activation(sg[:, 0:tn], pg[:, 0:tn], AF.Silu), [f"psb{gb}"], [f"sg{gb}"])
                P.add("vector", lambda e, sg=sg, pu=pu, jj=jj, t0=t0, tn=tn: e.tensor_tensor(act[:, jj, t0:t0 + tn], sg[:, 0:tn], pu[:, 0:tn], ALU.mult),
                      [f"sg{gb}", f"psb{2 + gb}"], [f"act{jj}_{ti}"])
        for ti, (t0, tn) in enumerate(tiles):
            v = "c" if t0 < TCTX else "x"
            for dc in range(KC):
                db = 4 + (dc % 2)
                pd = C.psum[db]
                for n, j in enumerate(grp):
                    jj = j % GFF
                    jw = j % (2 * GFF)
                    P.add("tensor", lambda e, pd=pd, jj=jj, jw=jw, dc=dc, t0=t0, tn=tn, n=n, ng=len(grp): e.matmul(pd[:, 0:tn], wd16[jw][:, dc * 128:(dc + 1) * 128], act[:, jj, t0:t0 + tn],
                                                                                                  start=(n == 0), stop=(n == ng - 1)),
                          [f"wd16_{jw}", f"act{jj}_{ti}"], [f"psb{db}"])
                P.add("vector", lambda e, pd=pd, dc=dc, t0=t0, tn=tn, v=v: e.scalar_tensor_tensor(xT[:, dc, t0:t0 + tn], pd[:, 0:tn], colsel(cols, v, "gate", s, dc),
                                                                                                 xT[:, dc, t0:t0 + tn], ALU.mult, ALU.add),
                      [f"psb{db}", f"gate_{v}", f"x{dc}_{ti}"], [f"x{dc}_{ti}"])


def alloc_common(P, C, with_ffn=True):
    K = {}
    K["ones"] = P.sb([128, 128], BF16, name="ones")
    P.add("vector", lambda e: e.memset(K["ones"][:], 1.0), [], ["ones"])
    K["epscol"] = P.sb([128, 1], F32, name="epscol")
    P.add("vector", lambda e: e.memset(K["epscol"][:], EPS), [], ["epscol"])
    K["sq"] = [P.sb([128, 512], BF16, name=f"sq{i}") for i in range(2)]
    K["rstd"] = [P.sb([128, 512], F32, name=f"rstd{i}") for i in range(2)]
    K["ntmp"] = [P.sb([128, 512], F32, name=f"ntmp{i}") for i in range(2)]
    if with_ffn:
        K["act"] = P.sb([128, GFF, T], BF16, name="act")
        K["wg16"] = [P.sb([128, 8, 128], BF16, name=f"wg16_{i}") for i in range(2)]
        K["wu16"] = [P.sb([128, 8, 128], BF16, name=f"wu16_{i}") for i in range(2)]
        K["wd16"] = [P.sb([128, 1024], BF16, name=f"wd16_{i}") for i in range(2 * GFF)]
        K["sg"] = [P.sb([128, 512], F32, name=f"sg{i}") for i in range(2)]
    return K


def emit_consts(P, C, K, rot_d, bones_d):
    K["rot"] = P.sb([128, 128], BF16, name="rot_sb")
    K["bones"] = P.sb([128, 128], BF16, name="bones_sb")
    C.load_cast(K["rot"][:], "rot", rot_d, 128)
    C.load_cast(K["bones"][:], "bones", bones_d, 128)


def emit_inproj(P, C, K, hT, win_d, n_fm, kinds, qkg, cosT, sinT, outs, v_d, vw, v_oc0, tiles=TILES):
    w16 = K["win16"]
    zq = K["zq"]
    for oc in range(n_fm):
        sl = oc % 2
        C.load_cast(w16[sl][:].rearrange("p k m -> p (k m)"), f"win16_{sl}", win_d[oc].rearrange("p k m -> p (k m)"), 1024)
        kind = kinds[oc]
        od, odt = outs[oc]
        for ti, (t0, tn) in enumerate(tiles):
            pb = ti % 2
            pz = C.psum[pb]
            for k in range(KC):
                P.add("tensor", lambda e, pz=pz, k=k, sl=sl, t0=t0, tn=tn: e.matmul(pz[:, 0:tn], w16[sl][:, k, :], hT[:, k, t0:t0 + tn], start=(k == 0), stop=(k == KC - 1)),
                      [f"win16_{sl}", f"h{k}_{ti}"], [f"psb{pb}"])
            ob = K["ob32"][pb] if odt == F32 else K["ob16"][pb]
            obk = f"ob{'32' if odt == F32 else '16'}_{pb}"
            if kind == "u":
                P.add("scalar", lambda e, ob=ob, pz=pz, tn=tn: e.copy(ob[:, 0:tn], pz[:, 0:tn]), [f"psb{pb}"], [obk])
            else:
                gcol = qkg[:, 0:1] if kind == "q" else qkg[:, 1:2]
                z = zq[pb]
                zk = f"zq{pb}"
                sq = K["sq"][pb]
                sqk = f"sq{pb}"
                P.add("scalar", lambda e, z=z, pz=pz, tn=tn: e.copy(z[:, 0:tn], pz[:, 0:tn]), [f"psb{pb}"], [zk])
                P.add("scalar", lambda e, sq=sq, z=z, tn=tn: e.activation(sq[:, 0:tn], z[:, 0:tn], AF.Square), [zk], [sqk])
                ph = C.psum[2 + pb]
                P.add("tensor", lambda e, ph=ph, sq=sq, tn=tn: e.matmul(ph[:, 0:tn], K["bones"][:], sq[:, 0:tn], start=True, stop=True), [sqk, "bones"], [f"psb{2 + pb}"])
                rs = K["rstd"][pb]
                rsk = f"rstd{pb}"
                P.add("scalar", lambda e, rs=rs, ph=ph, tn=tn: e.activation(rs[:, 0:tn], ph[:, 0:tn], AF.Sqrt, bias=K["epscol"][:, 0:1], scale=1.0 / 64), [f"psb{2 + pb}", "epscol"], [rsk])
                P.add("vector", lambda e, rs=rs, tn=tn: e.reciprocal(rs[:, 0:tn], rs[:, 0:tn]), [rsk], [rsk])
                P.add("vector", lambda e, z=z, rs=rs, tn=tn, gcol=gcol: e.scalar_tensor_tensor(z[:, 0:tn], z[:, 0:tn], gcol, rs[:, 0:tn], ALU.mult, ALU.mult), [zk, rsk, "qkg"], [zk])
                zb = K["zb"][pb]
                zbk = f"zb{pb}"
                P.add("scalar", lambda e, zb=zb, z=z, tn=tn: e.copy(zb[:, 0:tn], z[:, 0:tn]), [zk], [zbk])
                pr = C.psum[4 + pb]
                P.add("tensor", lambda e, pr=pr, zb=zb, tn=tn: e.matmul(pr[:, 0:tn], K["rot"][:], zb[:, 0:tn], start=True, stop=True), [zbk, "rot"], [f"psb{4 + pb}"])
                t2 = K["ntmp"][pb]
                t2k = f"ntmp{pb}"
                cb_ = pb % len(K["cst"])
                cst, snt = K["cst"][cb_], K["snt"][cb_]
                C.load(cst[:, 0:tn], f"cst{cb_}", cosT[:, t0:t0 + tn])
                C.load(snt[:, 0:tn], f"snt{cb_}", sinT[:, t0:t0 + tn])
                P.add("vector", lambda e, t2=t2, pr=pr, snt=snt, tn=tn: e.tensor_tensor(t2[:, 0:tn], pr[:, 0:tn], snt[:, 0:tn], ALU.mult), [f"psb{4 + pb}", f"snt{cb_}"], [t2k])
                P.add("vector", lambda e, z=z, cst=cst, tn=tn: e.tensor_tensor(z[:, 0:tn], z[:, 0:tn], cst[:, 0:tn], ALU.mult), [zk, f"cst{cb_}"], [zk])
                P.add("vector", lambda e, ob=ob, z=z, t2=t2, tn=tn: e.tensor_tensor(ob[:, 0:tn], z[:, 0:tn], t2[:, 0:tn], ALU.add), [zk, t2k], [obk])
            P.dma(od(t0, tn) if callable(od) else od[:, t0:t0 + tn], ob[:, 0:tn], [obk], [], f"st_{obk}")
    nvc = vw // 128
    wv = K["wv16"]
    for i in range(nvc):
        C.load_cast(wv[:, :, i * 128:(i + 1) * 128], f"wv16_{i}", win_d[v_oc0 + i], 1024, view=lambda a: a.rearrange("p (k m) -> p k m", m=128))
    for tt in range(T // 128):
        pb = tt % 2
        pv = C.psum[6 + pb]
        ti = 0 if tt < 2 else 1 + (tt - 2) // 4
        for k in range(KC):
            P.add("tensor", lambda e, pv=pv, k=k, tt=tt: e.matmul(pv[:, 0:vw], hT[:, k, tt * 128:(tt + 1) * 128], wv[:, k, :], start=(k == 0), stop=(k == KC - 1)),
                  [f"wv16_{i}" for i in range(nvc)] + [f"h{k}_{ti}"], [f"psb{6 + pb}"])
        vb = K["vb"][pb]
        P.add("scalar", lambda e, vb=vb, pv=pv: e.copy(vb[:, 0:vw], pv[:, 0:vw]), [f"psb{6 + pb}"], ["vb0"])
        P.dma(v_d[tt], vb[:, 0:vw], ["vb0"], [], "st_vb0")


def phase_att(P, C, layer, Dr):
    dense = layer == 0
    n_qc, n_kv = (4, 2) if dense else (8, 4)
    n_heads = 2 * n_qc
    NK = 66 if dense else 26
    qT = P.sb([128, n_qc, T], BF16, name="q_sb")
    kd = P.sb([128, n_kv, NK * 128], BF16, name="k_sb")
    va = P.sb([128, n_kv, NK, 65], BF16, name="v_sb")
    q_loc, o_loc = Dr["q_loc"], Dr["o_loc"]
    for c in range(n_qc):
        C.load(qT[:, c, :], f"q{c}", q_loc[:, c, :])
    for kv in range(n_kv):
        P.add("vector", lambda e, kv=kv: e.memset(va[:, kv, :, 64:65], 1.0), [], [f"v{kv}"])
    tkv = lambda s_: s_.rearrange("p (kt d) -> p kt d", d=64)
    tk = lambda ap: ap.rearrange("(kt p) d -> p kt d", p=128)

    def kload(kv, half, dst0, src_rows, src_cols0, ncols):
        pl = slice(64 * half, 64 * half + 64)
        for c0 in range(0, ncols, 1024):
            n = min(1024, ncols - c0)
            C.load_cast(kd[pl, kv, dst0 + c0:dst0 + c0 + n], f"k{kv}", src_rows[:, src_cols0 + c0:src_cols0 + c0 + n], n, psl=pl)

    def vload(kv, kt0, nkt, src):
        for t0 in range(0, nkt, 16):
            n = min(16, nkt - t0)
            C.load_cast(va[:, kv, kt0 + t0:kt0 + t0 + n, 0:64], f"v{kv}", tk(src[t0 * 128:(t0 + n) * 128, 64 * kv:64 * kv + 64]), n * 64, view=tkv)

    if dense:
        kA_all, kB_all, vA_all, vB_all = Dr["kA_all"], Dr["kB_all"], Dr["vA_all"], Dr["vB_all"]
        for kv in range(n_kv):
            for half in range(2):
                kload(kv, half, 0, kA_all[64 * kv:64 * kv + 64], 0, TCTX)
                for r in range(4):
                    kload(kv, half, TCTX + TLAT * r, kA_all[r * 128 + 64 * kv:r * 128 + 64 * kv + 64], TCTX, 1024)
                    kload(kv, half, TCTX + TLAT * r + 1024, kB_all[r * 128 + 64 * kv:r * 128 + 64 * kv + 64], 0, 1024)
            vload(kv, 0, 2, vA_all[0:TCTX])
            for r in range(4):
                vload(kv, 2 + 16 * r, 8, vA_all[r * 1280 + TCTX:(r + 1) * 1280])
                vload(kv, 2 + 16 * r + 8, 8, vB_all[r * 1024:(r + 1) * 1024])
    else:
        k_loc, v_loc, ek_all, ev_all = Dr["k_loc"], Dr["v_loc"], Dr["ek_all"], Dr["ev_all"]
        for kv in range(n_kv):
            ks = slice(64 * (kv % 2), 64 * (kv % 2) + 64)
            kc = kv // 2
            for half in range(2):
                kload(kv, half, 0, k_loc[ks, kc], TCTX, TLAT)
                kload(kv, half, 24 * 128, k_loc[ks, kc], 0, TCTX)
                for r in range(4):
                    kload(kv, half, 16 * 128 + 256 * r, ek_all[r * 128 + 64 * (kv % 2):r * 128 + 64 * (kv % 2) + 64], kc * 256, 256)
            vload(kv, 0, 16, v_loc[TCTX:T])
            vload(kv, 24, 2, v_loc[0:TCTX])
            for r in range(4):
                vload(kv, 16 + 2 * r, 2, ev_all[r * 256:(r + 1) * 256])
    ones32 = P.sb([128, 64], F32, name="ones32")
    P.add("vector", lambda e: e.memset(ones32[:], 1.0), [], ["ones32"])
    es = P.sb([128, 16], F32, name="es")
    if not dense:
        C.load(es[:], "es", Dr["sink"])
        P.add("scalar", lambda e: e.activation(es[:], es[:], AF.Exp), ["es"], ["es"])
        wm = P.sb([128, 4, 6, 512], BF16, name="wm")
        C.load(wm[:], "wm", Dr["wmask"])
        em = P.sb([128, 2, 4, 512], BF16, name="em")
        C.load(em[:], "em", Dr["emask"])
    pT = [P.sb([128, 512], BF16, name=f"pT{i}") for i in range(3)]
    osb = [P.sb([64, 512], F32, name=f"osb{i}") for i in range(2)]
    rden = [P.sb([128, 512], F32, name=f"rden{i}") for i in range(2)]
    ob = [P.sb([64, 512], BF16, name=f"ob{i}") for i in range(2)]
    work = []
    if dense:
        work.append((0, 256, [(0, None), (1, None)]))
        for m in range(4):
            work.append((256 + 512 * m, 512, [(kt, None) for kt in range(66)]))
    else:
        for m in range(4):
            keys = [(4 * m + r - 1, wm[:, m, r, :]) for r in range(6) if 0 <= 4 * m + r - 1 <= 15]
            if m == 0:
                keys += [(16 + 2 * r + 1, em[:, 0, r, :]) for r in range(4)]
            if m == 3:
                keys += [(16 + 2 * r, em[:, 1, r, :]) for r in range(4)]
            keys += [(24, None), (25, None)]
            work.append((256 + 512 * m, 512, keys))
    it = 0
    si = 0
    for (q0, qn, keys) in work:
        for h in range(n_heads):
            c, half, kv = h // 2, h % 2, h // 4
            pl = slice(64 * half, 64 * half + 64)
            ob_i = it % 2
            it += 1
            po = C.psum[6 + ob_i]
            pok = f"psb{6 + ob_i}"
            LA = 2
            nk = len(keys)
            slots = []
            for n in range(nk + LA):
                if n < nk:
                    kt, mk = keys[n]
                    sb_i = si % 3
                    si += 1
                    slots.append(sb_i)
                    pS = C.psum[sb_i]
                    P.add("tensor", lambda e, pS=pS, kv=kv, kt=kt, pl=pl, c=c, q0=q0, qn=qn: e.matmul(pS[:, 0:qn], kd[pl, kv, kt * 128:(kt + 1) * 128], qT[pl, c, q0:q0 + qn], start=True, stop=True),
                          [f"k{kv}", f"q{c}"], [f"psb{sb_i}"])
                    pt = pT[sb_i]
                    P.add("scalar", lambda e, pt=pt, pS=pS, qn=qn: e.activation(pt[:, 0:qn], pS[:, 0:qn], AF.Exp, scale=0.125), [f"psb{sb_i}"], [f"pT{sb_i}"])
                    if mk is not None:
                        P.add("vector", lambda e, pt=pt, mk=mk, qn=qn: e.tensor_tensor(pt[:, 0:qn], pt[:, 0:qn], mk[:, 0:qn], ALU.mult), [f"pT{sb_i}", "wm", "em"], [f"pT{sb_i}"])
                if n >= LA:
                    m_ = n - LA
                    kt2 = keys[m_][0]
                    sb2 = slots[m_]
                    pt2 = pT[sb2]
                    P.add("tensor", lambda e, po=po, pt2=pt2, kv=kv, kt2=kt2, qn=qn, m_=m_, nk=nk: e.matmul(po[0:65, 0:qn], va[:, kv, kt2, :], pt2[:, 0:qn], start=(m_ == 0), stop=(m_ == nk - 1)),
                          [f"v{kv}", f"pT{sb2}"], [pok])
            rd = rden[ob_i]
            rdk = f"rden{ob_i}"
            osx = osb[ob_i]
            P.add("scalar", lambda e, osx=osx, po=po, qn=qn: e.copy(osx[:, 0:qn], po[0:64, 0:qn]), [pok], [f"osb{ob_i}"])
            if dense:
                P.add("vector", lambda e, rd=rd, po=po, qn=qn: e.reciprocal(rd[64:65, 0:qn], po[64:65, 0:qn]), [pok], [rdk])
            else:
                P.add("vector", lambda e, rd=rd, po=po, qn=qn, h=h: e.tensor_scalar(rd[64:65, 0:qn], po[64:65, 0:qn], es[64:65, h:h + 1], None, ALU.add), [pok, "es"], [rdk])
                P.add("vector", lambda e, rd=rd, qn=qn: e.reciprocal(rd[64:65, 0:qn], rd[64:65, 0:qn]), [rdk], [rdk])
            pb = C.psum[3 + ob_i]
            P.add("tensor", lambda e, pb=pb, rd=rd, qn=qn: e.matmul(pb[0:64, 0:qn], ones32[64:65, 0:64], rd[64:65, 0:qn], start=True, stop=True), [rdk, "ones32"], [f"psb{3 + ob_i}"])
            o16 = ob[ob_i]
            P.add("vector", lambda e, o16=o16, osx=osx, pb=pb, qn=qn: e.tensor_tensor(o16[:, 0:qn], osx[:, 0:qn], pb[0:64, 0:qn], ALU.mult), [f"osb{ob_i}", f"psb{3 + ob_i}"], [f"ob{ob_i}"])
            P.dma(o_loc[64 * half:64 * half + 64, c, q0:q0 + qn], o16[:, 0:qn], [f"ob{ob_i}"], [], f"st_ob{ob_i}")


NS5 = TCTX + 4 * TLAT
TWO_PI = 2.0 * math.pi


def phase_s5(P, C, Dr):
    uA_all, uB_all, B_d, C_d, pc_d = Dr["uA_all"], Dr["uB_all"], Dr["Bl"], Dr["Cl"], Dr["pcols"]
    ic_d, ij_d, oh_d = Dr["iota_c"], Dr["iota_j"], Dr["onehot"]
    y_rs_in, yc_loc = Dr["y_rs_in"], Dr["yc_loc"]
    sm = lambda name, n=8, dtype=F32: P.sb([128, n], dtype, name=name)
    pc = P.sb([128, 3, 8], F32, name="pc")
    C.load(pc[:].rearrange("p a b -> p (a b)"), "pc", pc_d.rearrange("p a b -> p (a b)"))
    iota_c = P.sb([128, 2, 66], F32, name="iota_c_sb")
    iota_j = P.sb([128, 2, 128], F32, name="iota_j_sb")
    oh = sm("onehot_sb", 4)
    C.load(iota_c[:], "iota_c", ic_d)
    C.load(iota_j[:], "iota_j", ij_d)
    C.load(oh[:], "oh", oh_d)
    cnt = [0]

    def V(fn, reads, writes, eng="vector"):
        P.add(eng, fn, reads, writes)

    fr_i = P.sb([128, 128], I32, name="fr_i")
    fr_f = P.sb([128, 128], F32, name="fr_f")
    sc_t = P.sb([128, 128], F32, name="sc_t")
    sc_u = P.sb([128, 128], F32, name="sc_u")

    def fracp(dst, src, n, key_dst, key_src):
        V(lambda e: e.tensor_copy(fr_i[:, 0:n], src), [key_src], ["fr_i"])
        V(lambda e: e.tensor_copy(fr_f[:, 0:n], fr_i[:, 0:n]), ["fr_i"], ["fr_f"])
        V(lambda e: e.tensor_tensor(dst, src, fr_f[:, 0:n], ALU.subtract), [key_src, "fr_f"], [key_dst])

    def sincos(sin_dst, cos_dst, ph, n, key_s, key_c, key_ph):
        P.add("scalar", lambda e: e.activation(sin_dst, ph, AF.Sin, scale=TWO_PI - 1e-5), [key_ph], [key_s])
        V(lambda e: e.tensor_scalar(sc_t[:, 0:n], ph, 0.25, None, ALU.add), [key_ph], ["sc_t"])
        fracp(sc_u[:, 0:n], sc_t[:, 0:n], n, "sc_u", "sc_t")
        P.add("scalar", lambda e: e.activation(cos_dst, sc_u[:, 0:n], AF.Sin, scale=TWO_PI - 1e-5), ["sc_u"], [key_c])

    lr, li, dtv, a, th, rr, f = sm("lr"), sm("li"), sm("dtv"), sm("a_ln"), sm("th"), sm("rr"), sm("f")
    V(lambda e: e.tensor_scalar(lr[:], pc[:, 0, :], -1e-4, None, ALU.min), ["pc"], ["lr"])
    V(lambda e: e.tensor_copy(li[:], pc[:, 1, :]), ["pc"], ["li"])
    P.add("scalar", lambda e: e.activation(dtv[:], pc[:, 2, :], AF.Exp), ["pc"], ["dtv"])
    V(lambda e: e.tensor_tensor(a[:], lr[:], dtv[:], ALU.mult), ["lr", "dtv"], ["a"])
    V(lambda e: e.tensor_tensor(th[:], li[:], dtv[:], ALU.mult), ["li", "dtv"], ["th"])
    P.add("scalar", lambda e: e.activation(rr[:], a[:], AF.Exp), ["a"], ["rr"])
    V(lambda e: e.tensor_scalar(f[:], th[:], 1.0 / TWO_PI, None, ALU.mult), ["th"], ["f"])
    f0, sth, cth = sm("f0"), sm("sth"), sm("cth")
    fracp(f0[:], f[:], 8, "f0", "f")
    sincos(sth[:], cth[:], f0[:], 8, "sth", "cth", "f0")
    nr, ni, den, t8a, t8b, kr, ki, nkr, nki = [sm(n) for n in ("nr", "ni", "den", "t8a", "t8b", "kr", "ki", "nkr", "nki")]
    V(lambda e: e.tensor_tensor(nr[:], rr[:], cth[:], ALU.mult), ["rr", "cth"], ["nr"])
    V(lambda e: e.tensor_scalar(nr[:], nr[:], -1.0, None, ALU.add), ["nr"], ["nr"])
    V(lambda e: e.tensor_tensor(ni[:], rr[:], sth[:], ALU.mult), ["rr", "sth"], ["ni"])
    V(lambda e: e.tensor_tensor(den[:], lr[:], lr[:], ALU.mult), ["lr"], ["den"])
    V(lambda e: e.tensor_tensor(t8a[:], li[:], li[:], ALU.mult), ["li"], ["t8a"])
    V(lambda e: e.tensor_tensor(den[:], den[:], t8a[:], ALU.add), ["den", "t8a"], ["den"])
    V(lambda e: e.reciprocal(den[:], den[:]), ["den"], ["den"])
    V(lambda e: e.tensor_tensor(t8a[:], nr[:], lr[:], ALU.mult), ["nr", "lr"], ["t8a"])
    V(lambda e: e.tensor_tensor(t8b[:], ni[:], li[:], ALU.mult), ["ni", "li"], ["t8b"])
    V(lambda e: e.tensor_tensor(kr[:], t8a[:], t8b[:], ALU.add), ["t8a", "t8b"], ["kr"])
    V(lambda e: e.tensor_tensor(kr[:], kr[:], den[:], ALU.mult), ["kr", "den"], ["kr"])
    V(lambda e: e.tensor_tensor(t8a[:], ni[:], lr[:], ALU.mult), ["ni", "lr"], ["t8a"])
    V(lambda e: e.tensor_tensor(t8b[:], nr[:], li[:], ALU.mult), ["nr", "li"], ["t8b"])
    V(lambda e: e.tensor_tensor(ki[:], t8a[:], t8b[:], ALU.subtract), ["t8a", "t8b"], ["ki"])
    V(lambda e: e.tensor_tensor(ki[:], ki[:], den[:], ALU.mult), ["ki", "den"], ["ki"])
    V(lambda e: e.tensor_scalar(nkr[:], kr[:], -1.0, None, ALU.mult), ["kr"], ["nkr"])
    V(lambda e: e.tensor_scalar(nki[:], ki[:], -1.0, None, ALU.mult), ["ki"], ["nki"])
    a128, a128f = sm("a128"), sm("a128f")
    V(lambda e: e.tensor_scalar(a128[:], f0[:], 128.0, None, ALU.mult), ["f0"], ["a128"])
    fracp(a128f[:], a128[:], 8, "a128f", "a128")
    B16 = P.sb([128, 16, 4, 128], BF16, name="B16")
    CR = P.sb([128, 8, 128], BF16, name="CR16")
    CI = P.sb([128, 8, 128], BF16, name="CI16")
    c32 = [P.sb([128, 2, 128], F32, name=f"c32_{i}") for i in range(2)]
    ctmp = [P.sb([128, 128], F32, name=f"ctmp{i}") for i in range(2)]

    def wsetup(d, gp):
        q = d * 4 + gp
        for ri in range(2):
            C.load_cast(B16[:, q * 2 + ri, :, :], f"B16_{q}_{ri}", B_d[d, gp, ri].rearrange("c p m -> p c m"), 512,
                        view=lambda s_: s_.rearrange("p (c m) -> p c m", m=128))
        cb = c32[q % 2]
        ck = f"c32_{q % 2}"
        P.dma(cb[:], C_d[d, gp].rearrange("r k m -> k r m"), [], [ck], f"ld_c32_{q % 2}")
        tm = ctmp[q % 2]
        tk = f"ctmp{q % 2}"
        V(lambda e: e.tensor_scalar(tm[:], cb[:, 0, :], kr[:, q:q + 1], None, ALU.mult), [ck, "kr"], [tk])
        V(lambda e: e.scalar_tensor_tensor(CR[:, q, :], cb[:, 1, :], nki[:, q:q + 1], tm[:], ALU.mult, ALU.add), [ck, "nki", tk], [f"CR_{q}"])
        V(lambda e: e.tensor_scalar(tm[:], cb[:, 1, :], nkr[:, q:q + 1], None, ALU.mult), [ck, "nkr", f"CR_{q}"], [tk])
        V(lambda e: e.scalar_tensor_tensor(CI[:, q, :], cb[:, 0, :], nki[:, q:q + 1], tm[:], ALU.mult, ALU.add), [ck, "nki", tk], [f"CI_{q}"])

    for d in range(2):
        for gp in range(4):
            wsetup(d, gp)
    sinC = P.sb([128, 8, 66], F32, name="sinC")
    cosC = P.sb([128, 8, 66], F32, name="cosC")
    sinJ = P.sb([128, 8, 128], F32, name="sinJ")
    cosJ = P.sb([128, 8, 128], F32, name="cosJ")

    phA = P.sb([128, 128], F32, name="phA")
    phB = P.sb([128, 128], F32, name="phB")

    def tsetup(q):
        d = q // 4
        V(lambda e: e.tensor_scalar(phA[:, 0:66], iota_c[:, d, :], a128f[:, q:q + 1], None, ALU.mult), ["iota_c", "a128f"], ["phA"])
        fracp(phB[:, 0:66], phA[:, 0:66], 66, "phB", "phA")
        sincos(sinC[:, q, :], cosC[:, q, :], phB[:, 0:66], 66, f"sinC{q}", f"cosC{q}", "phB")
        V(lambda e: e.tensor_scalar(phA[:, 0:128], iota_j[:, d, :], f0[:, q:q + 1], None, ALU.mult), ["iota_j", "f0"], ["phA"])
        fracp(phB[:, 0:128], phA[:, 0:128], 128, "phB", "phA")
        sincos(sinJ[:, q, :], cosJ[:, q, :], phB[:, 0:128], 128, f"sinJ{q}", f"cosJ{q}", "phB")

    for q in range(8):
        tsetup(q)
    NB = 512
    big = lambda name, dtype=F32: P.sb([128, NB], dtype, name=name)
    u16 = [P.sb([128, 4, NB], BF16, name=f"u16_{i}") for i in range(2)]
    St = [big(f"St{i}") for i in range(2)]
    Ct = [big(f"Ct{i}") for i in range(2)]
    tB = big("tB")
    tA2 = [big("tA0"), big("tA1")]
    brs2, bis2 = [big("brs0"), big("brs1")], [big("bis0"), big("bis1")]
    btr2, bti2 = [big("btr0"), big("btr1")], [big("bti0"), big("bti1")]
    gr2, gi2 = [big("gr0"), big("gr1")], [big("gi0"), big("gi1")]
    hr16 = [big(f"hr16_{i}", BF16) for i in range(2)]
    hi16 = [big(f"hi16_{i}", BF16) for i in range(2)]
    carry = P.sb([128, 16], F32, name="carry")
    yb = [P.sb([128, 4, NB], F32, name=f"yb{i}") for i in range(2)]

    def rv(t, n):
        return bass.AP(t, n - 1, [[NB, 128], [-1, n]])

    def gp_body(d, first, lay0, sn, ub, gp, tb, hb, seg_n):
        q = d * 4 + gp
        c0, ncn = lay0 // 128, sn // 128
        nst = (sn + 511) // 512
        S, Cc = St[tb], Ct[tb]
        brs, bis = brs2[tb], bis2[tb]
        ch = gp % 2
        tA, btr, bti, gr, gi = tA2[ch], btr2[ch], bti2[ch], gr2[ch], gi2[ch]
        kA_, kbr, kbi, kgr, kgi = f"tA{ch}", f"btr{ch}", f"bti{ch}", f"gr{ch}", f"gi{ch}"
        yb_ = 4 + (seg_n % 2)
        W = slice(0, sn)
        v3 = lambda t: t[:, 0:sn].rearrange("p (c j) -> p c j", j=128)
        cC = lambda t: t[:, q, c0:c0 + ncn].unsqueeze(2).broadcast_to([128, ncn, 128])
        cJ = lambda t: t[:, q, :].unsqueeze(1).broadcast_to([128, ncn, 128])
        G = "vector"
        P.add(G, lambda e, a=cC(sinC), b=cJ(cosJ), v=v3(S): e.tensor_tensor(v, a, b, ALU.mult), [f"sinC{q}", f"cosJ{q}"], [f"St{tb}"])
        P.add(G, lambda e, a=cC(cosC), b=cJ(sinJ), v=v3(tB): e.tensor_tensor(v, a, b, ALU.mult), [f"cosC{q}", f"sinJ{q}"], ["tBg"])
        P.add(G, lambda e: e.tensor_tensor(S[:, W], S[:, W], tB[:, W], ALU.add), [f"St{tb}", "tBg"], [f"St{tb}"])
        P.add(G, lambda e, a=cC(cosC), b=cJ(cosJ), v=v3(Cc): e.tensor_tensor(v, a, b, ALU.mult), [f"cosC{q}", f"cosJ{q}"], [f"Ct{tb}"])
        P.add(G, lambda e, a=cC(sinC), b=cJ(sinJ), v=v3(tB): e.tensor_tensor(v, a, b, ALU.mult), [f"sinC{q}", f"sinJ{q}"], ["tBg"])
        P.add(G, lambda e: e.tensor_tensor(Cc[:, W], Cc[:, W], tB[:, W], ALU.subtract), [f"Ct{tb}", "tBg"], [f"Ct{tb}"])
        for st in range(nst):
            n0 = st * 512
            nn = min(512, sn - n0)
            for ri, (dst, dk) in enumerate(((brs, f"brs{tb}_"), (bis, f"bis{tb}_"))):
                pbi = ri * 2 + tb
                pb = C.psum[pbi]
                for c in range(4):
                    P.add("tensor", lambda e, pb=pb, ri=ri, n0=n0, nn=nn, c=c: e.matmul(pb[:, 0:nn], B16[:, q * 2 + ri, c, :], u16[ub][:, c, n0:n0 + nn], start=(c == 0), stop=(c == 3)),
                          [f"B16_{q}_{ri}", f"u16_{ub}_{c}"], [f"psb{pbi}"])
                P.add("scalar", lambda e, pb=pb, dst=dst, n0=n0, nn=nn: e.copy(dst[:, n0:n0 + nn], pb[:, 0:nn]), [f"psb{pbi}"], [f"{dk}{st}"])
        assert nst == 1
        bk = [f"brs{tb}_{st}" for st in range(nst)]
        ik = [f"bis{tb}_{st}" for st in range(nst)]
        V(lambda e: e.tensor_tensor(tA[:, W], Cc[:, W], brs[:, W], ALU.mult), [f"Ct{tb}"] + bk, [kA_])
        yield
        V(lambda e: e.tensor_tensor(btr[:, W], S[:, W], bis[:, W], ALU.mult), [f"St{tb}"] + ik, [kbr])
        yield
        V(lambda e: e.tensor_tensor(btr[:, W], btr[:, W], tA[:, W], ALU.add), [kbr, kA_], [kbr])
        yield
        V(lambda e: e.tensor_tensor(tA[:, W], Cc[:, W], bis[:, W], ALU.mult), [f"Ct{tb}"] + ik, [kA_])
        yield
        V(lambda e: e.tensor_tensor(bti[:, W], S[:, W], brs[:, W], ALU.mult), [f"St{tb}"] + bk, [kbi])
        yield
        V(lambda e: e.tensor_tensor(bti[:, W], tA[:, W], bti[:, W], ALU.subtract), [kbi, kA_], [kbi])
        yield
        rcol = rr[:, q:q + 1].broadcast_to([128, sn])
        for (g_, bt_, gk, btk, ci) in ((gr, btr, kgr, kbr, 2 * q), (gi, bti, kgi, kbi, 2 * q + 1)):
            init = 0.0 if first else carry[:, ci:ci + 1]
            if d == 0:
                V(lambda e, g_=g_, bt_=bt_, init=init: e.tensor_tensor_scan(g_[:, W], rcol, bt_[:, W], init, ALU.mult, ALU.add), [btk, "rr", f"carry{ci}"], [gk])
                yield
                P.add("scalar", lambda e, g_=g_, ci=ci: e.copy(carry[:, ci:ci + 1], g_[:, sn - 1:sn]), [gk], [f"carry{ci}"])
            else:
                V(lambda e, g_=g_, bt_=bt_, init=init: e.tensor_tensor_scan(rv(g_, sn), rcol, rv(bt_, sn), init, ALU.mult, ALU.add), [btk, "rr", f"carry{ci}"], [gk])
                yield
                P.add("scalar", lambda e, g_=g_, ci=ci: e.copy(carry[:, ci:ci + 1], g_[:, 0:1]), [gk], [f"carry{ci}"])
        hr_, hi_ = hr16[hb], hi16[hb]
        V(lambda e: e.tensor_tensor(tA[:, W], Cc[:, W], gr[:, W], ALU.mult), [f"Ct{tb}", kgr], [kA_])
        yield
        V(lambda e: e.tensor_tensor(btr[:, W], S[:, W], gi[:, W], ALU.mult), [f"St{tb}", kgi], [kbr])
        yield
        V(lambda e: e.tensor_tensor(hr_[:, W], tA[:, W], btr[:, W], ALU.subtract), [kA_, kbr], [f"hr16_{hb}"])
        yield
        V(lambda e: e.tensor_tensor(tA[:, W], S[:, W], gr[:, W], ALU.mult), [f"St{tb}", kgr], [kA_])
        yield
        V(lambda e: e.tensor_tensor(bti[:, W], Cc[:, W], gi[:, W], ALU.mult), [f"Ct{tb}", kgi], [kbi])
        yield
        V(lambda e: e.tensor_tensor(hi_[:, W], tA[:, W], bti[:, W], ALU.add), [kA_, kbi], [f"hi16_{hb}"])
        yield
        for st in range(nst):
            n0 = st * 512
            nn = min(512, sn - n0)
            py = C.psum[yb_]
            P.add("tensor", lambda e, py=py, n0=n0, nn=nn: e.matmul(py[:, 0:nn], CR[:, q, :], hr_[:, n0:n0 + nn], start=(gp == 0), stop=False),
                  [f"CR_{q}", f"hr16_{hb}"], [f"psb{yb_}"])
            P.add("tensor", lambda e, py=py, n0=n0, nn=nn: e.matmul(py[:, 0:nn], CI[:, q, :], hi_[:, n0:n0 + nn], start=False, stop=(gp == 3)),
                  [f"CI_{q}", f"hi16_{hb}"], [f"psb{yb_}"])

    def seg_body(d, si, first, kind, s, ub, it0, seg_n):
        if kind == "ctx":
            sn, r, t0 = 256, 0, 0
            lay0 = 0 if d == 0 else 4 * TLAT
        else:
            sn, r, t0 = 512, s // 4, TCTX + 512 * (s % 4)
            lay0 = (TCTX + 512 * s) if d == 0 else 512 * s
        for c in range(4):
            usrc = uA_all[c][r * 128:(r + 1) * 128, t0:t0 + sn] if t0 < 1280 else uB_all[c][r * 128:(r + 1) * 128, t0 - 1280:t0 - 1280 + sn]
            C.load_cast(u16[ub][:, c, 0:sn], f"u16_{ub}_{c}", usrc, sn, cast_eng="scalar")
        nst = (sn + 511) // 512
        it = it0
        for gp0 in (0, 2):
            gens = []
            for gp in (gp0, gp0 + 1):
                tb = it % 2
                it += 1
                gens.append(gp_body(d, first, lay0, sn, ub, gp, tb, it % 2, seg_n))
            while gens:
                for g_ in list(gens):
                    try:
                        next(g_)
                    except StopIteration:
                        gens.remove(g_)
        ybuf = yb[ub]
        ybk = 4 + (seg_n % 2)
        if kind == "ctx":
            P.add("scalar", lambda e: e.copy(ybuf[:, 0, 0:256], C.psum[ybk][:, 0:256]), [f"psb{ybk}"], [f"yb{ub}"])
            P.dma(yc_loc[:, d * 256:(d + 1) * 256], ybuf[:, 0, 0:256], [f"yb{ub}"], [], f"st_yb{ub}")
        else:
            for c in range(4):
                P.add("scalar", lambda e, c=c: e.activation(ybuf[:, c, 0:512], C.psum[ybk][:, 0:512], AF.Identity, scale=oh[:, c:c + 1]),
                      [f"psb{ybk}", "oh"], [f"yb{ub}"])
            for c in range(4):
                P.dma(y_rs_in[d][c][(s // 4) * 128:(s // 4 + 1) * 128, 512 * (s % 4):512 * (s % 4) + 512], ybuf[:, c, 0:512], [f"yb{ub}"], [], f"st_yb{ub}_{c}")
        return it

    it = 0
    n = 0
    for d in range(2):
        order = [("ctx", 0)] + ([("lat", s) for s in range(16)] if d == 0 else [("lat", s) for s in range(15, -1, -1)])
        for si, (kind, s) in enumerate(order):
            it = seg_body(d, si, si == 0, kind, s, n % 2, it, n)
            n += 1


def phase_po(P, C, layer, xT, cols, Dr, tiles=TILES):
    l0 = layer == 0
    o_loc, wo_d = Dr["o_loc"], Dr["wo"][layer]
    wo16 = P.sb([128, 8, 1024], BF16, name="wo16")
    for k in range(8):
        C.load_cast(wo16[:, k, :], f"wo16_{k}", wo_d[k], 1024)
    nch = 4 if l0 else 8
    ot = [P.sb([128, nch, 512], BF16, name=f"ot{i}") for i in range(2)]
    if l0:
        uA, uB, y_rs_out, yc_all = Dr["uA"], Dr["uB"], Dr["y_rs_out"], Dr["yc_all"]
        g0 = P.sb([128, 4, 512], BF16, name="glu0_sb")
        g1 = P.sb([128, 4, 512], BF16, name="glu1_sb")
        for k in range(4):
            C.load_cast(g0[:, k, :], f"g0_{k}", Dr["glu0"][k], 512)
            C.load_cast(g1[:, k, :], f"g1_{k}", Dr["glu1"][k], 512)
        dcol = P.sb([128, 4], F32, name="dcol_sb")
        C.load(dcol[:], "dcol", Dr["dcol"])
        ub = [P.sb([128, 512], F32, name=f"ub{i}") for i in range(2)]
        yfb = [P.sb([128, 512], F32, name=f"yfb{i}") for i in range(2)]
        yrb = [P.sb([128, 512], F32, name=f"yrb{i}") for i in range(2)]
        t1 = [P.sb([128, 512], F32, name=f"t1_{i}") for i in range(2)]
        gt = [P.sb([128, 4, 512], BF16, name=f"gt{i}") for i in range(2)]
        glt = [P.sb([128, 4, 512], BF16, name=f"glt{i}") for i in range(2)]
        sgb = [P.sb([128, 512], F32, name=f"sgb{i}") for i in range(2)]
    it = 0
    for ti, (t0, tn) in enumerate(tiles):
        v = "c" if t0 < TCTX else "x"
        tb = ti % 2
        P.dma(ot[tb][:, :, 0:tn], o_loc[:, 0:nch, t0:t0 + tn], [], [f"ot{tb}"], f"ld_ot{tb}")
        if l0:
            for c in range(4):
                b = it % 2
                it += 1
                if t0 < TCTX:
                    yf_src = yc_all[c * 128:(c + 1) * 128, 0:256]
                    yr_src = yc_all[c * 128:(c + 1) * 128, 256:512]
                else:
                    yf_src = y_rs_out[0][c][:, t0 - TCTX:t0 - TCTX + tn]
                    yr_src = y_rs_out[1][c][:, t0 - TCTX:t0 - TCTX + tn]
                u_src = uA[c][:, t0:t0 + tn] if t0 < 1280 else uB[c][:, t0 - 1280:t0 - 1280 + tn]
                P.dma(ub[b][:, 0:tn], u_src, [], [f"ub{b}"], f"ld_ub{b}")
                P.dma(yfb[b][:, 0:tn], yf_src, [], [f"yfb{b}"], f"ld_yfb{b}")
                P.dma(yrb[b][:, 0:tn], yr_src, [], [f"yrb{b}"], f"ld_yrb{b}")
                y, u_, yr_, tt = yfb[b], ub[b], yrb[b], t1[b]
                P.add("vector", lambda e, y=y, yr_=yr_, tn=tn: e.tensor_tensor(y[:, 0:tn], y[:, 0:tn], yr_[:, 0:tn], ALU.add), [f"yfb{b}", f"yrb{b}"], [f"yfb{b}"])
                P.add("vector", lambda e, y=y, u_=u_, c=c, tn=tn: e.scalar_tensor_tensor(y[:, 0:tn], u_[:, 0:tn], dcol[:, c:c + 1], y[:, 0:tn], ALU.mult, ALU.add),
                      [f"yfb{b}", f"ub{b}", "dcol"], [f"yfb{b}"])
                P.add("scalar", lambda e, tt=tt, y=y, tn=tn: e.activation(tt[:, 0:tn], y[:, 0:tn], AF.Square), [f"yfb{b}"], [f"t1_{b}"])
                P.add("vector", lambda e, tt=tt, tn=tn: e.tensor_scalar(tt[:, 0:tn], tt[:, 0:tn], 0.044715, 1.0, ALU.mult, ALU.add), [f"t1_{b}"], [f"t1_{b}"])
                P.add("vector", lambda e, tt=tt, y=y, tn=tn: e.tensor_tensor(tt[:, 0:tn], tt[:, 0:tn], y[:, 0:tn], ALU.mult), [f"t1_{b}", f"yfb{b}"], [f"t1_{b}"])
                P.add("scalar", lambda e, tt=tt, tn=tn: e.activation(tt[:, 0:tn], tt[:, 0:tn], AF.Sigmoid, scale=1.5957691216), [f"t1_{b}"], [f"t1_{b}"])
                P.add("vector", lambda e, tt=tt, y=y, c=c, tb=tb, tn=tn: e.tensor_tensor(gt[tb][:, c, 0:tn], tt[:, 0:tn], y[:, 0:tn], ALU.mult), [f"t1_{b}", f"yfb{b}"], [f"gt{tb}_{c}"])
            for oc in range(4):
                pb = oc % 2
                pa, pg = C.psum[pb], C.psum[2 + pb]
                for k in range(4):
                    P.add("tensor", lambda e, pa=pa, k=k, oc=oc, tb=tb, tn=tn: e.matmul(pa[:, 0:tn], g0[:, k, oc * 128:(oc + 1) * 128], gt[tb][:, k, 0:tn], start=(k == 0), stop=(k == 3)),
                          [f"g0_{k}", f"gt{tb}_{k}"], [f"psb{pb}"])
                for k in range(4):
                    P.add("tensor", lambda e, pg=pg, k=k, oc=oc, tb=tb, tn=tn: e.matmul(pg[:, 0:tn], g1[:, k, oc * 128:(oc + 1) * 128], gt[tb][:, k, 0:tn], start=(k == 0), stop=(k == 3)),
                          [f"g1_{k}", f"gt{tb}_{k}"], [f"psb{2 + pb}"])
                sg = sgb[pb]
                P.add("scalar", lambda e, sg=sg, pg=pg, tn=tn: e.activation(sg[:, 0:tn], pg[:, 0:tn], AF.Sigmoid), [f"psb{2 + pb}"], [f"sgb{pb}"])
                P.add("vector", lambda e, sg=sg, pa=pa, oc=oc, tb=tb, tn=tn: e.tensor_tensor(glt[tb][:, oc, 0:tn], sg[:, 0:tn], pa[:, 0:tn], ALU.mult), [f"sgb{pb}", f"psb{pb}"], [f"glt{tb}_{oc}"])
        for oc in range(8):
            pb = 4 + oc % 2
            po = C.psum[pb]
            srcs = ([(glt[tb][:, k, 0:tn], f"glt{tb}_{k}", k) for k in range(4)] if l0 else []) + \
                   [(ot[tb][:, k, 0:tn], f"ot{tb}", (4 + k) if l0 else k) for k in range(nch)]
            for n, (rhs, rk, kk) in enumerate(srcs):
                P.add("tensor", lambda e, po=po, rhs=rhs, kk=kk, oc=oc, tn=tn, n=n, ns=len(srcs): e.matmul(po[:, 0:tn], wo16[:, kk, oc * 128:(oc + 1) * 128], rhs, start=(n == 0), stop=(n == ns - 1)),
                      [f"wo16_{kk}", rk], [f"psb{pb}"])
            P.add("vector", lambda e, po=po, oc=oc, t0=t0, tn=tn, v=v: e.scalar_tensor_tensor(xT[:, oc, t0:t0 + tn], po[:, 0:tn], colsel(cols, v, "gate", 1, oc),
                                                                                             xT[:, oc, t0:t0 + tn], ALU.mult, ALU.add),
                  [f"psb{pb}", f"gate_{v}", f"x{oc}_{ti}"], [f"x{oc}_{ti}"])


GROUPS = [[0, 1, 2, 3], [4, 5, 6, 7]]


def build_fused(stop=999):
    nc = bass.Bass("TRN2", target_bir_lowering=False)
    ext = lambda name, shape, dtype=F32: nc.dram_tensor(name, list(shape), dtype, kind="ExternalInput").ap()
    scr = lambda name, shape, dtype=F32: nc.dram_tensor(name, list(shape), dtype).ap()
    xT_d = ext("xT", [128, 8, T])
    cT_d = ext("cT", [128, 16])
    modw_d = ext("modw", [36, 128, 8, 128])
    modb_d = ext("modb", [128, 36])
    normg_d = ext("normg", [128, 2, 24])
    wg_d = ext("wg", [4, FC, 128, 8, 128])
    wu_d = ext("wu", [4, FC, 128, 8, 128])
    wd_d = ext("wd", [4, FC, 128, 1024])
    win_d = [ext("win0", [10, 128, 8, 128]), ext("win1", [12, 128, 8, 128])]
    qkg_d = ext("qkg", [128, 2, 2])
    cos_d, sin_d = ext("cosT", [128, T]), ext("sinT", [128, T])
    rot_d, bones_d = ext("rot", [128, 128]), ext("bones", [128, 128])
    Dr = dict(
        Bl=ext("Bl", [2, 4, 2, 4, 128, 128]), Cl=ext("Cl", [2, 4, 2, 128, 128]), pcols=ext("pcols", [128, 3, 8]),
        iota_c=ext("iota_c", [128, 2, 66]), iota_j=ext("iota_j", [128, 2, 128]), onehot=ext("onehot", [128, 4]),
        dcol=ext("dcol", [128, 4]), glu0=ext("glu0", [4, 128, 512]), glu1=ext("glu1", [4, 128, 512]),
        wo=ext("wo", [2, 8, 128, 1024]), sink=ext("sink", [128, 16]),
        wmask=ext("wmask", [128, 4, 6, 512], BF16), emask=ext("emask", [128, 2, 4, 512], BF16),
    )
    out_d = nc.dram_tensor("outT", [128, 8, TLAT], F32, kind="ExternalOutput").ap()
    Dr.update(
        mod_loc=scr("mod_loc", [128, 72]), mod_all=scr("mod_all", [512, 72]),
        uA=[scr(f"uA{c}", [128, 1280]) for c in range(4)], uB=[scr(f"uB{c}", [128, 1024]) for c in range(4)],
        uA_all=[scr(f"uA_all{c}", [512, 1280]) for c in range(4)], uB_all=[scr(f"uB_all{c}", [512, 1024]) for c in range(4)],
        q_loc=scr("q_loc", [128, 8, T], BF16),
        k_loc=scr("k_loc", [128, 2, T]), kA=scr("kA", [128, 1280]), kB=scr("kB", [128, 1024]),
        kA_all=scr("kA_all", [512, 1280]), kB_all=scr("kB_all", [512, 1024]),
        v_loc=scr("v_loc", [T, 256]), vA=scr("vA", [1280, 128]), vB=scr("vB", [1024, 128]),
        vA_all=scr("vA_all", [5120, 128]), vB_all=scr("vB_all", [4096, 128]),
        y_rs_in=[[scr(f"y_rs_in{d}{c}", [512, TLAT]) for c in range(4)] for d in range(2)],
        y_rs_out=[[scr(f"y_rs_out{d}{c}", [128, TLAT]) for c in range(4)] for d in range(2)],
        yc_loc=scr("yc_loc", [128, 512]), yc_all=scr("yc_all", [512, 512]),
        o_loc=scr("o_loc", [128, 8, T], BF16),
        ek_loc=scr("ek_loc", [128, 512]), ek_all=scr("ek_all", [512, 512]),
        ev_loc=scr("ev_loc", [256, 256]), ev_all=scr("ev_all", [1024, 256]),
    )
    P = Prog(nc)
    xT = P.sb([128, 8, T], F32, name="xT_sb", persist=True)
    modall = P.sb([128, 2, 72, 2], F32, name="modall", persist=True)
    ng = P.sb([128, 2, 24], F32, name="normg_sb", persist=True)
    coltiles = {(l, v): (P.sb([128, 24], F32, name=f"gs_{v}{l}", persist=True), P.sb([128, 24], F32, name=f"gate_{v}{l}", persist=True))
                for l in range(2) for v in ("x", "c")}
    C = Ctx(P)

    c32 = P.sb([128, 16], F32, name="c32")
    c16 = P.sb([128, 16], BF16, name="c16")
    bt = P.sb([128, 36], F32, name="bt")
    res = P.sb([128, 36, 2], F32, name="res")
    w16 = [P.sb([128, 8, 128], BF16, name=f"w16_{i}") for i in range(2)]
    for c in range(KC):
        for ti, (t0, tn) in enumerate(TILES):
            P.dma(xT[:, c, t0:t0 + tn], xT_d[:, c, t0:t0 + tn], [], [f"x{c}_{ti}"], f"ld_x{(c * 5 + ti) % 4}")
    C.load(c32[:], "c32", cT_d)
    C.load(bt[:], "bt", modb_d)
    C.load(ng[:], "normg", normg_d)
    P.add("scalar", lambda e: e.activation(c16[:], c32[:], AF.Silu), ["c32"], ["c16"])
    for oc in range(36):
        sl = oc % 2
        C.load_cast(w16[sl][:].rearrange("p k m -> p (k m)"), f"w16_{sl}", modw_d[oc].rearrange("p k m -> p (k m)"), 1024)
        ps = C.psum[oc % 2]
        for k in range(8):
            P.add("tensor", lambda e, ps=ps, k=k, sl=sl: e.matmul(ps[:, 0:2], w16[sl][:, k, :], c16[:, k * 2:(k + 1) * 2], start=(k == 0), stop=(k == 7)),
                  [f"w16_{sl}", "c16"], [f"psb{oc % 2}"])
        P.add("vector", lambda e, ps=ps, oc=oc: e.tensor_scalar(res[:, oc, :], ps[:, 0:2], bt[:, oc:oc + 1], None, ALU.add),
              [f"psb{oc % 2}", "bt"], ["res"])
    P.dma(Dr["mod_loc"], res[:].rearrange("p a b -> p (a b)"), ["res"], ["mod_loc"], "st_mod")
    P.coll("AllGather", ALU.bypass, GROUPS, Dr["mod_loc"], Dr["mod_all"], ["mod_loc"], ["mod_all"], "cc_mod")
    for l in range(2):
        for r2 in range(2):
            r = 2 * l + r2
            C.load(modall[:, l, 36 * r2:36 * (r2 + 1), :].rearrange("p a b -> p (a b)"), "modall", Dr["mod_all"][r * 128:(r + 1) * 128, :], reads=["mod_all"])
    cols = [make_cols(P, modall, ng, l, coltiles) for l in range(2)]
    P.end_phase()
    if stop == 1:
        P.close()
        return nc

    def phase_ffn(layer, f_idx, s, tiles=TILES):
        C.new_phase()
        K = alloc_common(P, C)
        hT = P.sb([128, 8, T], BF16, name="hT_sb")
        emit_norm(P, C, K, xT, hT, cols[layer], s, tiles=tiles)
        emit_ffn(P, C, K, xT, hT, cols[layer], s, wg_d[2 * layer + f_idx], wu_d[2 * layer + f_idx], wd_d[2 * layer + f_idx], tiles=tiles)
        return K, hT

    def phase_a(layer):
        n_u, n_q, n_k, vw = (4, 4, 1, 128) if layer == 0 else (0, 8, 2, 256)
        n_fm = n_u + n_q + n_k
        K, hT = phase_ffn(layer, 0, 0)
        emit_norm(P, C, K, xT, hT, cols[layer], 1)
        emit_consts(P, C, K, rot_d, bones_d)
        qkg = P.sb([128, 2, 2], F32, name="qkg_sb")
        C.load(qkg[:], "qkg", qkg_d)
        K["cst"] = [P.sb([128, 512], F32, name=f"cst{i}") for i in range(1)]
        K["snt"] = [P.sb([128, 512], F32, name=f"snt{i}") for i in range(1)]
        K["win16"] = [P.sb([128, 8, 128], BF16, name=f"win16_{i}") for i in range(2)]
        K["zq"] = [P.sb([128, 512], F32, name=f"zq{i}") for i in range(2)]
        K["zb"] = [P.sb([128, 512], BF16, name=f"zb{i}") for i in range(2)]
        K["ob32"] = [P.sb([128, 512], F32, name=f"ob32_{i}") for i in range(2)]
        K["ob16"] = [P.sb([128, 512], BF16, name=f"ob16_{i}") for i in range(2)]
        K["wv16"] = P.sb([128, 8, vw], BF16, name="wv16")
        K["vb"] = [P.sb([128, 256], F32, name="vb0")] * 2
        kinds = ["u"] * n_u + ["q"] * n_q + ["k"] * n_k
        def split_dst(a_, b_):
            return lambda t0, tn: (a_[:, t0:t0 + tn] if t0 < 1280 else b_[:, t0 - 1280:t0 - 1280 + tn])
        kdst = [split_dst(Dr["kA"], Dr["kB"])] if layer == 0 else [Dr["k_loc"][:, i, :] for i in range(2)]
        outs = [(split_dst(Dr["uA"][i], Dr["uB"][i]), F32) for i in range(n_u)] + [(Dr["q_loc"][:, i, :], BF16) for i in range(n_q)] + [(kd_, F32) for kd_ in kdst]
        if layer == 0:
            v_d = [Dr["vA"][tt * 128:(tt + 1) * 128, :] if tt < 10 else Dr["vB"][(tt - 10) * 128:(tt - 9) * 128, :] for tt in range(T // 128)]
        else:
            v_d = Dr["v_loc"].rearrange("(tt p) f -> tt p f", p=128)
        emit_inproj(P, C, K, hT, win_d[layer], n_fm, kinds, qkg[:, layer, :], cos_d, sin_d, outs, v_d, vw, n_fm)
        P.end_phase()

    phase_a(0)
    if stop == 2:
        P.close()
        return nc
    for c in range(4):
        P.coll("AllGather", ALU.bypass, GROUPS, Dr["uA"][c], Dr["uA_all"][c], [], [f"uA_all{c}"], "cc_u")
        P.coll("AllGather", ALU.bypass, GROUPS, Dr["uB"][c], Dr["uB_all"][c], [], [f"uB_all{c}"], "cc_u")
    P.coll("AllGather", ALU.bypass, GROUPS, Dr["kA"], Dr["kA_all"], [], ["kA_all"], "cc_k")
    P.coll("AllGather", ALU.bypass, GROUPS, Dr["kB"], Dr["kB_all"], [], ["kB_all"], "cc_k")
    P.coll("AllGather", ALU.bypass, GROUPS, Dr["vA"], Dr["vA_all"], [], ["vA_all"], "cc_v")
    P.coll("AllGather", ALU.bypass, GROUPS, Dr["vB"], Dr["vB_all"], [], ["vB_all"], "cc_v")
    P.end_phase()
    if stop == 3:
        P.close()
        return nc
    C.new_phase()
    phase_s5(P, C, Dr)
    P.end_phase()
    if stop == 4:
        P.close()
        return nc
    for d in range(2):
        for c in range(4):
            P.coll("ReduceScatter", ALU.add, GROUPS, Dr["y_rs_in"][d][c], Dr["y_rs_out"][d][c], [], [f"y_rs_out{d}{c}"], "cc_y")
    P.coll("AllGather", ALU.bypass, GROUPS, Dr["yc_loc"], Dr["yc_all"], [], ["yc_all"], "cc_yc")
    P.end_phase()
    if stop == 5:
        P.close()
        return nc
    C.new_phase()
    phase_att(P, C, 0, Dr)
    P.end_phase()
    if stop == 6:
        P.close()
        return nc
    C.new_phase()
    phase_po(P, C, 0, xT, cols[0], Dr)
    P.end_phase()
    if stop == 7:
        P.close()
        return nc
    phase_ffn(0, 1, 2)
    P.end_phase()
    if stop == 8:
        P.close()
        return nc
    phase_a(1)
    if stop == 9:
        P.close()
        return nc
    for c in range(2):
        for lh, col0 in ((0, TCTX), (1, T - 128)):
            P.dma(Dr["ek_loc"][:, c * 256 + lh * 128:c * 256 + (lh + 1) * 128], Dr["k_loc"][:, c, col0:col0 + 128], [], ["ek_loc"], f"cp_ek{c}{lh}")
    for lh, col0 in ((0, TCTX), (1, T - 128)):
        P.dma(Dr["ev_loc"][lh * 128:(lh + 1) * 128, :], Dr["v_loc"][col0:col0 + 128, :], [], ["ev_loc"], f"cp_ev{lh}")
    P.coll("AllGather", ALU.bypass, GROUPS, Dr["ek_loc"], Dr["ek_all"], ["ek_loc"], ["ek_all"], "cc_ek")
    P.coll("AllGather", ALU.bypass, GROUPS, Dr["ev_loc"], Dr["ev_all"], ["ev_loc"], ["ev_all"], "cc_ev")
    P.end_phase()
    if stop == 10:
        P.close()
        return nc
    C.new_phase()
    phase_att(P, C, 1, Dr)
    P.end_phase()
    if stop == 11:
        P.close()
        return nc
    C.new_phase()
    phase_po(P, C, 1, xT, cols[1], Dr, tiles=TILES[1:])
    P.end_phase()
    if stop == 12:
        P.close()
        return nc
    phase_ffn(1, 1, 2, tiles=TILES[1:])
    for c in range(KC):
        for ti, (t0, tn) in enumerate(TILES[1:]):
            P.dma(out_d[:, c, t0 - TCTX:t0 - TCTX + tn], xT[:, c, t0:t0 + tn], [f"x{c}_{ti}"], [], f"st_x{(c * 5 + ti) % 4}")
    P.end_phase()
    if stop == 13:
        P.close()
        return nc
    P.close()
    return nc
def fm(x2d):
    t, f = x2d.shape
    return np.ascontiguousarray(x2d.T.reshape(f // 128, 128, t).transpose(1, 0, 2))


def unfm(a):
    p, c, t = a.shape
    return np.ascontiguousarray(a.transpose(1, 0, 2).reshape(c * 128, t).T)


def w_oc(W):
    k, n = W.shape
    return np.ascontiguousarray(W.reshape(k // 128, 128, n // 128, 128).transpose(2, 1, 0, 3))


def w_rows(W):
    k, n = W.shape
    return np.ascontiguousarray(W.reshape(k // 128, 128, n))


def rope_tables():
    rows = 8192 // 64
    r = np.repeat(np.arange(rows, dtype=np.float32), 64)
    col = np.tile(np.arange(64, dtype=np.float32), rows)
    inv = (10000.0 ** (-np.arange(16, dtype=np.float32) / 16)).astype(np.float32)
    ang = np.concatenate([r[:, None] * inv, col[:, None] * inv], axis=-1).astype(np.float32)
    cos = np.cos(ang).astype(np.float32).T
    sin = np.sin(ang).astype(np.float32).T
    idx = np.arange(128) % 32
    return cos[idx], sin[idx]


def const_mats():
    rot = np.zeros((128, 128), np.float32)
    for m in range(128):
        j = m % 64
        if j < 32:
            rot[m + 32, m] = -1.0
        else:
            rot[m - 32, m] = 1.0
    bones = np.zeros((128, 128), np.float32)
    bones[:64, :64] = 1.0
    bones[64:, 64:] = 1.0
    return rot, bones


def core_tokens(inp_x, ctx, i):
    b, q = i // 4, i % 4
    return np.concatenate([ctx[b], inp_x[b, q * TLAT:(q + 1) * TLAT]], axis=0)


def fused_inputs(inp):
    cos, sin = rope_tables()
    rot, bones = const_mats()
    W = np.concatenate([inp["mod_w"][0], inp["mod_w"][1]], axis=1)
    Bv = np.concatenate([inp["mod_b"][0], inp["mod_b"][1]], axis=0)
    Wl = W.reshape(8, 128, 144, 128).transpose(2, 1, 0, 3)
    Bl_mod = Bv.reshape(144, 128).T
    normg = np.ascontiguousarray(np.stack([inp["norm_g"][l].reshape(3, 8, 128).transpose(2, 0, 1).reshape(128, 24) for l in range(2)], axis=1))
    wg = np.stack([w_oc(inp["ffn_wg"][l, f]) for l in range(2) for f in range(2)])
    wu = np.stack([w_oc(inp["ffn_wu"][l, f]) for l in range(2) for f in range(2)])
    wd = np.stack([w_rows(inp["ffn_wd"][l, f]) for l in range(2) for f in range(2)])
    win0, win1 = w_oc(inp["ab_w_in"][0]), w_oc(inp["win_w_in"][0])
    qkg = np.ascontiguousarray(np.stack([inp["qk_norm"][l][:, np.arange(128) % 64].T for l in range(2)], axis=1))
    wo = np.stack([w_rows(inp["w_out"][l]) for l in range(2)])
    dcol = np.ascontiguousarray(inp["s5_d"][0].reshape(4, 128).T)
    glu0, glu1 = w_rows(inp["s5_glu_w"][0, 0]), w_rows(inp["s5_glu_w"][0, 1])
    sink = np.ascontiguousarray(np.broadcast_to(inp["win_sink"][0].reshape(1, 16), (128, 16))).astype(np.float32)
    iota_c = np.zeros((128, 2, 66), np.float32)
    iota_c[:, 0, :] = np.arange(66)
    iota_c[:, 1, :] = 65 - np.arange(66)
    iota_j = np.zeros((128, 2, 128), np.float32)
    iota_j[:, 0, :] = np.arange(128)
    iota_j[:, 1, :] = 127 - np.arange(128)
    one, zero = np.ones((1,), NPBF)[0], np.zeros((1,), NPBF)[0]
    kk = np.arange(128)[:, None]
    qq = np.arange(512)[None, :]
    wmask = np.zeros((128, 4, 6, 512), NPBF)
    for m in range(4):
        for r in range(6):
            ok = np.abs((4 * m + r - 1) * 128 + kk - (512 * m + qq)) <= 128
            wmask[:, m, r, :] = np.where(ok, one, zero)
    e = 0
    maps = []
    for i in range(NCORES):
        b, q = i // 4, i % 4
        cT = np.ascontiguousarray(np.stack([inp["c"][b], inp["c_ctx"]], axis=0).T.reshape(8, 128, 2).transpose(1, 0, 2).reshape(128, 16))
        cosT = np.concatenate([np.ones((128, TCTX), np.float32), cos[:, q * TLAT:(q + 1) * TLAT]], axis=1)
        sinT = np.concatenate([np.zeros((128, TCTX), np.float32), sin[:, q * TLAT:(q + 1) * TLAT]], axis=1)
        Bl = np.zeros((2, 4, 2, 4, 128, 128), np.float32)
        Cl = np.zeros((2, 4, 2, 128, 128), np.float32)
        pcols = np.zeros((128, 3, 8), np.float32)
        j = q
        for d in range(2):
            for gp in range(4):
                for gl in range(2):
                    g = 8 * j + 2 * gp + gl
                    ch0 = (2 * gp + gl) * 16
                    for ri, (bsrc, csrc) in enumerate(((inp["s5_b_re"], inp["s5_c_re"]), (inp["s5_b_im"], inp["s5_c_im"]))):
                        Bl[d, gp, ri, j, ch0:ch0 + 16, gl * 64:(gl + 1) * 64] = bsrc[e, d, g].T
                        Cl[d, gp, ri, gl * 64:(gl + 1) * 64, ch0:ch0 + 16] = csrc[e, d, g].T
                    pcols[gl * 64:(gl + 1) * 64, 0, d * 4 + gp] = inp["s5_lam_re"][e, d, g]
                    pcols[gl * 64:(gl + 1) * 64, 1, d * 4 + gp] = inp["s5_lam_im"][e, d, g]
                    pcols[gl * 64:(gl + 1) * 64, 2, d * 4 + gp] = inp["s5_log_step"][e, d, g]
        onehot = np.zeros((128, 4), np.float32)
        onehot[:, q] = 1.0
        emask = np.zeros((128, 2, 4, 512), NPBF)
        for r in range(4):
            if r == q - 1:
                emask[:, 0, r, :] = np.where(np.abs(-128 + kk - qq) <= 128, one, zero)
            if r == q + 1:
                emask[:, 1, r, :] = np.where(np.abs(2048 + kk - 1536 - qq) <= 128, one, zero)
        maps.append(dict(
            xT=fm(core_tokens(inp["x"], inp["ctx"], i)), cT=cT, modw=np.ascontiguousarray(Wl[36 * q:36 * q + 36]),
            modb=np.ascontiguousarray(Bl_mod[:, 36 * q:36 * q + 36]), normg=normg, wg=wg, wu=wu, wd=wd, win0=win0, win1=win1, qkg=qkg,
            cosT=np.ascontiguousarray(cosT), sinT=np.ascontiguousarray(sinT), rot=rot, bones=bones,
            Bl=Bl, Cl=Cl, pcols=pcols, iota_c=iota_c, iota_j=iota_j, onehot=onehot, dcol=dcol, glu0=glu0, glu1=glu1, wo=wo, sink=sink,
            wmask=wmask, emask=emask))
    return maps


def kernel(**inputs):
    inp = {k: np.asarray(v) for k, v in inputs.items()}
    nc = build_fused()
    res = run_bass_kernel_spmd(nc, fused_inputs(inp), core_ids=list(range(NCORES)))
    out = np.zeros((2, 4 * TLAT, D), np.float32)
    for i in range(NCORES):
        b, q = i // 4, i % 4
        out[b, q * TLAT:(q + 1) * TLAT] = unfm(np.asarray(res.results[i]["outT"]))
    return out
```

```python
import contextlib
import math
import numpy as np
import ml_dtypes
import concourse.bass as bass
import concourse.mybir as mybir
from concourse.bass_utils import run_bass_kernel_spmd

F32 = mybir.dt.float32
BF16 = mybir.dt.bfloat16
I32 = mybir.dt.int32
ALU = mybir.AluOpType
AF = mybir.ActivationFunctionType
AX = mybir.AxisListType
NPBF = ml_dtypes.bfloat16

NCORES = 8
D = 1024
KC = 8
DFF = 2816
FC = 22
TCTX = 256
TLAT = 2048
T = TCTX + TLAT
TILES = [(0, 256), (256, 512), (768, 512), (1280, 512), (1792, 512)]
EPS = 1e-6
GFF = 4


class _Op:
    __slots__ = ("eng", "pos", "fn", "cwaits", "dwaits", "signal", "dma_key", "dma_k")


class Prog:
    ENGS = ("tensor", "vector", "scalar", "gpsimd", "sync")
    CENG = ("tensor", "vector", "scalar", "gpsimd")

    def __init__(self, nc):
        self.nc = nc
        self.ges = contextlib.ExitStack()
        self.pes = contextlib.ExitStack()
        self.esem = {e: self.ges.enter_context(nc.semaphore(f"s_{e}")) for e in self.CENG}
        self.esig = {e: 0 for e in self.CENG}
        self.dsem = {}
        self.dma_cnt = {}
        self.dma_inc = {}
        self.n_sb = 0
        self.n_phase = 0
        self._reset()

    def _reset(self):
        self.ops = {e: [] for e in self.ENGS}
        self.last_w = {}
        self.readers = {}

    def sb(self, shape, dtype=F32, name=None, persist=False):
        self.n_sb += 1
        nm = (name or "sb") + f"_{self.n_sb}"
        es = self.ges if persist else self.pes
        return es.enter_context(self.nc.sbuf_tensor(nm, list(shape), dtype))

    def ps(self, shape, dtype=F32, name=None):
        self.n_sb += 1
        return self.ges.enter_context(self.nc.psum_tensor(name or f"ps{self.n_sb}", list(shape), dtype))

    def add(self, eng, fn, reads=(), writes=(), dma_key=None, inc=16):
        op = _Op()
        op.eng, op.fn, op.signal = eng, fn, False
        op.pos = len(self.ops[eng])
        op.dma_key = dma_key
        op.dma_k = None
        deps = {}
        for k in reads:
            d = self.last_w.get(k)
            if d is not None:
                deps[id(d)] = d
        for k in writes:
            d = self.last_w.get(k)
            if d is not None:
                deps[id(d)] = d
            for r in self.readers.get(k, ()):
                deps[id(r)] = r
        cw = {}
        dw = {}
        for i, d in deps.items():
            if d.dma_key is not None:
                v = self.dma_inc[d.dma_key] * (d.dma_k + 1)
                dw[d.dma_key] = max(dw.get(d.dma_key, 0), v)
            elif d.eng == eng:
                if eng != "tensor":
                    cw[d.eng] = max(cw.get(d.eng, -1), d.pos)
            else:
                cw[d.eng] = max(cw.get(d.eng, -1), d.pos)
        if dma_key is not None:
            self.dma_inc.setdefault(dma_key, inc)
            k = self.dma_cnt.get(dma_key, 0)
            op.dma_k = k
            self.dma_cnt[dma_key] = k + 1
            if k > 0:
                dw[dma_key] = max(dw.get(dma_key, 0), self.dma_inc[dma_key] * k)
        op.cwaits = []
        for e, p in cw.items():
            d = self.ops[e][p]
            d.signal = True
            op.cwaits.append(d)
        op.dwaits = list(dw.items())
        self.ops[eng].append(op)
        for k in reads:
            self.readers.setdefault(k, []).append(op)
        for k in writes:
            self.last_w[k] = op
            self.readers[k] = []
        return op

    def dma(self, out, in_, reads, writes, key, eng="sync", **kw):
        return self.add(eng, lambda e: e.dma_start(out=out, in_=in_, **kw), reads, writes, dma_key=key)

    def coll(self, kind, op, groups, in_ap, out_ap, reads, writes, key):
        self.n_coll = getattr(self, "n_coll", 0) + 1
        key = f"{key}_{self.n_coll}"
        return self.add("gpsimd", lambda e: e.collective_compute(kind, op, replica_groups=groups, ins=[in_ap], outs=[out_ap]),
                        reads, writes, dma_key=key, inc=1)

    def end_phase(self):
        nc = self.nc
        lasts = []
        for e in self.CENG:
            real = [o for o in self.ops[e] if o.dma_key is None and o.fn is not None]
            if real:
                lasts.append(real[-1])
        for d in lasts:
            d.signal = True
        for e in self.ENGS:
            op = _Op()
            op.eng, op.fn, op.signal, op.dma_key, op.dma_k = e, None, False, None, None
            op.pos = len(self.ops[e])
            op.cwaits = [d for d in lasts if d.eng != e]
            op.dwaits = [(k, self.dma_inc[k] * c) for k, c in self.dma_cnt.items()]
            self.ops[e].append(op)
        sig = {}
        for e in self.CENG:
            c = self.esig[e]
            for op in self.ops[e]:
                if op.signal:
                    c += 1
                    sig[id(op)] = c
            self.esig[e] = c
        for k in self.dma_cnt:
            if k not in self.dsem:
                self.dsem[k] = self.ges.enter_context(nc.semaphore(f"d_{len(self.dsem)}"))
        esem, dsem = self.esem, self.dsem
        with nc.Block() as block:
            def mk(ename):
                ops = self.ops[ename]

                def body(e):
                    for op in ops:
                        for d in op.cwaits:
                            e.wait_ge(esem[d.eng], sig[id(d)])
                        for k, v in op.dwaits:
                            e.wait_ge(dsem[k], v)
                        if op.fn is None:
                            continue
                        ins = op.fn(e)
                        if op.dma_key is not None:
                            ins.then_inc(dsem[op.dma_key], self.dma_inc[op.dma_key])
                        elif op.signal:
                            ins.then_inc(esem[ename], 1)
                return body

            for ename in self.ENGS:
                getattr(block, ename)(mk(ename))
        self.pes.close()
        self.pes = contextlib.ExitStack()
        self._reset()
        self.n_phase += 1

    def close(self):
        self.ges.close()


class Ctx:
    def __init__(self, P):
        self.P = P
        self.psum = [P.ps([128, 512], F32, name=f"psb{i}") for i in range(8)]
        self.ld_i = 0
        self.new_phase()

    def new_phase(self):
        P = self.P
        self.stage = [P.sb([128, 1024], F32, name=f"stage{i}") for i in range(3)]
        self.stage_i = 0

    def load_cast(self, dst16_ap, dst_key, src_ap, n, cast_eng="gpsimd", view=None, psl=None):
        P = self.P
        i = self.stage_i
        self.stage_i = (i + 1) % len(self.stage)
        st = self.stage[i]
        sk = f"stage{i}"
        sv = st[:, 0:n] if psl is None else st[psl, 0:n]
        P.dma(sv if view is None else view(sv), src_ap, [], [sk], f"ld_stage{i}")
        if cast_eng == "scalar":
            P.add("scalar", lambda e: e.copy(dst16_ap, sv if view is None else view(sv)), [sk], [dst_key])
        else:
            P.add(cast_eng, lambda e: e.tensor_copy(dst16_ap, sv if view is None else view(sv)), [sk], [dst_key])

    def load(self, dst_ap, dst_key, src_ap, reads=()):
        P = self.P
        self.ld_i += 1
        P.dma(dst_ap, src_ap, list(reads), [dst_key], f"ld_misc{self.ld_i % 4}")


def make_cols(P, modall, ng, layer, tiles):
    out = {}
    for vi, v in enumerate(("x", "c")):
        gs, gt = tiles[(layer, v)]
        for s in range(3):
            sc = modall[:, layer, (3 * s + 1) * 8:(3 * s + 2) * 8, vi]
            P.add("vector", lambda e, s=s, sc=sc, gs=gs: e.scalar_tensor_tensor(gs[:, s * 8:(s + 1) * 8], sc, 1.0, ng[:, layer, s * 8:(s + 1) * 8], ALU.add, ALU.mult),
                  ["modall", "normg"], [f"gs_{v}"])
            g = modall[:, layer, (3 * s + 2) * 8:(3 * s + 3) * 8, vi]
            fac = 1.0 if s == 1 else 0.5
            P.add("vector", lambda e, s=s, g=g, gt=gt, fac=fac: e.tensor_scalar(gt[:, s * 8:(s + 1) * 8], g, fac, None, ALU.mult),
                  ["modall"], [f"gate_{v}"])
        out[v] = dict(gs=gs, gate=gt, mod=modall, layer=layer, vi=vi)
    return out


def colsel(cols, v, kind, s, c):
    if kind == "gs":
        return cols[v]["gs"][:, s * 8 + c:s * 8 + c + 1]
    if kind == "gate":
        return cols[v]["gate"][:, s * 8 + c:s * 8 + c + 1]
    if kind == "shift":
        j = (3 * s) * 8 + c
        return cols[v]["mod"][:, cols[v]["layer"], j:j + 1, cols[v]["vi"]]
    raise ValueError(kind)


def emit_norm(P, C, K, xT, hT, cols, s, tiles=TILES):
    for ti, (t0, tn) in enumerate(tiles):
        v = "c" if t0 < TCTX else "x"
        pss = C.psum[6 + (ti % 2)]
        psk = f"psb{6 + (ti % 2)}"
        for c in range(KC):
            sq = K["sq"][c % 2]
            sqk = f"sq{c % 2}"
            P.add("scalar", lambda e, sq=sq, c=c, t0=t0, tn=tn: e.activation(sq[:, 0:tn], xT[:, c, t0:t0 + tn], AF.Square),
                  [f"x{c}_{ti}"], [sqk])
            P.add("tensor", lambda e, sq=sq, c=c, pss=pss, tn=tn: e.matmul(pss[:, 0:tn], K["ones"][:], sq[:, 0:tn], start=(c == 0), stop=(c == KC - 1)),
                  [sqk, "ones"], [psk])
        rs = K["rstd"][ti % 2]
        rsk = f"rstd{ti % 2}"
        P.add("scalar", lambda e, rs=rs, pss=pss, tn=tn: e.activation(rs[:, 0:tn], pss[:, 0:tn], AF.Sqrt, bias=K["epscol"][:, 0:1], scale=1.0 / D),
              [psk, "epscol"], [rsk])
        P.add("vector", lambda e, rs=rs, tn=tn: e.reciprocal(rs[:, 0:tn], rs[:, 0:tn]), [rsk], [rsk])
        for c in range(KC):
            tmp = K["ntmp"][c % 2]
            tk = f"ntmp{c % 2}"
            P.add("vector", lambda e, tmp=tmp, c=c, t0=t0, tn=tn, rs=rs: e.tensor_tensor(tmp[:, 0:tn], xT[:, c, t0:t0 + tn], rs[:, 0:tn], ALU.mult),
                  [f"x{c}_{ti}", rsk], [tk])
            P.add("scalar", lambda e, tmp=tmp, c=c, t0=t0, tn=tn, v=v: e.activation(hT[:, c, t0:t0 + tn], tmp[:, 0:tn], AF.Identity,
                                                                                    bias=colsel(cols, v, "shift", s, c), scale=colsel(cols, v, "gs", s, c)),
                  [tk, f"gs_{v}", "modall"], [f"h{c}_{ti}"])


def emit_ffn(P, C, K, xT, hT, cols, s, wg_d, wu_d, wd_d, tiles=TILES):
    act = K["act"]
    wg16, wu16, wd16 = K["wg16"], K["wu16"], K["wd16"]

    def load_chunk(j):
        sl = j % 2
        C.load_cast(wg16[sl][:].rearrange("p k m -> p (k m)"), f"wg16_{sl}", wg_d[j].rearrange("p k m -> p (k m)"), 1024)
        C.load_cast(wu16[sl][:].rearrange("p k m -> p (k m)"), f"wu16_{sl}", wu_d[j].rearrange("p k m -> p (k m)"), 1024)
        C.load_cast(wd16[j % (2 * GFF)][:], f"wd16_{j % (2 * GFF)}", wd_d[j], 1024)

    groups = [list(range(g, min(g + GFF, FC))) for g in range(0, FC, GFF)]
    load_chunk(0)
    pi = 0
    for grp in groups:
        for j in grp:
            if j + 1 < FC:
                load_chunk(j + 1)
            sl = j % 2
            jj = j % GFF
            for ti, (t0, tn) in enumerate(tiles):
                gb = pi % 2
                pi += 1
                pg, pu = C.psum[gb], C.psum[2 + gb]
                for k in range(KC):
                    P.add("tensor", lambda e, pg=pg, k=k, sl=sl, t0=t0, tn=tn: e.matmul(pg[:, 0:tn], wg16[sl][:, k, :], hT[:, k, t0:t0 + tn], start=(k == 0), stop=(k == KC - 1)),
                          [f"wg16_{sl}", f"h{k}_{ti}"], [f"psb{gb}"])
                for k in range(KC):
                    P.add("tensor", lambda e, pu=pu, k=k, sl=sl, t0=t0, tn=tn: e.matmul(pu[:, 0:tn], wu16[sl][:, k, :], hT[:, k, t0:t0 + tn], start=(k == 0), stop=(k == KC - 1)),
                          [f"wu16_{sl}", f"h{k}_{ti}"], [f"psb{2 + gb}"])
                sg = K["sg"][gb]
                P.add("scalar", lambda e, sg=sg, pg=pg, tn=tn: e.activation(sg[:, 0:tn], pg[:, 0:tn], AF.Silu), [f"psb{gb}"], [f"sg{gb}"])
                P.add("vector", lambda e, sg=sg, pu=pu, jj=jj, t0=t0, tn=tn: e.tensor_tensor(act[:, jj, t0:t0 + tn], sg[:, 0:tn], pu[:, 0:tn], ALU.mult),
                      [f"sg{gb}", f"psb{2 + gb}"], [f"act{jj}_{ti}"])
        for ti, (t0, tn) in enumerate(tiles):
            v = "c" if t0 < TCTX else "x"
            for dc in range(KC):
                db = 4 + (dc % 2)
                pd = C.psum[db]
                for n, j in enumerate(grp):
                    jj = j % GFF
                    jw = j % (2 * GFF)
                    P.add("tensor", lambda e, pd=pd, jj=jj, jw=jw, dc=dc, t0=t0, tn=tn, n=n, ng=len(grp): e.matmul(pd[:, 0:tn], wd16[jw][:, dc * 128:(dc + 1) * 128], act[:, jj, t0:t0 + tn],
                                                                                                  start=(n == 0), stop=(n == ng - 1)),
                          [f"wd16_{jw}", f"act{jj}_{ti}"], [f"psb{db}"])
                P.add("vector", lambda e, pd=pd, dc=dc, t0=t0, tn=tn, v=v: e.scalar_tensor_tensor(xT[:, dc, t0:t0 + tn], pd[:, 0:tn], colsel(cols, v, "gate", s, dc),
                                                                                                 xT[:, dc, t0:t0 + tn], ALU.mult, ALU.add),
                      [f"psb{db}", f"gate_{v}", f"x{dc}_{ti}"], [f"x{dc}_{ti}"])


def alloc_common(P, C, with_ffn=True):
    K = {}
    K["ones"] = P.sb([128, 128], BF16, name="ones")
    P.add("vector", lambda e: e.memset(K["ones"][:], 1.0), [], ["ones"])
    K["epscol"] = P.sb([128, 1], F32, name="epscol")
    P.add("vector", lambda e: e.memset(K["epscol"][:], EPS), [], ["epscol"])
    K["sq"] = [P.sb([128, 512], BF16, name=f"sq{i}") for i in range(2)]
    K["rstd"] = [P.sb([128, 512], F32, name=f"rstd{i}") for i in range(2)]
    K["ntmp"] = [P.sb([128, 512], F32, name=f"ntmp{i}") for i in range(2)]
    if with_ffn:
        K["act"] = P.sb([128, GFF, T], BF16, name="act")
        K["wg16"] = [P.sb([128, 8, 128], BF16, name=f"wg16_{i}") for i in range(2)]
        K["wu16"] = [P.sb([128, 8, 128], BF16, name=f"wu16_{i}") for i in range(2)]
        K["wd16"] = [P.sb([128, 1024], BF16, name=f"wd16_{i}") for i in range(2 * GFF)]
        K["sg"] = [P.sb([128, 512], F32, name=f"sg{i}") for i in range(2)]
    return K


def emit_consts(P, C, K, rot_d, bones_d):
    K["rot"] = P.sb([128, 128], BF16, name="rot_sb")
    K["bones"] = P.sb([128, 128], BF16, name="bones_sb")
    C.load_cast(K["rot"][:], "rot", rot_d, 128)
    C.load_cast(K["bones"][:], "bones", bones_d, 128)


def emit_inproj(P, C, K, hT, win_d, n_fm, kinds, qkg, cosT, sinT, outs, v_d, vw, v_oc0, tiles=TILES):
    w16 = K["win16"]
    zq = K["zq"]
    for oc in range(n_fm):
        sl = oc % 2
        C.load_cast(w16[sl][:].rearrange("p k m -> p (k m)"), f"win16_{sl}", win_d[oc].rearrange("p k m -> p (k m)"), 1024)
        kind = kinds[oc]
        od, odt = outs[oc]
        for ti, (t0, tn) in enumerate(tiles):
            pb = ti % 2
            pz = C.psum[pb]
            for k in range(KC):
                P.add("tensor", lambda e, pz=pz, k=k, sl=sl, t0=t0, tn=tn: e.matmul(pz[:, 0:tn], w16[sl][:, k, :], hT[:, k, t0:t0 + tn], start=(k == 0), stop=(k == KC - 1)),
                      [f"win16_{sl}", f"h{k}_{ti}"], [f"psb{pb}"])
            ob = K["ob32"][pb] if odt == F32 else K["ob16"][pb]
            obk = f"ob{'32' if odt == F32 else '16'}_{pb}"
            if kind == "u":
                P.add("scalar", lambda e, ob=ob, pz=pz, tn=tn: e.copy(ob[:, 0:tn], pz[:, 0:tn]), [f"psb{pb}"], [obk])
            else:
                gcol = qkg[:, 0:1] if kind == "q" else qkg[:, 1:2]
                z = zq[pb]
                zk = f"zq{pb}"
                sq = K["sq"][pb]
                sqk = f"sq{pb}"
                P.add("scalar", lambda e, z=z, pz=pz, tn=tn: e.copy(z[:, 0:tn], pz[:, 0:tn]), [f"psb{pb}"], [zk])
                P.add("scalar", lambda e, sq=sq, z=z, tn=tn: e.activation(sq[:, 0:tn], z[:, 0:tn], AF.Square), [zk], [sqk])
                ph = C.psum[2 + pb]
                P.add("tensor", lambda e, ph=ph, sq=sq, tn=tn: e.matmul(ph[:, 0:tn], K["bones"][:], sq[:, 0:tn], start=True, stop=True), [sqk, "bones"], [f"psb{2 + pb}"])
                rs = K["rstd"][pb]
                rsk = f"rstd{pb}"
                P.add("scalar", lambda e, rs=rs, ph=ph, tn=tn: e.activation(rs[:, 0:tn], ph[:, 0:tn], AF.Sqrt, bias=K["epscol"][:, 0:1], scale=1.0 / 64), [f"psb{2 + pb}", "epscol"], [rsk])
                P.add("vector", lambda e, rs=rs, tn=tn: e.reciprocal(rs[:, 0:tn], rs[:, 0:tn]), [rsk], [rsk])
                P.add("vector", lambda e, z=z, rs=rs, tn=tn, gcol=gcol: e.scalar_tensor_tensor(z[:, 0:tn], z[:, 0:tn], gcol, rs[:, 0:tn], ALU.mult, ALU.mult), [zk, rsk, "qkg"], [zk])
                zb = K["zb"][pb]
                zbk = f"zb{pb}"
                P.add("scalar", lambda e, zb=zb, z=z, tn=tn: e.copy(zb[:, 0:tn], z[:, 0:tn]), [zk], [zbk])
                pr = C.psum[4 + pb]
                P.add("tensor", lambda e, pr=pr, zb=zb, tn=tn: e.matmul(pr[:, 0:tn], K["rot"][:], zb[:, 0:tn], start=True, stop=True), [zbk, "rot"], [f"psb{4 + pb}"])
                t2 = K["ntmp"][pb]
                t2k = f"ntmp{pb}"
                cb_ = pb % len(K["cst"])
                cst, snt = K["cst"][cb_], K["snt"][cb_]
                C.load(cst[:, 0:tn], f"cst{cb_}", cosT[:, t0:t0 + tn])
                C.load(snt[:, 0:tn], f"snt{cb_}", sinT[:, t0:t0 + tn])
                P.add("vector", lambda e, t2=t2, pr=pr, snt=snt, tn=tn: e.tensor_tensor(t2[:, 0:tn], pr[:, 0:tn], snt[:, 0:tn], ALU.mult), [f"psb{4 + pb}", f"snt{cb_}"], [t2k])
                P.add("vector", lambda e, z=z, cst=cst, tn=tn: e.tensor_tensor(z[:, 0:tn], z[:, 0:tn], cst[:, 0:tn], ALU.mult), [zk, f"cst{cb_}"], [zk])
                P.add("vector", lambda e, ob=ob, z=z, t2=t2, tn=tn: e.tensor_tensor(ob[:, 0:tn], z[:, 0:tn], t2[:, 0:tn], ALU.add), [zk, t2k], [obk])
            P.dma(od(t0, tn) if callable(od) else od[:, t0:t0 + tn], ob[:, 0:tn], [obk], [], f"st_{obk}")
    nvc = vw // 128
    wv = K["wv16"]
    for i in range(nvc):
        C.load_cast(wv[:, :, i * 128:(i + 1) * 128], f"wv16_{i}", win_d[v_oc0 + i], 1024, view=lambda a: a.rearrange("p (k m) -> p k m", m=128))
    for tt in range(T // 128):
        pb = tt % 2
        pv = C.psum[6 + pb]
        ti = 0 if tt < 2 else 1 + (tt - 2) // 4
        for k in range(KC):
            P.add("tensor", lambda e, pv=pv, k=k, tt=tt: e.matmul(pv[:, 0:vw], hT[:, k, tt * 128:(tt + 1) * 128], wv[:, k, :], start=(k == 0), stop=(k == KC - 1)),
                  [f"wv16_{i}" for i in range(nvc)] + [f"h{k}_{ti}"], [f"psb{6 + pb}"])
        vb = K["vb"][pb]
        P.add("scalar", lambda e, vb=vb, pv=pv: e.copy(vb[:, 0:vw], pv[:, 0:vw]), [f"psb{6 + pb}"], ["vb0"])
        P.dma(v_d[tt], vb[:, 0:vw], ["vb0"], [], "st_vb0")


def phase_att(P, C, layer, Dr):
    dense = layer == 0
    n_qc, n_kv = (4, 2) if dense else (8, 4)
    n_heads = 2 * n_qc
    NK = 66 if dense else 26
    qT = P.sb([128, n_qc, T], BF16, name="q_sb")
    kd = P.sb([128, n_kv, NK * 128], BF16, name="k_sb")
    va = P.sb([128, n_kv, NK, 65], BF16, name="v_sb")
    q_loc, o_loc = Dr["q_loc"], Dr["o_loc"]
    for c in range(n_qc):
        C.load(qT[:, c, :], f"q{c}", q_loc[:, c, :])
    for kv in range(n_kv):
        P.add("vector", lambda e, kv=kv: e.memset(va[:, kv, :, 64:65], 1.0), [], [f"vone{kv}"])
    tkv = lambda s_: s_.rearrange("p (kt d) -> p kt d", d=64)
    tk = lambda ap: ap.rearrange("(kt p) d -> p kt d", p=128)

    kpieces, vpieces = {}, {}

    def kload(kv, half, dst0, src_rows, src_cols0, ncols):
        pl = slice(64 * half, 64 * half + 64)
        for c0 in range(0, ncols, 1024):
            n = min(1024, ncols - c0)
            key = f"k{kv}_{half}_{dst0 + c0}"
            kpieces.setdefault((kv, half), []).append((dst0 + c0, n, key))
            C.load_cast(kd[pl, kv, dst0 + c0:dst0 + c0 + n], key, src_rows[:, src_cols0 + c0:src_cols0 + c0 + n], n, psl=pl, cast_eng="vector")

    def vload(kv, kt0, nkt, src):
        for t0 in range(0, nkt, 16):
            n = min(16, nkt - t0)
            key = f"v{kv}_{kt0 + t0}"
            vpieces.setdefault(kv, []).append((kt0 + t0, n, key))
            C.load_cast(va[:, kv, kt0 + t0:kt0 + t0 + n, 0:64], key, tk(src[t0 * 128:(t0 + n) * 128, 64 * kv:64 * kv + 64]), n * 64, view=tkv, cast_eng="vector")

    def kkey(kv, half, kt):
        for (c0, n, key) in kpieces[(kv, half)]:
            if c0 <= kt * 128 < c0 + n:
                return key
        raise KeyError((kv, half, kt))

    def vkey(kv, kt):
        for (k0, n, key) in vpieces[kv]:
            if k0 <= kt < k0 + n:
                return key
        raise KeyError((kv, kt))

    if dense:
        kA_all, kB_all, vA_all, vB_all = Dr["kA_all"], Dr["kB_all"], Dr["vA_all"], Dr["vB_all"]
        for kv in range(n_kv):
            for half in range(2):
                kload(kv, half, 0, kA_all[64 * kv:64 * kv + 64], 0, TCTX)
                for r in range(4):
                    kload(kv, half, TCTX + TLAT * r, kA_all[r * 128 + 64 * kv:r * 128 + 64 * kv + 64], TCTX, 1024)
                    kload(kv, half, TCTX + TLAT * r + 1024, kB_all[r * 128 + 64 * kv:r * 128 + 64 * kv + 64], 0, 1024)
            vload(kv, 0, 2, vA_all[0:TCTX])
            for r in range(4):
                vload(kv, 2 + 16 * r, 8, vA_all[r * 1280 + TCTX:(r + 1) * 1280])
                vload(kv, 2 + 16 * r + 8, 8, vB_all[r * 1024:(r + 1) * 1024])
    else:
        k_loc, v_loc, ek_all, ev_all = Dr["k_loc"], Dr["v_loc"], Dr["ek_all"], Dr["ev_all"]
        for kv in range(n_kv):
            ks = slice(64 * (kv % 2), 64 * (kv % 2) + 64)
            kc = kv // 2
            for half in range(2):
                kload(kv, half, 0, k_loc[ks, kc], TCTX, TLAT)
                kload(kv, half, 24 * 128, k_loc[ks, kc], 0, TCTX)
                for r in range(4):
                    kload(kv, half, 16 * 128 + 256 * r, ek_all[r * 128 + 64 * (kv % 2):r * 128 + 64 * (kv % 2) + 64], kc * 256, 256)
            vload(kv, 0, 16, v_loc[TCTX:T])
            vload(kv, 24, 2, v_loc[0:TCTX])
            for r in range(4):
                vload(kv, 16 + 2 * r, 2, ev_all[r * 256:(r + 1) * 256])
    ones32 = P.sb([128, 64], F32, name="ones32")
    P.add("vector", lambda e: e.memset(ones32[:], 1.0), [], ["ones32"])
    es = P.sb([128, 16], F32, name="es")
    if not dense:
        C.load(es[:], "es", Dr["sink"])
        P.add("scalar", lambda e: e.activation(es[:], es[:], AF.Exp), ["es"], ["es"])
        wm = P.sb([128, 4, 6, 512], BF16, name="wm")
        C.load(wm[:], "wm", Dr["wmask"])
        em = P.sb([128, 2, 4, 512], BF16, name="em")
        C.load(em[:], "em", Dr["emask"])
    pT = [P.sb([128, 512], BF16, name=f"pT{i}") for i in range(3)]
    osb = [P.sb([64, 512], F32, name=f"osb{i}") for i in range(2)]
    rden = [P.sb([128, 512], F32, name=f"rden{i}") for i in range(2)]
    ob = [P.sb([64, 512], BF16, name=f"ob{i}") for i in range(2)]
    work = []
    if dense:
        work.append((0, 256, [(0, None), (1, None)]))
        for m in range(4):
            work.append((256 + 512 * m, 512, [(kt, None) for kt in range(66)]))
    else:
        for m in range(4):
            keys = [(4 * m + r - 1, wm[:, m, r, :]) for r in range(6) if 0 <= 4 * m + r - 1 <= 15]
            if m == 0:
                keys += [(16 + 2 * r + 1, em[:, 0, r, :]) for r in range(4)]
            if m == 3:
                keys += [(16 + 2 * r, em[:, 1, r, :]) for r in range(4)]
            keys += [(24, None), (25, None)]
            work.append((256 + 512 * m, 512, keys))
    it = 0
    si = 0
    for (q0, qn, keys) in work:
        for h in range(n_heads):
            c, half, kv = h // 2, h % 2, h // 4
            pl = slice(64 * half, 64 * half + 64)
            ob_i = it % 2
            it += 1
            po = C.psum[6 + ob_i]
            pok = f"psb{6 + ob_i}"
            LA = 2
            nk = len(keys)
            slots = []
            for n in range(nk + LA):
                if n < nk:
                    kt, mk = keys[n]
                    sb_i = si % 3
                    si += 1
                    slots.append(sb_i)
                    pS = C.psum[sb_i]
                    P.add("tensor", lambda e, pS=pS, kv=kv, kt=kt, pl=pl, c=c, q0=q0, qn=qn: e.matmul(pS[:, 0:qn], kd[pl, kv, kt * 128:(kt + 1) * 128], qT[pl, c, q0:q0 + qn], start=True, stop=True),
                          [kkey(kv, half, kt), f"q{c}"], [f"psb{sb_i}"])
                    pt = pT[sb_i]
                    P.add("scalar", lambda e, pt=pt, pS=pS, qn=qn: e.activation(pt[:, 0:qn], pS[:, 0:qn], AF.Exp, scale=0.125), [f"psb{sb_i}"], [f"pT{sb_i}"])
                    if mk is not None:
                        P.add("vector", lambda e, pt=pt, mk=mk, qn=qn: e.tensor_tensor(pt[:, 0:qn], pt[:, 0:qn], mk[:, 0:qn], ALU.mult), [f"pT{sb_i}", "wm", "em"], [f"pT{sb_i}"])
                if n >= LA:
                    m_ = n - LA
                    kt2 = keys[m_][0]
                    sb2 = slots[m_]
                    pt2 = pT[sb2]
                    P.add("tensor", lambda e, po=po, pt2=pt2, kv=kv, kt2=kt2, qn=qn, m_=m_, nk=nk: e.matmul(po[0:65, 0:qn], va[:, kv, kt2, :], pt2[:, 0:qn], start=(m_ == 0), stop=(m_ == nk - 1)),
                          [vkey(kv, kt2), f"vone{kv}", f"pT{sb2}"], [pok])
            rd = rden[ob_i]
            rdk = f"rden{ob_i}"
            osx = osb[ob_i]
            P.add("scalar", lambda e, osx=osx, po=po, qn=qn: e.copy(osx[:, 0:qn], po[0:64, 0:qn]), [pok], [f"osb{ob_i}"])
            if dense:
                P.add("vector", lambda e, rd=rd, po=po, qn=qn: e.reciprocal(rd[64:65, 0:qn], po[64:65, 0:qn]), [pok], [rdk])
            else:
                P.add("vector", lambda e, rd=rd, po=po, qn=qn, h=h: e.tensor_scalar(rd[64:65, 0:qn], po[64:65, 0:qn], es[64:65, h:h + 1], None, ALU.add), [pok, "es"], [rdk])
                P.add("vector", lambda e, rd=rd, qn=qn: e.reciprocal(rd[64:65, 0:qn], rd[64:65, 0:qn]), [rdk], [rdk])
            pb = C.psum[3 + ob_i]
            P.add("tensor", lambda e, pb=pb, rd=rd, qn=qn: e.matmul(pb[0:64, 0:qn], ones32[64:65, 0:64], rd[64:65, 0:qn], start=True, stop=True), [rdk, "ones32"], [f"psb{3 + ob_i}"])
            o16 = ob[ob_i]
            P.add("vector", lambda e, o16=o16, osx=osx, pb=pb, qn=qn: e.tensor_tensor(o16[:, 0:qn], osx[:, 0:qn], pb[0:64, 0:qn], ALU.mult), [f"osb{ob_i}", f"psb{3 + ob_i}"], [f"ob{ob_i}"])
            P.dma(o_loc[64 * half:64 * half + 64, c, q0:q0 + qn], o16[:, 0:qn], [f"ob{ob_i}"], [], f"st_ob{ob_i}")


NS5 = TCTX + 4 * TLAT
TWO_PI = 2.0 * math.pi


def phase_s5(P, C, Dr):
    uA_all, uB_all, B_d, C_d, pc_d = Dr["uA_all"], Dr["uB_all"], Dr["Bl"], Dr["Cl"], Dr["pcols"]
    ic_d, ij_d, oh_d = Dr["iota_c"], Dr["iota_j"], Dr["onehot"]
    y_rs_in, yc_loc = Dr["y_rs_in"], Dr["yc_loc"]
    sm = lambda name, n=8, dtype=F32: P.sb([128, n], dtype, name=name)
    pc = P.sb([128, 3, 8], F32, name="pc")
    C.load(pc[:].rearrange("p a b -> p (a b)"), "pc", pc_d.rearrange("p a b -> p (a b)"))
    iota_c = P.sb([128, 2, 66], F32, name="iota_c_sb")
    iota_j = P.sb([128, 2, 128], F32, name="iota_j_sb")
    oh = sm("onehot_sb", 4)
    C.load(iota_c[:], "iota_c", ic_d)
    C.load(iota_j[:], "iota_j", ij_d)
    C.load(oh[:], "oh", oh_d)
    cnt = [0]

    def V(fn, reads, writes, eng="vector"):
        P.add(eng, fn, reads, writes)

    fr_i = P.sb([128, 128], I32, name="fr_i")
    fr_f = P.sb([128, 128], F32, name="fr_f")
    sc_t = P.sb([128, 128], F32, name="sc_t")
    sc_u = P.sb([128, 128], F32, name="sc_u")

    def fracp(dst, src, n, key_dst, key_src):
        V(lambda e: e.tensor_copy(fr_i[:, 0:n], src), [key_src], ["fr_i"])
        V(lambda e: e.tensor_copy(fr_f[:, 0:n], fr_i[:, 0:n]), ["fr_i"], ["fr_f"])
        V(lambda e: e.tensor_tensor(dst, src, fr_f[:, 0:n], ALU.subtract), [key_src, "fr_f"], [key_dst])

    def sincos(sin_dst, cos_dst, ph, n, key_s, key_c, key_ph):
        P.add("scalar", lambda e: e.activation(sin_dst, ph, AF.Sin, scale=TWO_PI - 1e-5), [key_ph], [key_s])
        V(lambda e: e.tensor_scalar(sc_t[:, 0:n], ph, 0.25, None, ALU.add), [key_ph], ["sc_t"])
        fracp(sc_u[:, 0:n], sc_t[:, 0:n], n, "sc_u", "sc_t")
        P.add("scalar", lambda e: e.activation(cos_dst, sc_u[:, 0:n], AF.Sin, scale=TWO_PI - 1e-5), ["sc_u"], [key_c])

    lr, li, dtv, a, th, rr, f = sm("lr"), sm("li"), sm("dtv"), sm("a_ln"), sm("th"), sm("rr"), sm("f")
    V(lambda e: e.tensor_scalar(lr[:], pc[:, 0, :], -1e-4, None, ALU.min), ["pc"], ["lr"])
    V(lambda e: e.tensor_copy(li[:], pc[:, 1, :]), ["pc"], ["li"])
    P.add("scalar", lambda e: e.activation(dtv[:], pc[:, 2, :], AF.Exp), ["pc"], ["dtv"])
    V(lambda e: e.tensor_tensor(a[:], lr[:], dtv[:], ALU.mult), ["lr", "dtv"], ["a"])
    V(lambda e: e.tensor_tensor(th[:], li[:], dtv[:], ALU.mult), ["li", "dtv"], ["th"])
    P.add("scalar", lambda e: e.activation(rr[:], a[:], AF.Exp), ["a"], ["rr"])
    V(lambda e: e.tensor_scalar(f[:], th[:], 1.0 / TWO_PI, None, ALU.mult), ["th"], ["f"])
    f0, sth, cth = sm("f0"), sm("sth"), sm("cth")
    fracp(f0[:], f[:], 8, "f0", "f")
    sincos(sth[:], cth[:], f0[:], 8, "sth", "cth", "f0")
    nr, ni, den, t8a, t8b, kr, ki, nkr, nki = [sm(n) for n in ("nr", "ni", "den", "t8a", "t8b", "kr", "ki", "nkr", "nki")]
    V(lambda e: e.tensor_tensor(nr[:], rr[:], cth[:], ALU.mult), ["rr", "cth"], ["nr"])
    V(lambda e: e.tensor_scalar(nr[:], nr[:], -1.0, None, ALU.add), ["nr"], ["nr"])
    V(lambda e: e.tensor_tensor(ni[:], rr[:], sth[:], ALU.mult), ["rr", "sth"], ["ni"])
    V(lambda e: e.tensor_tensor(den[:], lr[:], lr[:], ALU.mult), ["lr"], ["den"])
    V(lambda e: e.tensor_tensor(t8a[:], li[:], li[:], ALU.mult), ["li"], ["t8a"])
    V(lambda e: e.tensor_tensor(den[:], den[:], t8a[:], ALU.add), ["den", "t8a"], ["den"])
    V(lambda e: e.reciprocal(den[:], den[:]), ["den"], ["den"])
    V(lambda e: e.tensor_tensor(t8a[:], nr[:], lr[:], ALU.mult), ["nr", "lr"], ["t8a"])
    V(lambda e: e.tensor_tensor(t8b[:], ni[:], li[:], ALU.mult), ["ni", "li"], ["t8b"])
    V(lambda e: e.tensor_tensor(kr[:], t8a[:], t8b[:], ALU.add), ["t8a", "t8b"], ["kr"])
    V(lambda e: e.tensor_tensor(kr[:], kr[:], den[:], ALU.mult), ["kr", "den"], ["kr"])
    V(lambda e: e.tensor_tensor(t8a[:], ni[:], lr[:], ALU.mult), ["ni", "lr"], ["t8a"])
    V(lambda e: e.tensor_tensor(t8b[:], nr[:], li[:], ALU.mult), ["nr", "li"], ["t8b"])
    V(lambda e: e.tensor_tensor(ki[:], t8a[:], t8b[:], ALU.subtract), ["t8a", "t8b"], ["ki"])
    V(lambda e: e.tensor_tensor(ki[:], ki[:], den[:], ALU.mult), ["ki", "den"], ["ki"])
    V(lambda e: e.tensor_scalar(nkr[:], kr[:], -1.0, None, ALU.mult), ["kr"], ["nkr"])
    V(lambda e: e.tensor_scalar(nki[:], ki[:], -1.0, None, ALU.mult), ["ki"], ["nki"])
    a128, a128f = sm("a128"), sm("a128f")
    V(lambda e: e.tensor_scalar(a128[:], f0[:], 128.0, None, ALU.mult), ["f0"], ["a128"])
    fracp(a128f[:], a128[:], 8, "a128f", "a128")
    B16 = P.sb([128, 16, 4, 128], BF16, name="B16")
    CR = P.sb([128, 8, 128], BF16, name="CR16")
    CI = P.sb([128, 8, 128], BF16, name="CI16")
    c32 = [P.sb([128, 2, 128], F32, name=f"c32_{i}") for i in range(2)]
    ctmp = [P.sb([128, 128], F32, name=f"ctmp{i}") for i in range(2)]

    def wsetup(d, gp):
        q = d * 4 + gp
        for ri in range(2):
            C.load_cast(B16[:, q * 2 + ri, :, :], f"B16_{q}_{ri}", B_d[d, gp, ri].rearrange("c p m -> p c m"), 512,
                        view=lambda s_: s_.rearrange("p (c m) -> p c m", m=128))
        cb = c32[q % 2]
        ck = f"c32_{q % 2}"
        P.dma(cb[:], C_d[d, gp].rearrange("r k m -> k r m"), [], [ck], f"ld_c32_{q % 2}")
        tm = ctmp[q % 2]
        tk = f"ctmp{q % 2}"
        V(lambda e: e.tensor_scalar(tm[:], cb[:, 0, :], kr[:, q:q + 1], None, ALU.mult), [ck, "kr"], [tk])
        V(lambda e: e.scalar_tensor_tensor(CR[:, q, :], cb[:, 1, :], nki[:, q:q + 1], tm[:], ALU.mult, ALU.add), [ck, "nki", tk], [f"CR_{q}"])
        V(lambda e: e.tensor_scalar(tm[:], cb[:, 1, :], nkr[:, q:q + 1], None, ALU.mult), [ck, "nkr", f"CR_{q}"], [tk])
        V(lambda e: e.scalar_tensor_tensor(CI[:, q, :], cb[:, 0, :], nki[:, q:q + 1], tm[:], ALU.mult, ALU.add), [ck, "nki", tk], [f"CI_{q}"])

    for d in range(2):
        for gp in range(4):
            wsetup(d, gp)
    sinC = P.sb([128, 8, 66], F32, name="sinC")
    cosC = P.sb([128, 8, 66], F32, name="cosC")
    sinJ = P.sb([128, 8, 128], F32, name="sinJ")
    cosJ = P.sb([128, 8, 128], F32, name="cosJ")

    phA = P.sb([128, 128], F32, name="phA")
    phB = P.sb([128, 128], F32, name="phB")

    def tsetup(q):
        d = q // 4
        V(lambda e: e.tensor_scalar(phA[:, 0:66], iota_c[:, d, :], a128f[:, q:q + 1], None, ALU.mult), ["iota_c", "a128f"], ["phA"])
        fracp(phB[:, 0:66], phA[:, 0:66], 66, "phB", "phA")
        sincos(sinC[:, q, :], cosC[:, q, :], phB[:, 0:66], 66, f"sinC{q}", f"cosC{q}", "phB")
        V(lambda e: e.tensor_scalar(phA[:, 0:128], iota_j[:, d, :], f0[:, q:q + 1], None, ALU.mult), ["iota_j", "f0"], ["phA"])
        fracp(phB[:, 0:128], phA[:, 0:128], 128, "phB", "phA")
        sincos(sinJ[:, q, :], cosJ[:, q, :], phB[:, 0:128], 128, f"sinJ{q}", f"cosJ{q}", "phB")

    for q in range(8):
        tsetup(q)
    NB = 512
    big = lambda name, dtype=F32: P.sb([128, NB], dtype, name=name)
    u16 = [P.sb([128, 4, NB], BF16, name=f"u16_{i}") for i in range(2)]
    St = [big(f"St{i}") for i in range(2)]
    Ct = [big(f"Ct{i}") for i in range(2)]
    tB = big("tB")
    tA2 = [big("tA0"), big("tA1")]
    brs2, bis2 = [big("brs0"), big("brs1")], [big("bis0"), big("bis1")]
    btr2, bti2 = [big("btr0"), big("btr1")], [big("bti0"), big("bti1")]
    gr2, gi2 = [big("gr0"), big("gr1")], [big("gi0"), big("gi1")]
    hr16 = [big(f"hr16_{i}", BF16) for i in range(2)]
    hi16 = [big(f"hi16_{i}", BF16) for i in range(2)]
    carry = P.sb([128, 16], F32, name="carry")
    yb = [P.sb([128, 4, NB], F32, name=f"yb{i}") for i in range(2)]

    def rv(t, n):
        return bass.AP(t, n - 1, [[NB, 128], [-1, n]])

    def gp_body(d, first, lay0, sn, ub, gp, tb, hb, seg_n):
        q = d * 4 + gp
        c0, ncn = lay0 // 128, sn // 128
        nst = (sn + 511) // 512
        S, Cc = St[tb], Ct[tb]
        brs, bis = brs2[tb], bis2[tb]
        ch = gp % 2
        tA, btr, bti, gr, gi = tA2[ch], btr2[ch], bti2[ch], gr2[ch], gi2[ch]
        kA_, kbr, kbi, kgr, kgi = f"tA{ch}", f"btr{ch}", f"bti{ch}", f"gr{ch}", f"gi{ch}"
        yb_ = 4 + (seg_n % 2)
        W = slice(0, sn)
        v3 = lambda t: t[:, 0:sn].rearrange("p (c j) -> p c j", j=128)
        cC = lambda t: t[:, q, c0:c0 + ncn].unsqueeze(2).broadcast_to([128, ncn, 128])
        cJ = lambda t: t[:, q, :].unsqueeze(1).broadcast_to([128, ncn, 128])
        G = "vector"
        P.add(G, lambda e, a=cC(sinC), b=cJ(cosJ), v=v3(S): e.tensor_tensor(v, a, b, ALU.mult), [f"sinC{q}", f"cosJ{q}"], [f"St{tb}"])
        P.add(G, lambda e, a=cC(cosC), b=cJ(sinJ), v=v3(tB): e.tensor_tensor(v, a, b, ALU.mult), [f"cosC{q}", f"sinJ{q}"], ["tBg"])
        P.add(G, lambda e: e.tensor_tensor(S[:, W], S[:, W], tB[:, W], ALU.add), [f"St{tb}", "tBg"], [f"St{tb}"])
        P.add(G, lambda e, a=cC(cosC), b=cJ(cosJ), v=v3(Cc): e.tensor_tensor(v, a, b, ALU.mult), [f"cosC{q}", f"cosJ{q}"], [f"Ct{tb}"])
        P.add(G, lambda e, a=cC(sinC), b=cJ(sinJ), v=v3(tB): e.tensor_tensor(v, a, b, ALU.mult), [f"sinC{q}", f"sinJ{q}"], ["tBg"])
        P.add(G, lambda e: e.tensor_tensor(Cc[:, W], Cc[:, W], tB[:, W], ALU.subtract), [f"Ct{tb}", "tBg"], [f"Ct{tb}"])
        for st in range(nst):
            n0 = st * 512
            nn = min(512, sn - n0)
            for ri, (dst, dk) in enumerate(((brs, f"brs{tb}_"), (bis, f"bis{tb}_"))):
                pbi = ri * 2 + tb
                pb = C.psum[pbi]
                for c in range(4):
                    P.add("tensor", lambda e, pb=pb, ri=ri, n0=n0, nn=nn, c=c: e.matmul(pb[:, 0:nn], B16[:, q * 2 + ri, c, :], u16[ub][:, c, n0:n0 + nn], start=(c == 0), stop=(c == 3)),
                          [f"B16_{q}_{ri}", f"u16_{ub}_{c}"], [f"psb{pbi}"])
                P.add("scalar", lambda e, pb=pb, dst=dst, n0=n0, nn=nn: e.copy(dst[:, n0:n0 + nn], pb[:, 0:nn]), [f"psb{pbi}"], [f"{dk}{st}"])
        assert nst == 1
        bk = [f"brs{tb}_{st}" for st in range(nst)]
        ik = [f"bis{tb}_{st}" for st in range(nst)]
        V(lambda e: e.tensor_tensor(tA[:, W], Cc[:, W], brs[:, W], ALU.mult), [f"Ct{tb}"] + bk, [kA_])
        yield
        V(lambda e: e.tensor_tensor(btr[:, W], S[:, W], bis[:, W], ALU.mult), [f"St{tb}"] + ik, [kbr])
        yield
        V(lambda e: e.tensor_tensor(btr[:, W], btr[:, W], tA[:, W], ALU.add), [kbr, kA_], [kbr])
        yield
        V(lambda e: e.tensor_tensor(tA[:, W], Cc[:, W], bis[:, W], ALU.mult), [f"Ct{tb}"] + ik, [kA_])
        yield
        V(lambda e: e.tensor_tensor(bti[:, W], S[:, W], brs[:, W], ALU.mult), [f"St{tb}"] + bk, [kbi])
        yield
        V(lambda e: e.tensor_tensor(bti[:, W], tA[:, W], bti[:, W], ALU.subtract), [kbi, kA_], [kbi])
        yield
        rcol = rr[:, q:q + 1].broadcast_to([128, sn])
        for (g_, bt_, gk, btk, ci) in ((gr, btr, kgr, kbr, 2 * q), (gi, bti, kgi, kbi, 2 * q + 1)):
            init = 0.0 if first else carry[:, ci:ci + 1]
            if d == 0:
                V(lambda e, g_=g_, bt_=bt_, init=init: e.tensor_tensor_scan(g_[:, W], rcol, bt_[:, W], init, ALU.mult, ALU.add), [btk, "rr", f"carry{ci}"], [gk])
                yield
                P.add("scalar", lambda e, g_=g_, ci=ci: e.copy(carry[:, ci:ci + 1], g_[:, sn - 1:sn]), [gk], [f"carry{ci}"])
            else:
                V(lambda e, g_=g_, bt_=bt_, init=init: e.tensor_tensor_scan(rv(g_, sn), rcol, rv(bt_, sn), init, ALU.mult, ALU.add), [btk, "rr", f"carry{ci}"], [gk])
                yield
                P.add("scalar", lambda e, g_=g_, ci=ci: e.copy(carry[:, ci:ci + 1], g_[:, 0:1]), [gk], [f"carry{ci}"])
        hr_, hi_ = hr16[hb], hi16[hb]
        V(lambda e: e.tensor_tensor(tA[:, W], Cc[:, W], gr[:, W], ALU.mult), [f"Ct{tb}", kgr], [kA_])
        yield
        V(lambda e: e.tensor_tensor(btr[:, W], S[:, W], gi[:, W], ALU.mult), [f"St{tb}", kgi], [kbr])
        yield
        V(lambda e: e.tensor_tensor(hr_[:, W], tA[:, W], btr[:, W], ALU.subtract), [kA_, kbr], [f"hr16_{hb}"])
        yield
        V(lambda e: e.tensor_tensor(tA[:, W], S[:, W], gr[:, W], ALU.mult), [f"St{tb}", kgr], [kA_])
        yield
        V(lambda e: e.tensor_tensor(bti[:, W], Cc[:, W], gi[:, W], ALU.mult), [f"Ct{tb}", kgi], [kbi])
        yield
        V(lambda e: e.tensor_tensor(hi_[:, W], tA[:, W], bti[:, W], ALU.add), [kA_, kbi], [f"hi16_{hb}"])
        yield
        for st in range(nst):
            n0 = st * 512
            nn = min(512, sn - n0)
            py = C.psum[yb_]
            P.add("tensor", lambda e, py=py, n0=n0, nn=nn: e.matmul(py[:, 0:nn], CR[:, q, :], hr_[:, n0:n0 + nn], start=(gp == 0), stop=False),
                  [f"CR_{q}", f"hr16_{hb}"], [f"psb{yb_}"])
            P.add("tensor", lambda e, py=py, n0=n0, nn=nn: e.matmul(py[:, 0:nn], CI[:, q, :], hi_[:, n0:n0 + nn], start=False, stop=(gp == 3)),
                  [f"CI_{q}", f"hi16_{hb}"], [f"psb{yb_}"])

    def seg_body(d, si, first, kind, s, ub, it0, seg_n):
        if kind == "ctx":
            sn, r, t0 = 256, 0, 0
            lay0 = 0 if d == 0 else 4 * TLAT
        else:
            sn, r, t0 = 512, s // 4, TCTX + 512 * (s % 4)
            lay0 = (TCTX + 512 * s) if d == 0 else 512 * s
        for c in range(4):
            usrc = uA_all[c][r * 128:(r + 1) * 128, t0:t0 + sn] if t0 < 1280 else uB_all[c][r * 128:(r + 1) * 128, t0 - 1280:t0 - 1280 + sn]
            C.load_cast(u16[ub][:, c, 0:sn], f"u16_{ub}_{c}", usrc, sn, cast_eng="scalar")
        nst = (sn + 511) // 512
        it = it0
        for gp0 in (0, 2):
            gens = []
            for gp in (gp0, gp0 + 1):
                tb = it % 2
                it += 1
                gens.append(gp_body(d, first, lay0, sn, ub, gp, tb, it % 2, seg_n))
            while gens:
                for g_ in list(gens):
                    try:
                        next(g_)
                    except StopIteration:
                        gens.remove(g_)
        ybuf = yb[ub]
        ybk = 4 + (seg_n % 2)
        if kind == "ctx":
            P.add("scalar", lambda e: e.copy(ybuf[:, 0, 0:256], C.psum[ybk][:, 0:256]), [f"psb{ybk}"], [f"yb{ub}"])
            P.dma(yc_loc[:, d * 256:(d + 1) * 256], ybuf[:, 0, 0:256], [f"yb{ub}"], [], f"st_yb{ub}")
        else:
            for c in range(4):
                P.add("scalar", lambda e, c=c: e.activation(ybuf[:, c, 0:512], C.psum[ybk][:, 0:512], AF.Identity, scale=oh[:, c:c + 1]),
                      [f"psb{ybk}", "oh"], [f"yb{ub}"])
            for c in range(4):
                P.dma(y_rs_in[d][c][(s // 4) * 128:(s // 4 + 1) * 128, 512 * (s % 4):512 * (s % 4) + 512], ybuf[:, c, 0:512], [f"yb{ub}"], [], f"st_yb{ub}_{c}")
        return it

    it = 0
    n = 0
    for d in range(2):
        order = [("ctx", 0)] + ([("lat", s) for s in range(16)] if d == 0 else [("lat", s) for s in range(15, -1, -1)])
        for si, (kind, s) in enumerate(order):
            it = seg_body(d, si, si == 0, kind, s, n % 2, it, n)
            n += 1


def phase_po(P, C, layer, xT, cols, Dr, tiles=TILES):
    l0 = layer == 0
    o_loc, wo_d = Dr["o_loc"], Dr["wo"][layer]
    wo16 = P.sb([128, 8, 1024], BF16, name="wo16")
    for k in range(8):
        C.load_cast(wo16[:, k, :], f"wo16_{k}", wo_d[k], 1024)
    nch = 4 if l0 else 8
    ot = [P.sb([128, nch, 512], BF16, name=f"ot{i}") for i in range(2)]
    if l0:
        uA, uB, y_rs_out, yc_all = Dr["uA"], Dr["uB"], Dr["y_rs_out"], Dr["yc_all"]
        g0 = P.sb([128, 4, 512], BF16, name="glu0_sb")
        g1 = P.sb([128, 4, 512], BF16, name="glu1_sb")
        for k in range(4):
            C.load_cast(g0[:, k, :], f"g0_{k}", Dr["glu0"][k], 512)
            C.load_cast(g1[:, k, :], f"g1_{k}", Dr["glu1"][k], 512)
        dcol = P.sb([128, 4], F32, name="dcol_sb")
        C.load(dcol[:], "dcol", Dr["dcol"])
        ub = [P.sb([128, 512], F32, name=f"ub{i}") for i in range(2)]
        yfb = [P.sb([128, 512], F32, name=f"yfb{i}") for i in range(2)]
        yrb = [P.sb([128, 512], F32, name=f"yrb{i}") for i in range(2)]
        t1 = [P.sb([128, 512], F32, name=f"t1_{i}") for i in range(2)]
        gt = [P.sb([128, 4, 512], BF16, name=f"gt{i}") for i in range(2)]
        glt = [P.sb([128, 4, 512], BF16, name=f"glt{i}") for i in range(2)]
        sgb = [P.sb([128, 512], F32, name=f"sgb{i}") for i in range(2)]
    it = 0
    for ti, (t0, tn) in enumerate(tiles):
        v = "c" if t0 < TCTX else "x"
        tb = ti % 2
        P.dma(ot[tb][:, :, 0:tn], o_loc[:, 0:nch, t0:t0 + tn], [], [f"ot{tb}"], f"ld_ot{tb}")
        if l0:
            for c in range(4):
                b = it % 2
                it += 1
                if t0 < TCTX:
                    yf_src = yc_all[c * 128:(c + 1) * 128, 0:256]
                    yr_src = yc_all[c * 128:(c + 1) * 128, 256:512]
                else:
                    yf_src = y_rs_out[0][c][:, t0 - TCTX:t0 - TCTX + tn]
                    yr_src = y_rs_out[1][c][:, t0 - TCTX:t0 - TCTX + tn]
                u_src = uA[c][:, t0:t0 + tn] if t0 < 1280 else uB[c][:, t0 - 1280:t0 - 1280 + tn]
                P.dma(ub[b][:, 0:tn], u_src, [], [f"ub{b}"], f"ld_ub{b}")
                P.dma(yfb[b][:, 0:tn], yf_src, [], [f"yfb{b}"], f"ld_yfb{b}")
                P.dma(yrb[b][:, 0:tn], yr_src, [], [f"yrb{b}"], f"ld_yrb{b}")
                y, u_, yr_, tt = yfb[b], ub[b], yrb[b], t1[b]
                P.add("vector", lambda e, y=y, yr_=yr_, tn=tn: e.tensor_tensor(y[:, 0:tn], y[:, 0:tn], yr_[:, 0:tn], ALU.add), [f"yfb{b}", f"yrb{b}"], [f"yfb{b}"])
                P.add("vector", lambda e, y=y, u_=u_, c=c, tn=tn: e.scalar_tensor_tensor(y[:, 0:tn], u_[:, 0:tn], dcol[:, c:c + 1], y[:, 0:tn], ALU.mult, ALU.add),
                      [f"yfb{b}", f"ub{b}", "dcol"], [f"yfb{b}"])
                P.add("scalar", lambda e, tt=tt, y=y, tn=tn: e.activation(tt[:, 0:tn], y[:, 0:tn], AF.Square), [f"yfb{b}"], [f"t1_{b}"])
                P.add("vector", lambda e, tt=tt, tn=tn: e.tensor_scalar(tt[:, 0:tn], tt[:, 0:tn], 0.044715, 1.0, ALU.mult, ALU.add), [f"t1_{b}"], [f"t1_{b}"])
                P.add("vector", lambda e, tt=tt, y=y, tn=tn: e.tensor_tensor(tt[:, 0:tn], tt[:, 0:tn], y[:, 0:tn], ALU.mult), [f"t1_{b}", f"yfb{b}"], [f"t1_{b}"])
                P.add("scalar", lambda e, tt=tt, tn=tn: e.activation(tt[:, 0:tn], tt[:, 0:tn], AF.Sigmoid, scale=1.5957691216), [f"t1_{b}"], [f"t1_{b}"])
                P.add("vector", lambda e, tt=tt, y=y, c=c, tb=tb, tn=tn: e.tensor_tensor(gt[tb][:, c, 0:tn], tt[:, 0:tn], y[:, 0:tn], ALU.mult), [f"t1_{b}", f"yfb{b}"], [f"gt{tb}_{c}"])
            for oc in range(4):
                pb = oc % 2
                pa, pg = C.psum[pb], C.psum[2 + pb]
                for k in range(4):
                    P.add("tensor", lambda e, pa=pa, k=k, oc=oc, tb=tb, tn=tn: e.matmul(pa[:, 0:tn], g0[:, k, oc * 128:(oc + 1) * 128], gt[tb][:, k, 0:tn], start=(k == 0), stop=(k == 3)),
                          [f"g0_{k}", f"gt{tb}_{k}"], [f"psb{pb}"])
                for k in range(4):
                    P.add("tensor", lambda e, pg=pg, k=k, oc=oc, tb=tb, tn=tn: e.matmul(pg[:, 0:tn], g1[:, k, oc * 128:(oc + 1) * 128], gt[tb][:, k, 0:tn], start=(k == 0), stop=(k == 3)),
                          [f"g1_{k}", f"gt{tb}_{k}"], [f"psb{2 + pb}"])
                sg = sgb[pb]
                P.add("scalar", lambda e, sg=sg, pg=pg, tn=tn: e.activation(sg[:, 0:tn], pg[:, 0:tn], AF.Sigmoid), [f"psb{2 + pb}"], [f"sgb{pb}"])
                P.add("vector", lambda e, sg=sg, pa=pa, oc=oc, tb=tb, tn=tn: e.tensor_tensor(glt[tb][:, oc, 0:tn], sg[:, 0:tn], pa[:, 0:tn], ALU.mult), [f"sgb{pb}", f"psb{pb}"], [f"glt{tb}_{oc}"])
        for oc in range(8):
            pb = 4 + oc % 2
            po = C.psum[pb]
            srcs = ([(glt[tb][:, k, 0:tn], f"glt{tb}_{k}", k) for k in range(4)] if l0 else []) + \
                   [(ot[tb][:, k, 0:tn], f"ot{tb}", (4 + k) if l0 else k) for k in range(nch)]
            for n, (rhs, rk, kk) in enumerate(srcs):
                P.add("tensor", lambda e, po=po, rhs=rhs, kk=kk, oc=oc, tn=tn, n=n, ns=len(srcs): e.matmul(po[:, 0:tn], wo16[:, kk, oc * 128:(oc + 1) * 128], rhs, start=(n == 0), stop=(n == ns - 1)),
                      [f"wo16_{kk}", rk], [f"psb{pb}"])
            P.add("vector", lambda e, po=po, oc=oc, t0=t0, tn=tn, v=v: e.scalar_tensor_tensor(xT[:, oc, t0:t0 + tn], po[:, 0:tn], colsel(cols, v, "gate", 1, oc),
                                                                                             xT[:, oc, t0:t0 + tn], ALU.mult, ALU.add),
                  [f"psb{pb}", f"gate_{v}", f"x{oc}_{ti}"], [f"x{oc}_{ti}"])


GROUPS = [[0, 1, 2, 3], [4, 5, 6, 7]]


def build_fused(stop=999):
    nc = bass.Bass("TRN2", target_bir_lowering=False)
    ext = lambda name, shape, dtype=F32: nc.dram_tensor(name, list(shape), dtype, kind="ExternalInput").ap()
    scr = lambda name, shape, dtype=F32: nc.dram_tensor(name, list(shape), dtype).ap()
    xT_d = ext("xT", [128, 8, T])
    cT_d = ext("cT", [128, 16])
    modw_d = ext("modw", [36, 128, 8, 128])
    modb_d = ext("modb", [128, 36])
    normg_d = ext("normg", [128, 2, 24])
    wg_d = ext("wg", [4, FC, 128, 8, 128])
    wu_d = ext("wu", [4, FC, 128, 8, 128])
    wd_d = ext("wd", [4, FC, 128, 1024])
    win_d = [ext("win0", [10, 128, 8, 128]), ext("win1", [12, 128, 8, 128])]
    qkg_d = ext("qkg", [128, 2, 2])
    cos_d, sin_d = ext("cosT", [128, T]), ext("sinT", [128, T])
    rot_d, bones_d = ext("rot", [128, 128]), ext("bones", [128, 128])
    Dr = dict(
        Bl=ext("Bl", [2, 4, 2, 4, 128, 128]), Cl=ext("Cl", [2, 4, 2, 128, 128]), pcols=ext("pcols", [128, 3, 8]),
        iota_c=ext("iota_c", [128, 2, 66]), iota_j=ext("iota_j", [128, 2, 128]), onehot=ext("onehot", [128, 4]),
        dcol=ext("dcol", [128, 4]), glu0=ext("glu0", [4, 128, 512]), glu1=ext("glu1", [4, 128, 512]),
        wo=ext("wo", [2, 8, 128, 1024]), sink=ext("sink", [128, 16]),
        wmask=ext("wmask", [128, 4, 6, 512], BF16), emask=ext("emask", [128, 2, 4, 512], BF16),
    )
    out_d = nc.dram_tensor("outT", [128, 8, TLAT], F32, kind="ExternalOutput").ap()
    Dr.update(
        mod_loc=scr("mod_loc", [128, 72]), mod_all=scr("mod_all", [512, 72]),
        uA=[scr(f"uA{c}", [128, 1280]) for c in range(4)], uB=[scr(f"uB{c}", [128, 1024]) for c in range(4)],
        uA_all=[scr(f"uA_all{c}", [512, 1280]) for c in range(4)], uB_all=[scr(f"uB_all{c}", [512, 1024]) for c in range(4)],
        q_loc=scr("q_loc", [128, 8, T], BF16),
        k_loc=scr("k_loc", [128, 2, T]), kA=scr("kA", [128, 1280]), kB=scr("kB", [128, 1024]),
        kA_all=scr("kA_all", [512, 1280]), kB_all=scr("kB_all", [512, 1024]),
        v_loc=scr("v_loc", [T, 256]), vA=scr("vA", [1280, 128]), vB=scr("vB", [1024, 128]),
        vA_all=scr("vA_all", [5120, 128]), vB_all=scr("vB_all", [4096, 128]),
        y_rs_in=[[scr(f"y_rs_in{d}{c}", [512, TLAT]) for c in range(4)] for d in range(2)],
        y_rs_out=[[scr(f"y_rs_out{d}{c}", [128, TLAT]) for c in range(4)] for d in range(2)],
        yc_loc=scr("yc_loc", [128, 512]), yc_all=scr("yc_all", [512, 512]),
        o_loc=scr("o_loc", [128, 8, T], BF16),
        ek_loc=scr("ek_loc", [128, 512]), ek_all=scr("ek_all", [512, 512]),
        ev_loc=scr("ev_loc", [256, 256]), ev_all=scr("ev_all", [1024, 256]),
    )
    P = Prog(nc)
    xT = P.sb([128, 8, T], F32, name="xT_sb", persist=True)
    modall = P.sb([128, 2, 72, 2], F32, name="modall", persist=True)
    ng = P.sb([128, 2, 24], F32, name="normg_sb", persist=True)
    coltiles = {(l, v): (P.sb([128, 24], F32, name=f"gs_{v}{l}", persist=True), P.sb([128, 24], F32, name=f"gate_{v}{l}", persist=True))
                for l in range(2) for v in ("x", "c")}
    C = Ctx(P)

    c32 = P.sb([128, 16], F32, name="c32")
    c16 = P.sb([128, 16], BF16, name="c16")
    bt = P.sb([128, 36], F32, name="bt")
    res = P.sb([128, 36, 2], F32, name="res")
    w16 = [P.sb([128, 8, 128], BF16, name=f"w16_{i}") for i in range(2)]
    for c in range(KC):
        for ti, (t0, tn) in enumerate(TILES):
            P.dma(xT[:, c, t0:t0 + tn], xT_d[:, c, t0:t0 + tn], [], [f"x{c}_{ti}"], f"ld_x{(c * 5 + ti) % 4}")
    C.load(c32[:], "c32", cT_d)
    C.load(bt[:], "bt", modb_d)
    C.load(ng[:], "normg", normg_d)
    P.add("scalar", lambda e: e.activation(c16[:], c32[:], AF.Silu), ["c32"], ["c16"])
    for oc in range(36):
        sl = oc % 2
        C.load_cast(w16[sl][:].rearrange("p k m -> p (k m)"), f"w16_{sl}", modw_d[oc].rearrange("p k m -> p (k m)"), 1024)
        ps = C.psum[oc % 2]
        for k in range(8):
            P.add("tensor", lambda e, ps=ps, k=k, sl=sl: e.matmul(ps[:, 0:2], w16[sl][:, k, :], c16[:, k * 2:(k + 1) * 2], start=(k == 0), stop=(k == 7)),
                  [f"w16_{sl}", "c16"], [f"psb{oc % 2}"])
        P.add("vector", lambda e, ps=ps, oc=oc: e.tensor_scalar(res[:, oc, :], ps[:, 0:2], bt[:, oc:oc + 1], None, ALU.add),
              [f"psb{oc % 2}", "bt"], ["res"])
    P.dma(Dr["mod_loc"], res[:].rearrange("p a b -> p (a b)"), ["res"], ["mod_loc"], "st_mod")
    P.coll("AllGather", ALU.bypass, GROUPS, Dr["mod_loc"], Dr["mod_all"], ["mod_loc"], ["mod_all"], "cc_mod")
    for l in range(2):
        for r2 in range(2):
            r = 2 * l + r2
            C.load(modall[:, l, 36 * r2:36 * (r2 + 1), :].rearrange("p a b -> p (a b)"), "modall", Dr["mod_all"][r * 128:(r + 1) * 128, :], reads=["mod_all"])
    cols = [make_cols(P, modall, ng, l, coltiles) for l in range(2)]
    P.end_phase()
    if stop == 1:
        P.close()
        return nc

    def phase_ffn(layer, f_idx, s, tiles=TILES):
        C.new_phase()
        K = alloc_common(P, C)
        hT = P.sb([128, 8, T], BF16, name="hT_sb")
        emit_norm(P, C, K, xT, hT, cols[layer], s, tiles=tiles)
        emit_ffn(P, C, K, xT, hT, cols[layer], s, wg_d[2 * layer + f_idx], wu_d[2 * layer + f_idx], wd_d[2 * layer + f_idx], tiles=tiles)
        return K, hT

    def phase_a(layer):
        n_u, n_q, n_k, vw = (4, 4, 1, 128) if layer == 0 else (0, 8, 2, 256)
        n_fm = n_u + n_q + n_k
        K, hT = phase_ffn(layer, 0, 0)
        emit_norm(P, C, K, xT, hT, cols[layer], 1)
        emit_consts(P, C, K, rot_d, bones_d)
        qkg = P.sb([128, 2, 2], F32, name="qkg_sb")
        C.load(qkg[:], "qkg", qkg_d)
        K["cst"] = [P.sb([128, 512], F32, name=f"cst{i}") for i in range(1)]
        K["snt"] = [P.sb([128, 512], F32, name=f"snt{i}") for i in range(1)]
        K["win16"] = [P.sb([128, 8, 128], BF16, name=f"win16_{i}") for i in range(2)]
        K["zq"] = [P.sb([128, 512], F32, name=f"zq{i}") for i in range(2)]
        K["zb"] = [P.sb([128, 512], BF16, name=f"zb{i}") for i in range(2)]
        K["ob32"] = [P.sb([128, 512], F32, name=f"ob32_{i}") for i in range(2)]
        K["ob16"] = [P.sb([128, 512], BF16, name=f"ob16_{i}") for i in range(2)]
        K["wv16"] = P.sb([128, 8, vw], BF16, name="wv16")
        K["vb"] = [P.sb([128, 256], F32, name="vb0")] * 2
        kinds = ["u"] * n_u + ["q"] * n_q + ["k"] * n_k
        def split_dst(a_, b_):
            return lambda t0, tn: (a_[:, t0:t0 + tn] if t0 < 1280 else b_[:, t0 - 1280:t0 - 1280 + tn])
        kdst = [split_dst(Dr["kA"], Dr["kB"])] if layer == 0 else [Dr["k_loc"][:, i, :] for i in range(2)]
        outs = [(split_dst(Dr["uA"][i], Dr["uB"][i]), F32) for i in range(n_u)] + [(Dr["q_loc"][:, i, :], BF16) for i in range(n_q)] + [(kd_, F32) for kd_ in kdst]
        if layer == 0:
            v_d = [Dr["vA"][tt * 128:(tt + 1) * 128, :] if tt < 10 else Dr["vB"][(tt - 10) * 128:(tt - 9) * 128, :] for tt in range(T // 128)]
        else:
            v_d = Dr["v_loc"].rearrange("(tt p) f -> tt p f", p=128)
        emit_inproj(P, C, K, hT, win_d[layer], n_fm, kinds, qkg[:, layer, :], cos_d, sin_d, outs, v_d, vw, n_fm)
        P.end_phase()

    phase_a(0)
    if stop == 2:
        P.close()
        return nc
    for c in range(4):
        P.coll("AllGather", ALU.bypass, GROUPS, Dr["uA"][c], Dr["uA_all"][c], [], [f"uA_all{c}"], "cc_u")
        P.coll("AllGather", ALU.bypass, GROUPS, Dr["uB"][c], Dr["uB_all"][c], [], [f"uB_all{c}"], "cc_u")
    P.coll("AllGather", ALU.bypass, GROUPS, Dr["kA"], Dr["kA_all"], [], ["kA_all"], "cc_k")
    P.coll("AllGather", ALU.bypass, GROUPS, Dr["kB"], Dr["kB_all"], [], ["kB_all"], "cc_k")
    P.coll("AllGather", ALU.bypass, GROUPS, Dr["vA"], Dr["vA_all"], [], ["vA_all"], "cc_v")
    P.coll("AllGather", ALU.bypass, GROUPS, Dr["vB"], Dr["vB_all"], [], ["vB_all"], "cc_v")
    P.end_phase()
    if stop == 3:
        P.close()
        return nc
    C.new_phase()
    phase_s5(P, C, Dr)
    P.end_phase()
    if stop == 4:
        P.close()
        return nc
    for d in range(2):
        for c in range(4):
            P.coll("ReduceScatter", ALU.add, GROUPS, Dr["y_rs_in"][d][c], Dr["y_rs_out"][d][c], [], [f"y_rs_out{d}{c}"], "cc_y")
    P.coll("AllGather", ALU.bypass, GROUPS, Dr["yc_loc"], Dr["yc_all"], [], ["yc_all"], "cc_yc")
    P.end_phase()
    if stop == 5:
        P.close()
        return nc
    C.new_phase()
    phase_att(P, C, 0, Dr)
    P.end_phase()
    if stop == 6:
        P.close()
        return nc
    C.new_phase()
    phase_po(P, C, 0, xT, cols[0], Dr)
    P.end_phase()
    if stop == 7:
        P.close()
        return nc
    phase_ffn(0, 1, 2)
    P.end_phase()
    if stop == 8:
        P.close()
        return nc
    phase_a(1)
    if stop == 9:
        P.close()
        return nc
    for c in range(2):
        for lh, col0 in ((0, TCTX), (1, T - 128)):
            P.dma(Dr["ek_loc"][:, c * 256 + lh * 128:c * 256 + (lh + 1) * 128], Dr["k_loc"][:, c, col0:col0 + 128], [], ["ek_loc"], f"cp_ek{c}{lh}")
    for lh, col0 in ((0, TCTX), (1, T - 128)):
        P.dma(Dr["ev_loc"][lh * 128:(lh + 1) * 128, :], Dr["v_loc"][col0:col0 + 128, :], [], ["ev_loc"], f"cp_ev{lh}")
    P.coll("AllGather", ALU.bypass, GROUPS, Dr["ek_loc"], Dr["ek_all"], ["ek_loc"], ["ek_all"], "cc_ek")
    P.coll("AllGather", ALU.bypass, GROUPS, Dr["ev_loc"], Dr["ev_all"], ["ev_loc"], ["ev_all"], "cc_ev")
    P.end_phase()
    if stop == 10:
        P.close()
        return nc
    C.new_phase()
    phase_att(P, C, 1, Dr)
    P.end_phase()
    if stop == 11:
        P.close()
        return nc
    C.new_phase()
    phase_po(P, C, 1, xT, cols[1], Dr, tiles=TILES[1:])
    P.end_phase()
    if stop == 12:
        P.close()
        return nc
    phase_ffn(1, 1, 2, tiles=TILES[1:])
    for c in range(KC):
        for ti, (t0, tn) in enumerate(TILES[1:]):
            P.dma(out_d[:, c, t0 - TCTX:t0 - TCTX + tn], xT[:, c, t0:t0 + tn], [f"x{c}_{ti}"], [], f"st_x{(c * 5 + ti) % 4}")
    P.end_phase()
    if stop == 13:
        P.close()
        return nc
    P.close()
    return nc
def fm(x2d):
    t, f = x2d.shape
    return np.ascontiguousarray(x2d.T.reshape(f // 128, 128, t).transpose(1, 0, 2))


def unfm(a):
    p, c, t = a.shape
    return np.ascontiguousarray(a.transpose(1, 0, 2).reshape(c * 128, t).T)


def w_oc(W):
    k, n = W.shape
    return np.ascontiguousarray(W.reshape(k // 128, 128, n // 128, 128).transpose(2, 1, 0, 3))


def w_rows(W):
    k, n = W.shape
    return np.ascontiguousarray(W.reshape(k // 128, 128, n))


def rope_tables():
    rows = 8192 // 64
    r = np.repeat(np.arange(rows, dtype=np.float32), 64)
    col = np.tile(np.arange(64, dtype=np.float32), rows)
    inv = (10000.0 ** (-np.arange(16, dtype=np.float32) / 16)).astype(np.float32)
    ang = np.concatenate([r[:, None] * inv, col[:, None] * inv], axis=-1).astype(np.float32)
    cos = np.cos(ang).astype(np.float32).T
    sin = np.sin(ang).astype(np.float32).T
    idx = np.arange(128) % 32
    return cos[idx], sin[idx]


def const_mats():
    rot = np.zeros((128, 128), np.float32)
    for m in range(128):
        j = m % 64
        if j < 32:
            rot[m + 32, m] = -1.0
        else:
            rot[m - 32, m] = 1.0
    bones = np.zeros((128, 128), np.float32)
    bones[:64, :64] = 1.0
    bones[64:, 64:] = 1.0
    return rot, bones


def core_tokens(inp_x, ctx, i):
    b, q = i // 4, i % 4
    return np.concatenate([ctx[b], inp_x[b, q * TLAT:(q + 1) * TLAT]], axis=0)


def fused_inputs(inp):
    cos, sin = rope_tables()
    rot, bones = const_mats()
    W = np.concatenate([inp["mod_w"][0], inp["mod_w"][1]], axis=1)
    Bv = np.concatenate([inp["mod_b"][0], inp["mod_b"][1]], axis=0)
    Wl = W.reshape(8, 128, 144, 128).transpose(2, 1, 0, 3)
    Bl_mod = Bv.reshape(144, 128).T
    normg = np.ascontiguousarray(np.stack([inp["norm_g"][l].reshape(3, 8, 128).transpose(2, 0, 1).reshape(128, 24) for l in range(2)], axis=1))
    wg = np.stack([w_oc(inp["ffn_wg"][l, f]) for l in range(2) for f in range(2)])
    wu = np.stack([w_oc(inp["ffn_wu"][l, f]) for l in range(2) for f in range(2)])
    wd = np.stack([w_rows(inp["ffn_wd"][l, f]) for l in range(2) for f in range(2)])
    win0, win1 = w_oc(inp["ab_w_in"][0]), w_oc(inp["win_w_in"][0])
    qkg = np.ascontiguousarray(np.stack([inp["qk_norm"][l][:, np.arange(128) % 64].T for l in range(2)], axis=1))
    wo = np.stack([w_rows(inp["w_out"][l]) for l in range(2)])
    dcol = np.ascontiguousarray(inp["s5_d"][0].reshape(4, 128).T)
    glu0, glu1 = w_rows(inp["s5_glu_w"][0, 0]), w_rows(inp["s5_glu_w"][0, 1])
    sink = np.ascontiguousarray(np.broadcast_to(inp["win_sink"][0].reshape(1, 16), (128, 16))).astype(np.float32)
    iota_c = np.zeros((128, 2, 66), np.float32)
    iota_c[:, 0, :] = np.arange(66)
    iota_c[:, 1, :] = 65 - np.arange(66)
    iota_j = np.zeros((128, 2, 128), np.float32)
    iota_j[:, 0, :] = np.arange(128)
    iota_j[:, 1, :] = 127 - np.arange(128)
    one, zero = np.ones((1,), NPBF)[0], np.zeros((1,), NPBF)[0]
    kk = np.arange(128)[:, None]
    qq = np.arange(512)[None, :]
    wmask = np.zeros((128, 4, 6, 512), NPBF)
    for m in range(4):
        for r in range(6):
            ok = np.abs((4 * m + r - 1) * 128 + kk - (512 * m + qq)) <= 128
            wmask[:, m, r, :] = np.where(ok, one, zero)
    e = 0
    maps = []
    for i in range(NCORES):
        b, q = i // 4, i % 4
        cT = np.ascontiguousarray(np.stack([inp["c"][b], inp["c_ctx"]], axis=0).T.reshape(8, 128, 2).transpose(1, 0, 2).reshape(128, 16))
        cosT = np.concatenate([np.ones((128, TCTX), np.float32), cos[:, q * TLAT:(q + 1) * TLAT]], axis=1)
        sinT = np.concatenate([np.zeros((128, TCTX), np.float32), sin[:, q * TLAT:(q + 1) * TLAT]], axis=1)
        Bl = np.zeros((2, 4, 2, 4, 128, 128), np.float32)
        Cl = np.zeros((2, 4, 2, 128, 128), np.float32)
        pcols = np.zeros((128, 3, 8), np.float32)
        j = q
        for d in range(2):
            for gp in range(4):
                for gl in range(2):
                    g = 8 * j + 2 * gp + gl
                    ch0 = (2 * gp + gl) * 16
                    for ri, (bsrc, csrc) in enumerate(((inp["s5_b_re"], inp["s5_c_re"]), (inp["s5_b_im"], inp["s5_c_im"]))):
                        Bl[d, gp, ri, j, ch0:ch0 + 16, gl * 64:(gl + 1) * 64] = bsrc[e, d, g].T
                        Cl[d, gp, ri, gl * 64:(gl + 1) * 64, ch0:ch0 + 16] = csrc[e, d, g].T
                    pcols[gl * 64:(gl + 1) * 64, 0, d * 4 + gp] = inp["s5_lam_re"][e, d, g]
                    pcols[gl * 64:(gl + 1) * 64, 1, d * 4 + gp] = inp["s5_lam_im"][e, d, g]
                    pcols[gl * 64:(gl + 1) * 64, 2, d * 4 + gp] = inp["s5_log_step"][e, d, g]
        onehot = np.zeros((128, 4), np.float32)
        onehot[:, q] = 1.0
        emask = np.zeros((128, 2, 4, 512), NPBF)
        for r in range(4):
            if r == q - 1:
                emask[:, 0, r, :] = np.where(np.abs(-128 + kk - qq) <= 128, one, zero)
            if r == q + 1:
                emask[:, 1, r, :] = np.where(np.abs(2048 + kk - 1536 - qq) <= 128, one, zero)
        maps.append(dict(
            xT=fm(core_tokens(inp["x"], inp["ctx"], i)), cT=cT, modw=np.ascontiguousarray(Wl[36 * q:36 * q + 36]),
            modb=np.ascontiguousarray(Bl_mod[:, 36 * q:36 * q + 36]), normg=normg, wg=wg, wu=wu, wd=wd, win0=win0, win1=win1, qkg=qkg,
            cosT=np.ascontiguousarray(cosT), sinT=np.ascontiguousarray(sinT), rot=rot, bones=bones,
            Bl=Bl, Cl=Cl, pcols=pcols, iota_c=iota_c, iota_j=iota_j, onehot=onehot, dcol=dcol, glu0=glu0, glu1=glu1, wo=wo, sink=sink,
            wmask=wmask, emask=emask))
    return maps


def kernel(**inputs):
    inp = {k: np.asarray(v) for k, v in inputs.items()}
    nc = build_fused()
    res = run_bass_kernel_spmd(nc, fused_inputs(inp), core_ids=list(range(NCORES)))
    out = np.zeros((2, 4 * TLAT, D), np.float32)
    for i in range(NCORES):
        b, q = i // 4, i % 4
        out[b, q * TLAT:(q + 1) * TLAT] = unfm(np.asarray(res.results[i]["outT"]))
    return out
```

```python
import contextlib
import math
import numpy as np
import ml_dtypes
import concourse.bass as bass
import concourse.mybir as mybir
from concourse.bass_utils import run_bass_kernel_spmd

F32 = mybir.dt.float32
BF16 = mybir.dt.bfloat16
I32 = mybir.dt.int32
ALU = mybir.AluOpType
AF = mybir.ActivationFunctionType
AX = mybir.AxisListType
NPBF = ml_dtypes.bfloat16

NCORES = 8
D = 1024
KC = 8
DFF = 2816
FC = 22
TCTX = 256
TLAT = 2048
T = TCTX + TLAT
TILES = [(0, 256), (256, 512), (768, 512), (1280, 512), (1792, 512)]
EPS = 1e-6
GFF = 4


class _Op:
    __slots__ = ("eng", "pos", "fn", "cwaits", "dwaits", "signal", "dma_key", "dma_k")


class Prog:
    ENGS = ("tensor", "vector", "scalar", "gpsimd", "sync")
    CENG = ("tensor", "vector", "scalar", "gpsimd")

    def __init__(self, nc):
        self.nc = nc
        self.ges = contextlib.ExitStack()
        self.pes = contextlib.ExitStack()
        self.esem = {e: self.ges.enter_context(nc.semaphore(f"s_{e}")) for e in self.CENG}
        self.esig = {e: 0 for e in self.CENG}
        self.dsem = {}
        self.dma_cnt = {}
        self.dma_inc = {}
        self.n_sb = 0
        self.n_phase = 0
        self._reset()

    def _reset(self):
        self.ops = {e: [] for e in self.ENGS}
        self.last_w = {}
        self.readers = {}

    def sb(self, shape, dtype=F32, name=None, persist=False):
        self.n_sb += 1
        nm = (name or "sb") + f"_{self.n_sb}"
        es = self.ges if persist else self.pes
        return es.enter_context(self.nc.sbuf_tensor(nm, list(shape), dtype))

    def ps(self, shape, dtype=F32, name=None):
        self.n_sb += 1
        return self.ges.enter_context(self.nc.psum_tensor(name or f"ps{self.n_sb}", list(shape), dtype))

    def add(self, eng, fn, reads=(), writes=(), dma_key=None, inc=16):
        op = _Op()
        op.eng, op.fn, op.signal = eng, fn, False
        op.pos = len(self.ops[eng])
        op.dma_key = dma_key
        op.dma_k = None
        deps = {}
        for k in reads:
            d = self.last_w.get(k)
            if d is not None:
                deps[id(d)] = d
        for k in writes:
            d = self.last_w.get(k)
            if d is not None:
                deps[id(d)] = d
            for r in self.readers.get(k, ()):
                deps[id(r)] = r
        cw = {}
        dw = {}
        for i, d in deps.items():
            if d.dma_key is not None:
                v = self.dma_inc[d.dma_key] * (d.dma_k + 1)
                dw[d.dma_key] = max(dw.get(d.dma_key, 0), v)
            elif d.eng == eng:
                if eng != "tensor":
                    cw[d.eng] = max(cw.get(d.eng, -1), d.pos)
            else:
                cw[d.eng] = max(cw.get(d.eng, -1), d.pos)
        if dma_key is not None:
            self.dma_inc.setdefault(dma_key, inc)
            k = self.dma_cnt.get(dma_key, 0)
            op.dma_k = k
            self.dma_cnt[dma_key] = k + 1
            if k > 0:
                dw[dma_key] = max(dw.get(dma_key, 0), self.dma_inc[dma_key] * k)
        op.cwaits = []
        for e, p in cw.items():
            d = self.ops[e][p]
            d.signal = True
            op.cwaits.append(d)
        op.dwaits = list(dw.items())
        self.ops[eng].append(op)
        for k in reads:
            self.readers.setdefault(k, []).append(op)
        for k in writes:
            self.last_w[k] = op
            self.readers[k] = []
        return op

    def dma(self, out, in_, reads, writes, key, eng="sync", **kw):
        return self.add(eng, lambda e: e.dma_start(out=out, in_=in_, **kw), reads, writes, dma_key=key)

    def coll(self, kind, op, groups, in_ap, out_ap, reads, writes, key):
        self.n_coll = getattr(self, "n_coll", 0) + 1
        key = f"{key}_{self.n_coll}"
        return self.add("gpsimd", lambda e: e.collective_compute(kind, op, replica_groups=groups, ins=[in_ap], outs=[out_ap]),
                        reads, writes, dma_key=key, inc=1)

    def end_phase(self):
        nc = self.nc
        lasts = []
        for e in self.CENG:
            real = [o for o in self.ops[e] if o.dma_key is None and o.fn is not None]
            if real:
                lasts.append(real[-1])
        for d in lasts:
            d.signal = True
        for e in self.ENGS:
            op = _Op()
            op.eng, op.fn, op.signal, op.dma_key, op.dma_k = e, None, False, None, None
            op.pos = len(self.ops[e])
            op.cwaits = [d for d in lasts if d.eng != e]
            op.dwaits = [(k, self.dma_inc[k] * c) for k, c in self.dma_cnt.items()]
            self.ops[e].append(op)
        sig = {}
        for e in self.CENG:
            c = self.esig[e]
            for op in self.ops[e]:
                if op.signal:
                    c += 1
                    sig[id(op)] = c
            self.esig[e] = c
        for k in self.dma_cnt:
            if k not in self.dsem:
                self.dsem[k] = self.ges.enter_context(nc.semaphore(f"d_{len(self.dsem)}"))
        esem, dsem = self.esem, self.dsem
        with nc.Block() as block:
            def mk(ename):
                ops = self.ops[ename]

                def body(e):
                    for op in ops:
                        for d in op.cwaits:
                            e.wait_ge(esem[d.eng], sig[id(d)])
                        for k, v in op.dwaits:
                            e.wait_ge(dsem[k], v)
                        if op.fn is None:
                            continue
                        ins = op.fn(e)
                        if op.dma_key is not None:
                            ins.then_inc(dsem[op.dma_key], self.dma_inc[op.dma_key])
                        elif op.signal:
                            ins.then_inc(esem[ename], 1)
                return body

            for ename in self.ENGS:
                getattr(block, ename)(mk(ename))
        self.pes.close()
        self.pes = contextlib.ExitStack()
        self._reset()
        self.n_phase += 1

    def close(self):
        self.ges.close()


class Ctx:
    def __init__(self, P):
        self.P = P
        self.psum = [P.ps([128, 512], F32, name=f"psb{i}") for i in range(8)]
        self.ld_i = 0
        self.new_phase()

    def new_phase(self):
        P = self.P
        self.stage = [P.sb([128, 1024], F32, name=f"stage{i}") for i in range(3)]
        self.stage_i = 0

    def load_cast(self, dst16_ap, dst_key, src_ap, n, cast_eng="gpsimd", view=None, psl=None, reads=()):
        P = self.P
        i = self.stage_i
        self.stage_i = (i + 1) % len(self.stage)
        st = self.stage[i]
        sk = f"stage{i}"
        sv = st[:, 0:n] if psl is None else st[psl, 0:n]
        P.dma(sv if view is None else view(sv), src_ap, list(reads), [sk], f"ld_stage{i}")
        if cast_eng == "scalar":
            P.add("scalar", lambda e: e.copy(dst16_ap, sv if view is None else view(sv)), [sk], [dst_key])
        else:
            P.add(cast_eng, lambda e: e.tensor_copy(dst16_ap, sv if view is None else view(sv)), [sk], [dst_key])

    def load(self, dst_ap, dst_key, src_ap, reads=()):
        P = self.P
        self.ld_i += 1
        P.dma(dst_ap, src_ap, list(reads), [dst_key], f"ld_misc{self.ld_i % 4}")


def make_cols(P, modall, ng, layer, tiles):
    out = {}
    for vi, v in enumerate(("x", "c")):
        gs, gt = tiles[(layer, v)]
        for s in range(3):
            sc = modall[:, layer, (3 * s + 1) * 8:(3 * s + 2) * 8, vi]
            P.add("vector", lambda e, s=s, sc=sc, gs=gs: e.scalar_tensor_tensor(gs[:, s * 8:(s + 1) * 8], sc, 1.0, ng[:, layer, s * 8:(s + 1) * 8], ALU.add, ALU.mult),
                  ["modall", "normg"], [f"gs_{v}"])
            g = modall[:, layer, (3 * s + 2) * 8:(3 * s + 3) * 8, vi]
            fac = 1.0 if s == 1 else 0.5
            P.add("vector", lambda e, s=s, g=g, gt=gt, fac=fac: e.tensor_scalar(gt[:, s * 8:(s + 1) * 8], g, fac, None, ALU.mult),
                  ["modall"], [f"gate_{v}"])
        out[v] = dict(gs=gs, gate=gt, mod=modall, layer=layer, vi=vi)
    return out


def colsel(cols, v, kind, s, c):
    if kind == "gs":
        return cols[v]["gs"][:, s * 8 + c:s * 8 + c + 1]
    if kind == "gate":
        return cols[v]["gate"][:, s * 8 + c:s * 8 + c + 1]
    if kind == "shift":
        j = (3 * s) * 8 + c
        return cols[v]["mod"][:, cols[v]["layer"], j:j + 1, cols[v]["vi"]]
    raise ValueError(kind)


def emit_norm(P, C, K, xT, hT, cols, s, tiles=TILES):
    for ti, (t0, tn) in enumerate(tiles):
        v = "c" if t0 < TCTX else "x"
        pss = C.psum[6 + (ti % 2)]
        psk = f"psb{6 + (ti % 2)}"
        for c in range(KC):
            sq = K["sq"][c % 2]
            sqk = f"sq{c % 2}"
            P.add("scalar", lambda e, sq=sq, c=c, t0=t0, tn=tn: e.activation(sq[:, 0:tn], xT[:, c, t0:t0 + tn], AF.Square),
                  [f"x{c}_{ti}"], [sqk])
            P.add("tensor", lambda e, sq=sq, c=c, pss=pss, tn=tn: e.matmul(pss[:, 0:tn], K["ones"][:], sq[:, 0:tn], start=(c == 0), stop=(c == KC - 1)),
                  [sqk, "ones"], [psk])
        rs = K["rstd"][ti % 2]
        rsk = f"rstd{ti % 2}"
        P.add("scalar", lambda e, rs=rs, pss=pss, tn=tn: e.activation(rs[:, 0:tn], pss[:, 0:tn], AF.Sqrt, bias=K["epscol"][:, 0:1], scale=1.0 / D),
              [psk, "epscol"], [rsk])
        P.add("vector", lambda e, rs=rs, tn=tn: e.reciprocal(rs[:, 0:tn], rs[:, 0:tn]), [rsk], [rsk])
        for c in range(KC):
            tmp = K["ntmp"][c % 2]
            tk = f"ntmp{c % 2}"
            P.add("vector", lambda e, tmp=tmp, c=c, t0=t0, tn=tn, rs=rs: e.tensor_tensor(tmp[:, 0:tn], xT[:, c, t0:t0 + tn], rs[:, 0:tn], ALU.mult),
                  [f"x{c}_{ti}", rsk], [tk])
            P.add("scalar", lambda e, tmp=tmp, c=c, t0=t0, tn=tn, v=v: e.activation(hT[:, c, t0:t0 + tn], tmp[:, 0:tn], AF.Identity,
                                                                                    bias=colsel(cols, v, "shift", s, c), scale=colsel(cols, v, "gs", s, c)),
                  [tk, f"gs_{v}", "modall"], [f"h{c}_{ti}"])


def emit_ffn(P, C, K, xT, hT, cols, s, wg_d, wu_d, wd_d, tiles=TILES):
    act = K["act"]
    wg16, wu16, wd16 = K["wg16"], K["wu16"], K["wd16"]

    def load_chunk(j):
        sl = j % 2
        C.load_cast(wg16[sl][:].rearrange("p k m -> p (k m)"), f"wg16_{sl}", wg_d[j].rearrange("p k m -> p (k m)"), 1024)
        C.load_cast(wu16[sl][:].rearrange("p k m -> p (k m)"), f"wu16_{sl}", wu_d[j].rearrange("p k m -> p (k m)"), 1024)
        C.load_cast(wd16[j % (2 * GFF)][:], f"wd16_{j % (2 * GFF)}", wd_d[j], 1024)

    groups = [list(range(g, min(g + GFF, FC))) for g in range(0, FC, GFF)]
    load_chunk(0)
    pi = 0
    for grp in groups:
        for j in grp:
            if j + 1 < FC:
                load_chunk(j + 1)
            sl = j % 2
            jj = j % GFF
            for ti, (t0, tn) in enumerate(tiles):
                gb = pi % 2
                pi += 1
                pg, pu = C.psum[gb], C.psum[2 + gb]
                for k in range(KC):
                    P.add("tensor", lambda e, pg=pg, k=k, sl=sl, t0=t0, tn=tn: e.matmul(pg[:, 0:tn], wg16[sl][:, k, :], hT[:, k, t0:t0 + tn], start=(k == 0), stop=(k == KC - 1)),
                          [f"wg16_{sl}", f"h{k}_{ti}"], [f"psb{gb}"])
                for k in range(KC):
                    P.add("tensor", lambda e, pu=pu, k=k, sl=sl, t0=t0, tn=tn: e.matmul(pu[:, 0:tn], wu16[sl][:, k, :], hT[:, k, t0:t0 + tn], start=(k == 0), stop=(k == KC - 1)),
                          [f"wu16_{sl}", f"h{k}_{ti}"], [f"psb{2 + gb}"])
                sg = K["sg"][gb]
                P.add("scalar", lambda e, sg=sg, pg=pg, tn=tn: e.activation(sg[:, 0:tn], pg[:, 0:tn], AF.Silu), [f"psb{gb}"], [f"sg{gb}"])
                P.add("vector", lambda e, sg=sg, pu=pu, jj=jj, t0=t0, tn=tn: e.tensor_tensor(act[:, jj, t0:t0 + tn], sg[:, 0:tn], pu[:, 0:tn], ALU.mult),
                      [f"sg{gb}", f"psb{2 + gb}"], [f"act{jj}_{ti}"])
        for ti, (t0, tn) in enumerate(tiles):
            v = "c" if t0 < TCTX else "x"
            for dc in range(KC):
                db = 4 + (dc % 2)
                pd = C.psum[db]
                for n, j in enumerate(grp):
                    jj = j % GFF
                    jw = j % (2 * GFF)
                    P.add("tensor", lambda e, pd=pd, jj=jj, jw=jw, dc=dc, t0=t0, tn=tn, n=n, ng=len(grp): e.matmul(pd[:, 0:tn], wd16[jw][:, dc * 128:(dc + 1) * 128], act[:, jj, t0:t0 + tn],
                                                                                                  start=(n == 0), stop=(n == ng - 1)),
                          [f"wd16_{jw}", f"act{jj}_{ti}"], [f"psb{db}"])
                P.add("vector", lambda e, pd=pd, dc=dc, t0=t0, tn=tn, v=v: e.scalar_tensor_tensor(xT[:, dc, t0:t0 + tn], pd[:, 0:tn], colsel(cols, v, "gate", s, dc),
                                                                                                 xT[:, dc, t0:t0 + tn], ALU.mult, ALU.add),
                      [f"psb{db}", f"gate_{v}", f"x{dc}_{ti}"], [f"x{dc}_{ti}"])


def alloc_common(P, C, with_ffn=True):
    K = {}
    K["ones"] = P.sb([128, 128], BF16, name="ones")
    P.add("vector", lambda e: e.memset(K["ones"][:], 1.0), [], ["ones"])
    K["epscol"] = P.sb([128, 1], F32, name="epscol")
    P.add("vector", lambda e: e.memset(K["epscol"][:], EPS), [], ["epscol"])
    K["sq"] = [P.sb([128, 512], BF16, name=f"sq{i}") for i in range(2)]
    K["rstd"] = [P.sb([128, 512], F32, name=f"rstd{i}") for i in range(2)]
    K["ntmp"] = [P.sb([128, 512], F32, name=f"ntmp{i}") for i in range(2)]
    if with_ffn:
        K["act"] = P.sb([128, GFF, T], BF16, name="act")
        K["wg16"] = [P.sb([128, 8, 128], BF16, name=f"wg16_{i}") for i in range(2)]
        K["wu16"] = [P.sb([128, 8, 128], BF16, name=f"wu16_{i}") for i in range(2)]
        K["wd16"] = [P.sb([128, 1024], BF16, name=f"wd16_{i}") for i in range(2 * GFF)]
        K["sg"] = [P.sb([128, 512], F32, name=f"sg{i}") for i in range(2)]
    return K


def emit_consts(P, C, K, rot_d, bones_d):
    K["rot"] = P.sb([128, 128], BF16, name="rot_sb")
    K["bones"] = P.sb([128, 128], BF16, name="bones_sb")
    C.load_cast(K["rot"][:], "rot", rot_d, 128)
    C.load_cast(K["bones"][:], "bones", bones_d, 128)


def emit_inproj(P, C, K, hT, win_d, n_fm, kinds, qkg, cosT, sinT, outs, v_d, vw, v_oc0, tiles=TILES):
    w16 = K["win16"]
    zq = K["zq"]
    for oc in range(n_fm):
        sl = oc % 2
        C.load_cast(w16[sl][:].rearrange("p k m -> p (k m)"), f"win16_{sl}", win_d[oc].rearrange("p k m -> p (k m)"), 1024)
        kind = kinds[oc]
        od, odt = outs[oc]
        for ti, (t0, tn) in enumerate(tiles):
            pb = ti % 2
            pz = C.psum[pb]
            for k in range(KC):
                P.add("tensor", lambda e, pz=pz, k=k, sl=sl, t0=t0, tn=tn: e.matmul(pz[:, 0:tn], w16[sl][:, k, :], hT[:, k, t0:t0 + tn], start=(k == 0), stop=(k == KC - 1)),
                      [f"win16_{sl}", f"h{k}_{ti}"], [f"psb{pb}"])
            ob = K["ob32"][pb] if odt == F32 else K["ob16"][pb]
            obk = f"ob{'32' if odt == F32 else '16'}_{pb}"
            if kind == "u":
                P.add("scalar", lambda e, ob=ob, pz=pz, tn=tn: e.copy(ob[:, 0:tn], pz[:, 0:tn]), [f"psb{pb}"], [obk])
            else:
                gcol = qkg[:, 0:1] if kind == "q" else qkg[:, 1:2]
                z = zq[pb]
                zk = f"zq{pb}"
                sq = K["sq"][pb]
                sqk = f"sq{pb}"
                P.add("scalar", lambda e, z=z, pz=pz, tn=tn: e.copy(z[:, 0:tn], pz[:, 0:tn]), [f"psb{pb}"], [zk])
                P.add("scalar", lambda e, sq=sq, z=z, tn=tn: e.activation(sq[:, 0:tn], z[:, 0:tn], AF.Square), [zk], [sqk])
                ph = C.psum[2 + pb]
                P.add("tensor", lambda e, ph=ph, sq=sq, tn=tn: e.matmul(ph[:, 0:tn], K["bones"][:], sq[:, 0:tn], start=True, stop=True), [sqk, "bones"], [f"psb{2 + pb}"])
                rs = K["rstd"][pb]
                rsk = f"rstd{pb}"
                P.add("scalar", lambda e, rs=rs, ph=ph, tn=tn: e.activation(rs[:, 0:tn], ph[:, 0:tn], AF.Sqrt, bias=K["epscol"][:, 0:1], scale=1.0 / 64), [f"psb{2 + pb}", "epscol"], [rsk])
                P.add("vector", lambda e, rs=rs, tn=tn: e.reciprocal(rs[:, 0:tn], rs[:, 0:tn]), [rsk], [rsk])
                P.add("vector", lambda e, z=z, rs=rs, tn=tn, gcol=gcol: e.scalar_tensor_tensor(z[:, 0:tn], z[:, 0:tn], gcol, rs[:, 0:tn], ALU.mult, ALU.mult), [zk, rsk, "qkg"], [zk])
                zb = K["zb"][pb]
                zbk = f"zb{pb}"
                P.add("scalar", lambda e, zb=zb, z=z, tn=tn: e.copy(zb[:, 0:tn], z[:, 0:tn]), [zk], [zbk])
                pr = C.psum[4 + pb]
                P.add("tensor", lambda e, pr=pr, zb=zb, tn=tn: e.matmul(pr[:, 0:tn], K["rot"][:], zb[:, 0:tn], start=True, stop=True), [zbk, "rot"], [f"psb{4 + pb}"])
                t2 = K["ntmp"][pb]
                t2k = f"ntmp{pb}"
                cb_ = pb % len(K["cst"])
                cst, snt = K["cst"][cb_], K["snt"][cb_]
                C.load(cst[:, 0:tn], f"cst{cb_}", cosT[:, t0:t0 + tn])
                C.load(snt[:, 0:tn], f"snt{cb_}", sinT[:, t0:t0 + tn])
                P.add("vector", lambda e, t2=t2, pr=pr, snt=snt, tn=tn: e.tensor_tensor(t2[:, 0:tn], pr[:, 0:tn], snt[:, 0:tn], ALU.mult), [f"psb{4 + pb}", f"snt{cb_}"], [t2k])
                P.add("vector", lambda e, z=z, cst=cst, tn=tn: e.tensor_tensor(z[:, 0:tn], z[:, 0:tn], cst[:, 0:tn], ALU.mult), [zk, f"cst{cb_}"], [zk])
                P.add("vector", lambda e, ob=ob, z=z, t2=t2, tn=tn: e.tensor_tensor(ob[:, 0:tn], z[:, 0:tn], t2[:, 0:tn], ALU.add), [zk, t2k], [obk])
            P.dma(od(t0, tn) if callable(od) else od[:, t0:t0 + tn], ob[:, 0:tn], [obk], [], f"st_{obk}")
    nvc = vw // 128
    wv = K["wv16"]
    for i in range(nvc):
        C.load_cast(wv[:, :, i * 128:(i + 1) * 128], f"wv16_{i}", win_d[v_oc0 + i], 1024, view=lambda a: a.rearrange("p (k m) -> p k m", m=128))
    for tt in range(T // 128):
        pb = tt % 2
        pv = C.psum[6 + pb]
        ti = 0 if tt < 2 else 1 + (tt - 2) // 4
        for k in range(KC):
            P.add("tensor", lambda e, pv=pv, k=k, tt=tt: e.matmul(pv[:, 0:vw], hT[:, k, tt * 128:(tt + 1) * 128], wv[:, k, :], start=(k == 0), stop=(k == KC - 1)),
                  [f"wv16_{i}" for i in range(nvc)] + [f"h{k}_{ti}"], [f"psb{6 + pb}"])
        vb = K["vb"][pb]
        P.add("scalar", lambda e, vb=vb, pv=pv: e.copy(vb[:, 0:vw], pv[:, 0:vw]), [f"psb{6 + pb}"], ["vb0"])
        P.dma(v_d[tt], vb[:, 0:vw], ["vb0"], [], "st_vb0")


def phase_att(P, C, layer, Dr):
    dense = layer == 0
    n_qc, n_kv = (4, 2) if dense else (8, 4)
    n_heads = 2 * n_qc
    NK = 66 if dense else 26
    qT = P.sb([128, n_qc, T], BF16, name="q_sb")
    kd = P.sb([128, n_kv, NK * 128], BF16, name="k_sb")
    va = P.sb([128, n_kv, NK, 65], BF16, name="v_sb")
    q_loc, o_loc = Dr["q_loc"], Dr["o_loc"]
    for c in range(n_qc):
        C.load(qT[:, c, :], f"q{c}", q_loc[:, c, :])
    for kv in range(n_kv):
        P.add("vector", lambda e, kv=kv: e.memset(va[:, kv, :, 64:65], 1.0), [], [f"vone{kv}"])
    tkv = lambda s_: s_.rearrange("p (kt d) -> p kt d", d=64)
    tk = lambda ap: ap.rearrange("(kt p) d -> p kt d", p=128)

    kpieces, vpieces = {}, {}

    def kload(kv, half, dst0, src_rows, src_cols0, ncols):
        pl = slice(64 * half, 64 * half + 64)
        for c0 in range(0, ncols, 1024):
            n = min(1024, ncols - c0)
            key = f"k{kv}_{half}_{dst0 + c0}"
            kpieces.setdefault((kv, half), []).append((dst0 + c0, n, key))
            C.load_cast(kd[pl, kv, dst0 + c0:dst0 + c0 + n], key, src_rows[:, src_cols0 + c0:src_cols0 + c0 + n], n, psl=pl, cast_eng="vector")

    def vload(kv, kt0, nkt, src):
        for t0 in range(0, nkt, 16):
            n = min(16, nkt - t0)
            key = f"v{kv}_{kt0 + t0}"
            vpieces.setdefault(kv, []).append((kt0 + t0, n, key))
            C.load_cast(va[:, kv, kt0 + t0:kt0 + t0 + n, 0:64], key, tk(src[t0 * 128:(t0 + n) * 128, 64 * kv:64 * kv + 64]), n * 64, view=tkv, cast_eng="vector")

    def kkey(kv, half, kt):
        for (c0, n, key) in kpieces[(kv, half)]:
            if c0 <= kt * 128 < c0 + n:
                return key
        raise KeyError((kv, half, kt))

    def vkey(kv, kt):
        for (k0, n, key) in vpieces[kv]:
            if k0 <= kt < k0 + n:
                return key
        raise KeyError((kv, kt))

    if dense:
        kA_all, kB_all, vA_all, vB_all = Dr["kA_all"], Dr["kB_all"], Dr["vA_all"], Dr["vB_all"]
        for kv in range(n_kv):
            for half in range(2):
                kload(kv, half, 0, kA_all[64 * kv:64 * kv + 64], 0, TCTX)
                for r in range(4):
                    kload(kv, half, TCTX + TLAT * r, kA_all[r * 128 + 64 * kv:r * 128 + 64 * kv + 64], TCTX, 1024)
                    kload(kv, half, TCTX + TLAT * r + 1024, kB_all[r * 128 + 64 * kv:r * 128 + 64 * kv + 64], 0, 1024)
            vload(kv, 0, 2, vA_all[0:TCTX])
            for r in range(4):
                vload(kv, 2 + 16 * r, 8, vA_all[r * 1280 + TCTX:(r + 1) * 1280])
                vload(kv, 2 + 16 * r + 8, 8, vB_all[r * 1024:(r + 1) * 1024])
    else:
        k_loc, v_loc, ek_all, ev_all = Dr["k_loc"], Dr["v_loc"], Dr["ek_all"], Dr["ev_all"]
        for kv in range(n_kv):
            ks = slice(64 * (kv % 2), 64 * (kv % 2) + 64)
            kc = kv // 2
            for half in range(2):
                kload(kv, half, 0, k_loc[ks, kc], TCTX, TLAT)
                kload(kv, half, 24 * 128, k_loc[ks, kc], 0, TCTX)
                for r in range(4):
                    kload(kv, half, 16 * 128 + 256 * r, ek_all[r * 128 + 64 * (kv % 2):r * 128 + 64 * (kv % 2) + 64], kc * 256, 256)
            vload(kv, 0, 16, v_loc[TCTX:T])
            vload(kv, 24, 2, v_loc[0:TCTX])
            for r in range(4):
                vload(kv, 16 + 2 * r, 2, ev_all[r * 256:(r + 1) * 256])
    ones32 = P.sb([128, 64], F32, name="ones32")
    P.add("vector", lambda e: e.memset(ones32[:], 1.0), [], ["ones32"])
    es = P.sb([128, 16], F32, name="es")
    if not dense:
        C.load(es[:], "es", Dr["sink"])
        P.add("scalar", lambda e: e.activation(es[:], es[:], AF.Exp), ["es"], ["es"])
        wm = P.sb([128, 4, 6, 512], BF16, name="wm")
        C.load(wm[:], "wm", Dr["wmask"])
        em = P.sb([128, 2, 4, 512], BF16, name="em")
        C.load(em[:], "em", Dr["emask"])
    pT = [P.sb([128, 512], BF16, name=f"pT{i}") for i in range(3)]
    osb = [P.sb([64, 512], F32, name=f"osb{i}") for i in range(2)]
    rden = [P.sb([128, 512], F32, name=f"rden{i}") for i in range(2)]
    ob = [P.sb([64, 512], BF16, name=f"ob{i}") for i in range(2)]
    work = []
    if dense:
        work.append((0, 256, [(0, None), (1, None)]))
        for m in range(4):
            work.append((256 + 512 * m, 512, [(kt, None) for kt in range(66)]))
    else:
        for m in range(4):
            keys = [(4 * m + r - 1, wm[:, m, r, :]) for r in range(6) if 0 <= 4 * m + r - 1 <= 15]
            if m == 0:
                keys += [(16 + 2 * r + 1, em[:, 0, r, :]) for r in range(4)]
            if m == 3:
                keys += [(16 + 2 * r, em[:, 1, r, :]) for r in range(4)]
            keys += [(24, None), (25, None)]
            work.append((256 + 512 * m, 512, keys))
    it = 0
    si = 0
    for (q0, qn, keys) in work:
        for h in range(n_heads):
            c, half, kv = h // 2, h % 2, h // 4
            pl = slice(64 * half, 64 * half + 64)
            ob_i = it % 2
            it += 1
            po = C.psum[6 + ob_i]
            pok = f"psb{6 + ob_i}"
            LA = 2
            nk = len(keys)
            slots = []
            for n in range(nk + LA):
                if n < nk:
                    kt, mk = keys[n]
                    sb_i = si % 3
                    si += 1
                    slots.append(sb_i)
                    pS = C.psum[sb_i]
                    P.add("tensor", lambda e, pS=pS, kv=kv, kt=kt, pl=pl, c=c, q0=q0, qn=qn: e.matmul(pS[:, 0:qn], kd[pl, kv, kt * 128:(kt + 1) * 128], qT[pl, c, q0:q0 + qn], start=True, stop=True),
                          [kkey(kv, half, kt), f"q{c}"], [f"psb{sb_i}"])
                    pt = pT[sb_i]
                    P.add("scalar", lambda e, pt=pt, pS=pS, qn=qn: e.activation(pt[:, 0:qn], pS[:, 0:qn], AF.Exp, scale=0.125), [f"psb{sb_i}"], [f"pT{sb_i}"])
                    if mk is not None:
                        P.add("vector", lambda e, pt=pt, mk=mk, qn=qn: e.tensor_tensor(pt[:, 0:qn], pt[:, 0:qn], mk[:, 0:qn], ALU.mult), [f"pT{sb_i}", "wm", "em"], [f"pT{sb_i}"])
                if n >= LA:
                    m_ = n - LA
                    kt2 = keys[m_][0]
                    sb2 = slots[m_]
                    pt2 = pT[sb2]
                    P.add("tensor", lambda e, po=po, pt2=pt2, kv=kv, kt2=kt2, qn=qn, m_=m_, nk=nk: e.matmul(po[0:65, 0:qn], va[:, kv, kt2, :], pt2[:, 0:qn], start=(m_ == 0), stop=(m_ == nk - 1)),
                          [vkey(kv, kt2), f"vone{kv}", f"pT{sb2}"], [pok])
            rd = rden[ob_i]
            rdk = f"rden{ob_i}"
            osx = osb[ob_i]
            P.add("scalar", lambda e, osx=osx, po=po, qn=qn: e.copy(osx[:, 0:qn], po[0:64, 0:qn]), [pok], [f"osb{ob_i}"])
            if dense:
                P.add("vector", lambda e, rd=rd, po=po, qn=qn: e.reciprocal(rd[64:65, 0:qn], po[64:65, 0:qn]), [pok], [rdk])
            else:
                P.add("vector", lambda e, rd=rd, po=po, qn=qn, h=h: e.tensor_scalar(rd[64:65, 0:qn], po[64:65, 0:qn], es[64:65, h:h + 1], None, ALU.add), [pok, "es"], [rdk])
                P.add("vector", lambda e, rd=rd, qn=qn: e.reciprocal(rd[64:65, 0:qn], rd[64:65, 0:qn]), [rdk], [rdk])
            pb = C.psum[3 + ob_i]
            P.add("tensor", lambda e, pb=pb, rd=rd, qn=qn: e.matmul(pb[0:64, 0:qn], ones32[64:65, 0:64], rd[64:65, 0:qn], start=True, stop=True), [rdk, "ones32"], [f"psb{3 + ob_i}"])
            o16 = ob[ob_i]
            P.add("vector", lambda e, o16=o16, osx=osx, pb=pb, qn=qn: e.tensor_tensor(o16[:, 0:qn], osx[:, 0:qn], pb[0:64, 0:qn], ALU.mult), [f"osb{ob_i}", f"psb{3 + ob_i}"], [f"ob{ob_i}"])
            P.dma(o_loc[64 * half:64 * half + 64, c, q0:q0 + qn], o16[:, 0:qn], [f"ob{ob_i}"], [], f"st_ob{ob_i}")


NS5 = TCTX + 4 * TLAT
TWO_PI = 2.0 * math.pi


def phase_s5(P, C, Dr):
    uA_all, uB_all, B_d, C_d, pc_d = Dr["uA_all"], Dr["uB_all"], Dr["Bl"], Dr["Cl"], Dr["pcols"]
    ic_d, ij_d, oh_d = Dr["iota_c"], Dr["iota_j"], Dr["onehot"]
    y_rs_in, yc_loc = Dr["y_rs_in"], Dr["yc_loc"]
    sm = lambda name, n=8, dtype=F32: P.sb([128, n], dtype, name=name)
    pc = P.sb([128, 3, 8], F32, name="pc")
    C.load(pc[:].rearrange("p a b -> p (a b)"), "pc", pc_d.rearrange("p a b -> p (a b)"))
    iota_c = P.sb([128, 2, 66], F32, name="iota_c_sb")
    iota_j = P.sb([128, 2, 128], F32, name="iota_j_sb")
    oh = sm("onehot_sb", 4)
    C.load(iota_c[:], "iota_c", ic_d)
    C.load(iota_j[:], "iota_j", ij_d)
    C.load(oh[:], "oh", oh_d)
    cnt = [0]

    def V(fn, reads, writes, eng="vector"):
        P.add(eng, fn, reads, writes)

    fr_i = P.sb([128, 128], I32, name="fr_i")
    fr_f = P.sb([128, 128], F32, name="fr_f")
    sc_t = P.sb([128, 128], F32, name="sc_t")
    sc_u = P.sb([128, 128], F32, name="sc_u")

    def fracp(dst, src, n, key_dst, key_src):
        V(lambda e: e.tensor_copy(fr_i[:, 0:n], src), [key_src], ["fr_i"])
        V(lambda e: e.tensor_copy(fr_f[:, 0:n], fr_i[:, 0:n]), ["fr_i"], ["fr_f"])
        V(lambda e: e.tensor_tensor(dst, src, fr_f[:, 0:n], ALU.subtract), [key_src, "fr_f"], [key_dst])

    def sincos(sin_dst, cos_dst, ph, n, key_s, key_c, key_ph):
        P.add("scalar", lambda e: e.activation(sin_dst, ph, AF.Sin, scale=TWO_PI - 1e-5), [key_ph], [key_s])
        V(lambda e: e.tensor_scalar(sc_t[:, 0:n], ph, 0.25, None, ALU.add), [key_ph], ["sc_t"])
        fracp(sc_u[:, 0:n], sc_t[:, 0:n], n, "sc_u", "sc_t")
        P.add("scalar", lambda e: e.activation(cos_dst, sc_u[:, 0:n], AF.Sin, scale=TWO_PI - 1e-5), ["sc_u"], [key_c])

    lr, li, dtv, a, th, rr, f = sm("lr"), sm("li"), sm("dtv"), sm("a_ln"), sm("th"), sm("rr"), sm("f")
    V(lambda e: e.tensor_scalar(lr[:], pc[:, 0, :], -1e-4, None, ALU.min), ["pc"], ["lr"])
    V(lambda e: e.tensor_copy(li[:], pc[:, 1, :]), ["pc"], ["li"])
    P.add("scalar", lambda e: e.activation(dtv[:], pc[:, 2, :], AF.Exp), ["pc"], ["dtv"])
    V(lambda e: e.tensor_tensor(a[:], lr[:], dtv[:], ALU.mult), ["lr", "dtv"], ["a"])
    V(lambda e: e.tensor_tensor(th[:], li[:], dtv[:], ALU.mult), ["li", "dtv"], ["th"])
    P.add("scalar", lambda e: e.activation(rr[:], a[:], AF.Exp), ["a"], ["rr"])
    V(lambda e: e.tensor_scalar(f[:], th[:], 1.0 / TWO_PI, None, ALU.mult), ["th"], ["f"])
    f0, sth, cth = sm("f0"), sm("sth"), sm("cth")
    fracp(f0[:], f[:], 8, "f0", "f")
    sincos(sth[:], cth[:], f0[:], 8, "sth", "cth", "f0")
    nr, ni, den, t8a, t8b, kr, ki, nkr, nki = [sm(n) for n in ("nr", "ni", "den", "t8a", "t8b", "kr", "ki", "nkr", "nki")]
    V(lambda e: e.tensor_tensor(nr[:], rr[:], cth[:], ALU.mult), ["rr", "cth"], ["nr"])
    V(lambda e: e.tensor_scalar(nr[:], nr[:], -1.0, None, ALU.add), ["nr"], ["nr"])
    V(lambda e: e.tensor_tensor(ni[:], rr[:], sth[:], ALU.mult), ["rr", "sth"], ["ni"])
    V(lambda e: e.tensor_tensor(den[:], lr[:], lr[:], ALU.mult), ["lr"], ["den"])
    V(lambda e: e.tensor_tensor(t8a[:], li[:], li[:], ALU.mult), ["li"], ["t8a"])
    V(lambda e: e.tensor_tensor(den[:], den[:], t8a[:], ALU.add), ["den", "t8a"], ["den"])
    V(lambda e: e.reciprocal(den[:], den[:]), ["den"], ["den"])
    V(lambda e: e.tensor_tensor(t8a[:], nr[:], lr[:], ALU.mult), ["nr", "lr"], ["t8a"])
    V(lambda e: e.tensor_tensor(t8b[:], ni[:], li[:], ALU.mult), ["ni", "li"], ["t8b"])
    V(lambda e: e.tensor_tensor(kr[:], t8a[:], t8b[:], ALU.add), ["t8a", "t8b"], ["kr"])
    V(lambda e: e.tensor_tensor(kr[:], kr[:], den[:], ALU.mult), ["kr", "den"], ["kr"])
    V(lambda e: e.tensor_tensor(t8a[:], ni[:], lr[:], ALU.mult), ["ni", "lr"], ["t8a"])
    V(lambda e: e.tensor_tensor(t8b[:], nr[:], li[:], ALU.mult), ["nr", "li"], ["t8b"])
    V(lambda e: e.tensor_tensor(ki[:], t8a[:], t8b[:], ALU.subtract), ["t8a", "t8b"], ["ki"])
    V(lambda e: e.tensor_tensor(ki[:], ki[:], den[:], ALU.mult), ["ki", "den"], ["ki"])
    V(lambda e: e.tensor_scalar(nkr[:], kr[:], -1.0, None, ALU.mult), ["kr"], ["nkr"])
    V(lambda e: e.tensor_scalar(nki[:], ki[:], -1.0, None, ALU.mult), ["ki"], ["nki"])
    a128, a128f = sm("a128"), sm("a128f")
    V(lambda e: e.tensor_scalar(a128[:], f0[:], 128.0, None, ALU.mult), ["f0"], ["a128"])
    fracp(a128f[:], a128[:], 8, "a128f", "a128")
    B16 = P.sb([128, 16, 4, 128], BF16, name="B16")
    CR = P.sb([128, 8, 128], BF16, name="CR16")
    CI = P.sb([128, 8, 128], BF16, name="CI16")
    c32 = [P.sb([128, 2, 128], F32, name=f"c32_{i}") for i in range(2)]
    ctmp = [P.sb([128, 128], F32, name=f"ctmp{i}") for i in range(2)]

    def wsetup(d, gp):
        q = d * 4 + gp
        for ri in range(2):
            C.load_cast(B16[:, q * 2 + ri, :, :], f"B16_{q}_{ri}", B_d[d, gp, ri].rearrange("c p m -> p c m"), 512,
                        view=lambda s_: s_.rearrange("p (c m) -> p c m", m=128))
        cb = c32[q % 2]
        ck = f"c32_{q % 2}"
        P.dma(cb[:], C_d[d, gp].rearrange("r k m -> k r m"), [], [ck], f"ld_c32_{q % 2}")
        tm = ctmp[q % 2]
        tk = f"ctmp{q % 2}"
        V(lambda e: e.tensor_scalar(tm[:], cb[:, 0, :], kr[:, q:q + 1], None, ALU.mult), [ck, "kr"], [tk])
        V(lambda e: e.scalar_tensor_tensor(CR[:, q, :], cb[:, 1, :], nki[:, q:q + 1], tm[:], ALU.mult, ALU.add), [ck, "nki", tk], [f"CR_{q}"])
        V(lambda e: e.tensor_scalar(tm[:], cb[:, 1, :], nkr[:, q:q + 1], None, ALU.mult), [ck, "nkr", f"CR_{q}"], [tk])
        V(lambda e: e.scalar_tensor_tensor(CI[:, q, :], cb[:, 0, :], nki[:, q:q + 1], tm[:], ALU.mult, ALU.add), [ck, "nki", tk], [f"CI_{q}"])

    for d in range(2):
        for gp in range(4):
            wsetup(d, gp)
    sinC = P.sb([128, 8, 66], F32, name="sinC")
    cosC = P.sb([128, 8, 66], F32, name="cosC")
    sinJ = P.sb([128, 8, 128], F32, name="sinJ")
    cosJ = P.sb([128, 8, 128], F32, name="cosJ")

    phA = P.sb([128, 128], F32, name="phA")
    phB = P.sb([128, 128], F32, name="phB")

    def tsetup(q):
        d = q // 4
        V(lambda e: e.tensor_scalar(phA[:, 0:66], iota_c[:, d, :], a128f[:, q:q + 1], None, ALU.mult), ["iota_c", "a128f"], ["phA"])
        fracp(phB[:, 0:66], phA[:, 0:66], 66, "phB", "phA")
        sincos(sinC[:, q, :], cosC[:, q, :], phB[:, 0:66], 66, f"sinC{q}", f"cosC{q}", "phB")
        V(lambda e: e.tensor_scalar(phA[:, 0:128], iota_j[:, d, :], f0[:, q:q + 1], None, ALU.mult), ["iota_j", "f0"], ["phA"])
        fracp(phB[:, 0:128], phA[:, 0:128], 128, "phB", "phA")
        sincos(sinJ[:, q, :], cosJ[:, q, :], phB[:, 0:128], 128, f"sinJ{q}", f"cosJ{q}", "phB")

    for q in range(8):
        tsetup(q)
    NB = 512
    big = lambda name, dtype=F32: P.sb([128, NB], dtype, name=name)
    u16 = [P.sb([128, 4, NB], BF16, name=f"u16_{i}") for i in range(2)]
    St = [big(f"St{i}") for i in range(2)]
    Ct = [big(f"Ct{i}") for i in range(2)]
    tB = big("tB")
    tA2 = [big("tA0"), big("tA1")]
    brs2, bis2 = [big("brs0"), big("brs1")], [big("bis0"), big("bis1")]
    btr2, bti2 = [big("btr0"), big("btr1")], [big("bti0"), big("bti1")]
    gr2, gi2 = [big("gr0"), big("gr1")], [big("gi0"), big("gi1")]
    hr16 = [big(f"hr16_{i}", BF16) for i in range(2)]
    hi16 = [big(f"hi16_{i}", BF16) for i in range(2)]
    carry = P.sb([128, 16], F32, name="carry")
    yb = [P.sb([128, 4, NB], F32, name=f"yb{i}") for i in range(2)]

    def rv(t, n):
        return bass.AP(t, n - 1, [[NB, 128], [-1, n]])

    def gp_body(d, first, lay0, sn, ub, gp, tb, hb, seg_n):
        q = d * 4 + gp
        c0, ncn = lay0 // 128, sn // 128
        nst = (sn + 511) // 512
        S, Cc = St[tb], Ct[tb]
        brs, bis = brs2[tb], bis2[tb]
        ch = gp % 2
        tA, btr, bti, gr, gi = tA2[ch], btr2[ch], bti2[ch], gr2[ch], gi2[ch]
        kA_, kbr, kbi, kgr, kgi = f"tA{ch}", f"btr{ch}", f"bti{ch}", f"gr{ch}", f"gi{ch}"
        yb_ = 4 + (seg_n % 2)
        W = slice(0, sn)
        v3 = lambda t: t[:, 0:sn].rearrange("p (c j) -> p c j", j=128)
        cC = lambda t: t[:, q, c0:c0 + ncn].unsqueeze(2).broadcast_to([128, ncn, 128])
        cJ = lambda t: t[:, q, :].unsqueeze(1).broadcast_to([128, ncn, 128])
        G = "vector"
        P.add(G, lambda e, a=cC(sinC), b=cJ(cosJ), v=v3(S): e.tensor_tensor(v, a, b, ALU.mult), [f"sinC{q}", f"cosJ{q}"], [f"St{tb}"])
        P.add(G, lambda e, a=cC(cosC), b=cJ(sinJ), v=v3(tB): e.tensor_tensor(v, a, b, ALU.mult), [f"cosC{q}", f"sinJ{q}"], ["tBg"])
        P.add(G, lambda e: e.tensor_tensor(S[:, W], S[:, W], tB[:, W], ALU.add), [f"St{tb}", "tBg"], [f"St{tb}"])
        P.add(G, lambda e, a=cC(cosC), b=cJ(cosJ), v=v3(Cc): e.tensor_tensor(v, a, b, ALU.mult), [f"cosC{q}", f"cosJ{q}"], [f"Ct{tb}"])
        P.add(G, lambda e, a=cC(sinC), b=cJ(sinJ), v=v3(tB): e.tensor_tensor(v, a, b, ALU.mult), [f"sinC{q}", f"sinJ{q}"], ["tBg"])
        P.add(G, lambda e: e.tensor_tensor(Cc[:, W], Cc[:, W], tB[:, W], ALU.subtract), [f"Ct{tb}", "tBg"], [f"Ct{tb}"])
        for st in range(nst):
            n0 = st * 512
            nn = min(512, sn - n0)
            for ri, (dst, dk) in enumerate(((brs, f"brs{tb}_"), (bis, f"bis{tb}_"))):
                pbi = ri * 2 + tb
                pb = C.psum[pbi]
                for c in range(4):
                    P.add("tensor", lambda e, pb=pb, ri=ri, n0=n0, nn=nn, c=c: e.matmul(pb[:, 0:nn], B16[:, q * 2 + ri, c, :], u16[ub][:, c, n0:n0 + nn], start=(c == 0), stop=(c == 3)),
                          [f"B16_{q}_{ri}", f"u16_{ub}_{c}"], [f"psb{pbi}"])
                P.add("scalar", lambda e, pb=pb, dst=dst, n0=n0, nn=nn: e.copy(dst[:, n0:n0 + nn], pb[:, 0:nn]), [f"psb{pbi}"], [f"{dk}{st}"])
        assert nst == 1
        bk = [f"brs{tb}_{st}" for st in range(nst)]
        ik = [f"bis{tb}_{st}" for st in range(nst)]
        V(lambda e: e.tensor_tensor(tA[:, W], Cc[:, W], brs[:, W], ALU.mult), [f"Ct{tb}"] + bk, [kA_])
        yield
        V(lambda e: e.tensor_tensor(btr[:, W], S[:, W], bis[:, W], ALU.mult), [f"St{tb}"] + ik, [kbr])
        yield
        V(lambda e: e.tensor_tensor(btr[:, W], btr[:, W], tA[:, W], ALU.add), [kbr, kA_], [kbr])
        yield
        V(lambda e: e.tensor_tensor(tA[:, W], Cc[:, W], bis[:, W], ALU.mult), [f"Ct{tb}"] + ik, [kA_])
        yield
        V(lambda e: e.tensor_tensor(bti[:, W], S[:, W], brs[:, W], ALU.mult), [f"St{tb}"] + bk, [kbi])
        yield
        V(lambda e: e.tensor_tensor(bti[:, W], tA[:, W], bti[:, W], ALU.subtract), [kbi, kA_], [kbi])
        yield
        rcol = rr[:, q:q + 1].broadcast_to([128, sn])
        for (g_, bt_, gk, btk, ci) in ((gr, btr, kgr, kbr, 2 * q), (gi, bti, kgi, kbi, 2 * q + 1)):
            init = 0.0 if first else carry[:, ci:ci + 1]
            if d == 0:
                V(lambda e, g_=g_, bt_=bt_, init=init: e.tensor_tensor_scan(g_[:, W], rcol, bt_[:, W], init, ALU.mult, ALU.add), [btk, "rr", f"carry{ci}"], [gk])
                yield
                P.add("scalar", lambda e, g_=g_, ci=ci: e.copy(carry[:, ci:ci + 1], g_[:, sn - 1:sn]), [gk], [f"carry{ci}"])
            else:
                V(lambda e, g_=g_, bt_=bt_, init=init: e.tensor_tensor_scan(rv(g_, sn), rcol, rv(bt_, sn), init, ALU.mult, ALU.add), [btk, "rr", f"carry{ci}"], [gk])
                yield
                P.add("scalar", lambda e, g_=g_, ci=ci: e.copy(carry[:, ci:ci + 1], g_[:, 0:1]), [gk], [f"carry{ci}"])
        hr_, hi_ = hr16[hb], hi16[hb]
        V(lambda e: e.tensor_tensor(tA[:, W], Cc[:, W], gr[:, W], ALU.mult), [f"Ct{tb}", kgr], [kA_])
        yield
        V(lambda e: e.tensor_tensor(btr[:, W], S[:, W], gi[:, W], ALU.mult), [f"St{tb}", kgi], [kbr])
        yield
        V(lambda e: e.tensor_tensor(hr_[:, W], tA[:, W], btr[:, W], ALU.subtract), [kA_, kbr], [f"hr16_{hb}"])
        yield
        V(lambda e: e.tensor_tensor(tA[:, W], S[:, W], gr[:, W], ALU.mult), [f"St{tb}", kgr], [kA_])
        yield
        V(lambda e: e.tensor_tensor(bti[:, W], Cc[:, W], gi[:, W], ALU.mult), [f"Ct{tb}", kgi], [kbi])
        yield
        V(lambda e: e.tensor_tensor(hi_[:, W], tA[:, W], bti[:, W], ALU.add), [kA_, kbi], [f"hi16_{hb}"])
        yield
        for st in range(nst):
            n0 = st * 512
            nn = min(512, sn - n0)
            py = C.psum[yb_]
            P.add("tensor", lambda e, py=py, n0=n0, nn=nn: e.matmul(py[:, 0:nn], CR[:, q, :], hr_[:, n0:n0 + nn], start=(gp == 0), stop=False),
                  [f"CR_{q}", f"hr16_{hb}"], [f"psb{yb_}"])
            P.add("tensor", lambda e, py=py, n0=n0, nn=nn: e.matmul(py[:, 0:nn], CI[:, q, :], hi_[:, n0:n0 + nn], start=False, stop=(gp == 3)),
                  [f"CI_{q}", f"hi16_{hb}"], [f"psb{yb_}"])

    def seg_body(d, si, first, kind, s, ub, it0, seg_n):
        if kind == "ctx":
            sn, r, t0 = 256, 0, 0
            lay0 = 0 if d == 0 else 4 * TLAT
        else:
            sn, r, t0 = 512, s // 4, TCTX + 512 * (s % 4)
            lay0 = (TCTX + 512 * s) if d == 0 else 512 * s
        for c in range(4):
            usrc = uA_all[c][r * 128:(r + 1) * 128, t0:t0 + sn] if t0 < 1280 else uB_all[c][r * 128:(r + 1) * 128, t0 - 1280:t0 - 1280 + sn]
            ukey = f"uA_all{c}" if t0 < 1280 else f"uB_all{c}"
            C.load_cast(u16[ub][:, c, 0:sn], f"u16_{ub}_{c}", usrc, sn, cast_eng="scalar", reads=[ukey])
        nst = (sn + 511) // 512
        it = it0
        for gp0 in (0, 2):
            gens = []
            for gp in (gp0, gp0 + 1):
                tb = it % 2
                it += 1
                gens.append(gp_body(d, first, lay0, sn, ub, gp, tb, it % 2, seg_n))
            while gens:
                for g_ in list(gens):
                    try:
                        next(g_)
                    except StopIteration:
                        gens.remove(g_)
        ybuf = yb[ub]
        ybk = 4 + (seg_n % 2)
        if kind == "ctx":
            P.add("scalar", lambda e: e.copy(ybuf[:, 0, 0:256], C.psum[ybk][:, 0:256]), [f"psb{ybk}"], [f"yb{ub}"])
            P.dma(yc_loc[:, d * 256:(d + 1) * 256], ybuf[:, 0, 0:256], [f"yb{ub}"], [], f"st_yb{ub}")
        else:
            for c in range(4):
                P.add("scalar", lambda e, c=c: e.activation(ybuf[:, c, 0:512], C.psum[ybk][:, 0:512], AF.Identity, scale=oh[:, c:c + 1]),
                      [f"psb{ybk}", "oh"], [f"yb{ub}"])
            for c in range(4):
                P.dma(y_rs_in[d][c][(s // 4) * 128:(s // 4 + 1) * 128, 512 * (s % 4):512 * (s % 4) + 512], ybuf[:, c, 0:512], [f"yb{ub}"], [], f"st_yb{ub}_{c}")
        return it

    it = 0
    n = 0
    for d in range(2):
        order = [("ctx", 0)] + ([("lat", s) for s in range(16)] if d == 0 else [("lat", s) for s in range(15, -1, -1)])
        for si, (kind, s) in enumerate(order):
            it = seg_body(d, si, si == 0, kind, s, n % 2, it, n)
            n += 1


def phase_po(P, C, layer, xT, cols, Dr, tiles=TILES):
    l0 = layer == 0
    o_loc, wo_d = Dr["o_loc"], Dr["wo"][layer]
    wo16 = P.sb([128, 8, 1024], BF16, name="wo16")
    for k in range(8):
        C.load_cast(wo16[:, k, :], f"wo16_{k}", wo_d[k], 1024)
    nch = 4 if l0 else 8
    ot = [P.sb([128, nch, 512], BF16, name=f"ot{i}") for i in range(2)]
    if l0:
        uA, uB, y_rs_out, yc_all = Dr["uA"], Dr["uB"], Dr["y_rs_out"], Dr["yc_all"]
        g0 = P.sb([128, 4, 512], BF16, name="glu0_sb")
        g1 = P.sb([128, 4, 512], BF16, name="glu1_sb")
        for k in range(4):
            C.load_cast(g0[:, k, :], f"g0_{k}", Dr["glu0"][k], 512)
            C.load_cast(g1[:, k, :], f"g1_{k}", Dr["glu1"][k], 512)
        dcol = P.sb([128, 4], F32, name="dcol_sb")
        C.load(dcol[:], "dcol", Dr["dcol"])
        ub = [P.sb([128, 512], F32, name=f"ub{i}") for i in range(2)]
        yfb = [P.sb([128, 512], F32, name=f"yfb{i}") for i in range(2)]
        yrb = [P.sb([128, 512], F32, name=f"yrb{i}") for i in range(2)]
        t1 = [P.sb([128, 512], F32, name=f"t1_{i}") for i in range(2)]
        gt = [P.sb([128, 4, 512], BF16, name=f"gt{i}") for i in range(2)]
        glt = [P.sb([128, 4, 512], BF16, name=f"glt{i}") for i in range(2)]
        sgb = [P.sb([128, 512], F32, name=f"sgb{i}") for i in range(2)]
    it = 0
    for ti, (t0, tn) in enumerate(tiles):
        v = "c" if t0 < TCTX else "x"
        tb = ti % 2
        P.dma(ot[tb][:, :, 0:tn], o_loc[:, 0:nch, t0:t0 + tn], [], [f"ot{tb}"], f"ld_ot{tb}")
        if l0:
            for c in range(4):
                b = it % 2
                it += 1
                if t0 < TCTX:
                    yf_src = yc_all[c * 128:(c + 1) * 128, 0:256]
                    yr_src = yc_all[c * 128:(c + 1) * 128, 256:512]
                else:
                    yf_src = y_rs_out[0][c][:, t0 - TCTX:t0 - TCTX + tn]
                    yr_src = y_rs_out[1][c][:, t0 - TCTX:t0 - TCTX + tn]
                u_src = uA[c][:, t0:t0 + tn] if t0 < 1280 else uB[c][:, t0 - 1280:t0 - 1280 + tn]
                P.dma(ub[b][:, 0:tn], u_src, [], [f"ub{b}"], f"ld_ub{b}")
                P.dma(yfb[b][:, 0:tn], yf_src, [], [f"yfb{b}"], f"ld_yfb{b}")
                P.dma(yrb[b][:, 0:tn], yr_src, [], [f"yrb{b}"], f"ld_yrb{b}")
                y, u_, yr_, tt = yfb[b], ub[b], yrb[b], t1[b]
                P.add("vector", lambda e, y=y, yr_=yr_, tn=tn: e.tensor_tensor(y[:, 0:tn], y[:, 0:tn], yr_[:, 0:tn], ALU.add), [f"yfb{b}", f"yrb{b}"], [f"yfb{b}"])
                P.add("vector", lambda e, y=y, u_=u_, c=c, tn=tn: e.scalar_tensor_tensor(y[:, 0:tn], u_[:, 0:tn], dcol[:, c:c + 1], y[:, 0:tn], ALU.mult, ALU.add),
                      [f"yfb{b}", f"ub{b}", "dcol"], [f"yfb{b}"])
                P.add("scalar", lambda e, tt=tt, y=y, tn=tn: e.activation(tt[:, 0:tn], y[:, 0:tn], AF.Square), [f"yfb{b}"], [f"t1_{b}"])
                P.add("vector", lambda e, tt=tt, tn=tn: e.tensor_scalar(tt[:, 0:tn], tt[:, 0:tn], 0.044715, 1.0, ALU.mult, ALU.add), [f"t1_{b}"], [f"t1_{b}"])
                P.add("vector", lambda e, tt=tt, y=y, tn=tn: e.tensor_tensor(tt[:, 0:tn], tt[:, 0:tn], y[:, 0:tn], ALU.mult), [f"t1_{b}", f"yfb{b}"], [f"t1_{b}"])
                P.add("scalar", lambda e, tt=tt, tn=tn: e.activation(tt[:, 0:tn], tt[:, 0:tn], AF.Sigmoid, scale=1.5957691216), [f"t1_{b}"], [f"t1_{b}"])
                P.add("vector", lambda e, tt=tt, y=y, c=c, tb=tb, tn=tn: e.tensor_tensor(gt[tb][:, c, 0:tn], tt[:, 0:tn], y[:, 0:tn], ALU.mult), [f"t1_{b}", f"yfb{b}"], [f"gt{tb}_{c}"])
            for oc in range(4):
                pb = oc % 2
                pa, pg = C.psum[pb], C.psum[2 + pb]
                for k in range(4):
                    P.add("tensor", lambda e, pa=pa, k=k, oc=oc, tb=tb, tn=tn: e.matmul(pa[:, 0:tn], g0[:, k, oc * 128:(oc + 1) * 128], gt[tb][:, k, 0:tn], start=(k == 0), stop=(k == 3)),
                          [f"g0_{k}", f"gt{tb}_{k}"], [f"psb{pb}"])
                for k in range(4):
                    P.add("tensor", lambda e, pg=pg, k=k, oc=oc, tb=tb, tn=tn: e.matmul(pg[:, 0:tn], g1[:, k, oc * 128:(oc + 1) * 128], gt[tb][:, k, 0:tn], start=(k == 0), stop=(k == 3)),
                          [f"g1_{k}", f"gt{tb}_{k}"], [f"psb{2 + pb}"])
                sg = sgb[pb]
                P.add("scalar", lambda e, sg=sg, pg=pg, tn=tn: e.activation(sg[:, 0:tn], pg[:, 0:tn], AF.Sigmoid), [f"psb{2 + pb}"], [f"sgb{pb}"])
                P.add("vector", lambda e, sg=sg, pa=pa, oc=oc, tb=tb, tn=tn: e.tensor_tensor(glt[tb][:, oc, 0:tn], sg[:, 0:tn], pa[:, 0:tn], ALU.mult), [f"sgb{pb}", f"psb{pb}"], [f"glt{tb}_{oc}"])
        for oc in range(8):
            pb = 4 + oc % 2
            po = C.psum[pb]
            srcs = ([(glt[tb][:, k, 0:tn], f"glt{tb}_{k}", k) for k in range(4)] if l0 else []) + \
                   [(ot[tb][:, k, 0:tn], f"ot{tb}", (4 + k) if l0 else k) for k in range(nch)]
            for n, (rhs, rk, kk) in enumerate(srcs):
                P.add("tensor", lambda e, po=po, rhs=rhs, kk=kk, oc=oc, tn=tn, n=n, ns=len(srcs): e.matmul(po[:, 0:tn], wo16[:, kk, oc * 128:(oc + 1) * 128], rhs, start=(n == 0), stop=(n == ns - 1)),
                      [f"wo16_{kk}", rk], [f"psb{pb}"])
            P.add("vector", lambda e, po=po, oc=oc, t0=t0, tn=tn, v=v: e.scalar_tensor_tensor(xT[:, oc, t0:t0 + tn], po[:, 0:tn], colsel(cols, v, "gate", 1, oc),
                                                                                             xT[:, oc, t0:t0 + tn], ALU.mult, ALU.add),
                  [f"psb{pb}", f"gate_{v}", f"x{oc}_{ti}"], [f"x{oc}_{ti}"])


GROUPS = [[0, 1, 2, 3], [4, 5, 6, 7]]


def build_fused(stop=999):
    nc = bass.Bass("TRN2", target_bir_lowering=False)
    ext = lambda name, shape, dtype=F32: nc.dram_tensor(name, list(shape), dtype, kind="ExternalInput").ap()
    scr = lambda name, shape, dtype=F32: nc.dram_tensor(name, list(shape), dtype).ap()
    xT_d = ext("xT", [128, 8, T])
    cT_d = ext("cT", [128, 16])
    modw_d = ext("modw", [36, 128, 8, 128])
    modb_d = ext("modb", [128, 36])
    normg_d = ext("normg", [128, 2, 24])
    wg_d = ext("wg", [4, FC, 128, 8, 128])
    wu_d = ext("wu", [4, FC, 128, 8, 128])
    wd_d = ext("wd", [4, FC, 128, 1024])
    win_d = [ext("win0", [10, 128, 8, 128]), ext("win1", [12, 128, 8, 128])]
    qkg_d = ext("qkg", [128, 2, 2])
    cos_d, sin_d = ext("cosT", [128, T]), ext("sinT", [128, T])
    rot_d, bones_d = ext("rot", [128, 128]), ext("bones", [128, 128])
    Dr = dict(
        Bl=ext("Bl", [2, 4, 2, 4, 128, 128]), Cl=ext("Cl", [2, 4, 2, 128, 128]), pcols=ext("pcols", [128, 3, 8]),
        iota_c=ext("iota_c", [128, 2, 66]), iota_j=ext("iota_j", [128, 2, 128]), onehot=ext("onehot", [128, 4]),
        dcol=ext("dcol", [128, 4]), glu0=ext("glu0", [4, 128, 512]), glu1=ext("glu1", [4, 128, 512]),
        wo=ext("wo", [2, 8, 128, 1024]), sink=ext("sink", [128, 16]),
        wmask=ext("wmask", [128, 4, 6, 512], BF16), emask=ext("emask", [128, 2, 4, 512], BF16),
    )
    out_d = nc.dram_tensor("outT", [128, 8, TLAT], F32, kind="ExternalOutput").ap()
    Dr.update(
        mod_loc=scr("mod_loc", [128, 72]), mod_all=scr("mod_all", [512, 72]),
        uA=[scr(f"uA{c}", [128, 1280]) for c in range(4)], uB=[scr(f"uB{c}", [128, 1024]) for c in range(4)],
        uA_all=[scr(f"uA_all{c}", [512, 1280]) for c in range(4)], uB_all=[scr(f"uB_all{c}", [512, 1024]) for c in range(4)],
        q_loc=scr("q_loc", [128, 8, T], BF16),
        k_loc=scr("k_loc", [128, 2, T]), kA=scr("kA", [128, 1280]), kB=scr("kB", [128, 1024]),
        kA_all=scr("kA_all", [512, 1280]), kB_all=scr("kB_all", [512, 1024]),
        v_loc=scr("v_loc", [T, 256]), vA=scr("vA", [1280, 128]), vB=scr("vB", [1024, 128]),
        vA_all=scr("vA_all", [5120, 128]), vB_all=scr("vB_all", [4096, 128]),
        y_rs_in=[[scr(f"y_rs_in{d}{c}", [512, TLAT]) for c in range(4)] for d in range(2)],
        y_rs_out=[[scr(f"y_rs_out{d}{c}", [128, TLAT]) for c in range(4)] for d in range(2)],
        yc_loc=scr("yc_loc", [128, 512]), yc_all=scr("yc_all", [512, 512]),
        o_loc=scr("o_loc", [128, 8, T], BF16),
        ek_loc=scr("ek_loc", [128, 512]), ek_all=scr("ek_all", [512, 512]),
        ev_loc=scr("ev_loc", [256, 256]), ev_all=scr("ev_all", [1024, 256]),
    )
    P = Prog(nc)
    xT = P.sb([128, 8, T], F32, name="xT_sb", persist=True)
    modall = P.sb([128, 2, 72, 2], F32, name="modall", persist=True)
    ng = P.sb([128, 2, 24], F32, name="normg_sb", persist=True)
    coltiles = {(l, v): (P.sb([128, 24], F32, name=f"gs_{v}{l}", persist=True), P.sb([128, 24], F32, name=f"gate_{v}{l}", persist=True))
                for l in range(2) for v in ("x", "c")}
    C = Ctx(P)

    c32 = P.sb([128, 16], F32, name="c32")
    c16 = P.sb([128, 16], BF16, name="c16")
    bt = P.sb([128, 36], F32, name="bt")
    res = P.sb([128, 36, 2], F32, name="res")
    w16 = [P.sb([128, 8, 128], BF16, name=f"w16_{i}") for i in range(2)]
    for c in range(KC):
        for ti, (t0, tn) in enumerate(TILES):
            P.dma(xT[:, c, t0:t0 + tn], xT_d[:, c, t0:t0 + tn], [], [f"x{c}_{ti}"], f"ld_x{(c * 5 + ti) % 4}")
    C.load(c32[:], "c32", cT_d)
    C.load(bt[:], "bt", modb_d)
    C.load(ng[:], "normg", normg_d)
    P.add("scalar", lambda e: e.activation(c16[:], c32[:], AF.Silu), ["c32"], ["c16"])
    for oc in range(36):
        sl = oc % 2
        C.load_cast(w16[sl][:].rearrange("p k m -> p (k m)"), f"w16_{sl}", modw_d[oc].rearrange("p k m -> p (k m)"), 1024)
        ps = C.psum[oc % 2]
        for k in range(8):
            P.add("tensor", lambda e, ps=ps, k=k, sl=sl: e.matmul(ps[:, 0:2], w16[sl][:, k, :], c16[:, k * 2:(k + 1) * 2], start=(k == 0), stop=(k == 7)),
                  [f"w16_{sl}", "c16"], [f"psb{oc % 2}"])
        P.add("vector", lambda e, ps=ps, oc=oc: e.tensor_scalar(res[:, oc, :], ps[:, 0:2], bt[:, oc:oc + 1], None, ALU.add),
              [f"psb{oc % 2}", "bt"], ["res"])
    P.dma(Dr["mod_loc"], res[:].rearrange("p a b -> p (a b)"), ["res"], ["mod_loc"], "st_mod")
    P.coll("AllGather", ALU.bypass, GROUPS, Dr["mod_loc"], Dr["mod_all"], ["mod_loc"], ["mod_all"], "cc_mod")
    for l in range(2):
        for r2 in range(2):
            r = 2 * l + r2
            C.load(modall[:, l, 36 * r2:36 * (r2 + 1), :].rearrange("p a b -> p (a b)"), "modall", Dr["mod_all"][r * 128:(r + 1) * 128, :], reads=["mod_all"])
    cols = [make_cols(P, modall, ng, l, coltiles) for l in range(2)]
    P.end_phase()
    if stop == 1:
        P.close()
        return nc

    def phase_ffn(layer, f_idx, s, tiles=TILES):
        C.new_phase()
        K = alloc_common(P, C)
        hT = P.sb([128, 8, T], BF16, name="hT_sb")
        emit_norm(P, C, K, xT, hT, cols[layer], s, tiles=tiles)
        emit_ffn(P, C, K, xT, hT, cols[layer], s, wg_d[2 * layer + f_idx], wu_d[2 * layer + f_idx], wd_d[2 * layer + f_idx], tiles=tiles)
        return K, hT

    def phase_a(layer):
        n_u, n_q, n_k, vw = (4, 4, 1, 128) if layer == 0 else (0, 8, 2, 256)
        n_fm = n_u + n_q + n_k
        K, hT = phase_ffn(layer, 0, 0)
        emit_norm(P, C, K, xT, hT, cols[layer], 1)
        emit_consts(P, C, K, rot_d, bones_d)
        qkg = P.sb([128, 2, 2], F32, name="qkg_sb")
        C.load(qkg[:], "qkg", qkg_d)
        K["cst"] = [P.sb([128, 512], F32, name=f"cst{i}") for i in range(1)]
        K["snt"] = [P.sb([128, 512], F32, name=f"snt{i}") for i in range(1)]
        K["win16"] = [P.sb([128, 8, 128], BF16, name=f"win16_{i}") for i in range(2)]
        K["zq"] = [P.sb([128, 512], F32, name=f"zq{i}") for i in range(2)]
        K["zb"] = [P.sb([128, 512], BF16, name=f"zb{i}") for i in range(2)]
        K["ob32"] = [P.sb([128, 512], F32, name=f"ob32_{i}") for i in range(2)]
        K["ob16"] = [P.sb([128, 512], BF16, name=f"ob16_{i}") for i in range(2)]
        K["wv16"] = P.sb([128, 8, vw], BF16, name="wv16")
        K["vb"] = [P.sb([128, 256], F32, name="vb0")] * 2
        kinds = ["u"] * n_u + ["q"] * n_q + ["k"] * n_k
        def split_dst(a_, b_):
            return lambda t0, tn: (a_[:, t0:t0 + tn] if t0 < 1280 else b_[:, t0 - 1280:t0 - 1280 + tn])
        kdst = [split_dst(Dr["kA"], Dr["kB"])] if layer == 0 else [Dr["k_loc"][:, i, :] for i in range(2)]
        outs = [(split_dst(Dr["uA"][i], Dr["uB"][i]), F32) for i in range(n_u)] + [(Dr["q_loc"][:, i, :], BF16) for i in range(n_q)] + [(kd_, F32) for kd_ in kdst]
        if layer == 0:
            v_d = [Dr["vA"][tt * 128:(tt + 1) * 128, :] if tt < 10 else Dr["vB"][(tt - 10) * 128:(tt - 9) * 128, :] for tt in range(T // 128)]
        else:
            v_d = Dr["v_loc"].rearrange("(tt p) f -> tt p f", p=128)
        emit_inproj(P, C, K, hT, win_d[layer], n_fm, kinds, qkg[:, layer, :], cos_d, sin_d, outs, v_d, vw, n_fm)
        P.end_phase()

    phase_a(0)
    if stop == 2:
        P.close()
        return nc
    for c in range(4):
        P.coll("AllGather", ALU.bypass, GROUPS, Dr["uA"][c], Dr["uA_all"][c], [], [f"uA_all{c}"], "cc_u")
        P.coll("AllGather", ALU.bypass, GROUPS, Dr["uB"][c], Dr["uB_all"][c], [], [f"uB_all{c}"], "cc_u")
    P.coll("AllGather", ALU.bypass, GROUPS, Dr["kA"], Dr["kA_all"], [], ["kA_all"], "cc_k")
    P.coll("AllGather", ALU.bypass, GROUPS, Dr["kB"], Dr["kB_all"], [], ["kB_all"], "cc_k")
    P.coll("AllGather", ALU.bypass, GROUPS, Dr["vA"], Dr["vA_all"], [], ["vA_all"], "cc_v")
    P.coll("AllGather", ALU.bypass, GROUPS, Dr["vB"], Dr["vB_all"], [], ["vB_all"], "cc_v")
    C.new_phase()
    phase_s5(P, C, Dr)
    P.end_phase()
    if stop == 4:
        P.close()
        return nc
    for d in range(2):
        for c in range(4):
            P.coll("ReduceScatter", ALU.add, GROUPS, Dr["y_rs_in"][d][c], Dr["y_rs_out"][d][c], [], [f"y_rs_out{d}{c}"], "cc_y")
    P.coll("AllGather", ALU.bypass, GROUPS, Dr["yc_loc"], Dr["yc_all"], [], ["yc_all"], "cc_yc")
    C.new_phase()
    phase_att(P, C, 0, Dr)
    P.end_phase()
    if stop == 6:
        P.close()
        return nc
    C.new_phase()
    phase_po(P, C, 0, xT, cols[0], Dr)
    P.end_phase()
    if stop == 7:
        P.close()
        return nc
    phase_ffn(0, 1, 2)
    P.end_phase()
    if stop == 8:
        P.close()
        return nc
    phase_a(1)
    if stop == 9:
        P.close()
        return nc
    for c in range(2):
        for lh, col0 in ((0, TCTX), (1, T - 128)):
            P.dma(Dr["ek_loc"][:, c * 256 + lh * 128:c * 256 + (lh + 1) * 128], Dr["k_loc"][:, c, col0:col0 + 128], [], ["ek_loc"], f"cp_ek{c}{lh}")
    for lh, col0 in ((0, TCTX), (1, T - 128)):
        P.dma(Dr["ev_loc"][lh * 128:(lh + 1) * 128, :], Dr["v_loc"][col0:col0 + 128, :], [], ["ev_loc"], f"cp_ev{lh}")
    P.coll("AllGather", ALU.bypass, GROUPS, Dr["ek_loc"], Dr["ek_all"], ["ek_loc"], ["ek_all"], "cc_ek")
    P.coll("AllGather", ALU.bypass, GROUPS, Dr["ev_loc"], Dr["ev_all"], ["ev_loc"], ["ev_all"], "cc_ev")
    P.end_phase()
    if stop == 10:
        P.close()
        return nc
    C.new_phase()
    phase_att(P, C, 1, Dr)
    P.end_phase()
    if stop == 11:
        P.close()
        return nc
    C.new_phase()
    phase_po(P, C, 1, xT, cols[1], Dr, tiles=TILES[1:])
    P.end_phase()
    if stop == 12:
        P.close()
        return nc
    phase_ffn(1, 1, 2, tiles=TILES[1:])
    for c in range(KC):
        for ti, (t0, tn) in enumerate(TILES[1:]):
            P.dma(out_d[:, c, t0 - TCTX:t0 - TCTX + tn], xT[:, c, t0:t0 + tn], [f"x{c}_{ti}"], [], f"st_x{(c * 5 + ti) % 4}")
    P.end_phase()
    if stop == 13:
        P.close()
        return nc
    P.close()
    return nc
def fm(x2d):
    t, f = x2d.shape
    return np.ascontiguousarray(x2d.T.reshape(f // 128, 128, t).transpose(1, 0, 2))


def unfm(a):
    p, c, t = a.shape
    return np.ascontiguousarray(a.transpose(1, 0, 2).reshape(c * 128, t).T)


def w_oc(W):
    k, n = W.shape
    return np.ascontiguousarray(W.reshape(k // 128, 128, n // 128, 128).transpose(2, 1, 0, 3))


def w_rows(W):
    k, n = W.shape
    return np.ascontiguousarray(W.reshape(k // 128, 128, n))


def rope_tables():
    rows = 8192 // 64
    r = np.repeat(np.arange(rows, dtype=np.float32), 64)
    col = np.tile(np.arange(64, dtype=np.float32), rows)
    inv = (10000.0 ** (-np.arange(16, dtype=np.float32) / 16)).astype(np.float32)
    ang = np.concatenate([r[:, None] * inv, col[:, None] * inv], axis=-1).astype(np.float32)
    cos = np.cos(ang).astype(np.float32).T
    sin = np.sin(ang).astype(np.float32).T
    idx = np.arange(128) % 32
    return cos[idx], sin[idx]


def const_mats():
    rot = np.zeros((128, 128), np.float32)
    for m in range(128):
        j = m % 64
        if j < 32:
            rot[m + 32, m] = -1.0
        else:
            rot[m - 32, m] = 1.0
    bones = np.zeros((128, 128), np.float32)
    bones[:64, :64] = 1.0
    bones[64:, 64:] = 1.0
    return rot, bones


def core_tokens(inp_x, ctx, i):
    b, q = i // 4, i % 4
    return np.concatenate([ctx[b], inp_x[b, q * TLAT:(q + 1) * TLAT]], axis=0)


def fused_inputs(inp):
    cos, sin = rope_tables()
    rot, bones = const_mats()
    W = np.concatenate([inp["mod_w"][0], inp["mod_w"][1]], axis=1)
    Bv = np.concatenate([inp["mod_b"][0], inp["mod_b"][1]], axis=0)
    Wl = W.reshape(8, 128, 144, 128).transpose(2, 1, 0, 3)
    Bl_mod = Bv.reshape(144, 128).T
    normg = np.ascontiguousarray(np.stack([inp["norm_g"][l].reshape(3, 8, 128).transpose(2, 0, 1).reshape(128, 24) for l in range(2)], axis=1))
    wg = np.stack([w_oc(inp["ffn_wg"][l, f]) for l in range(2) for f in range(2)])
    wu = np.stack([w_oc(inp["ffn_wu"][l, f]) for l in range(2) for f in range(2)])
    wd = np.stack([w_rows(inp["ffn_wd"][l, f]) for l in range(2) for f in range(2)])
    win0, win1 = w_oc(inp["ab_w_in"][0]), w_oc(inp["win_w_in"][0])
    qkg = np.ascontiguousarray(np.stack([inp["qk_norm"][l][:, np.arange(128) % 64].T for l in range(2)], axis=1))
    wo = np.stack([w_rows(inp["w_out"][l]) for l in range(2)])
    dcol = np.ascontiguousarray(inp["s5_d"][0].reshape(4, 128).T)
    glu0, glu1 = w_rows(inp["s5_glu_w"][0, 0]), w_rows(inp["s5_glu_w"][0, 1])
    sink = np.ascontiguousarray(np.broadcast_to(inp["win_sink"][0].reshape(1, 16), (128, 16))).astype(np.float32)
    iota_c = np.zeros((128, 2, 66), np.float32)
    iota_c[:, 0, :] = np.arange(66)
    iota_c[:, 1, :] = 65 - np.arange(66)
    iota_j = np.zeros((128, 2, 128), np.float32)
    iota_j[:, 0, :] = np.arange(128)
    iota_j[:, 1, :] = 127 - np.arange(128)
    one, zero = np.ones((1,), NPBF)[0], np.zeros((1,), NPBF)[0]
    kk = np.arange(128)[:, None]
    qq = np.arange(512)[None, :]
    wmask = np.zeros((128, 4, 6, 512), NPBF)
    for m in range(4):
        for r in range(6):
            ok = np.abs((4 * m + r - 1) * 128 + kk - (512 * m + qq)) <= 128
            wmask[:, m, r, :] = np.where(ok, one, zero)
    e = 0
    maps = []
    for i in range(NCORES):
        b, q = i // 4, i % 4
        cT = np.ascontiguousarray(np.stack([inp["c"][b], inp["c_ctx"]], axis=0).T.reshape(8, 128, 2).transpose(1, 0, 2).reshape(128, 16))
        cosT = np.concatenate([np.ones((128, TCTX), np.float32), cos[:, q * TLAT:(q + 1) * TLAT]], axis=1)
        sinT = np.concatenate([np.zeros((128, TCTX), np.float32), sin[:, q * TLAT:(q + 1) * TLAT]], axis=1)
        Bl = np.zeros((2, 4, 2, 4, 128, 128), np.float32)
        Cl = np.zeros((2, 4, 2, 128, 128), np.float32)
        pcols = np.zeros((128, 3, 8), np.float32)
        j = q
        for d in range(2):
            for gp in range(4):
                for gl in range(2):
                    g = 8 * j + 2 * gp + gl
                    ch0 = (2 * gp + gl) * 16
                    for ri, (bsrc, csrc) in enumerate(((inp["s5_b_re"], inp["s5_c_re"]), (inp["s5_b_im"], inp["s5_c_im"]))):
                        Bl[d, gp, ri, j, ch0:ch0 + 16, gl * 64:(gl + 1) * 64] = bsrc[e, d, g].T
                        Cl[d, gp, ri, gl * 64:(gl + 1) * 64, ch0:ch0 + 16] = csrc[e, d, g].T
                    pcols[gl * 64:(gl + 1) * 64, 0, d * 4 + gp] = inp["s5_lam_re"][e, d, g]
                    pcols[gl * 64:(gl + 1) * 64, 1, d * 4 + gp] = inp["s5_lam_im"][e, d, g]
                    pcols[gl * 64:(gl + 1) * 64, 2, d * 4 + gp] = inp["s5_log_step"][e, d, g]
        onehot = np.zeros((128, 4), np.float32)
        onehot[:, q] = 1.0
        emask = np.zeros((128, 2, 4, 512), NPBF)
        for r in range(4):
            if r == q - 1:
                emask[:, 0, r, :] = np.where(np.abs(-128 + kk - qq) <= 128, one, zero)
            if r == q + 1:
                emask[:, 1, r, :] = np.where(np.abs(2048 + kk - 1536 - qq) <= 128, one, zero)
        maps.append(dict(
            xT=fm(core_tokens(inp["x"], inp["ctx"], i)), cT=cT, modw=np.ascontiguousarray(Wl[36 * q:36 * q + 36]),
            modb=np.ascontiguousarray(Bl_mod[:, 36 * q:36 * q + 36]), normg=normg, wg=wg, wu=wu, wd=wd, win0=win0, win1=win1, qkg=qkg,
            cosT=np.ascontiguousarray(cosT), sinT=np.ascontiguousarray(sinT), rot=rot, bones=bones,
            Bl=Bl, Cl=Cl, pcols=pcols, iota_c=iota_c, iota_j=iota_j, onehot=onehot, dcol=dcol, glu0=glu0, glu1=glu1, wo=wo, sink=sink,
            wmask=wmask, emask=emask))
    return maps


def kernel(**inputs):
    inp = {k: np.asarray(v) for k, v in inputs.items()}
    nc = build_fused()
    res = run_bass_kernel_spmd(nc, fused_inputs(inp), core_ids=list(range(NCORES)))
    out = np.zeros((2, 4 * TLAT, D), np.float32)
    for i in range(NCORES):
        b, q = i // 4, i % 4
        out[b, q * TLAT:(q + 1) * TLAT] = unfm(np.asarray(res.results[i]["outT"]))
    return out
```

```python
import contextlib
import math
import numpy as np
import ml_dtypes
import concourse.bass as bass
import concourse.mybir as mybir
from concourse.bass_utils import run_bass_kernel_spmd

F32 = mybir.dt.float32
BF16 = mybir.dt.bfloat16
I32 = mybir.dt.int32
ALU = mybir.AluOpType
AF = mybir.ActivationFunctionType
AX = mybir.AxisListType
NPBF = ml_dtypes.bfloat16

NCORES = 8
D = 1024
KC = 8
DFF = 2816
FC = 22
TCTX = 256
TLAT = 2048
T = TCTX + TLAT
TILES = [(0, 256), (256, 512), (768, 512), (1280, 512), (1792, 512)]
EPS = 1e-6
GFF = 4


class _Op:
    __slots__ = ("eng", "pos", "fn", "cwaits", "dwaits", "signal", "dma_key", "dma_k")


class Prog:
    ENGS = ("tensor", "vector", "scalar", "gpsimd", "sync")
    CENG = ("tensor", "vector", "scalar", "gpsimd")

    def __init__(self, nc):
        self.nc = nc
        self.ges = contextlib.ExitStack()
        self.pes = contextlib.ExitStack()
        self.esem = {e: self.ges.enter_context(nc.semaphore(f"s_{e}")) for e in self.CENG}
        self.esig = {e: 0 for e in self.CENG}
        self.dsem = {}
        self.dma_cnt = {}
        self.dma_inc = {}
        self.n_sb = 0
        self.n_phase = 0
        self._reset()

    def _reset(self):
        self.ops = {e: [] for e in self.ENGS}
        self.last_w = {}
        self.readers = {}

    def sb(self, shape, dtype=F32, name=None, persist=False):
        self.n_sb += 1
        nm = (name or "sb") + f"_{self.n_sb}"
        es = self.ges if persist else self.pes
        return es.enter_context(self.nc.sbuf_tensor(nm, list(shape), dtype))

    def ps(self, shape, dtype=F32, name=None):
        self.n_sb += 1
        return self.ges.enter_context(self.nc.psum_tensor(name or f"ps{self.n_sb}", list(shape), dtype))

    def add(self, eng, fn, reads=(), writes=(), dma_key=None, inc=16):
        op = _Op()
        op.eng, op.fn, op.signal = eng, fn, False
        op.pos = len(self.ops[eng])
        op.dma_key = dma_key
        op.dma_k = None
        deps = {}
        for k in reads:
            d = self.last_w.get(k)
            if d is not None:
                deps[id(d)] = d
        for k in writes:
            d = self.last_w.get(k)
            if d is not None:
                deps[id(d)] = d
            for r in self.readers.get(k, ()):
                deps[id(r)] = r
        cw = {}
        dw = {}
        for i, d in deps.items():
            if d.dma_key is not None:
                v = self.dma_inc[d.dma_key] * (d.dma_k + 1)
                dw[d.dma_key] = max(dw.get(d.dma_key, 0), v)
            elif d.eng == eng:
                if eng != "tensor":
                    cw[d.eng] = max(cw.get(d.eng, -1), d.pos)
            else:
                cw[d.eng] = max(cw.get(d.eng, -1), d.pos)
        if dma_key is not None:
            self.dma_inc.setdefault(dma_key, inc)
            k = self.dma_cnt.get(dma_key, 0)
            op.dma_k = k
            self.dma_cnt[dma_key] = k + 1
            if k > 0:
                dw[dma_key] = max(dw.get(dma_key, 0), self.dma_inc[dma_key] * k)
        op.cwaits = []
        for e, p in cw.items():
            d = self.ops[e][p]
            d.signal = True
            op.cwaits.append(d)
        op.dwaits = list(dw.items())
        self.ops[eng].append(op)
        for k in reads:
            self.readers.setdefault(k, []).append(op)
        for k in writes:
            self.last_w[k] = op
            self.readers[k] = []
        return op

    def dma(self, out, in_, reads, writes, key, eng="sync", **kw):
        return self.add(eng, lambda e: e.dma_start(out=out, in_=in_, **kw), reads, writes, dma_key=key)

    def coll(self, kind, op, groups, in_ap, out_ap, reads, writes, key):
        self.n_coll = getattr(self, "n_coll", 0) + 1
        key = f"{key}_{self.n_coll}"
        return self.add("gpsimd", lambda e: e.collective_compute(kind, op, replica_groups=groups, ins=[in_ap], outs=[out_ap]),
                        reads, writes, dma_key=key, inc=1)

    def end_phase(self):
        nc = self.nc
        lasts = []
        for e in self.CENG:
            real = [o for o in self.ops[e] if o.dma_key is None and o.fn is not None]
            if real:
                lasts.append(real[-1])
        for d in lasts:
            d.signal = True
        for e in self.ENGS:
            op = _Op()
            op.eng, op.fn, op.signal, op.dma_key, op.dma_k = e, None, False, None, None
            op.pos = len(self.ops[e])
            op.cwaits = [d for d in lasts if d.eng != e]
            op.dwaits = [(k, self.dma_inc[k] * c) for k, c in self.dma_cnt.items()]
            self.ops[e].append(op)
        sig = {}
        for e in self.CENG:
            c = self.esig[e]
            for op in self.ops[e]:
                if op.signal:
                    c += 1
                    sig[id(op)] = c
            self.esig[e] = c
        for k in self.dma_cnt:
            if k not in self.dsem:
                self.dsem[k] = self.ges.enter_context(nc.semaphore(f"d_{len(self.dsem)}"))
        esem, dsem = self.esem, self.dsem
        with nc.Block() as block:
            def mk(ename):
                ops = self.ops[ename]

                def body(e):
                    for op in ops:
                        for d in op.cwaits:
                            e.wait_ge(esem[d.eng], sig[id(d)])
                        for k, v in op.dwaits:
                            e.wait_ge(dsem[k], v)
                        if op.fn is None:
                            continue
                        ins = op.fn(e)
                        if op.dma_key is not None:
                            ins.then_inc(dsem[op.dma_key], self.dma_inc[op.dma_key])
                        elif op.signal:
                            ins.then_inc(esem[ename], 1)
                return body

            for ename in self.ENGS:
                getattr(block, ename)(mk(ename))
        self.pes.close()
        self.pes = contextlib.ExitStack()
        self._reset()
        self.n_phase += 1

    def close(self):
        self.ges.close()


class Ctx:
    def __init__(self, P):
        self.P = P
        self.psum = [P.ps([128, 512], F32, name=f"psb{i}") for i in range(8)]
        self.ld_i = 0
        self.new_phase()

    def new_phase(self):
        P = self.P
        self.stage = [P.sb([128, 1024], F32, name=f"stage{i}") for i in range(3)]
        self.stage_i = 0

    def load_cast(self, dst16_ap, dst_key, src_ap, n, cast_eng="gpsimd", view=None, psl=None, reads=()):
        P = self.P
        i = self.stage_i
        self.stage_i = (i + 1) % len(self.stage)
        st = self.stage[i]
        sk = f"stage{i}"
        sv = st[:, 0:n] if psl is None else st[psl, 0:n]
        P.dma(sv if view is None else view(sv), src_ap, list(reads), [sk], f"ld_stage{i}")
        if cast_eng == "scalar":
            P.add("scalar", lambda e: e.copy(dst16_ap, sv if view is None else view(sv)), [sk], [dst_key])
        else:
            P.add(cast_eng, lambda e: e.tensor_copy(dst16_ap, sv if view is None else view(sv)), [sk], [dst_key])

    def load(self, dst_ap, dst_key, src_ap, reads=()):
        P = self.P
        self.ld_i += 1
        P.dma(dst_ap, src_ap, list(reads), [dst_key], f"ld_misc{self.ld_i % 4}")


def make_cols(P, modall, ng, layer, tiles):
    out = {}
    for vi, v in enumerate(("x", "c")):
        gs, gt = tiles[(layer, v)]
        for s in range(3):
            sc = modall[:, layer, (3 * s + 1) * 8:(3 * s + 2) * 8, vi]
            P.add("vector", lambda e, s=s, sc=sc, gs=gs: e.scalar_tensor_tensor(gs[:, s * 8:(s + 1) * 8], sc, 1.0, ng[:, layer, s * 8:(s + 1) * 8], ALU.add, ALU.mult),
                  ["modall", "normg"], [f"gs_{v}"])
            g = modall[:, layer, (3 * s + 2) * 8:(3 * s + 3) * 8, vi]
            fac = 1.0 if s == 1 else 0.5
            P.add("vector", lambda e, s=s, g=g, gt=gt, fac=fac: e.tensor_scalar(gt[:, s * 8:(s + 1) * 8], g, fac, None, ALU.mult),
                  ["modall"], [f"gate_{v}"])
        out[v] = dict(gs=gs, gate=gt, mod=modall, layer=layer, vi=vi)
    return out


def colsel(cols, v, kind, s, c):
    if kind == "gs":
        return cols[v]["gs"][:, s * 8 + c:s * 8 + c + 1]
    if kind == "gate":
        return cols[v]["gate"][:, s * 8 + c:s * 8 + c + 1]
    if kind == "shift":
        j = (3 * s) * 8 + c
        return cols[v]["mod"][:, cols[v]["layer"], j:j + 1, cols[v]["vi"]]
    raise ValueError(kind)


def emit_norm(P, C, K, xT, hT, cols, s, tiles=TILES):
    for ti, (t0, tn) in enumerate(tiles):
        v = "c" if t0 < TCTX else "x"
        pss = C.psum[6 + (ti % 2)]
        psk = f"psb{6 + (ti % 2)}"
        for c in range(KC):
            sq = K["sq"][c % 2]
            sqk = f"sq{c % 2}"
            P.add("scalar", lambda e, sq=sq, c=c, t0=t0, tn=tn: e.activation(sq[:, 0:tn], xT[:, c, t0:t0 + tn], AF.Square),
                  [f"x{c}_{ti}"], [sqk])
            P.add("tensor", lambda e, sq=sq, c=c, pss=pss, tn=tn: e.matmul(pss[:, 0:tn], K["ones"][:], sq[:, 0:tn], start=(c == 0), stop=(c == KC - 1)),
                  [sqk, "ones"], [psk])
        rs = K["rstd"][ti % 2]
        rsk = f"rstd{ti % 2}"
        P.add("scalar", lambda e, rs=rs, pss=pss, tn=tn: e.activation(rs[:, 0:tn], pss[:, 0:tn], AF.Sqrt, bias=K["epscol"][:, 0:1], scale=1.0 / D),
              [psk, "epscol"], [rsk])
        P.add("vector", lambda e, rs=rs, tn=tn: e.reciprocal(rs[:, 0:tn], rs[:, 0:tn]), [rsk], [rsk])
        for c in range(KC):
            tmp = K["ntmp"][c % 2]
            tk = f"ntmp{c % 2}"
            P.add("vector", lambda e, tmp=tmp, c=c, t0=t0, tn=tn, rs=rs: e.tensor_tensor(tmp[:, 0:tn], xT[:, c, t0:t0 + tn], rs[:, 0:tn], ALU.mult),
                  [f"x{c}_{ti}", rsk], [tk])
            P.add("scalar", lambda e, tmp=tmp, c=c, t0=t0, tn=tn, v=v: e.activation(hT[:, c, t0:t0 + tn], tmp[:, 0:tn], AF.Identity,
                                                                                    bias=colsel(cols, v, "shift", s, c), scale=colsel(cols, v, "gs", s, c)),
                  [tk, f"gs_{v}", "modall"], [f"h{c}_{ti}"])


def emit_ffn(P, C, K, xT, hT, cols, s, wg_d, wu_d, wd_d, tiles=TILES):
    act = K["act"]
    wg16, wu16, wd16 = K["wg16"], K["wu16"], K["wd16"]

    def load_chunk(j):
        sl = j % 2
        C.load_cast(wg16[sl][:].rearrange("p k m -> p (k m)"), f"wg16_{sl}", wg_d[j].rearrange("p k m -> p (k m)"), 1024)
        C.load_cast(wu16[sl][:].rearrange("p k m -> p (k m)"), f"wu16_{sl}", wu_d[j].rearrange("p k m -> p (k m)"), 1024)
        C.load_cast(wd16[j % (2 * GFF)][:], f"wd16_{j % (2 * GFF)}", wd_d[j], 1024)

    groups = [list(range(g, min(g + GFF, FC))) for g in range(0, FC, GFF)]
    load_chunk(0)
    pi = 0
    for grp in groups:
        for j in grp:
            if j + 1 < FC:
                load_chunk(j + 1)
            sl = j % 2
            jj = j % GFF
            for ti, (t0, tn) in enumerate(tiles):
                gb = pi % 2
                pi += 1
                pg, pu = C.psum[gb], C.psum[2 + gb]
                for k in range(KC):
                    P.add("tensor", lambda e, pg=pg, k=k, sl=sl, t0=t0, tn=tn: e.matmul(pg[:, 0:tn], wg16[sl][:, k, :], hT[:, k, t0:t0 + tn], start=(k == 0), stop=(k == KC - 1)),
                          [f"wg16_{sl}", f"h{k}_{ti}"], [f"psb{gb}"])
                for k in range(KC):
                    P.add("tensor", lambda e, pu=pu, k=k, sl=sl, t0=t0, tn=tn: e.matmul(pu[:, 0:tn], wu16[sl][:, k, :], hT[:, k, t0:t0 + tn], start=(k == 0), stop=(k == KC - 1)),
                          [f"wu16_{sl}", f"h{k}_{ti}"], [f"psb{2 + gb}"])
                sg = K["sg"][gb]
                P.add("scalar", lambda e, sg=sg, pg=pg, tn=tn: e.activation(sg[:, 0:tn], pg[:, 0:tn], AF.Silu), [f"psb{gb}"], [f"sg{gb}"])
                P.add("vector", lambda e, sg=sg, pu=pu, jj=jj, t0=t0, tn=tn: e.tensor_tensor(act[:, jj, t0:t0 + tn], sg[:, 0:tn], pu[:, 0:tn], ALU.mult),
                      [f"sg{gb}", f"psb{2 + gb}"], [f"act{jj}_{ti}"])
        for ti, (t0, tn) in enumerate(tiles):
            v = "c" if t0 < TCTX else "x"
            for dc in range(KC):
                db = 4 + (dc % 2)
                pd = C.psum[db]
                for n, j in enumerate(grp):
                    jj = j % GFF
                    jw = j % (2 * GFF)
                    P.add("tensor", lambda e, pd=pd, jj=jj, jw=jw, dc=dc, t0=t0, tn=tn, n=n, ng=len(grp): e.matmul(pd[:, 0:tn], wd16[jw][:, dc * 128:(dc + 1) * 128], act[:, jj, t0:t0 + tn],
                                                                                                  start=(n == 0), stop=(n == ng - 1)),
                          [f"wd16_{jw}", f"act{jj}_{ti}"], [f"psb{db}"])
                P.add("vector", lambda e, pd=pd, dc=dc, t0=t0, tn=tn, v=v: e.scalar_tensor_tensor(xT[:, dc, t0:t0 + tn], pd[:, 0:tn], colsel(cols, v, "gate", s, dc),
                                                                                                 xT[:, dc, t0:t0 + tn], ALU.mult, ALU.add),
                      [f"psb{db}", f"gate_{v}", f"x{dc}_{ti}"], [f"x{dc}_{ti}"])


def alloc_common(P, C, with_ffn=True):
    K = {}
    K["ones"] = P.sb([128, 128], BF16, name="ones")
    P.add("vector", lambda e: e.memset(K["ones"][:], 1.0), [], ["ones"])
    K["epscol"] = P.sb([128, 1], F32, name="epscol")
    P.add("vector", lambda e: e.memset(K["epscol"][:], EPS), [], ["epscol"])
    K["sq"] = [P.sb([128, 512], BF16, name=f"sq{i}") for i in range(2)]
    K["rstd"] = [P.sb([128, 512], F32, name=f"rstd{i}") for i in range(2)]
    K["ntmp"] = [P.sb([128, 512], F32, name=f"ntmp{i}") for i in range(2)]
    if with_ffn:
        K["act"] = P.sb([128, GFF, T], BF16, name="act")
        K["wg16"] = [P.sb([128, 8, 128], BF16, name=f"wg16_{i}") for i in range(2)]
        K["wu16"] = [P.sb([128, 8, 128], BF16, name=f"wu16_{i}") for i in range(2)]
        K["wd16"] = [P.sb([128, 1024], BF16, name=f"wd16_{i}") for i in range(2 * GFF)]
        K["sg"] = [P.sb([128, 512], F32, name=f"sg{i}") for i in range(2)]
    return K


def emit_consts(P, C, K, rot_d, bones_d):
    K["rot"] = P.sb([128, 128], BF16, name="rot_sb")
    K["bones"] = P.sb([128, 128], BF16, name="bones_sb")
    C.load_cast(K["rot"][:], "rot", rot_d, 128)
    C.load_cast(K["bones"][:], "bones", bones_d, 128)


def emit_inproj(P, C, K, hT, win_d, n_fm, kinds, qkg, cosT, sinT, outs, v_d, vw, v_oc0, tiles=TILES):
    w16 = K["win16"]
    zq = K["zq"]
    for oc in range(n_fm):
        sl = oc % 2
        C.load_cast(w16[sl][:].rearrange("p k m -> p (k m)"), f"win16_{sl}", win_d[oc].rearrange("p k m -> p (k m)"), 1024)
        kind = kinds[oc]
        od, odt = outs[oc]
        for ti, (t0, tn) in enumerate(tiles):
            pb = ti % 2
            pz = C.psum[pb]
            for k in range(KC):
                P.add("tensor", lambda e, pz=pz, k=k, sl=sl, t0=t0, tn=tn: e.matmul(pz[:, 0:tn], w16[sl][:, k, :], hT[:, k, t0:t0 + tn], start=(k == 0), stop=(k == KC - 1)),
                      [f"win16_{sl}", f"h{k}_{ti}"], [f"psb{pb}"])
            ob = K["ob32"][pb] if odt == F32 else K["ob16"][pb]
            obk = f"ob{'32' if odt == F32 else '16'}_{pb}"
            if kind == "u":
                P.add("scalar", lambda e, ob=ob, pz=pz, tn=tn: e.copy(ob[:, 0:tn], pz[:, 0:tn]), [f"psb{pb}"], [obk])
            else:
                gcol = qkg[:, 0:1] if kind == "q" else qkg[:, 1:2]
                z = zq[pb]
                zk = f"zq{pb}"
                sq = K["sq"][pb]
                sqk = f"sq{pb}"
                P.add("scalar", lambda e, z=z, pz=pz, tn=tn: e.copy(z[:, 0:tn], pz[:, 0:tn]), [f"psb{pb}"], [zk])
                P.add("scalar", lambda e, sq=sq, z=z, tn=tn: e.activation(sq[:, 0:tn], z[:, 0:tn], AF.Square), [zk], [sqk])
                ph = C.psum[2 + pb]
                P.add("tensor", lambda e, ph=ph, sq=sq, tn=tn: e.matmul(ph[:, 0:tn], K["bones"][:], sq[:, 0:tn], start=True, stop=True), [sqk, "bones"], [f"psb{2 + pb}"])
                rs = K["rstd"][pb]
                rsk = f"rstd{pb}"
                P.add("scalar", lambda e, rs=rs, ph=ph, tn=tn: e.activation(rs[:, 0:tn], ph[:, 0:tn], AF.Sqrt, bias=K["epscol"][:, 0:1], scale=1.0 / 64), [f"psb{2 + pb}", "epscol"], [rsk])
                P.add("vector", lambda e, rs=rs, tn=tn: e.reciprocal(rs[:, 0:tn], rs[:, 0:tn]), [rsk], [rsk])
                P.add("vector", lambda e, z=z, rs=rs, tn=tn, gcol=gcol: e.scalar_tensor_tensor(z[:, 0:tn], z[:, 0:tn], gcol, rs[:, 0:tn], ALU.mult, ALU.mult), [zk, rsk, "qkg"], [zk])
                zb = K["zb"][pb]
                zbk = f"zb{pb}"
                P.add("scalar", lambda e, zb=zb, z=z, tn=tn: e.copy(zb[:, 0:tn], z[:, 0:tn]), [zk], [zbk])
                pr = C.psum[4 + pb]
                P.add("tensor", lambda e, pr=pr, zb=zb, tn=tn: e.matmul(pr[:, 0:tn], K["rot"][:], zb[:, 0:tn], start=True, stop=True), [zbk, "rot"], [f"psb{4 + pb}"])
                t2 = K["ntmp"][pb]
                t2k = f"ntmp{pb}"
                cb_ = pb % len(K["cst"])
                cst, snt = K["cst"][cb_], K["snt"][cb_]
                C.load(cst[:, 0:tn], f"cst{cb_}", cosT[:, t0:t0 + tn])
                C.load(snt[:, 0:tn], f"snt{cb_}", sinT[:, t0:t0 + tn])
                P.add("vector", lambda e, t2=t2, pr=pr, snt=snt, tn=tn: e.tensor_tensor(t2[:, 0:tn], pr[:, 0:tn], snt[:, 0:tn], ALU.mult), [f"psb{4 + pb}", f"snt{cb_}"], [t2k])
                P.add("vector", lambda e, z=z, cst=cst, tn=tn: e.tensor_tensor(z[:, 0:tn], z[:, 0:tn], cst[:, 0:tn], ALU.mult), [zk, f"cst{cb_}"], [zk])
                P.add("vector", lambda e, ob=ob, z=z, t2=t2, tn=tn: e.tensor_tensor(ob[:, 0:tn], z[:, 0:tn], t2[:, 0:tn], ALU.add), [zk, t2k], [obk])
            P.dma(od(t0, tn) if callable(od) else od[:, t0:t0 + tn], ob[:, 0:tn], [obk], [], f"st_{obk}")
    nvc = vw // 128
    wv = K["wv16"]
    for i in range(nvc):
        C.load_cast(wv[:, :, i * 128:(i + 1) * 128], f"wv16_{i}", win_d[v_oc0 + i], 1024, view=lambda a: a.rearrange("p (k m) -> p k m", m=128))
    for tt in range(T // 128):
        pb = tt % 2
        pv = C.psum[6 + pb]
        ti = 0 if tt < 2 else 1 + (tt - 2) // 4
        for k in range(KC):
            P.add("tensor", lambda e, pv=pv, k=k, tt=tt: e.matmul(pv[:, 0:vw], hT[:, k, tt * 128:(tt + 1) * 128], wv[:, k, :], start=(k == 0), stop=(k == KC - 1)),
                  [f"wv16_{i}" for i in range(nvc)] + [f"h{k}_{ti}"], [f"psb{6 + pb}"])
        vb = K["vb"][pb]
        P.add("scalar", lambda e, vb=vb, pv=pv: e.copy(vb[:, 0:vw], pv[:, 0:vw]), [f"psb{6 + pb}"], ["vb0"])
        P.dma(v_d[tt], vb[:, 0:vw], ["vb0"], [], "st_vb0")


def phase_att(P, C, layer, Dr):
    dense = layer == 0
    n_qc, n_kv = (4, 2) if dense else (8, 4)
    n_heads = 2 * n_qc
    NK = 66 if dense else 26
    qT = P.sb([128, n_qc, T], BF16, name="q_sb")
    kd = P.sb([128, n_kv, NK * 128], BF16, name="k_sb")
    va = P.sb([128, n_kv, NK, 65], BF16, name="v_sb")
    q_loc, o_loc = Dr["q_loc"], Dr["o_loc"]
    for c in range(n_qc):
        C.load(qT[:, c, :], f"q{c}", q_loc[:, c, :])
    for kv in range(n_kv):
        P.add("vector", lambda e, kv=kv: e.memset(va[:, kv, :, 64:65], 1.0), [], [f"vone{kv}"])
    tkv = lambda s_: s_.rearrange("p (kt d) -> p kt d", d=64)
    tk = lambda ap: ap.rearrange("(kt p) d -> p kt d", p=128)

    kpieces, vpieces = {}, {}

    def kload(kv, half, dst0, src_rows, src_cols0, ncols, reads=()):
        pl = slice(64 * half, 64 * half + 64)
        for c0 in range(0, ncols, 1024):
            n = min(1024, ncols - c0)
            key = f"k{kv}_{half}_{dst0 + c0}"
            kpieces.setdefault((kv, half), []).append((dst0 + c0, n, key))
            C.load_cast(kd[pl, kv, dst0 + c0:dst0 + c0 + n], key, src_rows[:, src_cols0 + c0:src_cols0 + c0 + n], n, psl=pl, cast_eng="vector", reads=reads)

    def vload(kv, kt0, nkt, src, reads=()):
        for t0 in range(0, nkt, 16):
            n = min(16, nkt - t0)
            key = f"v{kv}_{kt0 + t0}"
            vpieces.setdefault(kv, []).append((kt0 + t0, n, key))
            C.load_cast(va[:, kv, kt0 + t0:kt0 + t0 + n, 0:64], key, tk(src[t0 * 128:(t0 + n) * 128, 64 * kv:64 * kv + 64]), n * 64, view=tkv, cast_eng="vector", reads=reads)

    def kkey(kv, half, kt):
        for (c0, n, key) in kpieces[(kv, half)]:
            if c0 <= kt * 128 < c0 + n:
                return key
        raise KeyError((kv, half, kt))

    def vkey(kv, kt):
        for (k0, n, key) in vpieces[kv]:
            if k0 <= kt < k0 + n:
                return key
        raise KeyError((kv, kt))

    if dense:
        kA_all, kB_all, vA_all, vB_all = Dr["kA_all"], Dr["kB_all"], Dr["vA_all"], Dr["vB_all"]
        for kv in range(n_kv):
            for half in range(2):
                kload(kv, half, 0, kA_all[64 * kv:64 * kv + 64], 0, TCTX)
                for r in range(4):
                    kload(kv, half, TCTX + TLAT * r, kA_all[r * 128 + 64 * kv:r * 128 + 64 * kv + 64], TCTX, 1024)
                    kload(kv, half, TCTX + TLAT * r + 1024, kB_all[r * 128 + 64 * kv:r * 128 + 64 * kv + 64], 0, 1024)
            vload(kv, 0, 2, vA_all[0:TCTX])
            for r in range(4):
                vload(kv, 2 + 16 * r, 8, vA_all[r * 1280 + TCTX:(r + 1) * 1280])
                vload(kv, 2 + 16 * r + 8, 8, vB_all[r * 1024:(r + 1) * 1024])
    else:
        k_loc, v_loc, ek_all, ev_all = Dr["k_loc"], Dr["v_loc"], Dr["ek_all"], Dr["ev_all"]
        for kv in range(n_kv):
            ks = slice(64 * (kv % 2), 64 * (kv % 2) + 64)
            kc = kv // 2
            for half in range(2):
                kload(kv, half, 0, k_loc[ks, kc], TCTX, TLAT)
                kload(kv, half, 24 * 128, k_loc[ks, kc], 0, TCTX)
                for r in range(4):
                    kload(kv, half, 16 * 128 + 256 * r, ek_all[r * 128 + 64 * (kv % 2):r * 128 + 64 * (kv % 2) + 64], kc * 256, 256, reads=["ek_all"])
            vload(kv, 0, 16, v_loc[TCTX:T])
            vload(kv, 24, 2, v_loc[0:TCTX])
            for r in range(4):
                vload(kv, 16 + 2 * r, 2, ev_all[r * 256:(r + 1) * 256], reads=["ev_all"])
    ones32 = P.sb([128, 64], F32, name="ones32")
    P.add("vector", lambda e: e.memset(ones32[:], 1.0), [], ["ones32"])
    es = P.sb([128, 16], F32, name="es")
    if not dense:
        C.load(es[:], "es", Dr["sink"])
        P.add("scalar", lambda e: e.activation(es[:], es[:], AF.Exp), ["es"], ["es"])
        wm = P.sb([128, 4, 6, 512], BF16, name="wm")
        C.load(wm[:], "wm", Dr["wmask"])
        em = P.sb([128, 2, 4, 512], BF16, name="em")
        C.load(em[:], "em", Dr["emask"])
    pT = [P.sb([128, 512], BF16, name=f"pT{i}") for i in range(3)]
    osb = [P.sb([64, 512], F32, name=f"osb{i}") for i in range(2)]
    rden = [P.sb([128, 512], F32, name=f"rden{i}") for i in range(2)]
    ob = [P.sb([64, 512], BF16, name=f"ob{i}") for i in range(2)]
    work = []
    if dense:
        work.append((0, 256, [(0, None), (1, None)]))
        for m in range(4):
            work.append((256 + 512 * m, 512, [(kt, None) for kt in range(66)]))
    else:
        for m in range(4):
            keys = [(4 * m + r - 1, wm[:, m, r, :]) for r in range(6) if 0 <= 4 * m + r - 1 <= 15]
            if m == 0:
                keys += [(16 + 2 * r + 1, em[:, 0, r, :]) for r in range(4)]
            if m == 3:
                keys += [(16 + 2 * r, em[:, 1, r, :]) for r in range(4)]
            keys += [(24, None), (25, None)]
            work.append((256 + 512 * m, 512, keys))
    it = 0
    si = 0
    for (q0, qn, keys) in work:
        for h in range(n_heads):
            c, half, kv = h // 2, h % 2, h // 4
            pl = slice(64 * half, 64 * half + 64)
            ob_i = it % 2
            it += 1
            po = C.psum[6 + ob_i]
            pok = f"psb{6 + ob_i}"
            LA = 2
            nk = len(keys)
            slots = []
            for n in range(nk + LA):
                if n < nk:
                    kt, mk = keys[n]
                    sb_i = si % 3
                    si += 1
                    slots.append(sb_i)
                    pS = C.psum[sb_i]
                    P.add("tensor", lambda e, pS=pS, kv=kv, kt=kt, pl=pl, c=c, q0=q0, qn=qn: e.matmul(pS[:, 0:qn], kd[pl, kv, kt * 128:(kt + 1) * 128], qT[pl, c, q0:q0 + qn], start=True, stop=True),
                          [kkey(kv, half, kt), f"q{c}"], [f"psb{sb_i}"])
                    pt = pT[sb_i]
                    P.add("scalar", lambda e, pt=pt, pS=pS, qn=qn: e.activation(pt[:, 0:qn], pS[:, 0:qn], AF.Exp, scale=0.125), [f"psb{sb_i}"], [f"pT{sb_i}"])
                    if mk is not None:
                        P.add("vector", lambda e, pt=pt, mk=mk, qn=qn: e.tensor_tensor(pt[:, 0:qn], pt[:, 0:qn], mk[:, 0:qn], ALU.mult), [f"pT{sb_i}", "wm", "em"], [f"pT{sb_i}"])
                if n >= LA:
                    m_ = n - LA
                    kt2 = keys[m_][0]
                    sb2 = slots[m_]
                    pt2 = pT[sb2]
                    P.add("tensor", lambda e, po=po, pt2=pt2, kv=kv, kt2=kt2, qn=qn, m_=m_, nk=nk: e.matmul(po[0:65, 0:qn], va[:, kv, kt2, :], pt2[:, 0:qn], start=(m_ == 0), stop=(m_ == nk - 1)),
                          [vkey(kv, kt2), f"vone{kv}", f"pT{sb2}"], [pok])
            rd = rden[ob_i]
            rdk = f"rden{ob_i}"
            osx = osb[ob_i]
            P.add("scalar", lambda e, osx=osx, po=po, qn=qn: e.copy(osx[:, 0:qn], po[0:64, 0:qn]), [pok], [f"osb{ob_i}"])
            if dense:
                P.add("vector", lambda e, rd=rd, po=po, qn=qn: e.reciprocal(rd[64:65, 0:qn], po[64:65, 0:qn]), [pok], [rdk])
            else:
                P.add("vector", lambda e, rd=rd, po=po, qn=qn, h=h: e.tensor_scalar(rd[64:65, 0:qn], po[64:65, 0:qn], es[64:65, h:h + 1], None, ALU.add), [pok, "es"], [rdk])
                P.add("vector", lambda e, rd=rd, qn=qn: e.reciprocal(rd[64:65, 0:qn], rd[64:65, 0:qn]), [rdk], [rdk])
            pb = C.psum[3 + ob_i]
            P.add("tensor", lambda e, pb=pb, rd=rd, qn=qn: e.matmul(pb[0:64, 0:qn], ones32[64:65, 0:64], rd[64:65, 0:qn], start=True, stop=True), [rdk, "ones32"], [f"psb{3 + ob_i}"])
            o16 = ob[ob_i]
            P.add("vector", lambda e, o16=o16, osx=osx, pb=pb, qn=qn: e.tensor_tensor(o16[:, 0:qn], osx[:, 0:qn], pb[0:64, 0:qn], ALU.mult), [f"osb{ob_i}", f"psb{3 + ob_i}"], [f"ob{ob_i}"])
            P.dma(o_loc[64 * half:64 * half + 64, c, q0:q0 + qn], o16[:, 0:qn], [f"ob{ob_i}"], [], f"st_ob{ob_i}")


NS5 = TCTX + 4 * TLAT
TWO_PI = 2.0 * math.pi


def phase_s5(P, C, Dr):
    uA_all, uB_all, B_d, C_d, pc_d = Dr["uA_all"], Dr["uB_all"], Dr["Bl"], Dr["Cl"], Dr["pcols"]
    ic_d, ij_d, oh_d = Dr["iota_c"], Dr["iota_j"], Dr["onehot"]
    y_rs_in, yc_loc = Dr["y_rs_in"], Dr["yc_loc"]
    sm = lambda name, n=8, dtype=F32: P.sb([128, n], dtype, name=name)
    pc = P.sb([128, 3, 8], F32, name="pc")
    C.load(pc[:].rearrange("p a b -> p (a b)"), "pc", pc_d.rearrange("p a b -> p (a b)"))
    iota_c = P.sb([128, 2, 66], F32, name="iota_c_sb")
    iota_j = P.sb([128, 2, 128], F32, name="iota_j_sb")
    oh = sm("onehot_sb", 4)
    C.load(iota_c[:], "iota_c", ic_d)
    C.load(iota_j[:], "iota_j", ij_d)
    C.load(oh[:], "oh", oh_d)
    cnt = [0]

    def V(fn, reads, writes, eng="vector"):
        P.add(eng, fn, reads, writes)

    fr_i = P.sb([128, 128], I32, name="fr_i")
    fr_f = P.sb([128, 128], F32, name="fr_f")
    sc_t = P.sb([128, 128], F32, name="sc_t")
    sc_u = P.sb([128, 128], F32, name="sc_u")

    def fracp(dst, src, n, key_dst, key_src):
        V(lambda e: e.tensor_copy(fr_i[:, 0:n], src), [key_src], ["fr_i"])
        V(lambda e: e.tensor_copy(fr_f[:, 0:n], fr_i[:, 0:n]), ["fr_i"], ["fr_f"])
        V(lambda e: e.tensor_tensor(dst, src, fr_f[:, 0:n], ALU.subtract), [key_src, "fr_f"], [key_dst])

    def sincos(sin_dst, cos_dst, ph, n, key_s, key_c, key_ph):
        P.add("scalar", lambda e: e.activation(sin_dst, ph, AF.Sin, scale=TWO_PI - 1e-5), [key_ph], [key_s])
        V(lambda e: e.tensor_scalar(sc_t[:, 0:n], ph, 0.25, None, ALU.add), [key_ph], ["sc_t"])
        fracp(sc_u[:, 0:n], sc_t[:, 0:n], n, "sc_u", "sc_t")
        P.add("scalar", lambda e: e.activation(cos_dst, sc_u[:, 0:n], AF.Sin, scale=TWO_PI - 1e-5), ["sc_u"], [key_c])

    lr, li, dtv, a, th, rr, f = sm("lr"), sm("li"), sm("dtv"), sm("a_ln"), sm("th"), sm("rr"), sm("f")
    V(lambda e: e.tensor_scalar(lr[:], pc[:, 0, :], -1e-4, None, ALU.min), ["pc"], ["lr"])
    V(lambda e: e.tensor_copy(li[:], pc[:, 1, :]), ["pc"], ["li"])
    P.add("scalar", lambda e: e.activation(dtv[:], pc[:, 2, :], AF.Exp), ["pc"], ["dtv"])
    V(lambda e: e.tensor_tensor(a[:], lr[:], dtv[:], ALU.mult), ["lr", "dtv"], ["a"])
    V(lambda e: e.tensor_tensor(th[:], li[:], dtv[:], ALU.mult), ["li", "dtv"], ["th"])
    P.add("scalar", lambda e: e.activation(rr[:], a[:], AF.Exp), ["a"], ["rr"])
    V(lambda e: e.tensor_scalar(f[:], th[:], 1.0 / TWO_PI, None, ALU.mult), ["th"], ["f"])
    f0, sth, cth = sm("f0"), sm("sth"), sm("cth")
    fracp(f0[:], f[:], 8, "f0", "f")
    sincos(sth[:], cth[:], f0[:], 8, "sth", "cth", "f0")
    nr, ni, den, t8a, t8b, kr, ki, nkr, nki = [sm(n) for n in ("nr", "ni", "den", "t8a", "t8b", "kr", "ki", "nkr", "nki")]
    V(lambda e: e.tensor_tensor(nr[:], rr[:], cth[:], ALU.mult), ["rr", "cth"], ["nr"])
    V(lambda e: e.tensor_scalar(nr[:], nr[:], -1.0, None, ALU.add), ["nr"], ["nr"])
    V(lambda e: e.tensor_tensor(ni[:], rr[:], sth[:], ALU.mult), ["rr", "sth"], ["ni"])
    V(lambda e: e.tensor_tensor(den[:], lr[:], lr[:], ALU.mult), ["lr"], ["den"])
    V(lambda e: e.tensor_tensor(t8a[:], li[:], li[:], ALU.mult), ["li"], ["t8a"])
    V(lambda e: e.tensor_tensor(den[:], den[:], t8a[:], ALU.add), ["den", "t8a"], ["den"])
    V(lambda e: e.reciprocal(den[:], den[:]), ["den"], ["den"])
    V(lambda e: e.tensor_tensor(t8a[:], nr[:], lr[:], ALU.mult), ["nr", "lr"], ["t8a"])
    V(lambda e: e.tensor_tensor(t8b[:], ni[:], li[:], ALU.mult), ["ni", "li"], ["t8b"])
    V(lambda e: e.tensor_tensor(kr[:], t8a[:], t8b[:], ALU.add), ["t8a", "t8b"], ["kr"])
    V(lambda e: e.tensor_tensor(kr[:], kr[:], den[:], ALU.mult), ["kr", "den"], ["kr"])
    V(lambda e: e.tensor_tensor(t8a[:], ni[:], lr[:], ALU.mult), ["ni", "lr"], ["t8a"])
    V(lambda e: e.tensor_tensor(t8b[:], nr[:], li[:], ALU.mult), ["nr", "li"], ["t8b"])
    V(lambda e: e.tensor_tensor(ki[:], t8a[:], t8b[:], ALU.subtract), ["t8a", "t8b"], ["ki"])
    V(lambda e: e.tensor_tensor(ki[:], ki[:], den[:], ALU.mult), ["ki", "den"], ["ki"])
    V(lambda e: e.tensor_scalar(nkr[:], kr[:], -1.0, None, ALU.mult), ["kr"], ["nkr"])
    V(lambda e: e.tensor_scalar(nki[:], ki[:], -1.0, None, ALU.mult), ["ki"], ["nki"])
    a128, a128f = sm("a128"), sm("a128f")
    V(lambda e: e.tensor_scalar(a128[:], f0[:], 128.0, None, ALU.mult), ["f0"], ["a128"])
    fracp(a128f[:], a128[:], 8, "a128f", "a128")
    B16 = P.sb([128, 16, 4, 128], BF16, name="B16")
    CR = P.sb([128, 8, 128], BF16, name="CR16")
    CI = P.sb([128, 8, 128], BF16, name="CI16")
    c32 = [P.sb([128, 2, 128], F32, name=f"c32_{i}") for i in range(2)]
    ctmp = [P.sb([128, 128], F32, name=f"ctmp{i}") for i in range(2)]

    def wsetup(d, gp):
        q = d * 4 + gp
        for ri in range(2):
            C.load_cast(B16[:, q * 2 + ri, :, :], f"B16_{q}_{ri}", B_d[d, gp, ri].rearrange("c p m -> p c m"), 512,
                        view=lambda s_: s_.rearrange("p (c m) -> p c m", m=128))
        cb = c32[q % 2]
        ck = f"c32_{q % 2}"
        P.dma(cb[:], C_d[d, gp].rearrange("r k m -> k r m"), [], [ck], f"ld_c32_{q % 2}")
        tm = ctmp[q % 2]
        tk = f"ctmp{q % 2}"
        V(lambda e: e.tensor_scalar(tm[:], cb[:, 0, :], kr[:, q:q + 1], None, ALU.mult), [ck, "kr"], [tk])
        V(lambda e: e.scalar_tensor_tensor(CR[:, q, :], cb[:, 1, :], nki[:, q:q + 1], tm[:], ALU.mult, ALU.add), [ck, "nki", tk], [f"CR_{q}"])
        V(lambda e: e.tensor_scalar(tm[:], cb[:, 1, :], nkr[:, q:q + 1], None, ALU.mult), [ck, "nkr", f"CR_{q}"], [tk])
        V(lambda e: e.scalar_tensor_tensor(CI[:, q, :], cb[:, 0, :], nki[:, q:q + 1], tm[:], ALU.mult, ALU.add), [ck, "nki", tk], [f"CI_{q}"])

    for d in range(2):
        for gp in range(4):
            wsetup(d, gp)
    sinC = P.sb([128, 8, 66], F32, name="sinC")
    cosC = P.sb([128, 8, 66], F32, name="cosC")
    sinJ = P.sb([128, 8, 128], F32, name="sinJ")
    cosJ = P.sb([128, 8, 128], F32, name="cosJ")

    phA = P.sb([128, 128], F32, name="phA")
    phB = P.sb([128, 128], F32, name="phB")

    def tsetup(q):
        d = q // 4
        V(lambda e: e.tensor_scalar(phA[:, 0:66], iota_c[:, d, :], a128f[:, q:q + 1], None, ALU.mult), ["iota_c", "a128f"], ["phA"])
        fracp(phB[:, 0:66], phA[:, 0:66], 66, "phB", "phA")
        sincos(sinC[:, q, :], cosC[:, q, :], phB[:, 0:66], 66, f"sinC{q}", f"cosC{q}", "phB")
        V(lambda e: e.tensor_scalar(phA[:, 0:128], iota_j[:, d, :], f0[:, q:q + 1], None, ALU.mult), ["iota_j", "f0"], ["phA"])
        fracp(phB[:, 0:128], phA[:, 0:128], 128, "phB", "phA")
        sincos(sinJ[:, q, :], cosJ[:, q, :], phB[:, 0:128], 128, f"sinJ{q}", f"cosJ{q}", "phB")

    for q in range(8):
        tsetup(q)
    NB = 512
    big = lambda name, dtype=F32: P.sb([128, NB], dtype, name=name)
    u16 = [P.sb([128, 4, NB], BF16, name=f"u16_{i}") for i in range(2)]
    St = [big(f"St{i}") for i in range(2)]
    Ct = [big(f"Ct{i}") for i in range(2)]
    tB = big("tB")
    tA2 = [big("tA0"), big("tA1")]
    brs2, bis2 = [big("brs0"), big("brs1")], [big("bis0"), big("bis1")]
    btr2, bti2 = [big("btr0"), big("btr1")], [big("bti0"), big("bti1")]
    gr2, gi2 = [big("gr0"), big("gr1")], [big("gi0"), big("gi1")]
    hr16 = [big(f"hr16_{i}", BF16) for i in range(2)]
    hi16 = [big(f"hi16_{i}", BF16) for i in range(2)]
    carry = P.sb([128, 16], F32, name="carry")
    yb = [P.sb([128, 4, NB], F32, name=f"yb{i}") for i in range(2)]

    def rv(t, n):
        return bass.AP(t, n - 1, [[NB, 128], [-1, n]])

    def gp_body(d, first, lay0, sn, ub, gp, tb, hb, seg_n):
        q = d * 4 + gp
        c0, ncn = lay0 // 128, sn // 128
        nst = (sn + 511) // 512
        S, Cc = St[tb], Ct[tb]
        brs, bis = brs2[tb], bis2[tb]
        ch = gp % 2
        tA, btr, bti, gr, gi = tA2[ch], btr2[ch], bti2[ch], gr2[ch], gi2[ch]
        kA_, kbr, kbi, kgr, kgi = f"tA{ch}", f"btr{ch}", f"bti{ch}", f"gr{ch}", f"gi{ch}"
        yb_ = 4 + (seg_n % 2)
        W = slice(0, sn)
        v3 = lambda t: t[:, 0:sn].rearrange("p (c j) -> p c j", j=128)
        cC = lambda t: t[:, q, c0:c0 + ncn].unsqueeze(2).broadcast_to([128, ncn, 128])
        cJ = lambda t: t[:, q, :].unsqueeze(1).broadcast_to([128, ncn, 128])
        G = "vector"
        P.add(G, lambda e, a=cC(sinC), b=cJ(cosJ), v=v3(S): e.tensor_tensor(v, a, b, ALU.mult), [f"sinC{q}", f"cosJ{q}"], [f"St{tb}"])
        P.add(G, lambda e, a=cC(cosC), b=cJ(sinJ), v=v3(tB): e.tensor_tensor(v, a, b, ALU.mult), [f"cosC{q}", f"sinJ{q}"], ["tBg"])
        P.add(G, lambda e: e.tensor_tensor(S[:, W], S[:, W], tB[:, W], ALU.add), [f"St{tb}", "tBg"], [f"St{tb}"])
        P.add(G, lambda e, a=cC(cosC), b=cJ(cosJ), v=v3(Cc): e.tensor_tensor(v, a, b, ALU.mult), [f"cosC{q}", f"cosJ{q}"], [f"Ct{tb}"])
        P.add(G, lambda e, a=cC(sinC), b=cJ(sinJ), v=v3(tB): e.tensor_tensor(v, a, b, ALU.mult), [f"sinC{q}", f"sinJ{q}"], ["tBg"])
        P.add(G, lambda e: e.tensor_tensor(Cc[:, W], Cc[:, W], tB[:, W], ALU.subtract), [f"Ct{tb}", "tBg"], [f"Ct{tb}"])
        for st in range(nst):
            n0 = st * 512
            nn = min(512, sn - n0)
            for ri, (dst, dk) in enumerate(((brs, f"brs{tb}_"), (bis, f"bis{tb}_"))):
                pbi = ri * 2 + tb
                pb = C.psum[pbi]
                for c in range(4):
                    P.add("tensor", lambda e, pb=pb, ri=ri, n0=n0, nn=nn, c=c: e.matmul(pb[:, 0:nn], B16[:, q * 2 + ri, c, :], u16[ub][:, c, n0:n0 + nn], start=(c == 0), stop=(c == 3)),
                          [f"B16_{q}_{ri}", f"u16_{ub}_{c}"], [f"psb{pbi}"])
                P.add("scalar", lambda e, pb=pb, dst=dst, n0=n0, nn=nn: e.copy(dst[:, n0:n0 + nn], pb[:, 0:nn]), [f"psb{pbi}"], [f"{dk}{st}"])
        assert nst == 1
        bk = [f"brs{tb}_{st}" for st in range(nst)]
        ik = [f"bis{tb}_{st}" for st in range(nst)]
        V(lambda e: e.tensor_tensor(tA[:, W], Cc[:, W], brs[:, W], ALU.mult), [f"Ct{tb}"] + bk, [kA_])
        yield
        V(lambda e: e.tensor_tensor(btr[:, W], S[:, W], bis[:, W], ALU.mult), [f"St{tb}"] + ik, [kbr])
        yield
        V(lambda e: e.tensor_tensor(btr[:, W], btr[:, W], tA[:, W], ALU.add), [kbr, kA_], [kbr])
        yield
        V(lambda e: e.tensor_tensor(tA[:, W], Cc[:, W], bis[:, W], ALU.mult), [f"Ct{tb}"] + ik, [kA_])
        yield
        V(lambda e: e.tensor_tensor(bti[:, W], S[:, W], brs[:, W], ALU.mult), [f"St{tb}"] + bk, [kbi])
        yield
        V(lambda e: e.tensor_tensor(bti[:, W], tA[:, W], bti[:, W], ALU.subtract), [kbi, kA_], [kbi])
        yield
        rcol = rr[:, q:q + 1].broadcast_to([128, sn])
        for (g_, bt_, gk, btk, ci) in ((gr, btr, kgr, kbr, 2 * q), (gi, bti, kgi, kbi, 2 * q + 1)):
            init = 0.0 if first else carry[:, ci:ci + 1]
            if d == 0:
                V(lambda e, g_=g_, bt_=bt_, init=init: e.tensor_tensor_scan(g_[:, W], rcol, bt_[:, W], init, ALU.mult, ALU.add), [btk, "rr", f"carry{ci}"], [gk])
                yield
                P.add("scalar", lambda e, g_=g_, ci=ci: e.copy(carry[:, ci:ci + 1], g_[:, sn - 1:sn]), [gk], [f"carry{ci}"])
            else:
                V(lambda e, g_=g_, bt_=bt_, init=init: e.tensor_tensor_scan(rv(g_, sn), rcol, rv(bt_, sn), init, ALU.mult, ALU.add), [btk, "rr", f"carry{ci}"], [gk])
                yield
                P.add("scalar", lambda e, g_=g_, ci=ci: e.copy(carry[:, ci:ci + 1], g_[:, 0:1]), [gk], [f"carry{ci}"])
        hr_, hi_ = hr16[hb], hi16[hb]
        V(lambda e: e.tensor_tensor(tA[:, W], Cc[:, W], gr[:, W], ALU.mult), [f"Ct{tb}", kgr], [kA_])
        yield
        V(lambda e: e.tensor_tensor(btr[:, W], S[:, W], gi[:, W], ALU.mult), [f"St{tb}", kgi], [kbr])
        yield
        V(lambda e: e.tensor_tensor(hr_[:, W], tA[:, W], btr[:, W], ALU.subtract), [kA_, kbr], [f"hr16_{hb}"])
        yield
        V(lambda e: e.tensor_tensor(tA[:, W], S[:, W], gr[:, W], ALU.mult), [f"St{tb}", kgr], [kA_])
        yield
        V(lambda e: e.tensor_tensor(bti[:, W], Cc[:, W], gi[:, W], ALU.mult), [f"Ct{tb}", kgi], [kbi])
        yield
        V(lambda e: e.tensor_tensor(hi_[:, W], tA[:, W], bti[:, W], ALU.add), [kA_, kbi], [f"hi16_{hb}"])
        yield
        for st in range(nst):
            n0 = st * 512
            nn = min(512, sn - n0)
            py = C.psum[yb_]
            P.add("tensor", lambda e, py=py, n0=n0, nn=nn: e.matmul(py[:, 0:nn], CR[:, q, :], hr_[:, n0:n0 + nn], start=(gp == 0), stop=False),
                  [f"CR_{q}", f"hr16_{hb}"], [f"psb{yb_}"])
            P.add("tensor", lambda e, py=py, n0=n0, nn=nn: e.matmul(py[:, 0:nn], CI[:, q, :], hi_[:, n0:n0 + nn], start=False, stop=(gp == 3)),
                  [f"CI_{q}", f"hi16_{hb}"], [f"psb{yb_}"])

    def seg_body(d, si, first, kind, s, ub, it0, seg_n):
        if kind == "ctx":
            sn, r, t0 = 256, 0, 0
            lay0 = 0 if d == 0 else 4 * TLAT
        else:
            sn, r, t0 = 512, s // 4, TCTX + 512 * (s % 4)
            lay0 = (TCTX + 512 * s) if d == 0 else 512 * s
        for c in range(4):
            usrc = uA_all[c][r * 128:(r + 1) * 128, t0:t0 + sn] if t0 < 1280 else uB_all[c][r * 128:(r + 1) * 128, t0 - 1280:t0 - 1280 + sn]
            ukey = f"uA_all{c}" if t0 < 1280 else f"uB_all{c}"
            C.load_cast(u16[ub][:, c, 0:sn], f"u16_{ub}_{c}", usrc, sn, cast_eng="scalar", reads=[ukey])
        nst = (sn + 511) // 512
        it = it0
        for gp0 in (0, 2):
            gens = []
            for gp in (gp0, gp0 + 1):
                tb = it % 2
                it += 1
                gens.append(gp_body(d, first, lay0, sn, ub, gp, tb, it % 2, seg_n))
            while gens:
                for g_ in list(gens):
                    try:
                        next(g_)
                    except StopIteration:
                        gens.remove(g_)
        ybuf = yb[ub]
        ybk = 4 + (seg_n % 2)
        if kind == "ctx":
            P.add("scalar", lambda e: e.copy(ybuf[:, 0, 0:256], C.psum[ybk][:, 0:256]), [f"psb{ybk}"], [f"yb{ub}"])
            P.dma(yc_loc[:, d * 256:(d + 1) * 256], ybuf[:, 0, 0:256], [f"yb{ub}"], [], f"st_yb{ub}")
        else:
            for c in range(4):
                P.add("scalar", lambda e, c=c: e.activation(ybuf[:, c, 0:512], C.psum[ybk][:, 0:512], AF.Identity, scale=oh[:, c:c + 1]),
                      [f"psb{ybk}", "oh"], [f"yb{ub}"])
            for c in range(4):
                P.dma(y_rs_in[d][c][(s // 4) * 128:(s // 4 + 1) * 128, 512 * (s % 4):512 * (s % 4) + 512], ybuf[:, c, 0:512], [f"yb{ub}"], [], f"st_yb{ub}_{c}")
        return it

    it = 0
    n = 0
    for d in range(2):
        order = [("ctx", 0)] + ([("lat", s) for s in range(16)] if d == 0 else [("lat", s) for s in range(15, -1, -1)])
        for si, (kind, s) in enumerate(order):
            it = seg_body(d, si, si == 0, kind, s, n % 2, it, n)
            n += 1


def phase_po(P, C, layer, xT, cols, Dr, tiles=TILES):
    l0 = layer == 0
    o_loc, wo_d = Dr["o_loc"], Dr["wo"][layer]
    wo16 = P.sb([128, 8, 1024], BF16, name="wo16")
    for k in range(8):
        C.load_cast(wo16[:, k, :], f"wo16_{k}", wo_d[k], 1024, cast_eng="vector")
    nch = 4 if l0 else 8
    ot = [P.sb([128, nch, 512], BF16, name=f"ot{i}") for i in range(2)]
    if l0:
        uA, uB, y_rs_out, yc_all = Dr["uA"], Dr["uB"], Dr["y_rs_out"], Dr["yc_all"]
        g0 = P.sb([128, 4, 512], BF16, name="glu0_sb")
        g1 = P.sb([128, 4, 512], BF16, name="glu1_sb")
        for k in range(4):
            C.load_cast(g0[:, k, :], f"g0_{k}", Dr["glu0"][k], 512, cast_eng="vector")
            C.load_cast(g1[:, k, :], f"g1_{k}", Dr["glu1"][k], 512, cast_eng="vector")
        dcol = P.sb([128, 4], F32, name="dcol_sb")
        C.load(dcol[:], "dcol", Dr["dcol"])
        ub = [P.sb([128, 512], F32, name=f"ub{i}") for i in range(2)]
        yfb = [P.sb([128, 512], F32, name=f"yfb{i}") for i in range(2)]
        yrb = [P.sb([128, 512], F32, name=f"yrb{i}") for i in range(2)]
        t1 = [P.sb([128, 512], F32, name=f"t1_{i}") for i in range(2)]
        gt = [P.sb([128, 4, 512], BF16, name=f"gt{i}") for i in range(2)]
        glt = [P.sb([128, 4, 512], BF16, name=f"glt{i}") for i in range(2)]
        sgb = [P.sb([128, 512], F32, name=f"sgb{i}") for i in range(2)]
    it = 0
    for ti, (t0, tn) in enumerate(tiles):
        v = "c" if t0 < TCTX else "x"
        tb = ti % 2
        P.dma(ot[tb][:, :, 0:tn], o_loc[:, 0:nch, t0:t0 + tn], [], [f"ot{tb}"], f"ld_ot{tb}")
        if l0:
            for c in range(4):
                b = it % 2
                it += 1
                if t0 < TCTX:
                    yf_src = yc_all[c * 128:(c + 1) * 128, 0:256]
                    yr_src = yc_all[c * 128:(c + 1) * 128, 256:512]
                else:
                    yf_src = y_rs_out[0][c][:, t0 - TCTX:t0 - TCTX + tn]
                    yr_src = y_rs_out[1][c][:, t0 - TCTX:t0 - TCTX + tn]
                u_src = uA[c][:, t0:t0 + tn] if t0 < 1280 else uB[c][:, t0 - 1280:t0 - 1280 + tn]
                P.dma(ub[b][:, 0:tn], u_src, [], [f"ub{b}"], f"ld_ub{b}")
                P.dma(yfb[b][:, 0:tn], yf_src, [], [f"yfb{b}"], f"ld_yfb{b}")
                P.dma(yrb[b][:, 0:tn], yr_src, [], [f"yrb{b}"], f"ld_yrb{b}")
                y, u_, yr_, tt = yfb[b], ub[b], yrb[b], t1[b]
                P.add("vector", lambda e, y=y, yr_=yr_, tn=tn: e.tensor_tensor(y[:, 0:tn], y[:, 0:tn], yr_[:, 0:tn], ALU.add), [f"yfb{b}", f"yrb{b}"], [f"yfb{b}"])
                P.add("vector", lambda e, y=y, u_=u_, c=c, tn=tn: e.scalar_tensor_tensor(y[:, 0:tn], u_[:, 0:tn], dcol[:, c:c + 1], y[:, 0:tn], ALU.mult, ALU.add),
                      [f"yfb{b}", f"ub{b}", "dcol"], [f"yfb{b}"])
                P.add("scalar", lambda e, tt=tt, y=y, tn=tn: e.activation(tt[:, 0:tn], y[:, 0:tn], AF.Square), [f"yfb{b}"], [f"t1_{b}"])
                P.add("vector", lambda e, tt=tt, tn=tn: e.tensor_scalar(tt[:, 0:tn], tt[:, 0:tn], 0.044715, 1.0, ALU.mult, ALU.add), [f"t1_{b}"], [f"t1_{b}"])
                P.add("vector", lambda e, tt=tt, y=y, tn=tn: e.tensor_tensor(tt[:, 0:tn], tt[:, 0:tn], y[:, 0:tn], ALU.mult), [f"t1_{b}", f"yfb{b}"], [f"t1_{b}"])
                P.add("scalar", lambda e, tt=tt, tn=tn: e.activation(tt[:, 0:tn], tt[:, 0:tn], AF.Sigmoid, scale=1.5957691216), [f"t1_{b}"], [f"t1_{b}"])
                P.add("vector", lambda e, tt=tt, y=y, c=c, tb=tb, tn=tn: e.tensor_tensor(gt[tb][:, c, 0:tn], tt[:, 0:tn], y[:, 0:tn], ALU.mult), [f"t1_{b}", f"yfb{b}"], [f"gt{tb}_{c}"])
            for oc in range(4):
                pb = oc % 2
                pa, pg = C.psum[pb], C.psum[2 + pb]
                for k in range(4):
                    P.add("tensor", lambda e, pa=pa, k=k, oc=oc, tb=tb, tn=tn: e.matmul(pa[:, 0:tn], g0[:, k, oc * 128:(oc + 1) * 128], gt[tb][:, k, 0:tn], start=(k == 0), stop=(k == 3)),
                          [f"g0_{k}", f"gt{tb}_{k}"], [f"psb{pb}"])
                for k in range(4):
                    P.add("tensor", lambda e, pg=pg, k=k, oc=oc, tb=tb, tn=tn: e.matmul(pg[:, 0:tn], g1[:, k, oc * 128:(oc + 1) * 128], gt[tb][:, k, 0:tn], start=(k == 0), stop=(k == 3)),
                          [f"g1_{k}", f"gt{tb}_{k}"], [f"psb{2 + pb}"])
                sg = sgb[pb]
                P.add("scalar", lambda e, sg=sg, pg=pg, tn=tn: e.activation(sg[:, 0:tn], pg[:, 0:tn], AF.Sigmoid), [f"psb{2 + pb}"], [f"sgb{pb}"])
                P.add("vector", lambda e, sg=sg, pa=pa, oc=oc, tb=tb, tn=tn: e.tensor_tensor(glt[tb][:, oc, 0:tn], sg[:, 0:tn], pa[:, 0:tn], ALU.mult), [f"sgb{pb}", f"psb{pb}"], [f"glt{tb}_{oc}"])
        for oc in range(8):
            pb = 4 + oc % 2
            po = C.psum[pb]
            srcs = ([(glt[tb][:, k, 0:tn], f"glt{tb}_{k}", k) for k in range(4)] if l0 else []) + \
                   [(ot[tb][:, k, 0:tn], f"ot{tb}", (4 + k) if l0 else k) for k in range(nch)]
            for n, (rhs, rk, kk) in enumerate(srcs):
                P.add("tensor", lambda e, po=po, rhs=rhs, kk=kk, oc=oc, tn=tn, n=n, ns=len(srcs): e.matmul(po[:, 0:tn], wo16[:, kk, oc * 128:(oc + 1) * 128], rhs, start=(n == 0), stop=(n == ns - 1)),
                      [f"wo16_{kk}", rk], [f"psb{pb}"])
            P.add("vector", lambda e, po=po, oc=oc, t0=t0, tn=tn, v=v: e.scalar_tensor_tensor(xT[:, oc, t0:t0 + tn], po[:, 0:tn], colsel(cols, v, "gate", 1, oc),
                                                                                             xT[:, oc, t0:t0 + tn], ALU.mult, ALU.add),
                  [f"psb{pb}", f"gate_{v}", f"x{oc}_{ti}"], [f"x{oc}_{ti}"])


GROUPS = [[0, 1, 2, 3], [4, 5, 6, 7]]


def build_fused(stop=999):
    nc = bass.Bass("TRN2", target_bir_lowering=False)
    ext = lambda name, shape, dtype=F32: nc.dram_tensor(name, list(shape), dtype, kind="ExternalInput").ap()
    scr = lambda name, shape, dtype=F32: nc.dram_tensor(name, list(shape), dtype).ap()
    xT_d = ext("xT", [128, 8, T])
    cT_d = ext("cT", [128, 16])
    modw_d = ext("modw", [36, 128, 8, 128])
    modb_d = ext("modb", [128, 36])
    normg_d = ext("normg", [128, 2, 24])
    wg_d = ext("wg", [4, FC, 128, 8, 128])
    wu_d = ext("wu", [4, FC, 128, 8, 128])
    wd_d = ext("wd", [4, FC, 128, 1024])
    win_d = [ext("win0", [10, 128, 8, 128]), ext("win1", [12, 128, 8, 128])]
    qkg_d = ext("qkg", [128, 2, 2])
    cos_d, sin_d = ext("cosT", [128, T]), ext("sinT", [128, T])
    rot_d, bones_d = ext("rot", [128, 128]), ext("bones", [128, 128])
    Dr = dict(
        Bl=ext("Bl", [2, 4, 2, 4, 128, 128]), Cl=ext("Cl", [2, 4, 2, 128, 128]), pcols=ext("pcols", [128, 3, 8]),
        iota_c=ext("iota_c", [128, 2, 66]), iota_j=ext("iota_j", [128, 2, 128]), onehot=ext("onehot", [128, 4]),
        dcol=ext("dcol", [128, 4]), glu0=ext("glu0", [4, 128, 512]), glu1=ext("glu1", [4, 128, 512]),
        wo=ext("wo", [2, 8, 128, 1024]), sink=ext("sink", [128, 16]),
        wmask=ext("wmask", [128, 4, 6, 512], BF16), emask=ext("emask", [128, 2, 4, 512], BF16),
    )
    out_d = nc.dram_tensor("outT", [128, 8, TLAT], F32, kind="ExternalOutput").ap()
    Dr.update(
        mod_loc=scr("mod_loc", [128, 72]), mod_all=scr("mod_all", [512, 72]),
        uA=[scr(f"uA{c}", [128, 1280]) for c in range(4)], uB=[scr(f"uB{c}", [128, 1024]) for c in range(4)],
        uA_all=[scr(f"uA_all{c}", [512, 1280]) for c in range(4)], uB_all=[scr(f"uB_all{c}", [512, 1024]) for c in range(4)],
        q_loc=scr("q_loc", [128, 8, T], BF16),
        k_loc=scr("k_loc", [128, 2, T]), kA=scr("kA", [128, 1280]), kB=scr("kB", [128, 1024]),
        kA_all=scr("kA_all", [512, 1280]), kB_all=scr("kB_all", [512, 1024]),
        v_loc=scr("v_loc", [T, 256]), vA=scr("vA", [1280, 128]), vB=scr("vB", [1024, 128]),
        vA_all=scr("vA_all", [5120, 128]), vB_all=scr("vB_all", [4096, 128]),
        y_rs_in=[[scr(f"y_rs_in{d}{c}", [512, TLAT]) for c in range(4)] for d in range(2)],
        y_rs_out=[[scr(f"y_rs_out{d}{c}", [128, TLAT]) for c in range(4)] for d in range(2)],
        yc_loc=scr("yc_loc", [128, 512]), yc_all=scr("yc_all", [512, 512]),
        o_loc=scr("o_loc", [128, 8, T], BF16),
        ek_loc=scr("ek_loc", [128, 512]), ek_all=scr("ek_all", [512, 512]),
        ev_loc=scr("ev_loc", [256, 256]), ev_all=scr("ev_all", [1024, 256]),
    )
    P = Prog(nc)
    xT = P.sb([128, 8, T], F32, name="xT_sb", persist=True)
    modall = P.sb([128, 2, 72, 2], F32, name="modall", persist=True)
    ng = P.sb([128, 2, 24], F32, name="normg_sb", persist=True)
    coltiles = {(l, v): (P.sb([128, 24], F32, name=f"gs_{v}{l}", persist=True), P.sb([128, 24], F32, name=f"gate_{v}{l}", persist=True))
                for l in range(2) for v in ("x", "c")}
    C = Ctx(P)

    c32 = P.sb([128, 16], F32, name="c32")
    c16 = P.sb([128, 16], BF16, name="c16")
    bt = P.sb([128, 36], F32, name="bt")
    res = P.sb([128, 36, 2], F32, name="res")
    w16 = [P.sb([128, 8, 128], BF16, name=f"w16_{i}") for i in range(2)]
    for c in range(KC):
        for ti, (t0, tn) in enumerate(TILES):
            P.dma(xT[:, c, t0:t0 + tn], xT_d[:, c, t0:t0 + tn], [], [f"x{c}_{ti}"], f"ld_x{(c * 5 + ti) % 4}")
    C.load(c32[:], "c32", cT_d)
    C.load(bt[:], "bt", modb_d)
    C.load(ng[:], "normg", normg_d)
    P.add("scalar", lambda e: e.activation(c16[:], c32[:], AF.Silu), ["c32"], ["c16"])
    for oc in range(36):
        sl = oc % 2
        C.load_cast(w16[sl][:].rearrange("p k m -> p (k m)"), f"w16_{sl}", modw_d[oc].rearrange("p k m -> p (k m)"), 1024, cast_eng="vector")
        ps = C.psum[oc % 2]
        for k in range(8):
            P.add("tensor", lambda e, ps=ps, k=k, sl=sl: e.matmul(ps[:, 0:2], w16[sl][:, k, :], c16[:, k * 2:(k + 1) * 2], start=(k == 0), stop=(k == 7)),
                  [f"w16_{sl}", "c16"], [f"psb{oc % 2}"])
        P.add("vector", lambda e, ps=ps, oc=oc: e.tensor_scalar(res[:, oc, :], ps[:, 0:2], bt[:, oc:oc + 1], None, ALU.add),
              [f"psb{oc % 2}", "bt"], ["res"])
    P.dma(Dr["mod_loc"], res[:].rearrange("p a b -> p (a b)"), ["res"], ["mod_loc"], "st_mod")
    P.coll("AllGather", ALU.bypass, GROUPS, Dr["mod_loc"], Dr["mod_all"], ["mod_loc"], ["mod_all"], "cc_mod")
    for l in range(2):
        for r2 in range(2):
            r = 2 * l + r2
            C.load(modall[:, l, 36 * r2:36 * (r2 + 1), :].rearrange("p a b -> p (a b)"), "modall", Dr["mod_all"][r * 128:(r + 1) * 128, :], reads=["mod_all"])
    cols = [make_cols(P, modall, ng, l, coltiles) for l in range(2)]
    P.end_phase()
    if stop == 1:
        P.close()
        return nc

    def phase_ffn(layer, f_idx, s, tiles=TILES):
        C.new_phase()
        K = alloc_common(P, C)
        hT = P.sb([128, 8, T], BF16, name="hT_sb")
        emit_norm(P, C, K, xT, hT, cols[layer], s, tiles=tiles)
        emit_ffn(P, C, K, xT, hT, cols[layer], s, wg_d[2 * layer + f_idx], wu_d[2 * layer + f_idx], wd_d[2 * layer + f_idx], tiles=tiles)
        return K, hT

    def phase_a(layer):
        n_u, n_q, n_k, vw = (4, 4, 1, 128) if layer == 0 else (0, 8, 2, 256)
        n_fm = n_u + n_q + n_k
        K, hT = phase_ffn(layer, 0, 0)
        emit_norm(P, C, K, xT, hT, cols[layer], 1)
        emit_consts(P, C, K, rot_d, bones_d)
        qkg = P.sb([128, 2, 2], F32, name="qkg_sb")
        C.load(qkg[:], "qkg", qkg_d)
        K["cst"] = [P.sb([128, 512], F32, name=f"cst{i}") for i in range(1)]
        K["snt"] = [P.sb([128, 512], F32, name=f"snt{i}") for i in range(1)]
        K["win16"] = [P.sb([128, 8, 128], BF16, name=f"win16_{i}") for i in range(2)]
        K["zq"] = [P.sb([128, 512], F32, name=f"zq{i}") for i in range(2)]
        K["zb"] = [P.sb([128, 512], BF16, name=f"zb{i}") for i in range(2)]
        K["ob32"] = [P.sb([128, 512], F32, name=f"ob32_{i}") for i in range(2)]
        K["ob16"] = [P.sb([128, 512], BF16, name=f"ob16_{i}") for i in range(2)]
        K["wv16"] = P.sb([128, 8, vw], BF16, name="wv16")
        K["vb"] = [P.sb([128, 256], F32, name="vb0")] * 2
        kinds = ["u"] * n_u + ["q"] * n_q + ["k"] * n_k
        def split_dst(a_, b_):
            return lambda t0, tn: (a_[:, t0:t0 + tn] if t0 < 1280 else b_[:, t0 - 1280:t0 - 1280 + tn])
        kdst = [split_dst(Dr["kA"], Dr["kB"])] if layer == 0 else [Dr["k_loc"][:, i, :] for i in range(2)]
        outs = [(split_dst(Dr["uA"][i], Dr["uB"][i]), F32) for i in range(n_u)] + [(Dr["q_loc"][:, i, :], BF16) for i in range(n_q)] + [(kd_, F32) for kd_ in kdst]
        if layer == 0:
            v_d = [Dr["vA"][tt * 128:(tt + 1) * 128, :] if tt < 10 else Dr["vB"][(tt - 10) * 128:(tt - 9) * 128, :] for tt in range(T // 128)]
        else:
            v_d = Dr["v_loc"].rearrange("(tt p) f -> tt p f", p=128)
        emit_inproj(P, C, K, hT, win_d[layer], n_fm, kinds, qkg[:, layer, :], cos_d, sin_d, outs, v_d, vw, n_fm)
        P.end_phase()

    phase_a(0)
    if stop == 2:
        P.close()
        return nc
    for c in range(4):
        P.coll("AllGather", ALU.bypass, GROUPS, Dr["uA"][c], Dr["uA_all"][c], [], [f"uA_all{c}"], "cc_u")
        P.coll("AllGather", ALU.bypass, GROUPS, Dr["uB"][c], Dr["uB_all"][c], [], [f"uB_all{c}"], "cc_u")
    P.coll("AllGather", ALU.bypass, GROUPS, Dr["kA"], Dr["kA_all"], [], ["kA_all"], "cc_k")
    P.coll("AllGather", ALU.bypass, GROUPS, Dr["kB"], Dr["kB_all"], [], ["kB_all"], "cc_k")
    P.coll("AllGather", ALU.bypass, GROUPS, Dr["vA"], Dr["vA_all"], [], ["vA_all"], "cc_v")
    P.coll("AllGather", ALU.bypass, GROUPS, Dr["vB"], Dr["vB_all"], [], ["vB_all"], "cc_v")
    C.new_phase()
    phase_s5(P, C, Dr)
    P.end_phase()
    if stop == 4:
        P.close()
        return nc
    for d in range(2):
        for c in range(4):
            P.coll("ReduceScatter", ALU.add, GROUPS, Dr["y_rs_in"][d][c], Dr["y_rs_out"][d][c], [], [f"y_rs_out{d}{c}"], "cc_y")
    P.coll("AllGather", ALU.bypass, GROUPS, Dr["yc_loc"], Dr["yc_all"], [], ["yc_all"], "cc_yc")
    C.new_phase()
    phase_att(P, C, 0, Dr)
    P.end_phase()
    if stop == 6:
        P.close()
        return nc
    C.new_phase()
    phase_po(P, C, 0, xT, cols[0], Dr)
    P.end_phase()
    if stop == 7:
        P.close()
        return nc
    phase_ffn(0, 1, 2)
    P.end_phase()
    if stop == 8:
        P.close()
        return nc
    phase_a(1)
    if stop == 9:
        P.close()
        return nc
    for c in range(2):
        for lh, col0 in ((0, TCTX), (1, T - 128)):
            P.dma(Dr["ek_loc"][:, c * 256 + lh * 128:c * 256 + (lh + 1) * 128], Dr["k_loc"][:, c, col0:col0 + 128], [], ["ek_loc"], f"cp_ek{c}{lh}")
    for lh, col0 in ((0, TCTX), (1, T - 128)):
        P.dma(Dr["ev_loc"][lh * 128:(lh + 1) * 128, :], Dr["v_loc"][col0:col0 + 128, :], [], ["ev_loc"], f"cp_ev{lh}")
    P.coll("AllGather", ALU.bypass, GROUPS, Dr["ek_loc"], Dr["ek_all"], ["ek_loc"], ["ek_all"], "cc_ek")
    P.coll("AllGather", ALU.bypass, GROUPS, Dr["ev_loc"], Dr["ev_all"], ["ev_loc"], ["ev_all"], "cc_ev")
    C.new_phase()
    phase_att(P, C, 1, Dr)
    P.end_phase()
    if stop == 11:
        P.close()
        return nc
    C.new_phase()
    phase_po(P, C, 1, xT, cols[1], Dr, tiles=TILES[1:])
    P.end_phase()
    if stop == 12:
        P.close()
        return nc
    phase_ffn(1, 1, 2, tiles=TILES[1:])
    for c in range(KC):
        for ti, (t0, tn) in enumerate(TILES[1:]):
            P.dma(out_d[:, c, t0 - TCTX:t0 - TCTX + tn], xT[:, c, t0:t0 + tn], [f"x{c}_{ti}"], [], f"st_x{(c * 5 + ti) % 4}")
    P.end_phase()
    if stop == 13:
        P.close()
        return nc
    P.close()
    return nc
def fm(x2d):
    t, f = x2d.shape
    return np.ascontiguousarray(x2d.T.reshape(f // 128, 128, t).transpose(1, 0, 2))


def unfm(a):
    p, c, t = a.shape
    return np.ascontiguousarray(a.transpose(1, 0, 2).reshape(c * 128, t).T)


def w_oc(W):
    k, n = W.shape
    return np.ascontiguousarray(W.reshape(k // 128, 128, n // 128, 128).transpose(2, 1, 0, 3))


def w_rows(W):
    k, n = W.shape
    return np.ascontiguousarray(W.reshape(k // 128, 128, n))


def rope_tables():
    rows = 8192 // 64
    r = np.repeat(np.arange(rows, dtype=np.float32), 64)
    col = np.tile(np.arange(64, dtype=np.float32), rows)
    inv = (10000.0 ** (-np.arange(16, dtype=np.float32) / 16)).astype(np.float32)
    ang = np.concatenate([r[:, None] * inv, col[:, None] * inv], axis=-1).astype(np.float32)
    cos = np.cos(ang).astype(np.float32).T
    sin = np.sin(ang).astype(np.float32).T
    idx = np.arange(128) % 32
    return cos[idx], sin[idx]


def const_mats():
    rot = np.zeros((128, 128), np.float32)
    for m in range(128):
        j = m % 64
        if j < 32:
            rot[m + 32, m] = -1.0
        else:
            rot[m - 32, m] = 1.0
    bones = np.zeros((128, 128), np.float32)
    bones[:64, :64] = 1.0
    bones[64:, 64:] = 1.0
    return rot, bones


def core_tokens(inp_x, ctx, i):
    b, q = i // 4, i % 4
    return np.concatenate([ctx[b], inp_x[b, q * TLAT:(q + 1) * TLAT]], axis=0)


def fused_inputs(inp):
    cos, sin = rope_tables()
    rot, bones = const_mats()
    W = np.concatenate([inp["mod_w"][0], inp["mod_w"][1]], axis=1)
    Bv = np.concatenate([inp["mod_b"][0], inp["mod_b"][1]], axis=0)
    Wl = W.reshape(8, 128, 144, 128).transpose(2, 1, 0, 3)
    Bl_mod = Bv.reshape(144, 128).T
    normg = np.ascontiguousarray(np.stack([inp["norm_g"][l].reshape(3, 8, 128).transpose(2, 0, 1).reshape(128, 24) for l in range(2)], axis=1))
    wg = np.stack([w_oc(inp["ffn_wg"][l, f]) for l in range(2) for f in range(2)])
    wu = np.stack([w_oc(inp["ffn_wu"][l, f]) for l in range(2) for f in range(2)])
    wd = np.stack([w_rows(inp["ffn_wd"][l, f]) for l in range(2) for f in range(2)])
    win0, win1 = w_oc(inp["ab_w_in"][0]), w_oc(inp["win_w_in"][0])
    qkg = np.ascontiguousarray(np.stack([inp["qk_norm"][l][:, np.arange(128) % 64].T for l in range(2)], axis=1))
    wo = np.stack([w_rows(inp["w_out"][l]) for l in range(2)])
    dcol = np.ascontiguousarray(inp["s5_d"][0].reshape(4, 128).T)
    glu0, glu1 = w_rows(inp["s5_glu_w"][0, 0]), w_rows(inp["s5_glu_w"][0, 1])
    sink = np.ascontiguousarray(np.broadcast_to(inp["win_sink"][0].reshape(1, 16), (128, 16))).astype(np.float32)
    iota_c = np.zeros((128, 2, 66), np.float32)
    iota_c[:, 0, :] = np.arange(66)
    iota_c[:, 1, :] = 65 - np.arange(66)
    iota_j = np.zeros((128, 2, 128), np.float32)
    iota_j[:, 0, :] = np.arange(128)
    iota_j[:, 1, :] = 127 - np.arange(128)
    one, zero = np.ones((1,), NPBF)[0], np.zeros((1,), NPBF)[0]
    kk = np.arange(128)[:, None]
    qq = np.arange(512)[None, :]
    wmask = np.zeros((128, 4, 6, 512), NPBF)
    for m in range(4):
        for r in range(6):
            ok = np.abs((4 * m + r - 1) * 128 + kk - (512 * m + qq)) <= 128
            wmask[:, m, r, :] = np.where(ok, one, zero)
    e = 0
    maps = []
    for i in range(NCORES):
        b, q = i // 4, i % 4
        cT = np.ascontiguousarray(np.stack([inp["c"][b], inp["c_ctx"]], axis=0).T.reshape(8, 128, 2).transpose(1, 0, 2).reshape(128, 16))
        cosT = np.concatenate([np.ones((128, TCTX), np.float32), cos[:, q * TLAT:(q + 1) * TLAT]], axis=1)
        sinT = np.concatenate([np.zeros((128, TCTX), np.float32), sin[:, q * TLAT:(q + 1) * TLAT]], axis=1)
        Bl = np.zeros((2, 4, 2, 4, 128, 128), np.float32)
        Cl = np.zeros((2, 4, 2, 128, 128), np.float32)
        pcols = np.zeros((128, 3, 8), np.float32)
        j = q
        for d in range(2):
            for gp in range(4):
                for gl in range(2):
                    g = 8 * j + 2 * gp + gl
                    ch0 = (2 * gp + gl) * 16
                    for ri, (bsrc, csrc) in enumerate(((inp["s5_b_re"], inp["s5_c_re"]), (inp["s5_b_im"], inp["s5_c_im"]))):
                        Bl[d, gp, ri, j, ch0:ch0 + 16, gl * 64:(gl + 1) * 64] = bsrc[e, d, g].T
                        Cl[d, gp, ri, gl * 64:(gl + 1) * 64, ch0:ch0 + 16] = csrc[e, d, g].T
                    pcols[gl * 64:(gl + 1) * 64, 0, d * 4 + gp] = inp["s5_lam_re"][e, d, g]
                    pcols[gl * 64:(gl + 1) * 64, 1, d * 4 + gp] = inp["s5_lam_im"][e, d, g]
                    pcols[gl * 64:(gl + 1) * 64, 2, d * 4 + gp] = inp["s5_log_step"][e, d, g]
        onehot = np.zeros((128, 4), np.float32)
        onehot[:, q] = 1.0
        emask = np.zeros((128, 2, 4, 512), NPBF)
        for r in range(4):
            if r == q - 1:
                emask[:, 0, r, :] = np.where(np.abs(-128 + kk - qq) <= 128, one, zero)
            if r == q + 1:
                emask[:, 1, r, :] = np.where(np.abs(2048 + kk - 1536 - qq) <= 128, one, zero)
        maps.append(dict(
            xT=fm(core_tokens(inp["x"], inp["ctx"], i)), cT=cT, modw=np.ascontiguousarray(Wl[36 * q:36 * q + 36]),
            modb=np.ascontiguousarray(Bl_mod[:, 36 * q:36 * q + 36]), normg=normg, wg=wg, wu=wu, wd=wd, win0=win0, win1=win1, qkg=qkg,
            cosT=np.ascontiguousarray(cosT), sinT=np.ascontiguousarray(sinT), rot=rot, bones=bones,
            Bl=Bl, Cl=Cl, pcols=pcols, iota_c=iota_c, iota_j=iota_j, onehot=onehot, dcol=dcol, glu0=glu0, glu1=glu1, wo=wo, sink=sink,
            wmask=wmask, emask=emask))
    return maps


def kernel(**inputs):
    inp = {k: np.asarray(v) for k, v in inputs.items()}
    nc = build_fused()
    res = run_bass_kernel_spmd(nc, fused_inputs(inp), core_ids=list(range(NCORES)))
    out = np.zeros((2, 4 * TLAT, D), np.float32)
    for i in range(NCORES):
        b, q = i // 4, i % 4
        out[b, q * TLAT:(q + 1) * TLAT] = unfm(np.asarray(res.results[i]["outT"]))
    return out
```

```python
import contextlib
import math
import numpy as np
import ml_dtypes
import concourse.bass as bass
import concourse.mybir as mybir
from concourse.bass_utils import run_bass_kernel_spmd

F32 = mybir.dt.float32
BF16 = mybir.dt.bfloat16
I32 = mybir.dt.int32
ALU = mybir.AluOpType
AF = mybir.ActivationFunctionType
AX = mybir.AxisListType
NPBF = ml_dtypes.bfloat16

NCORES = 8
D = 1024
KC = 8
DFF = 2816
FC = 22
TCTX = 256
TLAT = 2048
T = TCTX + TLAT
TILES = [(0, 256), (256, 512), (768, 512), (1280, 512), (1792, 512)]
EPS = 1e-6
GFF = 4


class _Op:
    __slots__ = ("eng", "pos", "fn", "cwaits", "dwaits", "signal", "dma_key", "dma_k")


class Prog:
    ENGS = ("tensor", "vector", "scalar", "gpsimd", "sync")
    CENG = ("tensor", "vector", "scalar", "gpsimd")

    def __init__(self, nc):
        self.nc = nc
        self.ges = contextlib.ExitStack()
        self.pes = contextlib.ExitStack()
        self.esem = {e: self.ges.enter_context(nc.semaphore(f"s_{e}")) for e in self.CENG}
        self.esig = {e: 0 for e in self.CENG}
        self.dsem = {}
        self.dma_cnt = {}
        self.dma_inc = {}
        self.n_sb = 0
        self.n_phase = 0
        self._reset()

    def _reset(self):
        self.ops = {e: [] for e in self.ENGS}
        self.last_w = {}
        self.readers = {}

    def sb(self, shape, dtype=F32, name=None, persist=False):
        self.n_sb += 1
        nm = (name or "sb") + f"_{self.n_sb}"
        es = self.ges if persist else self.pes
        return es.enter_context(self.nc.sbuf_tensor(nm, list(shape), dtype))

    def ps(self, shape, dtype=F32, name=None):
        self.n_sb += 1
        return self.ges.enter_context(self.nc.psum_tensor(name or f"ps{self.n_sb}", list(shape), dtype))

    def add(self, eng, fn, reads=(), writes=(), dma_key=None, inc=16):
        op = _Op()
        op.eng, op.fn, op.signal = eng, fn, False
        op.pos = len(self.ops[eng])
        op.dma_key = dma_key
        op.dma_k = None
        deps = {}
        for k in reads:
            d = self.last_w.get(k)
            if d is not None:
                deps[id(d)] = d
        for k in writes:
            d = self.last_w.get(k)
            if d is not None:
                deps[id(d)] = d
            for r in self.readers.get(k, ()):
                deps[id(r)] = r
        cw = {}
        dw = {}
        for i, d in deps.items():
            if d.dma_key is not None:
                v = self.dma_inc[d.dma_key] * (d.dma_k + 1)
                dw[d.dma_key] = max(dw.get(d.dma_key, 0), v)
            elif d.eng == eng:
                if eng != "tensor":
                    cw[d.eng] = max(cw.get(d.eng, -1), d.pos)
            else:
                cw[d.eng] = max(cw.get(d.eng, -1), d.pos)
        if dma_key is not None:
            self.dma_inc.setdefault(dma_key, inc)
            k = self.dma_cnt.get(dma_key, 0)
            op.dma_k = k
            self.dma_cnt[dma_key] = k + 1
            if k > 0:
                dw[dma_key] = max(dw.get(dma_key, 0), self.dma_inc[dma_key] * k)
        op.cwaits = []
        for e, p in cw.items():
            d = self.ops[e][p]
            d.signal = True
            op.cwaits.append(d)
        op.dwaits = list(dw.items())
        self.ops[eng].append(op)
        for k in reads:
            self.readers.setdefault(k, []).append(op)
        for k in writes:
            self.last_w[k] = op
            self.readers[k] = []
        return op

    def dma(self, out, in_, reads, writes, key, eng="sync", **kw):
        return self.add(eng, lambda e: e.dma_start(out=out, in_=in_, **kw), reads, writes, dma_key=key)

    def coll(self, kind, op, groups, in_ap, out_ap, reads, writes, key):
        self.n_coll = getattr(self, "n_coll", 0) + 1
        key = f"{key}_{self.n_coll}"
        return self.add("gpsimd", lambda e: e.collective_compute(kind, op, replica_groups=groups, ins=[in_ap], outs=[out_ap]),
                        reads, writes, dma_key=key, inc=1)

    def end_phase(self):
        nc = self.nc
        lasts = []
        for e in self.CENG:
            real = [o for o in self.ops[e] if o.dma_key is None and o.fn is not None]
            if real:
                lasts.append(real[-1])
        for d in lasts:
            d.signal = True
        for e in self.ENGS:
            op = _Op()
            op.eng, op.fn, op.signal, op.dma_key, op.dma_k = e, None, False, None, None
            op.pos = len(self.ops[e])
            op.cwaits = [d for d in lasts if d.eng != e]
            op.dwaits = [(k, self.dma_inc[k] * c) for k, c in self.dma_cnt.items()]
            self.ops[e].append(op)
        sig = {}
        for e in self.CENG:
            c = self.esig[e]
            for op in self.ops[e]:
                if op.signal:
                    c += 1
                    sig[id(op)] = c
            self.esig[e] = c
        for k in self.dma_cnt:
            if k not in self.dsem:
                self.dsem[k] = self.ges.enter_context(nc.semaphore(f"d_{len(self.dsem)}"))
        esem, dsem = self.esem, self.dsem
        with nc.Block() as block:
            def mk(ename):
                ops = self.ops[ename]

                def body(e):
                    for op in ops:
                        for d in op.cwaits:
                            e.wait_ge(esem[d.eng], sig[id(d)])
                        for k, v in op.dwaits:
                            e.wait_ge(dsem[k], v)
                        if op.fn is None:
                            continue
                        ins = op.fn(e)
                        if op.dma_key is not None:
                            ins.then_inc(dsem[op.dma_key], self.dma_inc[op.dma_key])
                        elif op.signal:
                            ins.then_inc(esem[ename], 1)
                return body

            for ename in self.ENGS:
                getattr(block, ename)(mk(ename))
        self.pes.close()
        self.pes = contextlib.ExitStack()
        self._reset()
        self.n_phase += 1

    def close(self):
        self.ges.close()


class Ctx:
    def __init__(self, P):
        self.P = P
        self.psum = [P.ps([128, 512], F32, name=f"psb{i}") for i in range(8)]
        self.ld_i = 0
        self.new_phase()

    def new_phase(self):
        P = self.P
        self.stage = [P.sb([128, 1024], F32, name=f"stage{i}") for i in range(3)]
        self.stage_i = 0

    def load_cast(self, dst16_ap, dst_key, src_ap, n, cast_eng="gpsimd", view=None, psl=None, reads=()):
        P = self.P
        i = self.stage_i
        self.stage_i = (i + 1) % len(self.stage)
        st = self.stage[i]
        sk = f"stage{i}"
        sv = st[:, 0:n] if psl is None else st[psl, 0:n]
        P.dma(sv if view is None else view(sv), src_ap, list(reads), [sk], f"ld_stage{i}")
        if cast_eng == "scalar":
            P.add("scalar", lambda e: e.copy(dst16_ap, sv if view is None else view(sv)), [sk], [dst_key])
        else:
            P.add(cast_eng, lambda e: e.tensor_copy(dst16_ap, sv if view is None else view(sv)), [sk], [dst_key])

    def load(self, dst_ap, dst_key, src_ap, reads=()):
        P = self.P
        self.ld_i += 1
        P.dma(dst_ap, src_ap, list(reads), [dst_key], f"ld_misc{self.ld_i % 4}")


def make_cols(P, modall, ng, layer, tiles):
    out = {}
    for vi, v in enumerate(("x", "c")):
        gs, gt = tiles[(layer, v)]
        for s in range(3):
            sc = modall[:, layer, (3 * s + 1) * 8:(3 * s + 2) * 8, vi]
            P.add("vector", lambda e, s=s, sc=sc, gs=gs: e.scalar_tensor_tensor(gs[:, s * 8:(s + 1) * 8], sc, 1.0, ng[:, layer, s * 8:(s + 1) * 8], ALU.add, ALU.mult),
                  ["modall", "normg"], [f"gs_{v}"])
            g = modall[:, layer, (3 * s + 2) * 8:(3 * s + 3) * 8, vi]
            fac = 1.0 if s == 1 else 0.5
            P.add("vector", lambda e, s=s, g=g, gt=gt, fac=fac: e.tensor_scalar(gt[:, s * 8:(s + 1) * 8], g, fac, None, ALU.mult),
                  ["modall"], [f"gate_{v}"])
        out[v] = dict(gs=gs, gate=gt, mod=modall, layer=layer, vi=vi)
    return out


def colsel(cols, v, kind, s, c):
    if kind == "gs":
        return cols[v]["gs"][:, s * 8 + c:s * 8 + c + 1]
    if kind == "gate":
        return cols[v]["gate"][:, s * 8 + c:s * 8 + c + 1]
    if kind == "shift":
        j = (3 * s) * 8 + c
        return cols[v]["mod"][:, cols[v]["layer"], j:j + 1, cols[v]["vi"]]
    raise ValueError(kind)


def emit_norm(P, C, K, xT, hT, cols, s, tiles=TILES):
    for ti, (t0, tn) in enumerate(tiles):
        v = "c" if t0 < TCTX else "x"
        pss = C.psum[6 + (ti % 2)]
        psk = f"psb{6 + (ti % 2)}"
        for c in range(KC):
            sq = K["sq"][c % 2]
            sqk = f"sq{c % 2}"
            P.add("scalar", lambda e, sq=sq, c=c, t0=t0, tn=tn: e.activation(sq[:, 0:tn], xT[:, c, t0:t0 + tn], AF.Square),
                  [f"x{c}_{ti}"], [sqk])
            P.add("tensor", lambda e, sq=sq, c=c, pss=pss, tn=tn: e.matmul(pss[:, 0:tn], K["ones"][:], sq[:, 0:tn], start=(c == 0), stop=(c == KC - 1)),
                  [sqk, "ones"], [psk])
        rs = K["rstd"][ti % 2]
        rsk = f"rstd{ti % 2}"
        P.add("scalar", lambda e, rs=rs, pss=pss, tn=tn: e.activation(rs[:, 0:tn], pss[:, 0:tn], AF.Sqrt, bias=K["epscol"][:, 0:1], scale=1.0 / D),
              [psk, "epscol"], [rsk])
        P.add("vector", lambda e, rs=rs, tn=tn: e.reciprocal(rs[:, 0:tn], rs[:, 0:tn]), [rsk], [rsk])
        for c in range(KC):
            tmp = K["ntmp"][c % 2]
            tk = f"ntmp{c % 2}"
            P.add("vector", lambda e, tmp=tmp, c=c, t0=t0, tn=tn, rs=rs: e.tensor_tensor(tmp[:, 0:tn], xT[:, c, t0:t0 + tn], rs[:, 0:tn], ALU.mult),
                  [f"x{c}_{ti}", rsk], [tk])
            P.add("scalar", lambda e, tmp=tmp, c=c, t0=t0, tn=tn, v=v: e.activation(hT[:, c, t0:t0 + tn], tmp[:, 0:tn], AF.Identity,
                                                                                    bias=colsel(cols, v, "shift", s, c), scale=colsel(cols, v, "gs", s, c)),
                  [tk, f"gs_{v}", "modall"], [f"h{c}_{ti}"])


def emit_ffn(P, C, K, xT, hT, cols, s, wg_d, wu_d, wd_d, tiles=TILES):
    act = K["act"]
    wg16, wu16, wd16 = K["wg16"], K["wu16"], K["wd16"]

    def load_chunk(j):
        sl = j % 2
        C.load_cast(wg16[sl][:].rearrange("p k m -> p (k m)"), f"wg16_{sl}", wg_d[j].rearrange("p k m -> p (k m)"), 1024)
        C.load_cast(wu16[sl][:].rearrange("p k m -> p (k m)"), f"wu16_{sl}", wu_d[j].rearrange("p k m -> p (k m)"), 1024)
        C.load_cast(wd16[j % (2 * GFF)][:], f"wd16_{j % (2 * GFF)}", wd_d[j], 1024)

    groups = [list(range(g, min(g + GFF, FC))) for g in range(0, FC, GFF)]
    load_chunk(0)
    pi = 0
    for grp in groups:
        for j in grp:
            if j + 1 < FC:
                load_chunk(j + 1)
            sl = j % 2
            jj = j % GFF
            for ti, (t0, tn) in enumerate(tiles):
                gb = pi % 2
                pi += 1
                pg, pu = C.psum[gb], C.psum[2 + gb]
                for k in range(KC):
                    P.add("tensor", lambda e, pg=pg, k=k, sl=sl, t0=t0, tn=tn: e.matmul(pg[:, 0:tn], wg16[sl][:, k, :], hT[:, k, t0:t0 + tn], start=(k == 0), stop=(k == KC - 1)),
                          [f"wg16_{sl}", f"h{k}_{ti}"], [f"psb{gb}"])
                for k in range(KC):
                    P.add("tensor", lambda e, pu=pu, k=k, sl=sl, t0=t0, tn=tn: e.matmul(pu[:, 0:tn], wu16[sl][:, k, :], hT[:, k, t0:t0 + tn], start=(k == 0), stop=(k == KC - 1)),
                          [f"wu16_{sl}", f"h{k}_{ti}"], [f"psb{2 + gb}"])
                sg = K["sg"][gb]
                P.add("scalar", lambda e, sg=sg, pg=pg, tn=tn: e.activation(sg[:, 0:tn], pg[:, 0:tn], AF.Silu), [f"psb{gb}"], [f"sg{gb}"])
                P.add("vector", lambda e, sg=sg, pu=pu, jj=jj, t0=t0, tn=tn: e.tensor_tensor(act[:, jj, t0:t0 + tn], sg[:, 0:tn], pu[:, 0:tn], ALU.mult),
                      [f"sg{gb}", f"psb{2 + gb}"], [f"act{jj}_{ti}"])
        for ti, (t0, tn) in enumerate(tiles):
            v = "c" if t0 < TCTX else "x"
            for dc in range(KC):
                db = 4 + (dc % 2)
                pd = C.psum[db]
                for n, j in enumerate(grp):
                    jj = j % GFF
                    jw = j % (2 * GFF)
                    P.add("tensor", lambda e, pd=pd, jj=jj, jw=jw, dc=dc, t0=t0, tn=tn, n=n, ng=len(grp): e.matmul(pd[:, 0:tn], wd16[jw][:, dc * 128:(dc + 1) * 128], act[:, jj, t0:t0 + tn],
                                                                                                  start=(n == 0), stop=(n == ng - 1)),
                          [f"wd16_{jw}", f"act{jj}_{ti}"], [f"psb{db}"])
                P.add("vector", lambda e, pd=pd, dc=dc, t0=t0, tn=tn, v=v: e.scalar_tensor_tensor(xT[:, dc, t0:t0 + tn], pd[:, 0:tn], colsel(cols, v, "gate", s, dc),
                                                                                                 xT[:, dc, t0:t0 + tn], ALU.mult, ALU.add),
                      [f"psb{db}", f"gate_{v}", f"x{dc}_{ti}"], [f"x{dc}_{ti}"])


def alloc_common(P, C, with_ffn=True):
    K = {}
    K["ones"] = P.sb([128, 128], BF16, name="ones")
    P.add("vector", lambda e: e.memset(K["ones"][:], 1.0), [], ["ones"])
    K["epscol"] = P.sb([128, 1], F32, name="epscol")
    P.add("vector", lambda e: e.memset(K["epscol"][:], EPS), [], ["epscol"])
    K["sq"] = [P.sb([128, 512], BF16, name=f"sq{i}") for i in range(2)]
    K["rstd"] = [P.sb([128, 512], F32, name=f"rstd{i}") for i in range(2)]
    K["ntmp"] = [P.sb([128, 512], F32, name=f"ntmp{i}") for i in range(2)]
    if with_ffn:
        K["act"] = P.sb([128, GFF, T], BF16, name="act")
        K["wg16"] = [P.sb([128, 8, 128], BF16, name=f"wg16_{i}") for i in range(2)]
        K["wu16"] = [P.sb([128, 8, 128], BF16, name=f"wu16_{i}") for i in range(2)]
        K["wd16"] = [P.sb([128, 1024], BF16, name=f"wd16_{i}") for i in range(2 * GFF)]
        K["sg"] = [P.sb([128, 512], F32, name=f"sg{i}") for i in range(2)]
    return K


def emit_consts(P, C, K, rot_d, bones_d):
    K["rot"] = P.sb([128, 128], BF16, name="rot_sb")
    K["bones"] = P.sb([128, 128], BF16, name="bones_sb")
    C.load_cast(K["rot"][:], "rot", rot_d, 128)
    C.load_cast(K["bones"][:], "bones", bones_d, 128)


def emit_inproj(P, C, K, hT, win_d, n_fm, kinds, qkg, cosT, sinT, outs, v_d, vw, v_oc0, tiles=TILES):
    w16 = K["win16"]
    zq = K["zq"]
    for oc in range(n_fm):
        sl = oc % 2
        C.load_cast(w16[sl][:].rearrange("p k m -> p (k m)"), f"win16_{sl}", win_d[oc].rearrange("p k m -> p (k m)"), 1024)
        kind = kinds[oc]
        od, odt = outs[oc]
        for ti, (t0, tn) in enumerate(tiles):
            pb = ti % 2
            pz = C.psum[pb]
            for k in range(KC):
                P.add("tensor", lambda e, pz=pz, k=k, sl=sl, t0=t0, tn=tn: e.matmul(pz[:, 0:tn], w16[sl][:, k, :], hT[:, k, t0:t0 + tn], start=(k == 0), stop=(k == KC - 1)),
                      [f"win16_{sl}", f"h{k}_{ti}"], [f"psb{pb}"])
            ob = K["ob32"][pb] if odt == F32 else K["ob16"][pb]
            obk = f"ob{'32' if odt == F32 else '16'}_{pb}"
            if kind == "u":
                P.add("scalar", lambda e, ob=ob, pz=pz, tn=tn: e.copy(ob[:, 0:tn], pz[:, 0:tn]), [f"psb{pb}"], [obk])
            else:
                gcol = qkg[:, 0:1] if kind == "q" else qkg[:, 1:2]
                z = zq[pb]
                zk = f"zq{pb}"
                sq = K["sq"][pb]
                sqk = f"sq{pb}"
                P.add("scalar", lambda e, z=z, pz=pz, tn=tn: e.copy(z[:, 0:tn], pz[:, 0:tn]), [f"psb{pb}"], [zk])
                P.add("scalar", lambda e, sq=sq, z=z, tn=tn: e.activation(sq[:, 0:tn], z[:, 0:tn], AF.Square), [zk], [sqk])
                ph = C.psum[2 + pb]
                P.add("tensor", lambda e, ph=ph, sq=sq, tn=tn: e.matmul(ph[:, 0:tn], K["bones"][:], sq[:, 0:tn], start=True, stop=True), [sqk, "bones"], [f"psb{2 + pb}"])
                rs = K["rstd"][pb]
                rsk = f"rstd{pb}"
                P.add("scalar", lambda e, rs=rs, ph=ph, tn=tn: e.activation(rs[:, 0:tn], ph[:, 0:tn], AF.Sqrt, bias=K["epscol"][:, 0:1], scale=1.0 / 64), [f"psb{2 + pb}", "epscol"], [rsk])
                P.add("vector", lambda e, rs=rs, tn=tn: e.reciprocal(rs[:, 0:tn], rs[:, 0:tn]), [rsk], [rsk])
                P.add("vector", lambda e, z=z, rs=rs, tn=tn, gcol=gcol: e.scalar_tensor_tensor(z[:, 0:tn], z[:, 0:tn], gcol, rs[:, 0:tn], ALU.mult, ALU.mult), [zk, rsk, "qkg"], [zk])
                zb = K["zb"][pb]
                zbk = f"zb{pb}"
                P.add("scalar", lambda e, zb=zb, z=z, tn=tn: e.copy(zb[:, 0:tn], z[:, 0:tn]), [zk], [zbk])
                pr = C.psum[4 + pb]
                P.add("tensor", lambda e, pr=pr, zb=zb, tn=tn: e.matmul(pr[:, 0:tn], K["rot"][:], zb[:, 0:tn], start=True, stop=True), [zbk, "rot"], [f"psb{4 + pb}"])
                t2 = K["ntmp"][pb]
                t2k = f"ntmp{pb}"
                cb_ = pb % len(K["cst"])
                cst, snt = K["cst"][cb_], K["snt"][cb_]
                C.load(cst[:, 0:tn], f"cst{cb_}", cosT[:, t0:t0 + tn])
                C.load(snt[:, 0:tn], f"snt{cb_}", sinT[:, t0:t0 + tn])
                P.add("vector", lambda e, t2=t2, pr=pr, snt=snt, tn=tn: e.tensor_tensor(t2[:, 0:tn], pr[:, 0:tn], snt[:, 0:tn], ALU.mult), [f"psb{4 + pb}", f"snt{cb_}"], [t2k])
                P.add("vector", lambda e, z=z, cst=cst, tn=tn: e.tensor_tensor(z[:, 0:tn], z[:, 0:tn], cst[:, 0:tn], ALU.mult), [zk, f"cst{cb_}"], [zk])
                P.add("vector", lambda e, ob=ob, z=z, t2=t2, tn=tn: e.tensor_tensor(ob[:, 0:tn], z[:, 0:tn], t2[:, 0:tn], ALU.add), [zk, t2k], [obk])
            P.dma(od(t0, tn) if callable(od) else od[:, t0:t0 + tn], ob[:, 0:tn], [obk], [], f"st_{obk}")
    nvc = vw // 128
    wv = K["wv16"]
    for i in range(nvc):
        C.load_cast(wv[:, :, i * 128:(i + 1) * 128], f"wv16_{i}", win_d[v_oc0 + i], 1024, view=lambda a: a.rearrange("p (k m) -> p k m", m=128))
    for tt in range(T // 128):
        pb = tt % 2
        pv = C.psum[6 + pb]
        ti = 0 if tt < 2 else 1 + (tt - 2) // 4
        for k in range(KC):
            P.add("tensor", lambda e, pv=pv, k=k, tt=tt: e.matmul(pv[:, 0:vw], hT[:, k, tt * 128:(tt + 1) * 128], wv[:, k, :], start=(k == 0), stop=(k == KC - 1)),
                  [f"wv16_{i}" for i in range(nvc)] + [f"h{k}_{ti}"], [f"psb{6 + pb}"])
        vb = K["vb"][pb]
        P.add("scalar", lambda e, vb=vb, pv=pv: e.copy(vb[:, 0:vw], pv[:, 0:vw]), [f"psb{6 + pb}"], ["vb0"])
        P.dma(v_d[tt], vb[:, 0:vw], ["vb0"], [], "st_vb0")


def phase_att(P, C, layer, Dr):
    dense = layer == 0
    n_qc, n_kv = (4, 2) if dense else (8, 4)
    n_heads = 2 * n_qc
    NK = 66 if dense else 26
    qT = P.sb([128, n_qc, T], BF16, name="q_sb")
    kd = P.sb([128, n_kv, NK * 128], BF16, name="k_sb")
    va = P.sb([128, n_kv, NK, 65], BF16, name="v_sb")
    q_loc, o_loc = Dr["q_loc"], Dr["o_loc"]
    for c in range(n_qc):
        C.load(qT[:, c, :], f"q{c}", q_loc[:, c, :])
    for kv in range(n_kv):
        P.add("vector", lambda e, kv=kv: e.memset(va[:, kv, :, 64:65], 1.0), [], [f"vone{kv}"])
    tkv = lambda s_: s_.rearrange("p (kt d) -> p kt d", d=64)
    tk = lambda ap: ap.rearrange("(kt p) d -> p kt d", p=128)

    kpieces, vpieces = {}, {}

    def kload(kv, half, dst0, src_rows, src_cols0, ncols, reads=()):
        pl = slice(64 * half, 64 * half + 64)
        for c0 in range(0, ncols, 1024):
            n = min(1024, ncols - c0)
            key = f"k{kv}_{half}_{dst0 + c0}"
            kpieces.setdefault((kv, half), []).append((dst0 + c0, n, key))
            C.load_cast(kd[pl, kv, dst0 + c0:dst0 + c0 + n], key, src_rows[:, src_cols0 + c0:src_cols0 + c0 + n], n, psl=pl, cast_eng="vector", reads=reads)

    def vload(kv, kt0, nkt, src, reads=()):
        for t0 in range(0, nkt, 16):
            n = min(16, nkt - t0)
            key = f"v{kv}_{kt0 + t0}"
            vpieces.setdefault(kv, []).append((kt0 + t0, n, key))
            C.load_cast(va[:, kv, kt0 + t0:kt0 + t0 + n, 0:64], key, tk(src[t0 * 128:(t0 + n) * 128, 64 * kv:64 * kv + 64]), n * 64, view=tkv, cast_eng="vector", reads=reads)

    def kkey(kv, half, kt):
        for (c0, n, key) in kpieces[(kv, half)]:
            if c0 <= kt * 128 < c0 + n:
                return key
        raise KeyError((kv, half, kt))

    def vkey(kv, kt):
        for (k0, n, key) in vpieces[kv]:
            if k0 <= kt < k0 + n:
                return key
        raise KeyError((kv, kt))

    if dense:
        kA_all, kB_all, vA_all, vB_all = Dr["kA_all"], Dr["kB_all"], Dr["vA_all"], Dr["vB_all"]
        for kv in range(n_kv):
            for half in range(2):
                kload(kv, half, 0, kA_all[64 * kv:64 * kv + 64], 0, TCTX)
                for r in range(4):
                    kload(kv, half, TCTX + TLAT * r, kA_all[r * 128 + 64 * kv:r * 128 + 64 * kv + 64], TCTX, 1024)
                    kload(kv, half, TCTX + TLAT * r + 1024, kB_all[r * 128 + 64 * kv:r * 128 + 64 * kv + 64], 0, 1024)
            vload(kv, 0, 2, vA_all[0:TCTX])
            for r in range(4):
                vload(kv, 2 + 16 * r, 8, vA_all[r * 1280 + TCTX:(r + 1) * 1280])
                vload(kv, 2 + 16 * r + 8, 8, vB_all[r * 1024:(r + 1) * 1024])
    else:
        k_loc, v_loc, ek_all, ev_all = Dr["k_loc"], Dr["v_loc"], Dr["ek_all"], Dr["ev_all"]
        for kv in range(n_kv):
            ks = slice(64 * (kv % 2), 64 * (kv % 2) + 64)
            kc = kv // 2
            for half in range(2):
                kload(kv, half, 0, k_loc[ks, kc], TCTX, TLAT)
                kload(kv, half, 24 * 128, k_loc[ks, kc], 0, TCTX)
                for r in range(4):
                    kload(kv, half, 16 * 128 + 256 * r, ek_all[r * 128 + 64 * (kv % 2):r * 128 + 64 * (kv % 2) + 64], kc * 256, 256, reads=["ek_all"])
            vload(kv, 0, 16, v_loc[TCTX:T])
            vload(kv, 24, 2, v_loc[0:TCTX])
            for r in range(4):
                vload(kv, 16 + 2 * r, 2, ev_all[r * 256:(r + 1) * 256], reads=["ev_all"])
    ones32 = P.sb([128, 64], F32, name="ones32")
    P.add("vector", lambda e: e.memset(ones32[:], 1.0), [], ["ones32"])
    es = P.sb([128, 16], F32, name="es")
    if not dense:
        C.load(es[:], "es", Dr["sink"])
        P.add("scalar", lambda e: e.activation(es[:], es[:], AF.Exp), ["es"], ["es"])
        wm = P.sb([128, 4, 6, 512], BF16, name="wm")
        C.load(wm[:], "wm", Dr["wmask"])
        em = P.sb([128, 2, 4, 512], BF16, name="em")
        C.load(em[:], "em", Dr["emask"])
    pT = [P.sb([128, 512], BF16, name=f"pT{i}") for i in range(4)]
    SBK = [0, 1, 2, 5]
    osb = [P.sb([64, 512], F32, name=f"osb{i}") for i in range(2)]
    rden = [P.sb([128, 512], F32, name=f"rden{i}") for i in range(2)]
    ob = [P.sb([64, 512], BF16, name=f"ob{i}") for i in range(2)]
    work = []
    if dense:
        work.append((0, 256, [(0, None), (1, None)]))
        for m in range(4):
            work.append((256 + 512 * m, 512, [(kt, None) for kt in range(66)]))
    else:
        for m in range(4):
            keys = [(4 * m + r - 1, wm[:, m, r, :]) for r in range(6) if 0 <= 4 * m + r - 1 <= 15]
            if m == 0:
                keys += [(16 + 2 * r + 1, em[:, 0, r, :]) for r in range(4)]
            if m == 3:
                keys += [(16 + 2 * r, em[:, 1, r, :]) for r in range(4)]
            keys += [(24, None), (25, None)]
            work.append((256 + 512 * m, 512, keys))
    it = 0
    si = 0
    for (q0, qn, keys) in work:
        for h in range(n_heads):
            c, half, kv = h // 2, h % 2, h // 4
            pl = slice(64 * half, 64 * half + 64)
            ob_i = it % 2
            it += 1
            po = C.psum[6 + ob_i]
            pok = f"psb{6 + ob_i}"
            LA = 3
            nk = len(keys)
            slots = []
            for n in range(nk + LA):
                if n < nk:
                    kt, mk = keys[n]
                    sb_i = si % 4
                    si += 1
                    slots.append(sb_i)
                    pS = C.psum[SBK[sb_i]]
                    P.add("tensor", lambda e, pS=pS, kv=kv, kt=kt, pl=pl, c=c, q0=q0, qn=qn: e.matmul(pS[:, 0:qn], kd[pl, kv, kt * 128:(kt + 1) * 128], qT[pl, c, q0:q0 + qn], start=True, stop=True),
                          [kkey(kv, half, kt), f"q{c}"], [f"psb{SBK[sb_i]}"])
                    pt = pT[sb_i]
                    P.add("scalar", lambda e, pt=pt, pS=pS, qn=qn: e.activation(pt[:, 0:qn], pS[:, 0:qn], AF.Exp, scale=0.125), [f"psb{SBK[sb_i]}"], [f"pT{sb_i}"])
                    if mk is not None:
                        P.add("vector", lambda e, pt=pt, mk=mk, qn=qn: e.tensor_tensor(pt[:, 0:qn], pt[:, 0:qn], mk[:, 0:qn], ALU.mult), [f"pT{sb_i}", "wm", "em"], [f"pT{sb_i}"])
                if n >= LA:
                    m_ = n - LA
                    kt2 = keys[m_][0]
                    sb2 = slots[m_]
                    pt2 = pT[sb2]
                    P.add("tensor", lambda e, po=po, pt2=pt2, kv=kv, kt2=kt2, qn=qn, m_=m_, nk=nk: e.matmul(po[0:65, 0:qn], va[:, kv, kt2, :], pt2[:, 0:qn], start=(m_ == 0), stop=(m_ == nk - 1)),
                          [vkey(kv, kt2), f"vone{kv}", f"pT{sb2}"], [pok])
            rd = rden[ob_i]
            rdk = f"rden{ob_i}"
            osx = osb[ob_i]
            P.add("scalar", lambda e, osx=osx, po=po, qn=qn: e.copy(osx[:, 0:qn], po[0:64, 0:qn]), [pok], [f"osb{ob_i}"])
            if dense:
                P.add("vector", lambda e, rd=rd, po=po, qn=qn: e.reciprocal(rd[64:65, 0:qn], po[64:65, 0:qn]), [pok], [rdk])
            else:
                P.add("vector", lambda e, rd=rd, po=po, qn=qn, h=h: e.tensor_scalar(rd[64:65, 0:qn], po[64:65, 0:qn], es[64:65, h:h + 1], None, ALU.add), [pok, "es"], [rdk])
                P.add("vector", lambda e, rd=rd, qn=qn: e.reciprocal(rd[64:65, 0:qn], rd[64:65, 0:qn]), [rdk], [rdk])
            pb = C.psum[3 + ob_i]
            P.add("tensor", lambda e, pb=pb, rd=rd, qn=qn: e.matmul(pb[0:64, 0:qn], ones32[64:65, 0:64], rd[64:65, 0:qn], start=True, stop=True), [rdk, "ones32"], [f"psb{3 + ob_i}"])
            o16 = ob[ob_i]
            P.add("vector", lambda e, o16=o16, osx=osx, pb=pb, qn=qn: e.tensor_tensor(o16[:, 0:qn], osx[:, 0:qn], pb[0:64, 0:qn], ALU.mult), [f"osb{ob_i}", f"psb{3 + ob_i}"], [f"ob{ob_i}"])
            P.dma(o_loc[64 * half:64 * half + 64, c, q0:q0 + qn], o16[:, 0:qn], [f"ob{ob_i}"], [], f"st_ob{ob_i}")


NS5 = TCTX + 4 * TLAT
TWO_PI = 2.0 * math.pi


def phase_s5(P, C, Dr):
    uA_all, uB_all, B_d, C_d, pc_d = Dr["uA_all"], Dr["uB_all"], Dr["Bl"], Dr["Cl"], Dr["pcols"]
    ic_d, ij_d, oh_d = Dr["iota_c"], Dr["iota_j"], Dr["onehot"]
    y_rs_in, yc_loc = Dr["y_rs_in"], Dr["yc_loc"]
    sm = lambda name, n=8, dtype=F32: P.sb([128, n], dtype, name=name)
    pc = P.sb([128, 3, 8], F32, name="pc")
    C.load(pc[:].rearrange("p a b -> p (a b)"), "pc", pc_d.rearrange("p a b -> p (a b)"))
    iota_c = P.sb([128, 2, 66], F32, name="iota_c_sb")
    iota_j = P.sb([128, 2, 128], F32, name="iota_j_sb")
    oh = sm("onehot_sb", 4)
    C.load(iota_c[:], "iota_c", ic_d)
    C.load(iota_j[:], "iota_j", ij_d)
    C.load(oh[:], "oh", oh_d)
    cnt = [0]

    def V(fn, reads, writes, eng="vector"):
        P.add(eng, fn, reads, writes)

    fr_i = P.sb([128, 128], I32, name="fr_i")
    fr_f = P.sb([128, 128], F32, name="fr_f")
    sc_t = P.sb([128, 128], F32, name="sc_t")
    sc_u = P.sb([128, 128], F32, name="sc_u")

    def fracp(dst, src, n, key_dst, key_src):
        V(lambda e: e.tensor_copy(fr_i[:, 0:n], src), [key_src], ["fr_i"])
        V(lambda e: e.tensor_copy(fr_f[:, 0:n], fr_i[:, 0:n]), ["fr_i"], ["fr_f"])
        V(lambda e: e.tensor_tensor(dst, src, fr_f[:, 0:n], ALU.subtract), [key_src, "fr_f"], [key_dst])

    def sincos(sin_dst, cos_dst, ph, n, key_s, key_c, key_ph):
        P.add("scalar", lambda e: e.activation(sin_dst, ph, AF.Sin, scale=TWO_PI - 1e-5), [key_ph], [key_s])
        V(lambda e: e.tensor_scalar(sc_t[:, 0:n], ph, 0.25, None, ALU.add), [key_ph], ["sc_t"])
        fracp(sc_u[:, 0:n], sc_t[:, 0:n], n, "sc_u", "sc_t")
        P.add("scalar", lambda e: e.activation(cos_dst, sc_u[:, 0:n], AF.Sin, scale=TWO_PI - 1e-5), ["sc_u"], [key_c])

    lr, li, dtv, a, th, rr, f = sm("lr"), sm("li"), sm("dtv"), sm("a_ln"), sm("th"), sm("rr"), sm("f")
    V(lambda e: e.tensor_scalar(lr[:], pc[:, 0, :], -1e-4, None, ALU.min), ["pc"], ["lr"])
    V(lambda e: e.tensor_copy(li[:], pc[:, 1, :]), ["pc"], ["li"])
    P.add("scalar", lambda e: e.activation(dtv[:], pc[:, 2, :], AF.Exp), ["pc"], ["dtv"])
    V(lambda e: e.tensor_tensor(a[:], lr[:], dtv[:], ALU.mult), ["lr", "dtv"], ["a"])
    V(lambda e: e.tensor_tensor(th[:], li[:], dtv[:], ALU.mult), ["li", "dtv"], ["th"])
    P.add("scalar", lambda e: e.activation(rr[:], a[:], AF.Exp), ["a"], ["rr"])
    V(lambda e: e.tensor_scalar(f[:], th[:], 1.0 / TWO_PI, None, ALU.mult), ["th"], ["f"])
    f0, sth, cth = sm("f0"), sm("sth"), sm("cth")
    fracp(f0[:], f[:], 8, "f0", "f")
    sincos(sth[:], cth[:], f0[:], 8, "sth", "cth", "f0")
    nr, ni, den, t8a, t8b, kr, ki, nkr, nki = [sm(n) for n in ("nr", "ni", "den", "t8a", "t8b", "kr", "ki", "nkr", "nki")]
    V(lambda e: e.tensor_tensor(nr[:], rr[:], cth[:], ALU.mult), ["rr", "cth"], ["nr"])
    V(lambda e: e.tensor_scalar(nr[:], nr[:], -1.0, None, ALU.add), ["nr"], ["nr"])
    V(lambda e: e.tensor_tensor(ni[:], rr[:], sth[:], ALU.mult), ["rr", "sth"], ["ni"])
    V(lambda e: e.tensor_tensor(den[:], lr[:], lr[:], ALU.mult), ["lr"], ["den"])
    V(lambda e: e.tensor_tensor(t8a[:], li[:], li[:], ALU.mult), ["li"], ["t8a"])
    V(lambda e: e.tensor_tensor(den[:], den[:], t8a[:], ALU.add), ["den", "t8a"], ["den"])
    V(lambda e: e.reciprocal(den[:], den[:]), ["den"], ["den"])
    V(lambda e: e.tensor_tensor(t8a[:], nr[:], lr[:], ALU.mult), ["nr", "lr"], ["t8a"])
    V(lambda e: e.tensor_tensor(t8b[:], ni[:], li[:], ALU.mult), ["ni", "li"], ["t8b"])
    V(lambda e: e.tensor_tensor(kr[:], t8a[:], t8b[:], ALU.add), ["t8a", "t8b"], ["kr"])
    V(lambda e: e.tensor_tensor(kr[:], kr[:], den[:], ALU.mult), ["kr", "den"], ["kr"])
    V(lambda e: e.tensor_tensor(t8a[:], ni[:], lr[:], ALU.mult), ["ni", "lr"], ["t8a"])
    V(lambda e: e.tensor_tensor(t8b[:], nr[:], li[:], ALU.mult), ["nr", "li"], ["t8b"])
    V(lambda e: e.tensor_tensor(ki[:], t8a[:], t8b[:], ALU.subtract), ["t8a", "t8b"], ["ki"])
    V(lambda e: e.tensor_tensor(ki[:], ki[:], den[:], ALU.mult), ["ki", "den"], ["ki"])
    V(lambda e: e.tensor_scalar(nkr[:], kr[:], -1.0, None, ALU.mult), ["kr"], ["nkr"])
    V(lambda e: e.tensor_scalar(nki[:], ki[:], -1.0, None, ALU.mult), ["ki"], ["nki"])
    a128, a128f = sm("a128"), sm("a128f")
    V(lambda e: e.tensor_scalar(a128[:], f0[:], 128.0, None, ALU.mult), ["f0"], ["a128"])
    fracp(a128f[:], a128[:], 8, "a128f", "a128")
    B16 = P.sb([128, 16, 4, 128], BF16, name="B16")
    CR = P.sb([128, 8, 128], BF16, name="CR16")
    CI = P.sb([128, 8, 128], BF16, name="CI16")
    c32 = [P.sb([128, 2, 128], F32, name=f"c32_{i}") for i in range(2)]
    ctmp = [P.sb([128, 128], F32, name=f"ctmp{i}") for i in range(2)]

    def wsetup(d, gp):
        q = d * 4 + gp
        for ri in range(2):
            C.load_cast(B16[:, q * 2 + ri, :, :], f"B16_{q}_{ri}", B_d[d, gp, ri].rearrange("c p m -> p c m"), 512,
                        view=lambda s_: s_.rearrange("p (c m) -> p c m", m=128))
        cb = c32[q % 2]
        ck = f"c32_{q % 2}"
        P.dma(cb[:], C_d[d, gp].rearrange("r k m -> k r m"), [], [ck], f"ld_c32_{q % 2}")
        tm = ctmp[q % 2]
        tk = f"ctmp{q % 2}"
        V(lambda e: e.tensor_scalar(tm[:], cb[:, 0, :], kr[:, q:q + 1], None, ALU.mult), [ck, "kr"], [tk])
        V(lambda e: e.scalar_tensor_tensor(CR[:, q, :], cb[:, 1, :], nki[:, q:q + 1], tm[:], ALU.mult, ALU.add), [ck, "nki", tk], [f"CR_{q}"])
        V(lambda e: e.tensor_scalar(tm[:], cb[:, 1, :], nkr[:, q:q + 1], None, ALU.mult), [ck, "nkr", f"CR_{q}"], [tk])
        V(lambda e: e.scalar_tensor_tensor(CI[:, q, :], cb[:, 0, :], nki[:, q:q + 1], tm[:], ALU.mult, ALU.add), [ck, "nki", tk], [f"CI_{q}"])

    for d in range(2):
        for gp in range(4):
            wsetup(d, gp)
    sinC = P.sb([128, 8, 66], F32, name="sinC")
    cosC = P.sb([128, 8, 66], F32, name="cosC")
    sinJ = P.sb([128, 8, 128], F32, name="sinJ")
    cosJ = P.sb([128, 8, 128], F32, name="cosJ")

    phA = P.sb([128, 128], F32, name="phA")
    phB = P.sb([128, 128], F32, name="phB")

    def tsetup(q):
        d = q // 4
        V(lambda e: e.tensor_scalar(phA[:, 0:66], iota_c[:, d, :], a128f[:, q:q + 1], None, ALU.mult), ["iota_c", "a128f"], ["phA"])
        fracp(phB[:, 0:66], phA[:, 0:66], 66, "phB", "phA")
        sincos(sinC[:, q, :], cosC[:, q, :], phB[:, 0:66], 66, f"sinC{q}", f"cosC{q}", "phB")
        V(lambda e: e.tensor_scalar(phA[:, 0:128], iota_j[:, d, :], f0[:, q:q + 1], None, ALU.mult), ["iota_j", "f0"], ["phA"])
        fracp(phB[:, 0:128], phA[:, 0:128], 128, "phB", "phA")
        sincos(sinJ[:, q, :], cosJ[:, q, :], phB[:, 0:128], 128, f"sinJ{q}", f"cosJ{q}", "phB")

    for q in range(8):
        tsetup(q)
    NB = 512
    big = lambda name, dtype=F32: P.sb([128, NB], dtype, name=name)
    u16 = [P.sb([128, 4, NB], BF16, name=f"u16_{i}") for i in range(2)]
    St = [big(f"St{i}") for i in range(2)]
    Ct = [big(f"Ct{i}") for i in range(2)]
    tB = big("tB")
    tA2 = [big("tA0"), big("tA1")]
    brs2, bis2 = [big("brs0"), big("brs1")], [big("bis0"), big("bis1")]
    btr2, bti2 = [big("btr0"), big("btr1")], [big("bti0"), big("bti1")]
    gr2, gi2 = [big("gr0"), big("gr1")], [big("gi0"), big("gi1")]
    hr16 = [big(f"hr16_{i}", BF16) for i in range(2)]
    hi16 = [big(f"hi16_{i}", BF16) for i in range(2)]
    carry = P.sb([128, 16], F32, name="carry")
    yb = [P.sb([128, 4, NB], F32, name=f"yb{i}") for i in range(2)]

    def rv(t, n):
        return bass.AP(t, n - 1, [[NB, 128], [-1, n]])

    def gp_body(d, first, lay0, sn, ub, gp, tb, hb, seg_n):
        q = d * 4 + gp
        c0, ncn = lay0 // 128, sn // 128
        nst = (sn + 511) // 512
        S, Cc = St[tb], Ct[tb]
        brs, bis = brs2[tb], bis2[tb]
        ch = gp % 2
        tA, btr, bti, gr, gi = tA2[ch], btr2[ch], bti2[ch], gr2[ch], gi2[ch]
        kA_, kbr, kbi, kgr, kgi = f"tA{ch}", f"btr{ch}", f"bti{ch}", f"gr{ch}", f"gi{ch}"
        yb_ = 4 + (seg_n % 2)
        W = slice(0, sn)
        v3 = lambda t: t[:, 0:sn].rearrange("p (c j) -> p c j", j=128)
        cC = lambda t: t[:, q, c0:c0 + ncn].unsqueeze(2).broadcast_to([128, ncn, 128])
        cJ = lambda t: t[:, q, :].unsqueeze(1).broadcast_to([128, ncn, 128])
        G = "vector"
        P.add(G, lambda e, a=cC(sinC), b=cJ(cosJ), v=v3(S): e.tensor_tensor(v, a, b, ALU.mult), [f"sinC{q}", f"cosJ{q}"], [f"St{tb}"])
        P.add(G, lambda e, a=cC(cosC), b=cJ(sinJ), v=v3(tB): e.tensor_tensor(v, a, b, ALU.mult), [f"cosC{q}", f"sinJ{q}"], ["tBg"])
        P.add(G, lambda e: e.tensor_tensor(S[:, W], S[:, W], tB[:, W], ALU.add), [f"St{tb}", "tBg"], [f"St{tb}"])
        P.add(G, lambda e, a=cC(cosC), b=cJ(cosJ), v=v3(Cc): e.tensor_tensor(v, a, b, ALU.mult), [f"cosC{q}", f"cosJ{q}"], [f"Ct{tb}"])
        P.add(G, lambda e, a=cC(sinC), b=cJ(sinJ), v=v3(tB): e.tensor_tensor(v, a, b, ALU.mult), [f"sinC{q}", f"sinJ{q}"], ["tBg"])
        P.add(G, lambda e: e.tensor_tensor(Cc[:, W], Cc[:, W], tB[:, W], ALU.subtract), [f"Ct{tb}", "tBg"], [f"Ct{tb}"])
        for st in range(nst):
            n0 = st * 512
            nn = min(512, sn - n0)
            for ri, (dst, dk) in enumerate(((brs, f"brs{tb}_"), (bis, f"bis{tb}_"))):
                pbi = ri * 2 + tb
                pb = C.psum[pbi]
                for c in range(4):
                    P.add("tensor", lambda e, pb=pb, ri=ri, n0=n0, nn=nn, c=c: e.matmul(pb[:, 0:nn], B16[:, q * 2 + ri, c, :], u16[ub][:, c, n0:n0 + nn], start=(c == 0), stop=(c == 3)),
                          [f"B16_{q}_{ri}", f"u16_{ub}_{c}"], [f"psb{pbi}"])
                P.add("scalar", lambda e, pb=pb, dst=dst, n0=n0, nn=nn: e.copy(dst[:, n0:n0 + nn], pb[:, 0:nn]), [f"psb{pbi}"], [f"{dk}{st}"])
        assert nst == 1
        bk = [f"brs{tb}_{st}" for st in range(nst)]
        ik = [f"bis{tb}_{st}" for st in range(nst)]
        V(lambda e: e.tensor_tensor(tA[:, W], Cc[:, W], brs[:, W], ALU.mult), [f"Ct{tb}"] + bk, [kA_])
        yield
        V(lambda e: e.tensor_tensor(btr[:, W], S[:, W], bis[:, W], ALU.mult), [f"St{tb}"] + ik, [kbr])
        yield
        V(lambda e: e.tensor_tensor(btr[:, W], btr[:, W], tA[:, W], ALU.add), [kbr, kA_], [kbr])
        yield
        V(lambda e: e.tensor_tensor(tA[:, W], Cc[:, W], bis[:, W], ALU.mult), [f"Ct{tb}"] + ik, [kA_])
        yield
        V(lambda e: e.tensor_tensor(bti[:, W], S[:, W], brs[:, W], ALU.mult), [f"St{tb}"] + bk, [kbi])
        yield
        V(lambda e: e.tensor_tensor(bti[:, W], tA[:, W], bti[:, W], ALU.subtract), [kbi, kA_], [kbi])
        yield
        rcol = rr[:, q:q + 1].broadcast_to([128, sn])
        for (g_, bt_, gk, btk, ci) in ((gr, btr, kgr, kbr, 2 * q), (gi, bti, kgi, kbi, 2 * q + 1)):
            init = 0.0 if first else carry[:, ci:ci + 1]
            if d == 0:
                V(lambda e, g_=g_, bt_=bt_, init=init: e.tensor_tensor_scan(g_[:, W], rcol, bt_[:, W], init, ALU.mult, ALU.add), [btk, "rr", f"carry{ci}"], [gk])
                yield
                P.add("scalar", lambda e, g_=g_, ci=ci: e.copy(carry[:, ci:ci + 1], g_[:, sn - 1:sn]), [gk], [f"carry{ci}"])
            else:
                V(lambda e, g_=g_, bt_=bt_, init=init: e.tensor_tensor_scan(rv(g_, sn), rcol, rv(bt_, sn), init, ALU.mult, ALU.add), [btk, "rr", f"carry{ci}"], [gk])
                yield
                P.add("scalar", lambda e, g_=g_, ci=ci: e.copy(carry[:, ci:ci + 1], g_[:, 0:1]), [gk], [f"carry{ci}"])
        hr_, hi_ = hr16[hb], hi16[hb]
        V(lambda e: e.tensor_tensor(tA[:, W], Cc[:, W], gr[:, W], ALU.mult), [f"Ct{tb}", kgr], [kA_])
        yield
        V(lambda e: e.tensor_tensor(btr[:, W], S[:, W], gi[:, W], ALU.mult), [f"St{tb}", kgi], [kbr])
        yield
        V(lambda e: e.tensor_tensor(hr_[:, W], tA[:, W], btr[:, W], ALU.subtract), [kA_, kbr], [f"hr16_{hb}"])
        yield
        V(lambda e: e.tensor_tensor(tA[:, W], S[:, W], gr[:, W], ALU.mult), [f"St{tb}", kgr], [kA_])
        yield
        V(lambda e: e.tensor_tensor(bti[:, W], Cc[:, W], gi[:, W], ALU.mult), [f"Ct{tb}", kgi], [kbi])
        yield
        V(lambda e: e.tensor_tensor(hi_[:, W], tA[:, W], bti[:, W], ALU.add), [kA_, kbi], [f"hi16_{hb}"])
        yield
        for st in range(nst):
            n0 = st * 512
            nn = min(512, sn - n0)
            py = C.psum[yb_]
            P.add("tensor", lambda e, py=py, n0=n0, nn=nn: e.matmul(py[:, 0:nn], CR[:, q, :], hr_[:, n0:n0 + nn], start=(gp == 0), stop=False),
                  [f"CR_{q}", f"hr16_{hb}"], [f"psb{yb_}"])
            P.add("tensor", lambda e, py=py, n0=n0, nn=nn: e.matmul(py[:, 0:nn], CI[:, q, :], hi_[:, n0:n0 + nn], start=False, stop=(gp == 3)),
                  [f"CI_{q}", f"hi16_{hb}"], [f"psb{yb_}"])

    def seg_body(d, si, first, kind, s, ub, it0, seg_n):
        if kind == "ctx":
            sn, r, t0 = 256, 0, 0
            lay0 = 0 if d == 0 else 4 * TLAT
        else:
            sn, r, t0 = 512, s // 4, TCTX + 512 * (s % 4)
            lay0 = (TCTX + 512 * s) if d == 0 else 512 * s
        for c in range(4):
            usrc = uA_all[c][r * 128:(r + 1) * 128, t0:t0 + sn] if t0 < 1280 else uB_all[c][r * 128:(r + 1) * 128, t0 - 1280:t0 - 1280 + sn]
            ukey = f"uA_all{c}" if t0 < 1280 else f"uB_all{c}"
            C.load_cast(u16[ub][:, c, 0:sn], f"u16_{ub}_{c}", usrc, sn, cast_eng="scalar", reads=[ukey])
        nst = (sn + 511) // 512
        it = it0
        for gp0 in (0, 2):
            gens = []
            for gp in (gp0, gp0 + 1):
                tb = it % 2
                it += 1
                gens.append(gp_body(d, first, lay0, sn, ub, gp, tb, it % 2, seg_n))
            while gens:
                for g_ in list(gens):
                    try:
                        next(g_)
                    except StopIteration:
                        gens.remove(g_)
        ybuf = yb[ub]
        ybk = 4 + (seg_n % 2)
        if kind == "ctx":
            P.add("scalar", lambda e: e.copy(ybuf[:, 0, 0:256], C.psum[ybk][:, 0:256]), [f"psb{ybk}"], [f"yb{ub}"])
            P.dma(yc_loc[:, d * 256:(d + 1) * 256], ybuf[:, 0, 0:256], [f"yb{ub}"], [], f"st_yb{ub}")
        else:
            for c in range(4):
                P.add("scalar", lambda e, c=c: e.activation(ybuf[:, c, 0:512], C.psum[ybk][:, 0:512], AF.Identity, scale=oh[:, c:c + 1]),
                      [f"psb{ybk}", "oh"], [f"yb{ub}"])
            for c in range(4):
                P.dma(y_rs_in[d][c][(s // 4) * 128:(s // 4 + 1) * 128, 512 * (s % 4):512 * (s % 4) + 512], ybuf[:, c, 0:512], [f"yb{ub}"], [], f"st_yb{ub}_{c}")
        return it

    it = 0
    n = 0
    for d in range(2):
        order = [("ctx", 0)] + ([("lat", s) for s in range(16)] if d == 0 else [("lat", s) for s in range(15, -1, -1)])
        for si, (kind, s) in enumerate(order):
            it = seg_body(d, si, si == 0, kind, s, n % 2, it, n)
            n += 1


def phase_po(P, C, layer, xT, cols, Dr, tiles=TILES):
    l0 = layer == 0
    o_loc, wo_d = Dr["o_loc"], Dr["wo"][layer]
    wo16 = P.sb([128, 8, 1024], BF16, name="wo16")
    for k in range(8):
        C.load_cast(wo16[:, k, :], f"wo16_{k}", wo_d[k], 1024, cast_eng="vector")
    nch = 4 if l0 else 8
    ot = [P.sb([128, nch, 512], BF16, name=f"ot{i}") for i in range(2)]
    if l0:
        uA, uB, y_rs_out, yc_all = Dr["uA"], Dr["uB"], Dr["y_rs_out"], Dr["yc_all"]
        g0 = P.sb([128, 4, 512], BF16, name="glu0_sb")
        g1 = P.sb([128, 4, 512], BF16, name="glu1_sb")
        for k in range(4):
            C.load_cast(g0[:, k, :], f"g0_{k}", Dr["glu0"][k], 512, cast_eng="vector")
            C.load_cast(g1[:, k, :], f"g1_{k}", Dr["glu1"][k], 512, cast_eng="vector")
        dcol = P.sb([128, 4], F32, name="dcol_sb")
        C.load(dcol[:], "dcol", Dr["dcol"])
        ub = [P.sb([128, 512], F32, name=f"ub{i}") for i in range(2)]
        yfb = [P.sb([128, 512], F32, name=f"yfb{i}") for i in range(2)]
        yrb = [P.sb([128, 512], F32, name=f"yrb{i}") for i in range(2)]
        t1 = [P.sb([128, 512], F32, name=f"t1_{i}") for i in range(2)]
        gt = [P.sb([128, 4, 512], BF16, name=f"gt{i}") for i in range(2)]
        glt = [P.sb([128, 4, 512], BF16, name=f"glt{i}") for i in range(2)]
        sgb = [P.sb([128, 512], F32, name=f"sgb{i}") for i in range(2)]
    it = 0
    for ti, (t0, tn) in enumerate(tiles):
        v = "c" if t0 < TCTX else "x"
        tb = ti % 2
        P.dma(ot[tb][:, :, 0:tn], o_loc[:, 0:nch, t0:t0 + tn], [], [f"ot{tb}"], f"ld_ot{tb}")
        if l0:
            for c in range(4):
                b = it % 2
                it += 1
                if t0 < TCTX:
                    yf_src = yc_all[c * 128:(c + 1) * 128, 0:256]
                    yr_src = yc_all[c * 128:(c + 1) * 128, 256:512]
                else:
                    yf_src = y_rs_out[0][c][:, t0 - TCTX:t0 - TCTX + tn]
                    yr_src = y_rs_out[1][c][:, t0 - TCTX:t0 - TCTX + tn]
                u_src = uA[c][:, t0:t0 + tn] if t0 < 1280 else uB[c][:, t0 - 1280:t0 - 1280 + tn]
                P.dma(ub[b][:, 0:tn], u_src, [], [f"ub{b}"], f"ld_ub{b}")
                P.dma(yfb[b][:, 0:tn], yf_src, [], [f"yfb{b}"], f"ld_yfb{b}")
                P.dma(yrb[b][:, 0:tn], yr_src, [], [f"yrb{b}"], f"ld_yrb{b}")
                y, u_, yr_, tt = yfb[b], ub[b], yrb[b], t1[b]
                P.add("vector", lambda e, y=y, yr_=yr_, tn=tn: e.tensor_tensor(y[:, 0:tn], y[:, 0:tn], yr_[:, 0:tn], ALU.add), [f"yfb{b}", f"yrb{b}"], [f"yfb{b}"])
                P.add("vector", lambda e, y=y, u_=u_, c=c, tn=tn: e.scalar_tensor_tensor(y[:, 0:tn], u_[:, 0:tn], dcol[:, c:c + 1], y[:, 0:tn], ALU.mult, ALU.add),
                      [f"yfb{b}", f"ub{b}", "dcol"], [f"yfb{b}"])
                P.add("scalar", lambda e, tt=tt, y=y, tn=tn: e.activation(tt[:, 0:tn], y[:, 0:tn], AF.Square), [f"yfb{b}"], [f"t1_{b}"])
                P.add("vector", lambda e, tt=tt, tn=tn: e.tensor_scalar(tt[:, 0:tn], tt[:, 0:tn], 0.044715, 1.0, ALU.mult, ALU.add), [f"t1_{b}"], [f"t1_{b}"])
                P.add("vector", lambda e, tt=tt, y=y, tn=tn: e.tensor_tensor(tt[:, 0:tn], tt[:, 0:tn], y[:, 0:tn], ALU.mult), [f"t1_{b}", f"yfb{b}"], [f"t1_{b}"])
                P.add("scalar", lambda e, tt=tt, tn=tn: e.activation(tt[:, 0:tn], tt[:, 0:tn], AF.Sigmoid, scale=1.5957691216), [f"t1_{b}"], [f"t1_{b}"])
                P.add("vector", lambda e, tt=tt, y=y, c=c, tb=tb, tn=tn: e.tensor_tensor(gt[tb][:, c, 0:tn], tt[:, 0:tn], y[:, 0:tn], ALU.mult), [f"t1_{b}", f"yfb{b}"], [f"gt{tb}_{c}"])
            for oc in range(4):
                pb = oc % 2
                pa, pg = C.psum[pb], C.psum[2 + pb]
                for k in range(4):
                    P.add("tensor", lambda e, pa=pa, k=k, oc=oc, tb=tb, tn=tn: e.matmul(pa[:, 0:tn], g0[:, k, oc * 128:(oc + 1) * 128], gt[tb][:, k, 0:tn], start=(k == 0), stop=(k == 3)),
                          [f"g0_{k}", f"gt{tb}_{k}"], [f"psb{pb}"])
                for k in range(4):
                    P.add("tensor", lambda e, pg=pg, k=k, oc=oc, tb=tb, tn=tn: e.matmul(pg[:, 0:tn], g1[:, k, oc * 128:(oc + 1) * 128], gt[tb][:, k, 0:tn], start=(k == 0), stop=(k == 3)),
                          [f"g1_{k}", f"gt{tb}_{k}"], [f"psb{2 + pb}"])
                sg = sgb[pb]
                P.add("scalar", lambda e, sg=sg, pg=pg, tn=tn: e.activation(sg[:, 0:tn], pg[:, 0:tn], AF.Sigmoid), [f"psb{2 + pb}"], [f"sgb{pb}"])
                P.add("vector", lambda e, sg=sg, pa=pa, oc=oc, tb=tb, tn=tn: e.tensor_tensor(glt[tb][:, oc, 0:tn], sg[:, 0:tn], pa[:, 0:tn], ALU.mult), [f"sgb{pb}", f"psb{pb}"], [f"glt{tb}_{oc}"])
        for oc in range(8):
            pb = 4 + oc % 2
            po = C.psum[pb]
            srcs = ([(glt[tb][:, k, 0:tn], f"glt{tb}_{k}", k) for k in range(4)] if l0 else []) + \
                   [(ot[tb][:, k, 0:tn], f"ot{tb}", (4 + k) if l0 else k) for k in range(nch)]
            for n, (rhs, rk, kk) in enumerate(srcs):
                P.add("tensor", lambda e, po=po, rhs=rhs, kk=kk, oc=oc, tn=tn, n=n, ns=len(srcs): e.matmul(po[:, 0:tn], wo16[:, kk, oc * 128:(oc + 1) * 128], rhs, start=(n == 0), stop=(n == ns - 1)),
                      [f"wo16_{kk}", rk], [f"psb{pb}"])
            P.add("vector", lambda e, po=po, oc=oc, t0=t0, tn=tn, v=v: e.scalar_tensor_tensor(xT[:, oc, t0:t0 + tn], po[:, 0:tn], colsel(cols, v, "gate", 1, oc),
                                                                                             xT[:, oc, t0:t0 + tn], ALU.mult, ALU.add),
                  [f"psb{pb}", f"gate_{v}", f"x{oc}_{ti}"], [f"x{oc}_{ti}"])


GROUPS = [[0, 1, 2, 3], [4, 5, 6, 7]]


def build_fused(stop=999):
    nc = bass.Bass("TRN2", target_bir_lowering=False)
    ext = lambda name, shape, dtype=F32: nc.dram_tensor(name, list(shape), dtype, kind="ExternalInput").ap()
    scr = lambda name, shape, dtype=F32: nc.dram_tensor(name, list(shape), dtype).ap()
    xT_d = ext("xT", [128, 8, T])
    cT_d = ext("cT", [128, 16])
    modw_d = ext("modw", [36, 128, 8, 128])
    modb_d = ext("modb", [128, 36])
    normg_d = ext("normg", [128, 2, 24])
    wg_d = ext("wg", [4, FC, 128, 8, 128])
    wu_d = ext("wu", [4, FC, 128, 8, 128])
    wd_d = ext("wd", [4, FC, 128, 1024])
    win_d = [ext("win0", [10, 128, 8, 128]), ext("win1", [12, 128, 8, 128])]
    qkg_d = ext("qkg", [128, 2, 2])
    cos_d, sin_d = ext("cosT", [128, T]), ext("sinT", [128, T])
    rot_d, bones_d = ext("rot", [128, 128]), ext("bones", [128, 128])
    Dr = dict(
        Bl=ext("Bl", [2, 4, 2, 4, 128, 128]), Cl=ext("Cl", [2, 4, 2, 128, 128]), pcols=ext("pcols", [128, 3, 8]),
        iota_c=ext("iota_c", [128, 2, 66]), iota_j=ext("iota_j", [128, 2, 128]), onehot=ext("onehot", [128, 4]),
        dcol=ext("dcol", [128, 4]), glu0=ext("glu0", [4, 128, 512]), glu1=ext("glu1", [4, 128, 512]),
        wo=ext("wo", [2, 8, 128, 1024]), sink=ext("sink", [128, 16]),
        wmask=ext("wmask", [128, 4, 6, 512], BF16), emask=ext("emask", [128, 2, 4, 512], BF16),
    )
    out_d = nc.dram_tensor("outT", [128, 8, TLAT], F32, kind="ExternalOutput").ap()
    Dr.update(
        mod_loc=scr("mod_loc", [128, 72]), mod_all=scr("mod_all", [512, 72]),
        uA=[scr(f"uA{c}", [128, 1280]) for c in range(4)], uB=[scr(f"uB{c}", [128, 1024]) for c in range(4)],
        uA_all=[scr(f"uA_all{c}", [512, 1280]) for c in range(4)], uB_all=[scr(f"uB_all{c}", [512, 1024]) for c in range(4)],
        q_loc=scr("q_loc", [128, 8, T], BF16),
        k_loc=scr("k_loc", [128, 2, T]), kA=scr("kA", [128, 1280]), kB=scr("kB", [128, 1024]),
        kA_all=scr("kA_all", [512, 1280]), kB_all=scr("kB_all", [512, 1024]),
        v_loc=scr("v_loc", [T, 256]), vA=scr("vA", [1280, 128]), vB=scr("vB", [1024, 128]),
        vA_all=scr("vA_all", [5120, 128]), vB_all=scr("vB_all", [4096, 128]),
        y_rs_in=[[scr(f"y_rs_in{d}{c}", [512, TLAT]) for c in range(4)] for d in range(2)],
        y_rs_out=[[scr(f"y_rs_out{d}{c}", [128, TLAT]) for c in range(4)] for d in range(2)],
        yc_loc=scr("yc_loc", [128, 512]), yc_all=scr("yc_all", [512, 512]),
        o_loc=scr("o_loc", [128, 8, T], BF16),
        ek_loc=scr("ek_loc", [128, 512]), ek_all=scr("ek_all", [512, 512]),
        ev_loc=scr("ev_loc", [256, 256]), ev_all=scr("ev_all", [1024, 256]),
    )
    P = Prog(nc)
    xT = P.sb([128, 8, T], F32, name="xT_sb", persist=True)
    modall = P.sb([128, 2, 72, 2], F32, name="modall", persist=True)
    ng = P.sb([128, 2, 24], F32, name="normg_sb", persist=True)
    coltiles = {(l, v): (P.sb([128, 24], F32, name=f"gs_{v}{l}", persist=True), P.sb([128, 24], F32, name=f"gate_{v}{l}", persist=True))
                for l in range(2) for v in ("x", "c")}
    C = Ctx(P)

    c32 = P.sb([128, 16], F32, name="c32")
    c16 = P.sb([128, 16], BF16, name="c16")
    bt = P.sb([128, 36], F32, name="bt")
    res = P.sb([128, 36, 2], F32, name="res")
    w16 = [P.sb([128, 8, 128], BF16, name=f"w16_{i}") for i in range(2)]
    for c in range(KC):
        for ti, (t0, tn) in enumerate(TILES):
            P.dma(xT[:, c, t0:t0 + tn], xT_d[:, c, t0:t0 + tn], [], [f"x{c}_{ti}"], f"ld_x{(c * 5 + ti) % 4}")
    C.load(c32[:], "c32", cT_d)
    C.load(bt[:], "bt", modb_d)
    C.load(ng[:], "normg", normg_d)
    P.add("scalar", lambda e: e.activation(c16[:], c32[:], AF.Silu), ["c32"], ["c16"])
    for oc in range(36):
        sl = oc % 2
        C.load_cast(w16[sl][:].rearrange("p k m -> p (k m)"), f"w16_{sl}", modw_d[oc].rearrange("p k m -> p (k m)"), 1024, cast_eng="vector")
        ps = C.psum[oc % 2]
        for k in range(8):
            P.add("tensor", lambda e, ps=ps, k=k, sl=sl: e.matmul(ps[:, 0:2], w16[sl][:, k, :], c16[:, k * 2:(k + 1) * 2], start=(k == 0), stop=(k == 7)),
                  [f"w16_{sl}", "c16"], [f"psb{oc % 2}"])
        P.add("vector", lambda e, ps=ps, oc=oc: e.tensor_scalar(res[:, oc, :], ps[:, 0:2], bt[:, oc:oc + 1], None, ALU.add),
              [f"psb{oc % 2}", "bt"], ["res"])
    P.dma(Dr["mod_loc"], res[:].rearrange("p a b -> p (a b)"), ["res"], ["mod_loc"], "st_mod")
    P.coll("AllGather", ALU.bypass, GROUPS, Dr["mod_loc"], Dr["mod_all"], ["mod_loc"], ["mod_all"], "cc_mod")
    for l in range(2):
        for r2 in range(2):
            r = 2 * l + r2
            C.load(modall[:, l, 36 * r2:36 * (r2 + 1), :].rearrange("p a b -> p (a b)"), "modall", Dr["mod_all"][r * 128:(r + 1) * 128, :], reads=["mod_all"])
    cols = [make_cols(P, modall, ng, l, coltiles) for l in range(2)]
    P.end_phase()
    if stop == 1:
        P.close()
        return nc

    def phase_ffn(layer, f_idx, s, tiles=TILES):
        C.new_phase()
        K = alloc_common(P, C)
        hT = P.sb([128, 8, T], BF16, name="hT_sb")
        emit_norm(P, C, K, xT, hT, cols[layer], s, tiles=tiles)
        emit_ffn(P, C, K, xT, hT, cols[layer], s, wg_d[2 * layer + f_idx], wu_d[2 * layer + f_idx], wd_d[2 * layer + f_idx], tiles=tiles)
        return K, hT

    def phase_a(layer):
        n_u, n_q, n_k, vw = (4, 4, 1, 128) if layer == 0 else (0, 8, 2, 256)
        n_fm = n_u + n_q + n_k
        K, hT = phase_ffn(layer, 0, 0)
        emit_norm(P, C, K, xT, hT, cols[layer], 1)
        emit_consts(P, C, K, rot_d, bones_d)
        qkg = P.sb([128, 2, 2], F32, name="qkg_sb")
        C.load(qkg[:], "qkg", qkg_d)
        K["cst"] = [P.sb([128, 512], F32, name=f"cst{i}") for i in range(1)]
        K["snt"] = [P.sb([128, 512], F32, name=f"snt{i}") for i in range(1)]
        K["win16"] = [P.sb([128, 8, 128], BF16, name=f"win16_{i}") for i in range(2)]
        K["zq"] = [P.sb([128, 512], F32, name=f"zq{i}") for i in range(2)]
        K["zb"] = [P.sb([128, 512], BF16, name=f"zb{i}") for i in range(2)]
        K["ob32"] = [P.sb([128, 512], F32, name=f"ob32_{i}") for i in range(2)]
        K["ob16"] = [P.sb([128, 512], BF16, name=f"ob16_{i}") for i in range(2)]
        K["wv16"] = P.sb([128, 8, vw], BF16, name="wv16")
        K["vb"] = [P.sb([128, 256], F32, name="vb0")] * 2
        kinds = ["u"] * n_u + ["q"] * n_q + ["k"] * n_k
        def split_dst(a_, b_):
            return lambda t0, tn: (a_[:, t0:t0 + tn] if t0 < 1280 else b_[:, t0 - 1280:t0 - 1280 + tn])
        kdst = [split_dst(Dr["kA"], Dr["kB"])] if layer == 0 else [Dr["k_loc"][:, i, :] for i in range(2)]
        outs = [(split_dst(Dr["uA"][i], Dr["uB"][i]), F32) for i in range(n_u)] + [(Dr["q_loc"][:, i, :], BF16) for i in range(n_q)] + [(kd_, F32) for kd_ in kdst]
        if layer == 0:
            v_d = [Dr["vA"][tt * 128:(tt + 1) * 128, :] if tt < 10 else Dr["vB"][(tt - 10) * 128:(tt - 9) * 128, :] for tt in range(T // 128)]
        else:
            v_d = Dr["v_loc"].rearrange("(tt p) f -> tt p f", p=128)
        emit_inproj(P, C, K, hT, win_d[layer], n_fm, kinds, qkg[:, layer, :], cos_d, sin_d, outs, v_d, vw, n_fm)
        P.end_phase()

    phase_a(0)
    if stop == 2:
        P.close()
        return nc
    for c in range(4):
        P.coll("AllGather", ALU.bypass, GROUPS, Dr["uA"][c], Dr["uA_all"][c], [], [f"uA_all{c}"], "cc_u")
        P.coll("AllGather", ALU.bypass, GROUPS, Dr["uB"][c], Dr["uB_all"][c], [], [f"uB_all{c}"], "cc_u")
    P.coll("AllGather", ALU.bypass, GROUPS, Dr["kA"], Dr["kA_all"], [], ["kA_all"], "cc_k")
    P.coll("AllGather", ALU.bypass, GROUPS, Dr["kB"], Dr["kB_all"], [], ["kB_all"], "cc_k")
    P.coll("AllGather", ALU.bypass, GROUPS, Dr["vA"], Dr["vA_all"], [], ["vA_all"], "cc_v")
    P.coll("AllGather", ALU.bypass, GROUPS, Dr["vB"], Dr["vB_all"], [], ["vB_all"], "cc_v")
    C.new_phase()
    phase_s5(P, C, Dr)
    P.end_phase()
    if stop == 4:
        P.close()
        return nc
    for d in range(2):
        for c in range(4):
            P.coll("ReduceScatter", ALU.add, GROUPS, Dr["y_rs_in"][d][c], Dr["y_rs_out"][d][c], [], [f"y_rs_out{d}{c}"], "cc_y")
    P.coll("AllGather", ALU.bypass, GROUPS, Dr["yc_loc"], Dr["yc_all"], [], ["yc_all"], "cc_yc")
    C.new_phase()
    phase_att(P, C, 0, Dr)
    P.end_phase()
    if stop == 6:
        P.close()
        return nc
    C.new_phase()
    phase_po(P, C, 0, xT, cols[0], Dr)
    P.end_phase()
    if stop == 7:
        P.close()
        return nc
    phase_ffn(0, 1, 2)
    P.end_phase()
    if stop == 8:
        P.close()
        return nc
    phase_a(1)
    if stop == 9:
        P.close()
        return nc
    for c in range(2):
        for lh, col0 in ((0, TCTX), (1, T - 128)):
            P.dma(Dr["ek_loc"][:, c * 256 + lh * 128:c * 256 + (lh + 1) * 128], Dr["k_loc"][:, c, col0:col0 + 128], [], ["ek_loc"], f"cp_ek{c}{lh}")
    for lh, col0 in ((0, TCTX), (1, T - 128)):
        P.dma(Dr["ev_loc"][lh * 128:(lh + 1) * 128, :], Dr["v_loc"][col0:col0 + 128, :], [], ["ev_loc"], f"cp_ev{lh}")
    P.coll("AllGather", ALU.bypass, GROUPS, Dr["ek_loc"], Dr["ek_all"], ["ek_loc"], ["ek_all"], "cc_ek")
    P.coll("AllGather", ALU.bypass, GROUPS, Dr["ev_loc"], Dr["ev_all"], ["ev_loc"], ["ev_all"], "cc_ev")
    C.new_phase()
    phase_att(P, C, 1, Dr)
    P.end_phase()
    if stop == 11:
        P.close()
        return nc
    C.new_phase()
    phase_po(P, C, 1, xT, cols[1], Dr, tiles=TILES[1:])
    P.end_phase()
    if stop == 12:
        P.close()
        return nc
    phase_ffn(1, 1, 2, tiles=TILES[1:])
    for c in range(KC):
        for ti, (t0, tn) in enumerate(TILES[1:]):
            P.dma(out_d[:, c, t0 - TCTX:t0 - TCTX + tn], xT[:, c, t0:t0 + tn], [f"x{c}_{ti}"], [], f"st_x{(c * 5 + ti) % 4}")
    P.end_phase()
    if stop == 13:
        P.close()
        return nc
    P.close()
    return nc
def fm(x2d):
    t, f = x2d.shape
    return np.ascontiguousarray(x2d.T.reshape(f // 128, 128, t).transpose(1, 0, 2))


def unfm(a):
    p, c, t = a.shape
    return np.ascontiguousarray(a.transpose(1, 0, 2).reshape(c * 128, t).T)


def w_oc(W):
    k, n = W.shape
    return np.ascontiguousarray(W.reshape(k // 128, 128, n // 128, 128).transpose(2, 1, 0, 3))


def w_rows(W):
    k, n = W.shape
    return np.ascontiguousarray(W.reshape(k // 128, 128, n))


def rope_tables():
    rows = 8192 // 64
    r = np.repeat(np.arange(rows, dtype=np.float32), 64)
    col = np.tile(np.arange(64, dtype=np.float32), rows)
    inv = (10000.0 ** (-np.arange(16, dtype=np.float32) / 16)).astype(np.float32)
    ang = np.concatenate([r[:, None] * inv, col[:, None] * inv], axis=-1).astype(np.float32)
    cos = np.cos(ang).astype(np.float32).T
    sin = np.sin(ang).astype(np.float32).T
    idx = np.arange(128) % 32
    return cos[idx], sin[idx]


def const_mats():
    rot = np.zeros((128, 128), np.float32)
    for m in range(128):
        j = m % 64
        if j < 32:
            rot[m + 32, m] = -1.0
        else:
            rot[m - 32, m] = 1.0
    bones = np.zeros((128, 128), np.float32)
    bones[:64, :64] = 1.0
    bones[64:, 64:] = 1.0
    return rot, bones


def core_tokens(inp_x, ctx, i):
    b, q = i // 4, i % 4
    return np.concatenate([ctx[b], inp_x[b, q * TLAT:(q + 1) * TLAT]], axis=0)


def fused_inputs(inp):
    cos, sin = rope_tables()
    rot, bones = const_mats()
    W = np.concatenate([inp["mod_w"][0], inp["mod_w"][1]], axis=1)
    Bv = np.concatenate([inp["mod_b"][0], inp["mod_b"][1]], axis=0)
    Wl = W.reshape(8, 128, 144, 128).transpose(2, 1, 0, 3)
    Bl_mod = Bv.reshape(144, 128).T
    normg = np.ascontiguousarray(np.stack([inp["norm_g"][l].reshape(3, 8, 128).transpose(2, 0, 1).reshape(128, 24) for l in range(2)], axis=1))
    wg = np.stack([w_oc(inp["ffn_wg"][l, f]) for l in range(2) for f in range(2)])
    wu = np.stack([w_oc(inp["ffn_wu"][l, f]) for l in range(2) for f in range(2)])
    wd = np.stack([w_rows(inp["ffn_wd"][l, f]) for l in range(2) for f in range(2)])
    win0, win1 = w_oc(inp["ab_w_in"][0]), w_oc(inp["win_w_in"][0])
    qkg = np.ascontiguousarray(np.stack([inp["qk_norm"][l][:, np.arange(128) % 64].T for l in range(2)], axis=1))
    wo = np.stack([w_rows(inp["w_out"][l]) for l in range(2)])
    dcol = np.ascontiguousarray(inp["s5_d"][0].reshape(4, 128).T)
    glu0, glu1 = w_rows(inp["s5_glu_w"][0, 0]), w_rows(inp["s5_glu_w"][0, 1])
    sink = np.ascontiguousarray(np.broadcast_to(inp["win_sink"][0].reshape(1, 16), (128, 16))).astype(np.float32)
    iota_c = np.zeros((128, 2, 66), np.float32)
    iota_c[:, 0, :] = np.arange(66)
    iota_c[:, 1, :] = 65 - np.arange(66)
    iota_j = np.zeros((128, 2, 128), np.float32)
    iota_j[:, 0, :] = np.arange(128)
    iota_j[:, 1, :] = 127 - np.arange(128)
    one, zero = np.ones((1,), NPBF)[0], np.zeros((1,), NPBF)[0]
    kk = np.arange(128)[:, None]
    qq = np.arange(512)[None, :]
    wmask = np.zeros((128, 4, 6, 512), NPBF)
    for m in range(4):
        for r in range(6):
            ok = np.abs((4 * m + r - 1) * 128 + kk - (512 * m + qq)) <= 128
            wmask[:, m, r, :] = np.where(ok, one, zero)
    e = 0
    maps = []
    for i in range(NCORES):
        b, q = i // 4, i % 4
        cT = np.ascontiguousarray(np.stack([inp["c"][b], inp["c_ctx"]], axis=0).T.reshape(8, 128, 2).transpose(1, 0, 2).reshape(128, 16))
        cosT = np.concatenate([np.ones((128, TCTX), np.float32), cos[:, q * TLAT:(q + 1) * TLAT]], axis=1)
        sinT = np.concatenate([np.zeros((128, TCTX), np.float32), sin[:, q * TLAT:(q + 1) * TLAT]], axis=1)
        Bl = np.zeros((2, 4, 2, 4, 128, 128), np.float32)
        Cl = np.zeros((2, 4, 2, 128, 128), np.float32)
        pcols = np.zeros((128, 3, 8), np.float32)
        j = q
        for d in range(2):
            for gp in range(4):
                for gl in range(2):
                    g = 8 * j + 2 * gp + gl
                    ch0 = (2 * gp + gl) * 16
                    for ri, (bsrc, csrc) in enumerate(((inp["s5_b_re"], inp["s5_c_re"]), (inp["s5_b_im"], inp["s5_c_im"]))):
                        Bl[d, gp, ri, j, ch0:ch0 + 16, gl * 64:(gl + 1) * 64] = bsrc[e, d, g].T
                        Cl[d, gp, ri, gl * 64:(gl + 1) * 64, ch0:ch0 + 16] = csrc[e, d, g].T
                    pcols[gl * 64:(gl + 1) * 64, 0, d * 4 + gp] = inp["s5_lam_re"][e, d, g]
                    pcols[gl * 64:(gl + 1) * 64, 1, d * 4 + gp] = inp["s5_lam_im"][e, d, g]
                    pcols[gl * 64:(gl + 1) * 64, 2, d * 4 + gp] = inp["s5_log_step"][e, d, g]
        onehot = np.zeros((128, 4), np.float32)
        onehot[:, q] = 1.0
        emask = np.zeros((128, 2, 4, 512), NPBF)
        for r in range(4):
            if r == q - 1:
                emask[:, 0, r, :] = np.where(np.abs(-128 + kk - qq) <= 128, one, zero)
            if r == q + 1:
                emask[:, 1, r, :] = np.where(np.abs(2048 + kk - 1536 - qq) <= 128, one, zero)
        maps.append(dict(
            xT=fm(core_tokens(inp["x"], inp["ctx"], i)), cT=cT, modw=np.ascontiguousarray(Wl[36 * q:36 * q + 36]),
            modb=np.ascontiguousarray(Bl_mod[:, 36 * q:36 * q + 36]), normg=normg, wg=wg, wu=wu, wd=wd, win0=win0, win1=win1, qkg=qkg,
            cosT=np.ascontiguousarray(cosT), sinT=np.ascontiguousarray(sinT), rot=rot, bones=bones,
            Bl=Bl, Cl=Cl, pcols=pcols, iota_c=iota_c, iota_j=iota_j, onehot=onehot, dcol=dcol, glu0=glu0, glu1=glu1, wo=wo, sink=sink,
            wmask=wmask, emask=emask))
    return maps


def kernel(**inputs):
    inp = {k: np.asarray(v) for k, v in inputs.items()}
    nc = build_fused()
    res = run_bass_kernel_spmd(nc, fused_inputs(inp), core_ids=list(range(NCORES)))
    out = np.zeros((2, 4 * TLAT, D), np.float32)
    for i in range(NCORES):
        b, q = i // 4, i % 4
        out[b, q * TLAT:(q + 1) * TLAT] = unfm(np.asarray(res.results[i]["outT"]))
    return out
```
